# Optimizing a Trainium2 kernel written in Bass

```python
import math
import jax, jax.numpy as jnp
from jax import lax
import numpy as np

D_MODEL = 1024
BATCH = 8
SEQ = 4096
DEPTH = 4

EPS = 1e-6
ROPE_THETA = 10000.0
NEG_INF = -1e30
HEAD_DIM = 64
N_Q_HEADS = 8
N_KV_HEADS = 2
GQA_GROUP = N_Q_HEADS // N_KV_HEADS
WINDOW = 128
BLOCK = 128
ATTN_WIDTH = N_Q_HEADS * HEAD_DIM
KV_WIDTH = N_KV_HEADS * HEAD_DIM
SSM_CH = 16
SSM_GROUPS = 32
SSM_STATE = 64
SSM_WIDTH = SSM_GROUPS * SSM_CH
N_MEM = 256
X_HEADS = 4
X_HEAD_DIM = 128
X_WIDTH = X_HEADS * X_HEAD_DIM

MIX_WIDTH = ATTN_WIDTH + SSM_WIDTH + X_WIDTH
SPLIT_SIZES = (ATTN_WIDTH, KV_WIDTH, KV_WIDTH, ATTN_WIDTH, SSM_WIDTH, SSM_WIDTH, X_WIDTH, X_WIDTH)
IN_WIDTH = sum(SPLIT_SIZES)
SPLIT_POINTS = tuple(int(v) for v in np.cumsum(SPLIT_SIZES)[:-1])

kernel_name = "hymba_style_swa_s5_memxattn_trunk"


def _rms(x, g):
    xf = x.astype(jnp.float32)
    y = xf * lax.rsqrt(jnp.mean(xf * xf, axis=-1, keepdims=True) + EPS)
    return (y * g.astype(jnp.float32)).astype(x.dtype)


def _rope(x, pos):
    half = x.shape[-1] // 2
    inv = ROPE_THETA ** (-jnp.arange(half, dtype=jnp.float32) / half)
    ang = pos.astype(jnp.float32)[..., None] * inv
    cos = jnp.cos(ang)[:, :, None, :]
    sin = jnp.sin(ang)[:, :, None, :]
    xf = x.astype(jnp.float32)
    x1, x2 = xf[..., :half], xf[..., half:]
    out = jnp.concatenate([x1 * cos - x2 * sin, x2 * cos + x1 * sin], axis=-1)
    return out.astype(x.dtype)


def _sliding_window_attn(q, k, v, sinks):
    b, s = q.shape[0], q.shape[1]
    nb = s // BLOCK
    qb = q.reshape(b, nb, BLOCK, N_KV_HEADS, GQA_GROUP, HEAD_DIM)
    kb = k.reshape(b, nb, BLOCK, N_KV_HEADS, HEAD_DIM)
    vb = v.reshape(b, nb, BLOCK, N_KV_HEADS, HEAD_DIM)
    zk = jnp.zeros_like(kb[:, :1])
    kk = jnp.concatenate([jnp.concatenate([zk, kb[:, :-1]], axis=1), kb], axis=2)
    vv = jnp.concatenate([jnp.concatenate([zk, vb[:, :-1]], axis=1), vb], axis=2)
    scale = 1.0 / math.sqrt(HEAD_DIM)
    sc = jnp.einsum('bnqhgd,bnkhd->bnhgqk', qb, kk).astype(jnp.float32) * scale
    i = jnp.arange(BLOCK)[:, None]
    j = jnp.arange(2 * BLOCK)[None, :]
    band = (j >= i + BLOCK - WINDOW + 1) & (j <= i + BLOCK)
    first = (jnp.arange(nb) > 0)[:, None, None] | (j >= BLOCK)[None]
    valid = band[None] & first
    sc = jnp.where(valid[None, :, None, None], sc, NEG_INF)
    sink = sinks.astype(jnp.float32).reshape(N_KV_HEADS, GQA_GROUP)[None, None, :, :, None, None]
    m = jnp.maximum(jnp.max(sc, axis=-1, keepdims=True), sink)
    p = jnp.exp(sc - m)
    p = p / (jnp.sum(p, axis=-1, keepdims=True) + jnp.exp(sink - m))
    o = jnp.einsum('bnhgqk,bnkhd->bnqhgd', p.astype(v.dtype), vv)
    return o.reshape(b, s, ATTN_WIDTH)


def _ssm_scan_combine(e1, e2):
    a1r, a1i, b1r, b1i = e1
    a2r, a2i, b2r, b2i = e2
    return (a1r * a2r - a1i * a2i,
            a1r * a2i + a1i * a2r,
            a2r * b1r - a2i * b1i + b2r,
            a2r * b1i + a2i * b1r + b2i)


def _s5(u, lam_re, lam_im, log_dt, b_re, b_im, c_re, c_im, d_skip, w_glu, b_glu):
    b, s = u.shape[0], u.shape[1]
    f32 = jnp.float32
    uf = u.astype(f32).reshape(b, s, SSM_GROUPS, SSM_CH)
    lr, li = lam_re.astype(f32), lam_im.astype(f32)
    dt = jnp.exp(log_dt.astype(f32))[:, None]
    mag = jnp.exp(lr * dt)
    ar, ai = mag * jnp.cos(li * dt), mag * jnp.sin(li * dt)
    den = lr * lr + li * li
    fr = ((ar - 1.0) * lr + ai * li) / den
    fi = (ai * lr - (ar - 1.0) * li) / den
    br, bi = b_re.astype(f32), b_im.astype(f32)
    bbr = fr[..., None] * br - fi[..., None] * bi
    bbi = fr[..., None] * bi + fi[..., None] * br
    xr = jnp.einsum('bsgc,gpc->bsgp', uf, bbr)
    xi = jnp.einsum('bsgc,gpc->bsgp', uf, bbi)
    a_r = jnp.broadcast_to(ar[None, None], (1, s, SSM_GROUPS, SSM_STATE))
    a_i = jnp.broadcast_to(ai[None, None], (1, s, SSM_GROUPS, SSM_STATE))
    _, _, hr, hi = lax.associative_scan(_ssm_scan_combine, (a_r, a_i, xr, xi), axis=1)
    y = (jnp.einsum('bsgp,gcp->bsgc', hr, c_re.astype(f32))
         - jnp.einsum('bsgp,gcp->bsgc', hi, c_im.astype(f32)))
    y = (y + d_skip.astype(f32).reshape(SSM_GROUPS, SSM_CH) * uf).reshape(b, s, SSM_WIDTH)
    y = jax.nn.gelu(y)
    y = y * jax.nn.sigmoid(y @ w_glu.astype(f32) + b_glu.astype(f32))
    return y.astype(u.dtype)


def _mem_cross_attn(q, k, v):
    b, s = q.shape[0], q.shape[1]
    sc = jnp.einsum('bshd,bmhd->bhsm', q, k).astype(jnp.float32) / math.sqrt(X_HEAD_DIM)
    p = jax.nn.softmax(sc, axis=-1)
    o = jnp.einsum('bhsm,bmhd->bshd', p.astype(v.dtype), v)
    return o.reshape(b, s, X_WIDTH)


def setup_inputs(seed: int = 0) -> dict:
    key = jax.random.key(seed)
    ks = jax.random.split(key, 24)
    f32 = jnp.float32
    nrm = lambda k, shape, std: jax.random.normal(k, shape, f32) * std
    x = nrm(ks[0], (BATCH, SEQ, D_MODEL), 1.0)
    mem = nrm(ks[1], (BATCH, N_MEM, D_MODEL), 1.0)
    offset = jax.random.randint(ks[2], (BATCH, 1), 0, 4096, dtype=jnp.int32)
    positions = offset + jnp.arange(SEQ, dtype=jnp.int32)[None, :]
    norm_g = 1.0 + nrm(ks[3], (DEPTH, D_MODEL), 0.02)
    w_in = nrm(ks[4], (DEPTH, D_MODEL, IN_WIDTH), D_MODEL ** -0.5)
    q_norm_g = 1.0 + nrm(ks[5], (DEPTH, HEAD_DIM), 0.02)
    k_norm_g = 1.0 + nrm(ks[6], (DEPTH, HEAD_DIM), 0.02)
    sinks = nrm(ks[7], (DEPTH, N_Q_HEADS), 0.5)
    lam_re = -0.5 + nrm(ks[8], (DEPTH, SSM_GROUPS, SSM_STATE), 0.01)
    lam_im = (math.pi * jnp.arange(SSM_STATE, dtype=f32))[None, None, :] + nrm(ks[9], (DEPTH, SSM_GROUPS, SSM_STATE), 0.01)
    log_dt = jax.random.uniform(ks[10], (DEPTH, SSM_GROUPS), f32, math.log(1e-3), math.log(1e-1))
    b_re = nrm(ks[11], (DEPTH, SSM_GROUPS, SSM_STATE, SSM_CH), (2.0 * SSM_CH) ** -0.5)
    b_im = nrm(ks[12], (DEPTH, SSM_GROUPS, SSM_STATE, SSM_CH), (2.0 * SSM_CH) ** -0.5)
    c_re = nrm(ks[13], (DEPTH, SSM_GROUPS, SSM_CH, SSM_STATE), (2.0 * SSM_STATE) ** -0.5)
    c_im = nrm(ks[14], (DEPTH, SSM_GROUPS, SSM_CH, SSM_STATE), (2.0 * SSM_STATE) ** -0.5)
    d_skip = nrm(ks[15], (DEPTH, SSM_WIDTH), 1.0)
    w_glu = nrm(ks[16], (DEPTH, SSM_WIDTH, SSM_WIDTH), SSM_WIDTH ** -0.5)
    b_glu = nrm(ks[17], (DEPTH, SSM_WIDTH), 0.02)
    mem_norm_g = 1.0 + nrm(ks[18], (DEPTH, D_MODEL), 0.02)
    w_mem_kv = nrm(ks[19], (DEPTH, D_MODEL, 2 * X_WIDTH), D_MODEL ** -0.5)
    xq_norm_g = 1.0 + nrm(ks[20], (DEPTH, X_HEAD_DIM), 0.02)
    xk_norm_g = 1.0 + nrm(ks[21], (DEPTH, X_HEAD_DIM), 0.02)
    w_out = nrm(ks[22], (DEPTH, MIX_WIDTH, D_MODEL), (MIX_WIDTH * 2.0 * DEPTH) ** -0.5)
    return {"x": x, "mem": mem, "positions": positions, "norm_g": norm_g, "w_in": w_in,
            "q_norm_g": q_norm_g, "k_norm_g": k_norm_g, "sinks": sinks,
            "lam_re": lam_re, "lam_im": lam_im, "log_dt": log_dt,
            "b_re": b_re, "b_im": b_im, "c_re": c_re, "c_im": c_im, "d_skip": d_skip,
            "w_glu": w_glu, "b_glu": b_glu, "mem_norm_g": mem_norm_g, "w_mem_kv": w_mem_kv,
            "xq_norm_g": xq_norm_g, "xk_norm_g": xk_norm_g, "w_out": w_out}


def reference(x, mem, positions, norm_g, w_in, q_norm_g, k_norm_g, sinks, lam_re, lam_im, log_dt,
              b_re, b_im, c_re, c_im, d_skip, w_glu, b_glu, mem_norm_g, w_mem_kv,
              xq_norm_g, xk_norm_g, w_out):
    b, s = x.shape[0], x.shape[1]
    for l in range(DEPTH):
        h = _rms(x, norm_g[l])
        z = h @ w_in[l]
        aq, ak, av, ag, su, sg, xq, xg = jnp.split(z, SPLIT_POINTS, axis=-1)
        aq = _rope(_rms(aq.reshape(b, s, N_Q_HEADS, HEAD_DIM), q_norm_g[l]), positions)
        ak = _rope(_rms(ak.reshape(b, s, N_KV_HEADS, HEAD_DIM), k_norm_g[l]), positions)
        av = av.reshape(b, s, N_KV_HEADS, HEAD_DIM)
        out_a = _sliding_window_attn(aq, ak, av, sinks[l]) * jax.nn.silu(ag)
        out_b = _s5(su, lam_re[l], lam_im[l], log_dt[l], b_re[l], b_im[l], c_re[l], c_im[l],
                    d_skip[l], w_glu[l], b_glu[l]) * jax.nn.silu(sg)
        mkv = _rms(mem, mem_norm_g[l]) @ w_mem_kv[l]
        mk, mv = jnp.split(mkv, 2, axis=-1)
        mk = _rms(mk.reshape(b, N_MEM, X_HEADS, X_HEAD_DIM), xk_norm_g[l])
        mv = mv.reshape(b, N_MEM, X_HEADS, X_HEAD_DIM)
        xq = _rms(xq.reshape(b, s, X_HEADS, X_HEAD_DIM), xq_norm_g[l])
        out_c = _mem_cross_attn(xq, mk, mv) * jax.nn.silu(xg)
        y = jnp.concatenate([out_a, out_b, out_c], axis=-1) @ w_out[l]
        x = x + y.astype(x.dtype)
    return x
```

```python
import math
import numpy as np
from contextlib import ExitStack
import concourse.bass as bass
import concourse.mybir as mybir
from concourse.bass_utils import run_bass_kernel_spmd

F32 = mybir.dt.float32
BF16 = mybir.dt.bfloat16
I32 = mybir.dt.int32
ALU = mybir.AluOpType
AF = mybir.ActivationFunctionType
AX = mybir.AxisListType

D = 1024
S = 4096
NMEM = 256
INW = 3328
DEPTH = 4
C_AQ, C_AK, C_AV, C_AG, C_SU, C_SG, C_XQ, C_XG = 0, 512, 640, 768, 1280, 1792, 2304, 2816
EPS = 1e-6
SBT = 512
BPS = SBT // 128
NCH = SBT // 8
N_DMA_SEMS = 40
TWO_PI = 2.0 * math.pi
CW1 = 6.28125
CW2 = TWO_PI - 6.28125


class Trk:
    def __init__(self, nc, stack):
        self.nc = nc
        self.eng = {"pe": nc.tensor, "act": nc.scalar, "dve": nc.vector, "pool": nc.gpsimd, "sp": nc.sync}
        self.sem = {k: stack.enter_context(nc.semaphore("s_" + k)) for k in ("pe", "act", "dve", "pool")}
        self.cnt = {k: 0 for k in self.sem}
        self.dsem = [stack.enter_context(nc.semaphore("d%d" % i)) for i in range(N_DMA_SEMS)]
        self.dcnt = [0] * N_DMA_SEMS
        self.dnext = 0
        self.known = {k: {} for k in self.eng}
        self.lastw = {}
        self.reads = {}
        self.nwaits = 0
        self.nops = 0

    def _wait(self, e, tok):
        kind, key, val = tok
        if kind == "E" and key == e and val > self.cnt[e]:
            return
        kn = self.known[e]
        if kn.get((kind, key), 0) >= val:
            return
        kn[(kind, key)] = val
        sem = self.sem[key] if kind == "E" else self.dsem[key]
        self.eng[e].wait_ge(sem, val)
        self.nwaits += 1

    ALIAS = {"junk": "hb", "sq": "rt1", "xq_f": "rt2", "y2s": "rt2", "ysum": "rt2", "y2k": "rt1", "sig_t": "qr",
             "pX0": "pTm0", "pX1": "pTm1", "mix_c": "mix_a"}

    def _deps(self, e, r, w):
        r = [self.ALIAS.get(k, k) for k in r]
        w = [self.ALIAS.get(k, k) for k in w]
        for k in r:
            t = self.lastw.get(k)
            if t is not None:
                self._wait(e, t)
        for k in w:
            t = self.lastw.get(k)
            if t is not None:
                self._wait(e, t)
            for t in self.reads.get(k, ()):
                self._wait(e, t)

    def _commit(self, tok, r, w):
        r = [self.ALIAS.get(k, k) for k in r]
        w = [self.ALIAS.get(k, k) for k in w]
        for k in r:
            self.reads.setdefault(k, []).append(tok)
        for k in w:
            self.lastw[k] = tok
            self.reads[k] = []

    def op(self, e, fn, r=(), w=(), sig=True):
        self._deps(e, r, w)
        inst = fn(self.eng[e])
        self.nops += 1
        if sig:
            self.cnt[e] += 1
            inst.then_inc(self.sem[e], 1)
            tok = ("E", e, self.cnt[e])
        else:
            tok = ("E", e, self.cnt[e] + 1)
        self._commit(tok, r, w)

    def dma(self, out, in_, r=(), w=(), q="sp"):
        self._deps(q, r, w)
        i = self.dnext
        self.dnext = (self.dnext + 1) % N_DMA_SEMS
        if self.dcnt[i] > 0:
            self._wait(q, ("D", i, self.dcnt[i]))
        self.dcnt[i] += 16
        self.eng[q].dma_start(out=out, in_=in_).then_inc(self.dsem[i], 16)
        tok = ("D", i, self.dcnt[i])
        self.nops += 1
        self._commit(tok, r, w)

    def drain(self, e="sp"):
        for k in self.sem:
            if self.cnt[k] > 0:
                self._wait(e, ("E", k, self.cnt[k]))
        for i in range(N_DMA_SEMS):
            if self.dcnt[i] > 0:
                self._wait(e, ("D", i, self.dcnt[i]))

    def finish(self, keys, e="sp"):
        for k in keys:
            t = self.lastw.get(k)
            if t is not None:
                self._wait(e, t)


class StopBuild(Exception):
    pass


def build_nc(n_layers=DEPTH, n_sb=S // SBT, dbg=False, stop=None, wd=DEPTH):
    nc = bass.Bass("TRN2", target_bir_lowering=False)
    SEQ = n_sb * SBT
    di = lambda n, s, dt=F32: nc.dram_tensor(n, s, dt, kind="ExternalInput").ap()
    x_d = di("x", [SEQ, D]); mem_d = di("mem", [NMEM, D]); pos_d = di("positions", [SEQ], I32)
    norm_g_d = di("norm_g", [wd, D]); w_in_d = di("w_in", [wd, D, INW])
    qg_d = di("q_norm_g", [wd, 64]); kg_d = di("k_norm_g", [wd, 64]); sinks_d = di("sinks", [wd, 8])
    lre_d = di("lam_re", [wd, 32, 64]); lim_d = di("lam_im", [wd, 32, 64]); ldt_d = di("log_dt", [wd, 32])
    bre_d = di("b_re", [wd, 32, 64, 16]); bim_d = di("b_im", [wd, 32, 64, 16])
    cre_d = di("c_re", [wd, 32, 16, 64]); cim_d = di("c_im", [wd, 32, 16, 64])
    dskip_d = di("d_skip", [wd, 512]); wglu_d = di("w_glu", [wd, 512, 512]); bglu_d = di("b_glu", [wd, 512])
    mng_d = di("mem_norm_g", [wd, D]); wkv_d = di("w_mem_kv", [wd, D, D])
    xqg_d = di("xq_norm_g", [wd, 128]); xkg_d = di("xk_norm_g", [wd, 128]); wout_d = di("w_out", [wd, 1536, D])
    out_d = nc.dram_tensor("out", [SEQ, D], F32, kind="ExternalOutput").ap()
    NB = SEQ // 128

    with ExitStack() as st:
        T = Trk(nc, st)
        sbt = lambda name, shape, dt: st.enter_context(nc.sbuf_tensor(name, shape, dt))
        pst = lambda name, shape, dt: st.enter_context(nc.psum_tensor(name, shape, dt))
        dve = lambda fn, r, w: T.op("dve", fn, r=r, w=w)
        act = lambda fn, r, w: T.op("act", fn, r=r, w=w)
        pool = lambda fn, r, w: T.op("pool", fn, r=r, w=w)

        def mm(out, lhsT, rhs, start, stop, r, w, sig=None, **kw):
            T.op("pe", lambda e: e.matmul(out, lhsT=lhsT, rhs=rhs, start=start, stop=stop, **kw), r=r, w=w,
                 sig=(stop if sig is None else sig))

        def tr(out, in_, ident, r, w, sig=True):
            T.op("pe", lambda e: e.transpose(out=out, in_=in_, identity=ident), r=r, w=w, sig=sig)

        ps_t = pst("ps_t", [128, 8, 128], BF16)
        ps_z = pst("ps_z", [128, 512], F32)
        ps_s = [pst("ps_s0", [128, 512], F32), pst("ps_s1", [128, 512], F32)]
        ps_f = pst("ps_f", [128, 512], F32)
        ps_y = pst("ps_y", [128, 512], F32)
        ps_y2 = pst("ps_y2", [128, 512], F32)
        ps_zs = pst("ps_zs", [128, 512], F32)

        Wsb = sbt("Wsb", [128, 8, INW], BF16)
        Wout = sbt("Wout", [128, 12, D], BF16)
        Wglu = sbt("Wglu", [128, 4, 512], BF16)
        Kstrip = sbt("Kstrip", [128, 4, 2, 15, 16], BF16)
        Wssm = sbt("Wssm", [128, 4, 8, 2, 128], BF16)
        Vf = sbt("Vf", [128, 16, 2, 256], BF16)
        mcT = sbt("mcT", [128, 16, NCH], F32)
        msT = sbt("msT", [128, 16, NCH], F32)
        Hb = sbt("Hb", [128, 16, 2, NCH], BF16)
        carry = sbt("carry", [128, 16, 2], F32)
        magA = sbt("magA", [128, 16], F32)
        nms1 = sbt("nms1", [128, 16], F32)
        mkT = sbt("mkT", [128, 4, NMEM], BF16)
        mv_aug = sbt("mv_aug", [128, 2, 4, 129], BF16)
        ident_b = sbt("ident_b", [128, 128], BF16)
        ident_f = sbt("ident_f", [128, 128], F32)
        mask_cur = sbt("mask_cur", [128, 4, 128], BF16)
        mask_prev = sbt("mask_prev", [128, 4, 128], BF16)
        dmask = sbt("dmask", [128, 32], F32)
        cosT = sbt("cosT", [128, NB, 32], F32)
        sinT = sbt("sinT", [128, NB, 32], F32)
        g10 = sbt("g10", [128, 10, 64], F32)
        gq_bc = sbt("gq_bc", [128, 64], F32)
        gk_bc = sbt("gk_bc", [128, 64], F32)
        esink = sbt("esink", [128, 8], F32)
        gxk_bc = sbt("gxk_bc", [128, 128], F32)
        gxq_col = sbt("gxq_col", [128, 1], F32)
        bglu_col = sbt("bglu_col", [128, 4], F32)
        d_col = sbt("d_col", [128, 4], F32)
        g_col = sbt("g_col", [128, 8], F32)
        gm_col = sbt("gm_col", [128, 8], F32)
        ldT = sbt("ldT", [32, 128], F32)
        nvec = sbt("nvec", [128, 9], F32)
        kvec = sbt("kvec", [128, NCH], F32)
        dummy = sbt("dummy_t", [128, 4], F32)

        ARW = 14600
        arena = sbt("arena", [128, ARW], F32)
        aoff = {"main": 0, "setup": 0, "setupB": 0}
        akeys = {"main": [], "setup": [], "setupB": []}

        def carve(phase, name, shape, dt):
            n = int(np.prod(shape))
            words = n if dt in (F32, I32) else (n + 1) // 2
            o = aoff[phase]
            aoff[phase] = o + words
            assert aoff[phase] <= ARW, (phase, name, aoff[phase])
            v = arena[:, o:o + words]
            if dt != F32:
                v = v.bitcast(dt)
                if dt == BF16 and n % 2:
                    v = v[:, 0:n]
            if len(shape) > 1:
                names = " ".join("d%d" % i for i in range(len(shape)))
                v = v.rearrange("p (%s) -> p %s" % (names, names), **{"d%d" % i: shape[i] for i in range(1, len(shape))})
            if name not in akeys[phase]:
                akeys[phase].append(name)
            return v

        M = lambda name, shape, dt: carve("main", name, shape, dt)
        U = lambda name, shape, dt: carve("setup", name, shape, dt)
        UB = lambda name, shape, dt: carve("setupB", name, shape, dt)

        uT = M("uT", [4, SBT], BF16)
        gT = M("gT", [4, SBT], BF16)
        mixT = M("mixT", [12, SBT], BF16)
        xt = [M("xt0", [D], F32), M("xt1", [D], F32)]
        xr = [M("xr0", [D], F32)]
        hb = M("hb", [D], BF16)
        junk = hb
        hT = M("hT", [8, 128], BF16)
        st1 = M("st1", [16], F32)
        qk = M("qk", [10, 64], F32)
        rt1 = M("rt1", [10, 64], F32)
        rt2 = M("rt2", [10, 64], F32)
        sq = rt1
        qr = M("qr", [10, 64], BF16)
        qT = M("qT", [4, 128], BF16)
        kT = [M("kT0", [128], BF16), M("kT1", [128], BF16)]
        vaug = [M("vaug0", [2, 65], BF16), M("vaug1", [2, 65], BF16)]
        pTm = [M("pTm0", [512], BF16), M("pTm1", [512], BF16)]
        pX = pTm
        gate_a = M("gate_a", [512], BF16)
        gate_x = M("gate_x", [512], BF16)
        mix_a = M("mix_a", [512], BF16)
        mix_c = mix_a
        xq_f = rt2.rearrange("p h d -> p (h d)")[:, 0:512].rearrange("p (h d) -> p h d", h=4)
        xq_b = M("xq_b", [4, 128], BF16)
        xqT = M("xqT", [4, 128], BF16)
        den = M("den", [8], F32)
        zt = M("zt", [4, NCH], F32)
        sct = M("sct", [4, NCH], F32)
        Gs = M("Gs", [2, NCH], F32)
        Hf = M("Hf", [2, NCH], F32)
        init2 = M("init2", [2], F32)
        y2s = rt2.rearrange("p h d -> p (h d)")[:, 0:256]
        ysum = rt2.rearrange("p h d -> p (h d)")[:, 256:512]
        y2k = rt1.rearrange("p h d -> p (h d)")[:, 0:512].bitcast(BF16).rearrange("p (t c) -> p t c", t=8)
        sig_t = qr.rearrange("p h d -> p (h d)")[:, 0:512]
        print("arena main words", aoff["main"])

        def load_T(dst, src, n, wkey):
            T.dma(ldT[0:n, :], src, w=["ldT"])
            tr(ps_z[:, 0:n], ldT[0:n, :], ident_f[0:n, 0:n], ["ldT", "ident_f"], ["ps_z"])
            dve(lambda e: e.tensor_copy(out=dst, in_=ps_z[:, 0:n]), ["ps_z"], [wkey])

        def sin_of(dst, src, n, shift, rk, wk, tmp=None):
            for c0 in range(0, n, 256):
                c1 = min(n, c0 + 256)
                m = c1 - c0
                ki, kf, yy = s_ki[:, 0:m], s_kf[:, 0:m], s_y[:, 0:m]
                sr = src[:, c0:c1]
                dve(lambda e: e.tensor_scalar(out=ki, in0=sr, scalar1=float(shift), scalar2=float(1.0 / TWO_PI), op0=ALU.add, op1=ALU.mult),
                    rk, ["sr_ki"])
                dve(lambda e: e.tensor_copy(out=kf, in_=ki), ["sr_ki"], ["sr_kf"])
                dve(lambda e: e.tensor_scalar(out=yy, in0=sr, scalar1=float(shift), scalar2=None, op0=ALU.add), rk, ["sr_y"])
                dve(lambda e: e.scalar_tensor_tensor(out=yy, in0=kf, scalar=float(-CW1), in1=yy, op0=ALU.mult, op1=ALU.add),
                    ["sr_kf", "sr_y"], ["sr_y"])
                dve(lambda e: e.scalar_tensor_tensor(out=yy, in0=kf, scalar=float(-CW2), in1=yy, op0=ALU.mult, op1=ALU.add),
                    ["sr_kf", "sr_y"], ["sr_y"])
                dve(lambda e: e.tensor_scalar(out=yy, in0=yy, scalar1=float(math.pi), scalar2=float(-math.pi), op0=ALU.min, op1=ALU.max),
                    ["sr_y"], ["sr_y"])
                act(lambda e: e.activation(out=dst[:, c0:c1], in_=yy, func=AF.Sin), ["sr_y"], wk)

        def rsqrt_of(dst, src, scale, rk, wk):
            dve(lambda e: e.tensor_scalar(out=dst, in0=src, scalar1=float(scale), scalar2=float(EPS), op0=ALU.mult, op1=ALU.add), rk, wk)
            act(lambda e: e.activation(out=dst, in_=dst, func=AF.Sqrt), wk, wk)
            dve(lambda e: e.reciprocal(out=dst, in_=dst), wk, wk)

        def join(keys, e="dve"):
            T.op(e, lambda en: en.memset(dummy[:, 0:1], 0.0), r=[], w=list(keys) + ["dummy"])

        s_ki = U("sr_ki", [256], I32); s_kf = U("sr_kf", [256], F32); s_y = U("sr_y", [256], F32)
        s_ang = U("s_ang", [256], F32)
        UB("sr_ki", [256], I32); UB("sr_kf", [256], F32); UB("sr_y", [256], F32); UB("s_ang", [256], F32)
        ones_f = U("ones_f", [128], F32)
        stage = [U("stage0", [1024], F32), U("stage1", [1024], F32)]
        s_posi = U("s_posi", [128], I32); s_posf = U("s_posf", [128], F32)
        s_posT = U("s_posT", [32], F32); s_inv = U("s_inv", [32], F32)
        s_memx = U("s_memx", [D], F32); s_memn = U("s_memn", [D], BF16); s_memT = U("s_memT", [8, NMEM], BF16)
        s_mkf = U("s_mkf", [4, 128], F32); s_mkb = U("s_mkb", [4, 128], BF16)
        s_smA = U("s_smA", [16], F32)
        s_sm = UB("s_sm", [16, 12], F32)
        s_e9 = UB("s_e9", [16, 9], F32); s_a9 = UB("s_a9", [16, 9], F32); s_mag9 = UB("s_mag9", [16, 9], F32)
        s_cos9 = UB("s_cos9", [16, 9], F32); s_sin9 = UB("s_sin9", [16, 9], F32)
        s_ar9 = UB("s_ar9", [16, 9], F32); s_ai9 = UB("s_ai9", [16, 9], F32)
        s_Br = UB("s_Br", [16, 16], F32); s_Bi = UB("s_Bi", [16, 16], F32)
        s_bbr = UB("s_bbr", [16, 16], F32); s_bbi = UB("s_bbi", [16, 16], F32); s_bt = UB("s_bt", [16, 16], F32)
        s_Cst = UB("s_Cst", [2, 128], F32)
        s_Cr = UB("s_Cr", [16, 16], F32); s_Ci = UB("s_Ci", [16, 16], F32)
        s_CAr = UB("s_CAr", [4, 9, 16], F32); s_CAi = UB("s_CAi", [4, 9, 16], F32); s_CAt = UB("s_CAt", [4, 9, 16], F32)
        s_CAzr = UB("s_CAzr", [4, 2, 8, 16], F32); s_CAzi = UB("s_CAzi", [4, 2, 8, 16], F32)
        s_Bzr = UB("s_Bzr", [4, 128], F32); s_Bzi = UB("s_Bzi", [4, 128], F32)
        s_Kt = UB("s_Kt", [2, 8, 16], F32); s_Kd = UB("s_Kd", [32], F32)
        s_Xr = UB("s_Xr", [4, 8, 16], F32); s_Xi = UB("s_Xi", [4, 8, 16], F32); s_Xt = UB("s_Xt", [4, 8, 16], F32)
        s_Xzr = UB("s_Xzr", [8, 4, 32], BF16); s_Xzi = UB("s_Xzi", [8, 4, 32], BF16)
        s_lg2 = UB("s_lg2", [2], F32)
        print("arena setup words", aoff["setup"], aoff["setupB"])
        ALLK = list(dict.fromkeys(akeys["main"] + akeys["setup"] + akeys["setupB"]))

        pool(lambda e: e.memset(ones_f, 1.0), [], ["ones_f"])
        pool(lambda e: e.memset(dummy[:], 0.0), [], ["dummy"])
        pool(lambda e: e.affine_select(out=ident_f[:], in_=ones_f, pattern=[[-1, 128]], compare_op=ALU.is_equal, fill=0.0,
                                       base=0, channel_multiplier=1), ["ones_f"], ["ident_f"])
        pool(lambda e: e.affine_select(out=ident_b[:], in_=ones_f, pattern=[[-1, 128]], compare_op=ALU.is_equal, fill=0.0,
                                       base=0, channel_multiplier=1), ["ones_f"], ["ident_b"])
        for j in range(4):
            pool(lambda e: e.affine_select(out=mask_cur[:, j, :], in_=ones_f, pattern=[[1, 128]], compare_op=ALU.is_ge, fill=0.0,
                                           base=0, channel_multiplier=-1), ["ones_f"], ["mask_cur"])
            pool(lambda e: e.affine_select(out=mask_prev[:, j, :], in_=ones_f, pattern=[[-1, 128]], compare_op=ALU.is_ge, fill=0.0,
                                           base=-1, channel_multiplier=1), ["ones_f"], ["mask_prev"])
        pool(lambda e: e.tensor_tensor(out=dmask[:], in0=ident_f[:, 0:32], in1=ident_f[:, 32:64], op=ALU.add), ["ident_f"], ["dmask"])
        pool(lambda e: e.tensor_tensor(out=dmask[:], in0=dmask[:], in1=ident_f[:, 64:96], op=ALU.add), ["ident_f", "dmask"], ["dmask"])
        pool(lambda e: e.tensor_tensor(out=dmask[:], in0=dmask[:], in1=ident_f[:, 96:128], op=ALU.add), ["ident_f", "dmask"], ["dmask"])
        pool(lambda e: e.iota(nvec[:], pattern=[[1, 9]], base=0, channel_multiplier=0, allow_small_or_imprecise_dtypes=True), [], ["nvec"])
        pool(lambda e: e.iota(kvec[:], pattern=[[1, NCH]], base=0, channel_multiplier=0, allow_small_or_imprecise_dtypes=True), [], ["kvec"])
        pool(lambda e: e.memset(mv_aug[:], 1.0), [], ["mv_aug"])
        pool(lambda e: e.memset(Kstrip[:], 0.0), [], ["Kstrip"])

        for bb in range(0, NB, 32):
            nbk = min(32, NB - bb)
            T.dma(s_posi[0:nbk, :], pos_d[bb * 128:(bb + nbk) * 128].rearrange("(b p) -> b p", p=128), w=["s_posi"])
            dve(lambda e: e.tensor_copy(out=s_posf[0:nbk, :], in_=s_posi[0:nbk, :]), ["s_posi"], ["s_posf"])
            tr(ps_z[:, 0:nbk], s_posf[0:nbk, :], ident_f[0:nbk, 0:nbk], ["s_posf", "ident_f"], ["ps_z"])
            dve(lambda e: e.tensor_copy(out=s_posT[:, 0:nbk], in_=ps_z[:, 0:nbk]), ["ps_z"], ["s_posT"])
        pool(lambda e: e.iota(s_inv, pattern=[[1, 32]], base=0, channel_multiplier=0, allow_small_or_imprecise_dtypes=True), [], ["s_inv"])
        act(lambda e: e.activation(out=s_inv, in_=s_inv, func=AF.Exp, scale=float(-math.log(10000.0) / 32.0)), ["s_inv"], ["s_inv"])
        for b0 in range(0, NB, 8):
            nb8 = min(8, NB - b0)
            ang3 = s_ang[:, 0:nb8 * 32].rearrange("p (b j) -> p b j", j=32)
            dve(lambda e: e.tensor_tensor(out=ang3, in0=s_posT[:, b0:b0 + nb8, None].to_broadcast([128, nb8, 32]),
                                          in1=s_inv[:, None, :].to_broadcast([128, nb8, 32]), op=ALU.mult), ["s_posT", "s_inv"], ["s_ang"])
            sin_of(sinT[:, b0:b0 + nb8, :].rearrange("p b j -> p (b j)"), s_ang[:, 0:nb8 * 32], nb8 * 32, 0.0, ["s_ang"], ["sinT"])
            sin_of(cosT[:, b0:b0 + nb8, :].rearrange("p b j -> p (b j)"), s_ang[:, 0:nb8 * 32], nb8 * 32, math.pi / 2, ["s_ang"], ["cosT"])

        def mark(name):
            if stop == name:
                raise StopBuild()

        def layer_setup(l):
            join(ALLK)
            load_T(g_col[:], norm_g_d[l].rearrange("(k p) -> k p", p=128), 8, "g_col")
            load_T(gm_col[:], mng_d[l].rearrange("(k p) -> k p", p=128), 8, "gm_col")
            load_T(bglu_col[:], bglu_d[l].rearrange("(k p) -> k p", p=128), 4, "bglu_col")
            load_T(d_col[:], dskip_d[l].rearrange("(k p) -> k p", p=128), 4, "d_col")
            load_T(gxq_col[:], xqg_d[l].rearrange("(k p) -> k p", p=128), 1, "gxq_col")
            dve(lambda e: e.tensor_scalar(out=gxq_col[:], in0=gxq_col[:], scalar1=float(1.0 / math.sqrt(128.0)), scalar2=None, op0=ALU.mult),
                ["gxq_col"], ["gxq_col"])
            T.dma(gq_bc[:], qg_d[l].partition_broadcast(128), w=["gq_bc"])
            T.dma(gk_bc[:], kg_d[l].partition_broadcast(128), w=["gk_bc"])
            T.dma(gxk_bc[:], xkg_d[l].partition_broadcast(128), w=["gxk_bc"])
            T.dma(esink[:], sinks_d[l].partition_broadcast(128), w=["esink"])
            act(lambda e: e.activation(out=esink[:], in_=esink[:], func=AF.Exp), ["esink"], ["esink"])
            dve(lambda e: e.tensor_copy(out=g10[:, 0:8, :], in_=gq_bc[:, None, :].to_broadcast([128, 8, 64])), ["gq_bc"], ["g10"])
            dve(lambda e: e.tensor_copy(out=g10[:, 8:10, :], in_=gk_bc[:, None, :].to_broadcast([128, 2, 64])), ["gk_bc", "g10"], ["g10"])

            mark('vecs')
            for mt in range(2):
                T.dma(s_memx, mem_d[mt * 128:(mt + 1) * 128, :], w=["s_memx"])
                act(lambda e: e.activation(out=s_memn, in_=s_memx, func=AF.Square, accum_out=s_smA[:, 0:1]), ["s_memx"], ["s_memn", "s_smA"])
                rsqrt_of(s_smA[:, 1:2], s_smA[:, 0:1], 1.0 / D, ["s_smA"], ["s_smA"])
                act(lambda e: e.activation(out=s_memn, in_=s_memx, func=AF.Copy, scale=s_smA[:, 1:2]), ["s_memx", "s_smA"], ["s_memn"])
                for kc in range(8):
                    tr(ps_t[:, kc, :], s_memn[:, kc * 128:(kc + 1) * 128], ident_b[:], ["s_memn", "ident_b"], ["ps_t"], sig=(kc == 7))
                dve(lambda e: e.tensor_copy(out=s_memT[:, :, mt * 128:(mt + 1) * 128], in_=ps_t[:]), ["ps_t"], ["s_memT"])
            Wkv = Wsb[:, :, 0:D]
            for kc in range(8):
                sg_ = stage[kc % 2]
                T.dma(sg_, wkv_d[l, kc * 128:(kc + 1) * 128, :], w=["stage%d" % (kc % 2)])
                eng = "act" if kc % 2 == 0 else "pool"
                if eng == "act":
                    act(lambda e: e.activation(out=Wkv[:, kc, :], in_=sg_, func=AF.Copy, scale=gm_col[:, kc:kc + 1]),
                        ["stage%d" % (kc % 2), "gm_col"], ["Wsb"])
                else:
                    pool(lambda e: e.tensor_scalar(out=Wkv[:, kc, :], in0=sg_, scalar1=gm_col[:, kc:kc + 1], scalar2=None, op0=ALU.mult),
                         ["stage%d" % (kc % 2), "gm_col"], ["Wsb"])
            for mt in range(2):
                for n in range(2):
                    for kc in range(8):
                        mm(ps_z[:], s_memT[:, kc, mt * 128:(mt + 1) * 128], Wkv[:, kc, n * 512:(n + 1) * 512], kc == 0, kc == 7,
                           ["s_memT", "Wsb"], ["ps_z"])
                    if n == 0:
                        act(lambda e: e.copy(out=s_mkf.rearrange("p h d -> p (h d)"), in_=ps_z[:]), ["ps_z"], ["s_mkf"])
                        dve(lambda e: e.tensor_tensor(out=s_memx[:, 0:512], in0=s_mkf.rearrange("p h d -> p (h d)"),
                                                      in1=s_mkf.rearrange("p h d -> p (h d)"), op=ALU.mult), ["s_mkf"], ["s_memx"])
                        dve(lambda e: e.tensor_reduce(out=s_smA[:, 4:8], in_=s_memx[:, 0:512].rearrange("p (h d) -> p h d", h=4),
                                                      axis=AX.X, op=ALU.add), ["s_memx"], ["s_smA"])
                        rsqrt_of(s_smA[:, 8:12], s_smA[:, 4:8], 1.0 / 128, ["s_smA"], ["s_smA"])
                        dve(lambda e: e.tensor_tensor(out=s_mkf, in0=s_mkf, in1=s_smA[:, 8:12, None].to_broadcast([128, 4, 128]), op=ALU.mult),
                            ["s_mkf", "s_smA"], ["s_mkf"])
                        dve(lambda e: e.tensor_tensor(out=s_mkb, in0=s_mkf, in1=gxk_bc[:, None, :].to_broadcast([128, 4, 128]), op=ALU.mult),
                            ["s_mkf", "gxk_bc"], ["s_mkb"])
                        for h in range(4):
                            tr(ps_t[:, h, :], s_mkb[:, h, :], ident_b[:], ["s_mkb", "ident_b"], ["ps_t"], sig=(h == 3))
                        dve(lambda e: e.tensor_scalar(out=mkT[:, :, mt * 128:(mt + 1) * 128], in0=ps_t[:, 0:4, :], scalar1=gxq_col[:, 0:1],
                                                      scalar2=None, op0=ALU.mult), ["ps_t", "gxq_col"], ["mkT"])
                    else:
                        act(lambda e: e.copy(out=mv_aug[:, mt, :, 0:128], in_=ps_z[:].rearrange("p (h d) -> p h d", h=4)), ["ps_z"], ["mv_aug"])

            mark('memkv')
            cnt = 0
            for kc in range(8):
                for (c0, c1) in ((0, 1024), (1024, 2048), (2048, 3072), (3072, INW)):
                    sg_ = stage[cnt % 2]; sk = "stage%d" % (cnt % 2)
                    T.dma(sg_[:, 0:c1 - c0], w_in_d[l, kc * 128:(kc + 1) * 128, c0:c1], w=[sk])
                    if cnt % 2 == 0:
                        act(lambda e: e.activation(out=Wsb[:, kc, c0:c1], in_=sg_[:, 0:c1 - c0], func=AF.Copy, scale=g_col[:, kc:kc + 1]),
                            [sk, "g_col"], ["Wsb"])
                    else:
                        pool(lambda e: e.tensor_scalar(out=Wsb[:, kc, c0:c1], in0=sg_[:, 0:c1 - c0], scalar1=g_col[:, kc:kc + 1], scalar2=None,
                                                       op0=ALU.mult), [sk, "g_col"], ["Wsb"])
                    cnt += 1
            for kc in range(12):
                sg_ = stage[cnt % 2]; sk = "stage%d" % (cnt % 2)
                T.dma(sg_, wout_d[l, kc * 128:(kc + 1) * 128, :], w=[sk])
                if cnt % 2 == 0:
                    act(lambda e: e.copy(out=Wout[:, kc, :], in_=sg_), [sk], ["Wout"])
                else:
                    pool(lambda e: e.tensor_copy(out=Wout[:, kc, :], in_=sg_), [sk], ["Wout"])
                cnt += 1
            for kc in range(4):
                sg_ = stage[cnt % 2]; sk = "stage%d" % (cnt % 2)
                T.dma(sg_[:, 0:512], wglu_d[l, kc * 128:(kc + 1) * 128, :], w=[sk])
                dve(lambda e: e.tensor_copy(out=Wglu[:, kc, :], in_=sg_[:, 0:512]), [sk], ["Wglu"])
                cnt += 1

            mark('weights')
            join(ALLK)
            lr = s_sm[:, :, 2]; li = s_sm[:, :, 3]; dtv = s_sm[:, :, 4]; lrdt = s_sm[:, :, 5]; lidt = s_sm[:, :, 6]
            t7 = s_sm[:, :, 7]; t8 = s_sm[:, :, 8]; fr = s_sm[:, :, 9]; fi = s_sm[:, :, 10]; t11 = s_sm[:, :, 11]
            load_T(lr, lre_d[l].rearrange("(pr two) p -> pr (two p)", two=2), 16, "s_sm")
            load_T(li, lim_d[l].rearrange("(pr two) p -> pr (two p)", two=2), 16, "s_sm")
            T.dma(s_lg2[0:16, :], ldt_d[l].rearrange("(pr two) -> pr two", two=2), w=["s_lg2"])
            dve(lambda e: e.tensor_copy(out=ldT[0:16, :].rearrange("q (two p) -> q two p", two=2),
                                        in_=s_lg2[0:16, :, None].to_broadcast([16, 2, 64])), ["s_lg2"], ["ldT"])
            tr(ps_z[:, 0:16], ldT[0:16, :], ident_f[0:16, 0:16], ["ldT", "ident_f"], ["ps_z"])
            act(lambda e: e.activation(out=dtv, in_=ps_z[:, 0:16], func=AF.Exp), ["ps_z"], ["s_sm"])
            dve(lambda e: e.tensor_tensor(out=lrdt, in0=lr, in1=dtv, op=ALU.mult), ["s_sm"], ["s_sm"])
            dve(lambda e: e.tensor_tensor(out=lidt, in0=li, in1=dtv, op=ALU.mult), ["s_sm"], ["s_sm"])
            dve(lambda e: e.tensor_tensor(out=s_e9, in0=lrdt[:, :, None].to_broadcast([128, 16, 9]), in1=nvec[:, None, :].to_broadcast([128, 16, 9]),
                                          op=ALU.mult), ["s_sm", "nvec"], ["s_e9"])
            dve(lambda e: e.tensor_tensor(out=s_a9, in0=lidt[:, :, None].to_broadcast([128, 16, 9]), in1=nvec[:, None, :].to_broadcast([128, 16, 9]),
                                          op=ALU.mult), ["s_sm", "nvec"], ["s_a9"])
            act(lambda e: e.activation(out=s_mag9, in_=s_e9, func=AF.Exp), ["s_e9"], ["s_mag9"])
            f2 = lambda v: v.rearrange("p a b -> p (a b)")
            sin_of(f2(s_sin9), f2(s_a9), 144, 0.0, ["s_a9"], ["s_sin9"])
            sin_of(f2(s_cos9), f2(s_a9), 144, math.pi / 2, ["s_a9"], ["s_cos9"])
            dve(lambda e: e.tensor_tensor(out=s_ar9, in0=s_mag9, in1=s_cos9, op=ALU.mult), ["s_mag9", "s_cos9"], ["s_ar9"])
            dve(lambda e: e.tensor_tensor(out=s_ai9, in0=s_mag9, in1=s_sin9, op=ALU.mult), ["s_mag9", "s_sin9"], ["s_ai9"])
            dve(lambda e: e.tensor_copy(out=magA[:], in_=s_mag9[:, :, 8]), ["s_mag9"], ["magA"])
            tmp16 = (s_ki[:, 0:16], s_kf[:, 0:16])
            dve(lambda e: e.tensor_scalar(out=tmp16[0], in0=lidt, scalar1=float(8.0 / TWO_PI), scalar2=None, op0=ALU.mult), ["s_sm"], ["sr_ki"])
            dve(lambda e: e.tensor_copy(out=tmp16[1], in_=tmp16[0]), ["sr_ki"], ["sr_kf"])
            dve(lambda e: e.tensor_scalar(out=t7, in0=lidt, scalar1=8.0, scalar2=None, op0=ALU.mult), ["s_sm"], ["s_sm"])
            dve(lambda e: e.scalar_tensor_tensor(out=t7, in0=tmp16[1], scalar=float(-CW1), in1=t7, op0=ALU.mult, op1=ALU.add), ["sr_kf", "s_sm"], ["s_sm"])
            dve(lambda e: e.scalar_tensor_tensor(out=t7, in0=tmp16[1], scalar=float(-CW2), in1=t7, op0=ALU.mult, op1=ALU.add), ["sr_kf", "s_sm"], ["s_sm"])
            for p0 in range(0, 16, 4):
                ta3 = s_ang[:, 0:4 * NCH].rearrange("p (a k) -> p a k", k=NCH)
                dve(lambda e: e.tensor_tensor(out=ta3, in0=t7[:, p0:p0 + 4, None].to_broadcast([128, 4, NCH]),
                                              in1=kvec[:, None, :].to_broadcast([128, 4, NCH]), op=ALU.mult), ["s_sm", "kvec"], ["s_ang"])
                sin_of(msT[:, p0:p0 + 4, :].rearrange("p a k -> p (a k)"), s_ang[:, 0:4 * NCH], 4 * NCH, 0.0, ["s_ang"], ["msT"])
                sin_of(mcT[:, p0:p0 + 4, :].rearrange("p a k -> p (a k)"), s_ang[:, 0:4 * NCH], 4 * NCH, math.pi / 2, ["s_ang"], ["mcT"])
            dve(lambda e: e.tensor_scalar(out=nms1[:], in0=msT[:, :, 1], scalar1=-1.0, scalar2=None, op0=ALU.mult), ["msT"], ["nms1"])
            dve(lambda e: e.tensor_tensor(out=t8, in0=lr, in1=lr, op=ALU.mult), ["s_sm"], ["s_sm"])
            dve(lambda e: e.tensor_tensor(out=t11, in0=li, in1=li, op=ALU.mult), ["s_sm"], ["s_sm"])
            dve(lambda e: e.tensor_tensor(out=t8, in0=t8, in1=t11, op=ALU.add), ["s_sm"], ["s_sm"])
            dve(lambda e: e.reciprocal(out=t8, in_=t8), ["s_sm"], ["s_sm"])
            dve(lambda e: e.tensor_scalar(out=t11, in0=s_ar9[:, :, 1], scalar1=-1.0, scalar2=None, op0=ALU.add), ["s_ar9"], ["s_sm"])
            dve(lambda e: e.tensor_tensor(out=fr, in0=t11, in1=lr, op=ALU.mult), ["s_sm"], ["s_sm"])
            dve(lambda e: e.tensor_tensor(out=t7, in0=s_ai9[:, :, 1], in1=li, op=ALU.mult), ["s_sm", "s_ai9"], ["s_sm"])
            dve(lambda e: e.tensor_tensor(out=fr, in0=fr, in1=t7, op=ALU.add), ["s_sm"], ["s_sm"])
            dve(lambda e: e.tensor_tensor(out=fr, in0=fr, in1=t8, op=ALU.mult), ["s_sm"], ["s_sm"])
            dve(lambda e: e.tensor_tensor(out=fi, in0=s_ai9[:, :, 1], in1=lr, op=ALU.mult), ["s_sm", "s_ai9"], ["s_sm"])
            dve(lambda e: e.tensor_tensor(out=t7, in0=t11, in1=li, op=ALU.mult), ["s_sm"], ["s_sm"])
            dve(lambda e: e.tensor_tensor(out=fi, in0=fi, in1=t7, op=ALU.subtract), ["s_sm"], ["s_sm"])
            dve(lambda e: e.tensor_tensor(out=fi, in0=fi, in1=t8, op=ALU.mult), ["s_sm"], ["s_sm"])
            mark('ssm_tabs')
            T.dma(s_Br, bre_d[l].rearrange("(pr two) p c -> (two p) pr c", two=2), w=["s_Br"])
            T.dma(s_Bi, bim_d[l].rearrange("(pr two) p c -> (two p) pr c", two=2), w=["s_Bi"])
            frb = fr[:, :, None].to_broadcast([128, 16, 16]); fib = fi[:, :, None].to_broadcast([128, 16, 16])
            dve(lambda e: e.tensor_tensor(out=s_bbr, in0=s_Br, in1=frb, op=ALU.mult), ["s_Br", "s_sm"], ["s_bbr"])
            dve(lambda e: e.tensor_tensor(out=s_bt, in0=s_Bi, in1=fib, op=ALU.mult), ["s_Bi", "s_sm"], ["s_bt"])
            dve(lambda e: e.tensor_tensor(out=s_bbr, in0=s_bbr, in1=s_bt, op=ALU.subtract), ["s_bbr", "s_bt"], ["s_bbr"])
            dve(lambda e: e.tensor_tensor(out=s_bbi, in0=s_Bi, in1=frb, op=ALU.mult), ["s_Bi", "s_sm"], ["s_bbi"])
            dve(lambda e: e.tensor_tensor(out=s_bt, in0=s_Br, in1=fib, op=ALU.mult), ["s_Br", "s_sm", "s_bbr"], ["s_bt"])
            dve(lambda e: e.tensor_tensor(out=s_bbi, in0=s_bbi, in1=s_bt, op=ALU.add), ["s_bbi", "s_bt"], ["s_bbi"])
            for (cd, dst, dk) in ((cre_d, s_Cr, "s_Cr"), (cim_d, s_Ci, "s_Ci")):
                for pr in range(16):
                    T.dma(s_Cst[(pr % 8) * 16:(pr % 8) * 16 + 16, pr // 8, :].rearrange("c (two p) -> c two p", two=2),
                          cd[l, 2 * pr:2 * pr + 2].rearrange("two c p -> c two p"), w=["s_Cst"])
                for hh in range(2):
                    tr(ps_z[:, hh * 128:(hh + 1) * 128], s_Cst[:, hh, :], ident_f[:], ["s_Cst", "ident_f"], ["ps_z"], sig=(hh == 1))
                dve(lambda e: e.tensor_copy(out=dst.rearrange("p a c -> p (a c)"), in_=ps_z[:, 0:256]), ["ps_z"], [dk])
            mark('ssm_BC')
            for q in range(4):
                prs = slice(4 * q, 4 * q + 4)
                arb = s_ar9[:, prs, :, None].to_broadcast([128, 4, 9, 16]); aib = s_ai9[:, prs, :, None].to_broadcast([128, 4, 9, 16])
                crb = s_Cr[:, prs, None, :].to_broadcast([128, 4, 9, 16]); cib = s_Ci[:, prs, None, :].to_broadcast([128, 4, 9, 16])
                dve(lambda e: e.tensor_tensor(out=s_CAr, in0=crb, in1=arb, op=ALU.mult), ["s_Cr", "s_ar9"], ["s_CAr"])
                dve(lambda e: e.tensor_tensor(out=s_CAt, in0=cib, in1=aib, op=ALU.mult), ["s_Ci", "s_ai9"], ["s_CAt"])
                dve(lambda e: e.tensor_tensor(out=s_CAr, in0=s_CAr, in1=s_CAt, op=ALU.subtract), ["s_CAr", "s_CAt"], ["s_CAr"])
                dve(lambda e: e.tensor_tensor(out=s_CAi, in0=crb, in1=aib, op=ALU.mult), ["s_Cr", "s_ai9"], ["s_CAi"])
                dve(lambda e: e.tensor_tensor(out=s_CAt, in0=cib, in1=arb, op=ALU.mult), ["s_Ci", "s_ar9", "s_CAr"], ["s_CAt"])
                dve(lambda e: e.tensor_tensor(out=s_CAi, in0=s_CAi, in1=s_CAt, op=ALU.add), ["s_CAi", "s_CAt"], ["s_CAi"])
                pool(lambda e: e.memset(Vf[:, prs, :, :], 0.0), [], ["Vf"])
                for two in range(2):
                    rows = slice(64 * two, 64 * two + 64)
                    dve(lambda e: e.tensor_copy(out=Vf[rows, prs, 0, two * 128:(two + 1) * 128].rearrange("p a (t c) -> p a t c", c=16),
                                                in_=s_CAr[rows, :, 1:9, :]), ["s_CAr", "Vf"], ["Vf"])
                    dve(lambda e: e.tensor_scalar(out=Vf[rows, prs, 1, two * 128:(two + 1) * 128].rearrange("p a (t c) -> p a t c", c=16),
                                                  in0=s_CAi[rows, :, 1:9, :], scalar1=-1.0, scalar2=None, op0=ALU.mult), ["s_CAi", "Vf"], ["Vf"])
                pool(lambda e: e.memset(s_CAzr, 0.0), [], ["s_CAzr"])
                pool(lambda e: e.memset(s_CAzi, 0.0), [], ["s_CAzi"])
                pool(lambda e: e.memset(s_Bzr, 0.0), [], ["s_Bzr"])
                pool(lambda e: e.memset(s_Bzi, 0.0), [], ["s_Bzi"])
                for two in range(2):
                    rows = slice(64 * two, 64 * two + 64)
                    dve(lambda e: e.tensor_copy(out=s_CAzr[rows, :, two, :, :], in_=s_CAr[rows, :, 0:8, :]), ["s_CAr", "s_CAzr"], ["s_CAzr"])
                    dve(lambda e: e.tensor_scalar(out=s_CAzi[rows, :, two, :, :], in0=s_CAi[rows, :, 0:8, :], scalar1=-1.0, scalar2=None, op0=ALU.mult),
                        ["s_CAi", "s_CAzi"], ["s_CAzi"])
                    for j in range(4):
                        c0 = 32 * j + 16 * two
                        dve(lambda e: e.tensor_copy(out=s_Bzr[rows, j, c0:c0 + 16], in_=s_bbr[rows, 4 * q + j, :]), ["s_bbr", "s_Bzr"], ["s_Bzr"])
                        dve(lambda e: e.tensor_copy(out=s_Bzi[rows, j, c0:c0 + 16], in_=s_bbi[rows, 4 * q + j, :]), ["s_bbi", "s_Bzi"], ["s_Bzi"])
                for j in range(4):
                    mm(ps_z[:, 0:256], s_Bzr[:, j, :], s_CAzr[:, j].rearrange("p a t c -> p (a t c)"), j == 0, False, ["s_Bzr", "s_CAzr"], ["ps_z"])
                    mm(ps_z[:, 0:256], s_Bzi[:, j, :], s_CAzi[:, j].rearrange("p a t c -> p (a t c)"), False, j == 3, ["s_Bzi", "s_CAzi"], ["ps_z"])
                dve(lambda e: e.tensor_copy(out=s_Kt.rearrange("p a t c -> p (a t c)"), in_=ps_z[:, 0:256]), ["ps_z"], ["s_Kt"])
                dve(lambda e: e.tensor_scalar(out=s_Kd, in0=dmask[:], scalar1=d_col[:, q:q + 1], scalar2=None, op0=ALU.mult), ["dmask", "d_col"], ["s_Kd"])
                dve(lambda e: e.tensor_tensor(out=s_Kt[:, :, 0, :], in0=s_Kt[:, :, 0, :], in1=s_Kd.rearrange("p (a c) -> p a c", a=2), op=ALU.add),
                    ["s_Kt", "s_Kd"], ["s_Kt"])
                dve(lambda e: e.tensor_copy(out=Kstrip[:, q, :, 7:15, :], in_=s_Kt), ["s_Kt"], ["Kstrip"])
                brb = s_bbr[:, prs, None, :].to_broadcast([128, 4, 8, 16]); bib = s_bbi[:, prs, None, :].to_broadcast([128, 4, 8, 16])
                ar8 = s_ar9[:, prs, 0:8, None].to_broadcast([128, 4, 8, 16]); ai8 = s_ai9[:, prs, 0:8, None].to_broadcast([128, 4, 8, 16])
                dve(lambda e: e.tensor_tensor(out=s_Xr, in0=brb, in1=ar8, op=ALU.mult), ["s_bbr", "s_ar9"], ["s_Xr"])
                dve(lambda e: e.tensor_tensor(out=s_Xt, in0=bib, in1=ai8, op=ALU.mult), ["s_bbi", "s_ai9"], ["s_Xt"])
                dve(lambda e: e.tensor_tensor(out=s_Xr, in0=s_Xr, in1=s_Xt, op=ALU.subtract), ["s_Xr", "s_Xt"], ["s_Xr"])
                dve(lambda e: e.tensor_tensor(out=s_Xi, in0=bib, in1=ar8, op=ALU.mult), ["s_bbi", "s_ar9"], ["s_Xi"])
                dve(lambda e: e.tensor_tensor(out=s_Xt, in0=brb, in1=ai8, op=ALU.mult), ["s_bbr", "s_ai9", "s_Xr"], ["s_Xt"])
                dve(lambda e: e.tensor_tensor(out=s_Xi, in0=s_Xi, in1=s_Xt, op=ALU.add), ["s_Xi", "s_Xt"], ["s_Xi"])
                pool(lambda e: e.memset(s_Xzr, 0.0), [], ["s_Xzr"])
                pool(lambda e: e.memset(s_Xzi, 0.0), [], ["s_Xzi"])
                for two in range(2):
                    rows = slice(64 * two, 64 * two + 64)
                    dve(lambda e: e.tensor_copy(out=s_Xzr[rows, :, :, 16 * two:16 * two + 16].rearrange("p n a c -> p a n c"), in_=s_Xr[rows]), ["s_Xr", "s_Xzr"], ["s_Xzr"])
                    dve(lambda e: e.tensor_copy(out=s_Xzi[rows, :, :, 16 * two:16 * two + 16].rearrange("p n a c -> p a n c"), in_=s_Xi[rows]), ["s_Xi", "s_Xzi"], ["s_Xzi"])
                for hh in range(2):
                    for s4 in range(4):
                        s_ = 4 * hh + s4
                        for ri, xz in ((0, s_Xzr), (1, s_Xzi)):
                            tr(ps_t[:, 2 * s4 + ri, :], xz[:, 7 - s_, :, :].rearrange("p a c -> p (a c)"), ident_b[:], ["s_Xzr", "s_Xzi", "ident_b"], ["ps_t"],
                               sig=(s4 == 3 and ri == 1))
                    dve(lambda e: e.tensor_copy(out=Wssm[:, q, 4 * hh:4 * hh + 4, :, :].rearrange("p s r m -> p (s r) m"), in_=ps_t[:]),
                        ["ps_t"], ["Wssm"])
            pool(lambda e: e.memset(carry[:], 0.0), [], ["carry"])
            pool(lambda e: e.memset(Hb[:], 0.0), [], ["Hb"])
            join(ALLK)
            pool(lambda e: e.memset(vaug[0][:, :, 64:65], 1.0), [], ["vaug0"])
            pool(lambda e: e.memset(vaug[1][:, :, 64:65], 1.0), [], ["vaug1"])

        def block_main(l, b, src):
            slot = b % 2
            bl = b % BPS
            cols = slice(bl * 128, (bl + 1) * 128)
            xk, kTk, vk, pk = "xt%d" % slot, "kT%d" % slot, "vaug%d" % slot, None
            x_t = xt[slot]
            T.dma(x_t, src[b * 128:(b + 1) * 128, :], r=[("res", b)], w=[xk])
            act(lambda e: e.activation(out=junk, in_=x_t, func=AF.Square, accum_out=st1[:, 0:1]), [xk], ["junk", "st1"])
            rsqrt_of(st1[:, 1:2], st1[:, 0:1], 1.0 / D, ["st1"], ["st1"])
            act(lambda e: e.activation(out=hb, in_=x_t, func=AF.Copy, scale=st1[:, 1:2]), [xk, "st1"], ["hb"])
            for kc in range(8):
                tr(ps_t[:, kc, :], hb[:, kc * 128:(kc + 1) * 128], ident_b[:], ["hb", "ident_b"], ["ps_t"], sig=(kc == 7))
            dve(lambda e: e.tensor_copy(out=hT, in_=ps_t[:]), ["ps_t"], ["hT"])

            def zgroup(c0, n):
                for kc in range(8):
                    mm(ps_z[:, 0:n], hT[:, kc, :], Wsb[:, kc, c0:c0 + n], kc == 0, kc == 7, ["hT", "Wsb"], ["ps_z"])

            if b == 0: mark('bm_norm')
            zgroup(C_AQ, 512)
            act(lambda e: e.copy(out=qk[:, 0:8, :].rearrange("p h d -> p (h d)"), in_=ps_z[:]), ["ps_z"], ["qk"])
            zgroup(C_AK, 256)
            act(lambda e: e.copy(out=qk[:, 8:10, :].rearrange("p h d -> p (h d)"), in_=ps_z[:, 0:128]), ["ps_z", "qk"], ["qk"])
            act(lambda e: e.copy(out=vaug[slot][:, :, 0:64], in_=ps_z[:, 128:256].rearrange("p (h d) -> p h d", h=2)), ["ps_z"], [vk])
            zgroup(C_AG, 512)
            act(lambda e: e.activation(out=gate_a, in_=ps_z[:], func=AF.Silu), ["ps_z"], ["gate_a"])
            if b == 0: mark('bm_z')
            dve(lambda e: e.tensor_tensor(out=sq, in0=qk, in1=qk, op=ALU.mult), ["qk"], ["sq"])
            dve(lambda e: e.tensor_reduce(out=st1[:, 2:12], in_=sq, axis=AX.X, op=ALU.add), ["sq"], ["st1"])
            if b == 0: mark("r1")
            rsqrt_of(st1[:, 2:12], st1[:, 2:12], 1.0 / 64, ["st1"], ["st1"])
            if b == 0: mark("r2")
            dve(lambda e: e.tensor_tensor(out=qk, in0=qk, in1=st1[:, 2:12, None].to_broadcast([128, 10, 64]), op=ALU.mult), ["qk", "st1"], ["qk"])
            dve(lambda e: e.tensor_tensor(out=qk, in0=qk, in1=g10[:], op=ALU.mult), ["qk", "g10"], ["qk"])
            if b == 0: mark("r3")
            qk4 = qk.rearrange("p h (a j) -> p h a j", a=2)
            r14 = rt1.rearrange("p h (a j) -> p h a j", a=2)
            r24 = rt2.rearrange("p h (a j) -> p h a j", a=2)
            qr4 = qr.rearrange("p h (a j) -> p h a j", a=2)
            cb = cosT[:, b, None, None, :].to_broadcast([128, 10, 2, 32])
            sb_ = sinT[:, b, None, :].to_broadcast([128, 10, 32])
            dve(lambda e: e.tensor_tensor(out=r14, in0=qk4, in1=cb, op=ALU.mult), ["qk", "cosT"], ["rt1"])
            if b == 0: mark("r4")
            dve(lambda e: e.tensor_tensor(out=r24[:, :, 0, :], in0=qk4[:, :, 1, :], in1=sb_, op=ALU.mult), ["qk", "sinT"], ["rt2"])
            dve(lambda e: e.tensor_tensor(out=r24[:, :, 1, :], in0=qk4[:, :, 0, :], in1=sb_, op=ALU.mult), ["qk", "sinT", "rt2"], ["rt2"])
            if b == 0: mark("r5")
            dve(lambda e: e.tensor_tensor(out=qr4[:, :, 0, :], in0=r14[:, :, 0, :], in1=r24[:, :, 0, :], op=ALU.subtract), ["rt1", "rt2"], ["qr"])
            dve(lambda e: e.tensor_tensor(out=qr4[:, :, 1, :], in0=r14[:, :, 1, :], in1=r24[:, :, 1, :], op=ALU.add), ["rt1", "rt2", "qr"], ["qr"])
            if b == 0: mark("r6")
            for j in range(5):
                tr(ps_t[:, j, :], qr[:, 2 * j:2 * j + 2, :].rearrange("p h d -> p (h d)"), ident_b[:], ["qr", "ident_b"], ["ps_t"], sig=(j == 4))
            if b == 0: mark("r7")
            dve(lambda e: e.tensor_copy(out=qT, in_=ps_t[:, 0:4, :]), ["ps_t"], ["qT"])
            dve(lambda e: e.tensor_copy(out=kT[slot], in_=ps_t[:, 4, :]), ["ps_t"], [kTk])
            if b == 0: mark('bm_rope')
            tiles = ([] if b == 0 else [(1 - slot, mask_prev, "mask_prev")]) + [(slot, mask_cur, "mask_cur")]
            for kvh in range(2):
                rows = slice(64 * kvh, 64 * kvh + 64)
                for ti, (sl, mk, mkk) in enumerate(tiles):
                    mm(ps_s[ti][:], kT[sl][rows, :], qT[rows, :, :].rearrange("p j q -> p (j q)"), True, True,
                       ["kT%d" % sl, "qT"], ["ps_s%d" % ti])
                    act(lambda e: e.activation(out=pTm[ti], in_=ps_s[ti][:], func=AF.Exp, scale=0.125), ["ps_s%d" % ti], ["pTm%d" % ti])
                    pool(lambda e: e.tensor_tensor(out=pTm[ti], in0=pTm[ti], in1=mk[:].rearrange("p j q -> p (j q)"), op=ALU.mult),
                         ["pTm%d" % ti, mkk], ["pTm%d" % ti])
                for j in range(4):
                    for ti, (sl, mk, mkk) in enumerate(tiles):
                        mm(ps_f[:, j * 65:(j + 1) * 65], pTm[ti][:, j * 128:(j + 1) * 128], vaug[sl][:, kvh, :], ti == 0, ti == len(tiles) - 1,
                           ["pTm%d" % ti, "vaug%d" % sl], ["ps_f"], sig=(j == 3 and ti == len(tiles) - 1))
                o4 = ps_f[:, 0:260].rearrange("p (j d) -> p j d", d=65)
                dve(lambda e: e.tensor_tensor(out=den[:, 0:4], in0=o4[:, :, 64], in1=esink[:, 4 * kvh:4 * kvh + 4], op=ALU.add), ["ps_f", "esink"], ["den"])
                dve(lambda e: e.reciprocal(out=den[:, 0:4], in_=den[:, 0:4]), ["den"], ["den"])
                for j in range(4):
                    h = 4 * kvh + j
                    dve(lambda e: e.scalar_tensor_tensor(out=mix_a[:, h * 64:(h + 1) * 64], in0=o4[:, j, 0:64], scalar=den[:, j:j + 1],
                                                         in1=gate_a[:, h * 64:(h + 1) * 64], op0=ALU.mult, op1=ALU.mult),
                        ["ps_f", "den", "gate_a"], ["mix_a"])
            for j in range(4):
                tr(ps_t[:, j, :], mix_a[:, j * 128:(j + 1) * 128], ident_b[:], ["mix_a", "ident_b"], ["ps_t"], sig=(j == 3))
            dve(lambda e: e.tensor_copy(out=mixT[:, 0:4, cols], in_=ps_t[:, 0:4, :]), ["ps_t"], ["mixT"])
            if b == 0: mark('bm_attn')
            zgroup(C_XQ, 512)
            act(lambda e: e.copy(out=xq_f.rearrange("p h d -> p (h d)"), in_=ps_z[:]), ["ps_z"], ["xq_f"])
            zgroup(C_XG, 512)
            act(lambda e: e.activation(out=gate_x, in_=ps_z[:], func=AF.Silu), ["ps_z"], ["gate_x"])
            sq4 = sq.rearrange("p h d -> p (h d)")[:, 0:512].rearrange("p (h d) -> p h d", h=4)
            dve(lambda e: e.tensor_tensor(out=sq4, in0=xq_f, in1=xq_f, op=ALU.mult), ["xq_f"], ["sq"])
            dve(lambda e: e.tensor_reduce(out=st1[:, 12:16], in_=sq4, axis=AX.X, op=ALU.add), ["sq"], ["st1"])
            rsqrt_of(st1[:, 12:16], st1[:, 12:16], 1.0 / 128, ["st1"], ["st1"])
            dve(lambda e: e.tensor_tensor(out=xq_b, in0=xq_f, in1=st1[:, 12:16, None].to_broadcast([128, 4, 128]), op=ALU.mult), ["xq_f", "st1"], ["xq_b"])
            for h in range(4):
                tr(ps_t[:, h, :], xq_b[:, h, :], ident_b[:], ["xq_b", "ident_b"], ["ps_t"], sig=(h == 3))
            dve(lambda e: e.tensor_copy(out=xqT, in_=ps_t[:, 0:4, :]), ["ps_t"], ["xqT"])
            for mt in range(2):
                for h in range(4):
                    mm(ps_s[mt][:, h * 128:(h + 1) * 128], mkT[:, h, mt * 128:(mt + 1) * 128], xqT[:, h, :], True, True, ["mkT", "xqT"],
                       ["ps_s%d" % mt], sig=(h == 3))
                act(lambda e: e.activation(out=pX[mt], in_=ps_s[mt][:], func=AF.Exp), ["ps_s%d" % mt], ["pX%d" % mt])
            for hp in range(2):
                for i in range(2):
                    h = 2 * hp + i
                    for mt in range(2):
                        mm(ps_f[:, i * 129:(i + 1) * 129], pX[mt][:, h * 128:(h + 1) * 128], mv_aug[:, mt, h, :], mt == 0, mt == 1,
                           ["pX%d" % mt, "mv_aug"], ["ps_f"], sig=(i == 1 and mt == 1))
                o2 = ps_f[:, 0:258].rearrange("p (i d) -> p i d", d=129)
                dve(lambda e: e.reciprocal(out=den[:, 4:6], in_=o2[:, :, 128]), ["ps_f"], ["den"])
                for i in range(2):
                    h = 2 * hp + i
                    dve(lambda e: e.scalar_tensor_tensor(out=mix_c[:, h * 128:(h + 1) * 128], in0=o2[:, i, 0:128], scalar=den[:, 4 + i:5 + i],
                                                         in1=gate_x[:, h * 128:(h + 1) * 128], op0=ALU.mult, op1=ALU.mult),
                        ["ps_f", "den", "gate_x"], ["mix_c"])
            for j in range(4):
                tr(ps_t[:, j, :], mix_c[:, j * 128:(j + 1) * 128], ident_b[:], ["mix_c", "ident_b"], ["ps_t"], sig=(j == 3))
            dve(lambda e: e.tensor_copy(out=mixT[:, 8:12, cols], in_=ps_t[:, 0:4, :]), ["ps_t"], ["mixT"])
            if b == 0: mark('bm_xattn')
            for (c0, dst, dk, fn) in ((C_SU, uT, "uT", None), (C_SG, gT, "gT", AF.Silu)):
                for ct in range(4):
                    for kc in range(8):
                        mm(ps_z[:, ct * 128:(ct + 1) * 128], Wsb[:, kc, c0 + ct * 128:c0 + (ct + 1) * 128], hT[:, kc, :], kc == 0, kc == 7,
                           ["hT", "Wsb"], ["ps_z"], sig=(ct == 3 and kc == 7))
                pz3 = ps_z[:].rearrange("p (c t) -> p c t", c=4)
                if fn is None:
                    act(lambda e: e.copy(out=dst[:, :, cols], in_=pz3), ["ps_z"], [dk])
                else:
                    act(lambda e: e.activation(out=dst[:, :, cols], in_=pz3, func=fn), ["ps_z"], [dk])

        def ssm_superblock(l, sbi):
            for q in range(4):
                for prl in range(4):
                    pr = 4 * q + prl
                    rows = slice(32 * prl, 32 * prl + 32)
                    kw = {"tile_position": (96, 0)} if prl == 3 else {}
                    u3 = uT[rows, q, :].rearrange("p (k s) -> p s k", s=8)
                    for ri in range(2):
                        for s_ in range(8):
                            mm(ps_zs[:, ri * NCH:(ri + 1) * NCH], Wssm[rows, q, s_, ri, :], u3[:, s_, :], s_ == 0, s_ == 7, ["Wssm", "uT"], ["ps_zs"],
                               sig=(ri == 1 and s_ == 7), **kw)
                    zr = ps_zs[:, 0:NCH]; zi = ps_zs[:, NCH:2 * NCH]
                    mc = mcT[:, pr, :]; ms = msT[:, pr, :]
                    dve(lambda e: e.tensor_tensor(out=sct[:, 0, :], in0=zr, in1=mc, op=ALU.mult), ["ps_zs", "mcT"], ["sct"])
                    dve(lambda e: e.tensor_tensor(out=sct[:, 1, :], in0=zi, in1=ms, op=ALU.mult), ["ps_zs", "msT", "sct"], ["sct"])
                    dve(lambda e: e.tensor_tensor(out=sct[:, 2, :], in0=zi, in1=mc, op=ALU.mult), ["ps_zs", "mcT", "sct"], ["sct"])
                    dve(lambda e: e.tensor_tensor(out=sct[:, 3, :], in0=zr, in1=ms, op=ALU.mult), ["ps_zs", "msT", "sct"], ["sct"])
                    pool(lambda e: e.tensor_tensor(out=zt[:, 0, :], in0=sct[:, 0, :], in1=sct[:, 1, :], op=ALU.add), ["sct"], ["zt"])
                    pool(lambda e: e.tensor_tensor(out=zt[:, 1, :], in0=sct[:, 2, :], in1=sct[:, 3, :], op=ALU.subtract), ["sct", "zt"], ["zt"])
                    if sbi == 0:
                        ini = (0.0, 0.0)
                        inik = []
                    else:
                        hr_ = carry[:, pr, 0:1]; hi_ = carry[:, pr, 1:2]
                        dve(lambda e: e.tensor_scalar(out=init2[:, 0:1], in0=hr_, scalar1=mcT[:, pr, 1:2], scalar2=None, op0=ALU.mult), ["carry", "mcT"], ["init2"])
                        dve(lambda e: e.scalar_tensor_tensor(out=init2[:, 0:1], in0=hi_, scalar=nms1[:, pr:pr + 1], in1=init2[:, 0:1], op0=ALU.mult, op1=ALU.add),
                            ["carry", "nms1", "init2"], ["init2"])
                        dve(lambda e: e.tensor_scalar(out=init2[:, 1:2], in0=hi_, scalar1=mcT[:, pr, 1:2], scalar2=None, op0=ALU.mult), ["carry", "mcT", "init2"], ["init2"])
                        dve(lambda e: e.scalar_tensor_tensor(out=init2[:, 1:2], in0=hr_, scalar=msT[:, pr, 1:2], in1=init2[:, 1:2], op0=ALU.mult, op1=ALU.add),
                            ["carry", "msT", "init2"], ["init2"])
                        ini = (init2[:, 0:1], init2[:, 1:2])
                        inik = ["init2"]
                    mg = magA[:, pr:pr + 1].to_broadcast([128, NCH])
                    for ri in range(2):
                        dve(lambda e: e.tensor_tensor_scan(out=Gs[:, ri, :], data0=mg, data1=zt[:, ri, :], initial=ini[ri], op0=ALU.mult, op1=ALU.add),
                            ["zt", "magA", "Gs"] + inik, ["Gs"])
                    pool(lambda e: e.tensor_tensor(out=sct[:, 0, :], in0=Gs[:, 0, :], in1=mc, op=ALU.mult), ["Gs", "mcT"], ["sct"])
                    pool(lambda e: e.tensor_tensor(out=sct[:, 1, :], in0=Gs[:, 1, :], in1=ms, op=ALU.mult), ["Gs", "msT", "sct"], ["sct"])
                    pool(lambda e: e.tensor_tensor(out=sct[:, 2, :], in0=Gs[:, 1, :], in1=mc, op=ALU.mult), ["Gs", "mcT", "sct"], ["sct"])
                    pool(lambda e: e.tensor_tensor(out=sct[:, 3, :], in0=Gs[:, 0, :], in1=ms, op=ALU.mult), ["Gs", "msT", "sct"], ["sct"])
                    pool(lambda e: e.tensor_tensor(out=Hf[:, 0, :], in0=sct[:, 0, :], in1=sct[:, 1, :], op=ALU.subtract), ["sct"], ["Hf"])
                    pool(lambda e: e.tensor_tensor(out=Hf[:, 1, :], in0=sct[:, 2, :], in1=sct[:, 3, :], op=ALU.add), ["sct", "Hf"], ["Hf"])
                    act(lambda e: e.copy(out=Hb[:, pr, :, 0:1], in_=carry[:, pr, :, None]), ["carry"], ["Hb"])
                    act(lambda e: e.copy(out=Hb[:, pr, :, 1:NCH], in_=Hf[:, :, 0:NCH - 1]), ["Hf", "Hb"], ["Hb"])
                    act(lambda e: e.copy(out=carry[:, pr, :], in_=Hf[:, :, NCH - 1]), ["Hf", "Hb"], ["carry"])
                    for s_ in range(8):
                        mm(ps_y[0:NCH, 0:256].rearrange("k (a t c) -> k a t c", a=2, t=8), u3[:, s_, :], Kstrip[rows, q, :, 7 - s_:15 - s_, :], s_ == 0, s_ == 7,
                           ["uT", "Kstrip"], ["ps_y"], **kw)
                    for ri in range(2):
                        mm(ps_y2[0:NCH, 0:256], Hb[:, pr, ri, :], Vf[:, pr, ri, :], ri == 0, ri == 1, ["Hb", "Vf"], ["ps_y2"])
                    act(lambda e: e.copy(out=y2s[0:NCH, :], in_=ps_y2[0:NCH, 0:256]), ["ps_y2"], ["y2s"])
                    dve(lambda e: e.tensor_tensor(out=ysum[0:NCH, :], in0=ps_y[0:NCH, 0:256], in1=y2s[0:NCH, :], op=ALU.add), ["ps_y", "y2s"], ["ysum"])
                    act(lambda e: e.activation(out=y2k[0:NCH, :, 32 * prl:32 * prl + 32].rearrange("k t (a c) -> k t a c", a=2),
                                               in_=ysum[0:NCH, :].rearrange("k (a t c) -> k t a c", a=2, t=8), func=AF.Gelu_apprx_tanh),
                        ["ysum"], ["y2k"])
                for t in range(8):
                    tr(ps_t[:, t, 0:NCH], y2k[0:NCH, t, :], ident_b[0:NCH, 0:NCH], ["y2k", "ident_b"], ["ps_t"], sig=(t == 7))
                dve(lambda e: e.tensor_copy(out=uT[:, q, :].rearrange("c (k t) -> c t k", t=8), in_=ps_t[:, :, 0:NCH]), ["ps_t", "uT"], ["uT"])
            for oc in range(4):
                for kc in range(4):
                    mm(ps_y[:, 0:SBT], Wglu[:, kc, oc * 128:(oc + 1) * 128], uT[:, kc, :], kc == 0, kc == 3, ["Wglu", "uT"], ["ps_y"])
                act(lambda e: e.activation(out=sig_t, in_=ps_y[:, 0:SBT], func=AF.Sigmoid, bias=bglu_col[:, oc:oc + 1]), ["ps_y", "bglu_col"], ["sig_t"])
                pool(lambda e: e.tensor_tensor(out=sig_t, in0=sig_t, in1=gT[:, oc, :], op=ALU.mult), ["sig_t", "gT"], ["sig_t"])
                dve(lambda e: e.tensor_tensor(out=mixT[:, 4 + oc, :], in0=sig_t, in1=uT[:, oc, :], op=ALU.mult), ["sig_t", "uT"], ["mixT"])

        def block_out(l, b, src):
            slot = b % 2
            bl = b % BPS
            cols = slice(bl * 128, (bl + 1) * 128)
            xk = "xr0"
            slot = 0
            T.dma(xr[slot], src[b * 128:(b + 1) * 128, :], r=[("res", b)], w=[xk], q="sp")
            for half in range(2):
                for kc in range(12):
                    mm(ps_f[:], mixT[:, kc, cols], Wout[:, kc, half * 512:(half + 1) * 512], kc == 0, kc == 11, ["mixT", "Wout"], ["ps_f"])
                dve(lambda e: e.tensor_tensor(out=xr[slot][:, half * 512:(half + 1) * 512], in0=ps_f[:], in1=xr[slot][:, half * 512:(half + 1) * 512],
                                              op=ALU.add), ["ps_f", xk], [xk])
            T.dma(out_d[b * 128:(b + 1) * 128, :], xr[slot], r=[xk], w=[("res", b)], q="sp")

        try:
            mark('consts')
            for l in range(n_layers):
                src = x_d if l == 0 else out_d
                layer_setup(l)
                mark('setup')
                for sbi in range(n_sb):
                    for bl in range(BPS):
                        block_main(l, sbi * BPS + bl, src)
                        mark('main%d' % bl)
                    ssm_superblock(l, sbi)
                    mark('ssm')
                    for bl in range(BPS):
                        block_out(l, sbi * BPS + bl, src)
        except StopBuild:
            print("build stopped at", stop)
        T.drain("sp")
        print("ops", T.nops, "waits", T.nwaits)
    return nc


Q_PERM = [0, 4, 1, 5, 2, 6, 3, 7]


def prep_inputs(inputs, n_sb=S // SBT, layers=None, x_override=None):
    SEQ = n_sb * SBT
    lsl = slice(None) if layers is None else slice(layers[0], layers[1])
    w_in = np.asarray(inputs["w_in"], dtype=np.float32)[lsl]
    qcols = np.concatenate([np.arange(h * 64, (h + 1) * 64) for h in Q_PERM])
    perm = np.concatenate([qcols, np.arange(512, INW)])
    w_in_p = np.ascontiguousarray(w_in[:, :, perm])
    shared = {k: np.ascontiguousarray(np.asarray(v)[lsl]) for k, v in inputs.items() if k not in ("x", "mem", "positions", "w_in")}
    shared["w_in"] = w_in_p
    xs = np.asarray(inputs["x"]) if x_override is None else x_override
    maps = []
    for c in range(8):
        m = dict(shared)
        m["x"] = np.ascontiguousarray(xs[c, :SEQ])
        m["mem"] = np.ascontiguousarray(np.asarray(inputs["mem"])[c])
        m["positions"] = np.ascontiguousarray(np.asarray(inputs["positions"])[c, :SEQ]).astype(np.int32)
        maps.append(m)
    return maps


LAYERS_PER_LAUNCH = 1


def kernel(**inputs):
    nc = build_nc(n_layers=LAYERS_PER_LAUNCH, wd=LAYERS_PER_LAUNCH)
    x = np.asarray(inputs["x"], dtype=np.float32)
    for l0 in range(0, DEPTH, LAYERS_PER_LAUNCH):
        maps = prep_inputs(inputs, layers=(l0, l0 + LAYERS_PER_LAUNCH), x_override=x)
        res = run_bass_kernel_spmd(nc, maps, core_ids=list(range(8)))
        x = np.stack([np.asarray(r["out"]) for r in res.results], axis=0).astype(np.float32)
    return x
```

```python
import math
import numpy as np
from contextlib import ExitStack
import concourse.bass as bass
import concourse.mybir as mybir
from concourse.bass_utils import run_bass_kernel_spmd

F32 = mybir.dt.float32
BF16 = mybir.dt.bfloat16
I32 = mybir.dt.int32
ALU = mybir.AluOpType
AF = mybir.ActivationFunctionType
AX = mybir.AxisListType

D = 1024
S = 4096
NMEM = 256
INW = 3328
DEPTH = 4
C_AQ, C_AK, C_AV, C_AG, C_SU, C_SG, C_XQ, C_XG = 0, 512, 640, 768, 1280, 1792, 2304, 2816
EPS = 1e-6
SBT = 512
BPS = SBT // 128
NCH = SBT // 8
N_DMA_SEMS = 40
TWO_PI = 2.0 * math.pi
CW1 = 6.28125
CW2 = TWO_PI - 6.28125


class Trk:
    def __init__(self, nc, stack):
        self.nc = nc
        self.eng = {"pe": nc.tensor, "act": nc.scalar, "dve": nc.vector, "pool": nc.gpsimd, "sp": nc.sync}
        self.sem = {k: stack.enter_context(nc.semaphore("s_" + k)) for k in ("pe", "act", "dve", "pool")}
        self.cnt = {k: 0 for k in self.sem}
        self.dsem = [stack.enter_context(nc.semaphore("d%d" % i)) for i in range(N_DMA_SEMS)]
        self.dcnt = [0] * N_DMA_SEMS
        self.dnext = 0
        self.known = {k: {} for k in self.eng}
        self.lastw = {}
        self.reads = {}
        self.nwaits = 0
        self.nops = 0

    def _wait(self, e, tok):
        kind, key, val = tok
        if kind == "E" and key == e and val > self.cnt[e]:
            return
        kn = self.known[e]
        if kn.get((kind, key), 0) >= val:
            return
        kn[(kind, key)] = val
        sem = self.sem[key] if kind == "E" else self.dsem[key]
        self.eng[e].wait_ge(sem, val)
        self.nwaits += 1

    ALIAS = {"junk": "hb", "sq": "rt1", "xq_f": "rt2", "y2s": "rt2", "ysum": "rt2", "y2k": "rt1", "sig_t": "qr",
             "pX0": "pTm0", "pX1": "pTm1", "mix_c": "mix_a"}

    def _deps(self, e, r, w):
        r = [self.ALIAS.get(k, k) for k in r]
        w = [self.ALIAS.get(k, k) for k in w]
        for k in r:
            t = self.lastw.get(k)
            if t is not None:
                self._wait(e, t)
        for k in w:
            t = self.lastw.get(k)
            if t is not None:
                self._wait(e, t)
            for t in self.reads.get(k, ()):
                self._wait(e, t)

    def _commit(self, tok, r, w):
        r = [self.ALIAS.get(k, k) for k in r]
        w = [self.ALIAS.get(k, k) for k in w]
        for k in r:
            self.reads.setdefault(k, []).append(tok)
        for k in w:
            self.lastw[k] = tok
            self.reads[k] = []

    def op(self, e, fn, r=(), w=(), sig=True):
        self._deps(e, r, w)
        inst = fn(self.eng[e])
        self.nops += 1
        if sig:
            self.cnt[e] += 1
            inst.then_inc(self.sem[e], 1)
            tok = ("E", e, self.cnt[e])
        else:
            tok = ("E", e, self.cnt[e] + 1)
        self._commit(tok, r, w)

    def dma(self, out, in_, r=(), w=(), q="sp"):
        self._deps(q, r, w)
        i = self.dnext
        self.dnext = (self.dnext + 1) % N_DMA_SEMS
        if self.dcnt[i] > 0:
            self._wait(q, ("D", i, self.dcnt[i]))
        self.dcnt[i] += 16
        self.eng[q].dma_start(out=out, in_=in_).then_inc(self.dsem[i], 16)
        tok = ("D", i, self.dcnt[i])
        self.nops += 1
        self._commit(tok, r, w)

    def drain(self, e="sp"):
        for k in self.sem:
            if self.cnt[k] > 0:
                self._wait(e, ("E", k, self.cnt[k]))
        for i in range(N_DMA_SEMS):
            if self.dcnt[i] > 0:
                self._wait(e, ("D", i, self.dcnt[i]))

    def finish(self, keys, e="sp"):
        for k in keys:
            t = self.lastw.get(k)
            if t is not None:
                self._wait(e, t)


class StopBuild(Exception):
    pass


def build_nc(n_layers=DEPTH, n_sb=S // SBT, dbg=False, stop=None, wd=DEPTH):
    nc = bass.Bass("TRN2", target_bir_lowering=False)
    SEQ = n_sb * SBT
    di = lambda n, s, dt=F32: nc.dram_tensor(n, s, dt, kind="ExternalInput").ap()
    x_d = di("x", [SEQ, D]); mem_d = di("mem", [NMEM, D]); pos_d = di("positions", [SEQ], I32)
    norm_g_d = di("norm_g", [wd, D]); w_in_d = di("w_in", [wd, D, INW])
    qg_d = di("q_norm_g", [wd, 64]); kg_d = di("k_norm_g", [wd, 64]); sinks_d = di("sinks", [wd, 8])
    lre_d = di("lam_re", [wd, 32, 64]); lim_d = di("lam_im", [wd, 32, 64]); ldt_d = di("log_dt", [wd, 32])
    bre_d = di("b_re", [wd, 32, 64, 16]); bim_d = di("b_im", [wd, 32, 64, 16])
    cre_d = di("c_re", [wd, 32, 16, 64]); cim_d = di("c_im", [wd, 32, 16, 64])
    dskip_d = di("d_skip", [wd, 512]); wglu_d = di("w_glu", [wd, 512, 512]); bglu_d = di("b_glu", [wd, 512])
    mng_d = di("mem_norm_g", [wd, D]); wkv_d = di("w_mem_kv", [wd, D, D])
    xqg_d = di("xq_norm_g", [wd, 128]); xkg_d = di("xk_norm_g", [wd, 128]); wout_d = di("w_out", [wd, 1536, D])
    out_d = nc.dram_tensor("out", [SEQ, D], F32, kind="ExternalOutput").ap()
    NB = SEQ // 128

    with ExitStack() as st:
        T = Trk(nc, st)
        sbt = lambda name, shape, dt: st.enter_context(nc.sbuf_tensor(name, shape, dt))
        pst = lambda name, shape, dt: st.enter_context(nc.psum_tensor(name, shape, dt))
        dve = lambda fn, r, w: T.op("dve", fn, r=r, w=w)
        act = lambda fn, r, w: T.op("act", fn, r=r, w=w)
        pool = lambda fn, r, w: T.op("pool", fn, r=r, w=w)

        def mm(out, lhsT, rhs, start, stop, r, w, sig=None, **kw):
            T.op("pe", lambda e: e.matmul(out, lhsT=lhsT, rhs=rhs, start=start, stop=stop, **kw), r=r, w=w,
                 sig=(stop if sig is None else sig))

        def tr(out, in_, ident, r, w, sig=True):
            T.op("pe", lambda e: e.transpose(out=out, in_=in_, identity=ident), r=r, w=w, sig=sig)

        ps_t = pst("ps_t", [128, 8, 128], BF16)
        ps_z = pst("ps_z", [128, 512], F32)
        ps_s = [pst("ps_s0", [128, 512], F32), pst("ps_s1", [128, 512], F32)]
        ps_f = pst("ps_f", [128, 512], F32)
        ps_y = pst("ps_y", [128, 512], F32)
        ps_y2 = pst("ps_y2", [128, 512], F32)
        ps_zs = pst("ps_zs", [128, 512], F32)

        Wsb = sbt("Wsb", [128, 8, INW], BF16)
        Wout = sbt("Wout", [128, 12, D], BF16)
        Wglu = sbt("Wglu", [128, 4, 512], BF16)
        Kstrip = sbt("Kstrip", [128, 4, 2, 15, 16], BF16)
        Wssm = sbt("Wssm", [128, 4, 8, 2, 128], BF16)
        Vf = sbt("Vf", [128, 16, 2, 256], BF16)
        mcT = sbt("mcT", [128, 16, NCH], F32)
        msT = sbt("msT", [128, 16, NCH], F32)
        Hb = sbt("Hb", [128, 16, 2, NCH], BF16)
        carry = sbt("carry", [128, 16, 2], F32)
        magA = sbt("magA", [128, 16], F32)
        nms1 = sbt("nms1", [128, 16], F32)
        mkT = sbt("mkT", [128, 4, NMEM], BF16)
        mv_aug = sbt("mv_aug", [128, 2, 4, 129], BF16)
        ident_b = sbt("ident_b", [128, 128], BF16)
        ident_f = sbt("ident_f", [128, 128], F32)
        mask_cur = sbt("mask_cur", [128, 4, 128], BF16)
        mask_prev = sbt("mask_prev", [128, 4, 128], BF16)
        dmask = sbt("dmask", [128, 32], F32)
        cosT = sbt("cosT", [128, NB, 32], F32)
        sinT = sbt("sinT", [128, NB, 32], F32)
        g10 = sbt("g10", [128, 10, 64], F32)
        gq_bc = sbt("gq_bc", [128, 64], F32)
        gk_bc = sbt("gk_bc", [128, 64], F32)
        esink = sbt("esink", [128, 8], F32)
        gxk_bc = sbt("gxk_bc", [128, 128], F32)
        gxq_col = sbt("gxq_col", [128, 1], F32)
        bglu_col = sbt("bglu_col", [128, 4], F32)
        d_col = sbt("d_col", [128, 4], F32)
        g_col = sbt("g_col", [128, 8], F32)
        gm_col = sbt("gm_col", [128, 8], F32)
        ldT = sbt("ldT", [32, 128], F32)
        nvec = sbt("nvec", [128, 9], F32)
        kvec = sbt("kvec", [128, NCH], F32)
        dummy = sbt("dummy_t", [128, 4], F32)

        ARW = 14600
        arena = sbt("arena", [128, ARW], F32)
        aoff = {"main": 0, "setup": 0, "setupB": 0}
        akeys = {"main": [], "setup": [], "setupB": []}

        def carve(phase, name, shape, dt):
            n = int(np.prod(shape))
            words = n if dt in (F32, I32) else (n + 1) // 2
            o = aoff[phase]
            aoff[phase] = o + words
            assert aoff[phase] <= ARW, (phase, name, aoff[phase])
            v = arena[:, o:o + words]
            if dt != F32:
                v = v.bitcast(dt)
                if dt == BF16 and n % 2:
                    v = v[:, 0:n]
            if len(shape) > 1:
                names = " ".join("d%d" % i for i in range(len(shape)))
                v = v.rearrange("p (%s) -> p %s" % (names, names), **{"d%d" % i: shape[i] for i in range(1, len(shape))})
            if name not in akeys[phase]:
                akeys[phase].append(name)
            return v

        M = lambda name, shape, dt: carve("main", name, shape, dt)
        U = lambda name, shape, dt: carve("setup", name, shape, dt)
        UB = lambda name, shape, dt: carve("setupB", name, shape, dt)

        uT = M("uT", [4, SBT], BF16)
        gT = M("gT", [4, SBT], BF16)
        mixT = M("mixT", [12, SBT], BF16)
        xt = [M("xt0", [D], F32), M("xt1", [D], F32)]
        xr = [M("xr0", [D], F32)]
        hb = M("hb", [D], BF16)
        junk = hb
        hT = M("hT", [8, 128], BF16)
        st1 = M("st1", [16], F32)
        qk = M("qk", [10, 64], F32)
        rt1 = M("rt1", [10, 64], F32)
        rt2 = M("rt2", [10, 64], F32)
        sq = rt1
        qr = M("qr", [10, 64], BF16)
        qT = M("qT", [4, 128], BF16)
        kT = [M("kT0", [128], BF16), M("kT1", [128], BF16)]
        vaug = [M("vaug0", [2, 65], BF16), M("vaug1", [2, 65], BF16)]
        pTm = [M("pTm0", [512], BF16), M("pTm1", [512], BF16)]
        pX = pTm
        gate_a = M("gate_a", [512], BF16)
        gate_x = M("gate_x", [512], BF16)
        mix_a = M("mix_a", [512], BF16)
        mix_c = mix_a
        xq_f = rt2.rearrange("p h d -> p (h d)")[:, 0:512].rearrange("p (h d) -> p h d", h=4)
        xq_b = M("xq_b", [4, 128], BF16)
        xqT = M("xqT", [4, 128], BF16)
        den = M("den", [8], F32)
        zt = M("zt", [4, NCH], F32)
        sct = M("sct", [4, NCH], F32)
        Gs = M("Gs", [2, NCH], F32)
        Hf = M("Hf", [2, NCH], F32)
        init2 = M("init2", [2], F32)
        y2s = rt2.rearrange("p h d -> p (h d)")[:, 0:256]
        ysum = rt2.rearrange("p h d -> p (h d)")[:, 256:512]
        y2k = rt1.rearrange("p h d -> p (h d)")[:, 0:512].bitcast(BF16).rearrange("p (t c) -> p t c", t=8)
        sig_t = qr.rearrange("p h d -> p (h d)")[:, 0:512]
        print("arena main words", aoff["main"])

        def load_T(dst, src, n, wkey):
            T.dma(ldT[0:n, :], src, w=["ldT"])
            tr(ps_z[:, 0:n], ldT[0:n, :], ident_f[0:n, 0:n], ["ldT", "ident_f"], ["ps_z"])
            dve(lambda e: e.tensor_copy(out=dst, in_=ps_z[:, 0:n]), ["ps_z"], [wkey])

        def sin_of(dst, src, n, shift, rk, wk, tmp=None):
            for c0 in range(0, n, 256):
                c1 = min(n, c0 + 256)
                m = c1 - c0
                ki, kf, yy = s_ki[:, 0:m], s_kf[:, 0:m], s_y[:, 0:m]
                sr = src[:, c0:c1]
                dve(lambda e: e.tensor_scalar(out=ki, in0=sr, scalar1=float(shift), scalar2=float(1.0 / TWO_PI), op0=ALU.add, op1=ALU.mult),
                    rk, ["sr_ki"])
                dve(lambda e: e.tensor_copy(out=kf, in_=ki), ["sr_ki"], ["sr_kf"])
                dve(lambda e: e.tensor_scalar(out=yy, in0=sr, scalar1=float(shift), scalar2=None, op0=ALU.add), rk, ["sr_y"])
                dve(lambda e: e.scalar_tensor_tensor(out=yy, in0=kf, scalar=float(-CW1), in1=yy, op0=ALU.mult, op1=ALU.add),
                    ["sr_kf", "sr_y"], ["sr_y"])
                dve(lambda e: e.scalar_tensor_tensor(out=yy, in0=kf, scalar=float(-CW2), in1=yy, op0=ALU.mult, op1=ALU.add),
                    ["sr_kf", "sr_y"], ["sr_y"])
                dve(lambda e: e.tensor_scalar(out=yy, in0=yy, scalar1=float(math.pi), scalar2=float(-math.pi), op0=ALU.min, op1=ALU.max),
                    ["sr_y"], ["sr_y"])
                act(lambda e: e.activation(out=dst[:, c0:c1], in_=yy, func=AF.Sin), ["sr_y"], wk)

        def rsqrt_of(dst, src, scale, rk, wk):
            dve(lambda e: e.tensor_scalar(out=dst, in0=src, scalar1=float(scale), scalar2=float(EPS), op0=ALU.mult, op1=ALU.add), rk, wk)
            act(lambda e: e.activation(out=dst, in_=dst, func=AF.Sqrt), wk, wk)
            dve(lambda e: e.reciprocal(out=dst, in_=dst), wk, wk)

        def join(keys, e="dve"):
            T.op(e, lambda en: en.memset(dummy[:, 0:1], 0.0), r=[], w=list(keys) + ["dummy"])

        s_ki = U("sr_ki", [256], I32); s_kf = U("sr_kf", [256], F32); s_y = U("sr_y", [256], F32)
        s_ang = U("s_ang", [256], F32)
        UB("sr_ki", [256], I32); UB("sr_kf", [256], F32); UB("sr_y", [256], F32); UB("s_ang", [256], F32)
        ones_f = U("ones_f", [128], F32)
        stage = [U("stage0", [1024], F32), U("stage1", [1024], F32)]
        s_posi = U("s_posi", [128], I32); s_posf = U("s_posf", [128], F32)
        s_posT = U("s_posT", [32], F32); s_inv = U("s_inv", [32], F32)
        s_memx = U("s_memx", [D], F32); s_memn = U("s_memn", [D], BF16); s_memT = U("s_memT", [8, NMEM], BF16)
        s_mkf = U("s_mkf", [4, 128], F32); s_mkb = U("s_mkb", [4, 128], BF16)
        s_smA = U("s_smA", [16], F32)
        s_sm = UB("s_sm", [16, 12], F32)
        s_e9 = UB("s_e9", [16, 9], F32); s_a9 = UB("s_a9", [16, 9], F32); s_mag9 = UB("s_mag9", [16, 9], F32)
        s_cos9 = UB("s_cos9", [16, 9], F32); s_sin9 = UB("s_sin9", [16, 9], F32)
        s_ar9 = UB("s_ar9", [16, 9], F32); s_ai9 = UB("s_ai9", [16, 9], F32)
        s_Br = UB("s_Br", [16, 16], F32); s_Bi = UB("s_Bi", [16, 16], F32)
        s_bbr = UB("s_bbr", [16, 16], F32); s_bbi = UB("s_bbi", [16, 16], F32); s_bt = UB("s_bt", [16, 16], F32)
        s_Cst = UB("s_Cst", [2, 128], F32)
        s_Cr = UB("s_Cr", [16, 16], F32); s_Ci = UB("s_Ci", [16, 16], F32)
        s_CAr = UB("s_CAr", [4, 9, 16], F32); s_CAi = UB("s_CAi", [4, 9, 16], F32); s_CAt = UB("s_CAt", [4, 9, 16], F32)
        s_CAzr = UB("s_CAzr", [4, 2, 8, 16], F32); s_CAzi = UB("s_CAzi", [4, 2, 8, 16], F32)
        s_Bzr = UB("s_Bzr", [4, 128], F32); s_Bzi = UB("s_Bzi", [4, 128], F32)
        s_Kt = UB("s_Kt", [2, 8, 16], F32); s_Kd = UB("s_Kd", [32], F32)
        s_Xr = UB("s_Xr", [4, 8, 16], F32); s_Xi = UB("s_Xi", [4, 8, 16], F32); s_Xt = UB("s_Xt", [4, 8, 16], F32)
        s_Xzr = UB("s_Xzr", [8, 4, 32], BF16); s_Xzi = UB("s_Xzi", [8, 4, 32], BF16)
        s_lg2 = UB("s_lg2", [2], F32)
        print("arena setup words", aoff["setup"], aoff["setupB"])
        ALLK = list(dict.fromkeys(akeys["main"] + akeys["setup"] + akeys["setupB"]))

        pool(lambda e: e.memset(ones_f, 1.0), [], ["ones_f"])
        pool(lambda e: e.memset(dummy[:], 0.0), [], ["dummy"])
        pool(lambda e: e.affine_select(out=ident_f[:], in_=ones_f, pattern=[[-1, 128]], compare_op=ALU.is_equal, fill=0.0,
                                       base=0, channel_multiplier=1), ["ones_f"], ["ident_f"])
        pool(lambda e: e.affine_select(out=ident_b[:], in_=ones_f, pattern=[[-1, 128]], compare_op=ALU.is_equal, fill=0.0,
                                       base=0, channel_multiplier=1), ["ones_f"], ["ident_b"])
        for j in range(4):
            pool(lambda e: e.affine_select(out=mask_cur[:, j, :], in_=ones_f, pattern=[[1, 128]], compare_op=ALU.is_ge, fill=0.0,
                                           base=0, channel_multiplier=-1), ["ones_f"], ["mask_cur"])
            pool(lambda e: e.affine_select(out=mask_prev[:, j, :], in_=ones_f, pattern=[[-1, 128]], compare_op=ALU.is_ge, fill=0.0,
                                           base=-1, channel_multiplier=1), ["ones_f"], ["mask_prev"])
        pool(lambda e: e.tensor_tensor(out=dmask[:], in0=ident_f[:, 0:32], in1=ident_f[:, 32:64], op=ALU.add), ["ident_f"], ["dmask"])
        pool(lambda e: e.tensor_tensor(out=dmask[:], in0=dmask[:], in1=ident_f[:, 64:96], op=ALU.add), ["ident_f", "dmask"], ["dmask"])
        pool(lambda e: e.tensor_tensor(out=dmask[:], in0=dmask[:], in1=ident_f[:, 96:128], op=ALU.add), ["ident_f", "dmask"], ["dmask"])
        pool(lambda e: e.iota(nvec[:], pattern=[[1, 9]], base=0, channel_multiplier=0, allow_small_or_imprecise_dtypes=True), [], ["nvec"])
        pool(lambda e: e.iota(kvec[:], pattern=[[1, NCH]], base=0, channel_multiplier=0, allow_small_or_imprecise_dtypes=True), [], ["kvec"])
        pool(lambda e: e.memset(mv_aug[:], 1.0), [], ["mv_aug"])
        pool(lambda e: e.memset(Kstrip[:], 0.0), [], ["Kstrip"])

        for bb in range(0, NB, 32):
            nbk = min(32, NB - bb)
            T.dma(s_posi[0:nbk, :], pos_d[bb * 128:(bb + nbk) * 128].rearrange("(b p) -> b p", p=128), w=["s_posi"])
            dve(lambda e: e.tensor_copy(out=s_posf[0:nbk, :], in_=s_posi[0:nbk, :]), ["s_posi"], ["s_posf"])
            tr(ps_z[:, 0:nbk], s_posf[0:nbk, :], ident_f[0:nbk, 0:nbk], ["s_posf", "ident_f"], ["ps_z"])
            dve(lambda e: e.tensor_copy(out=s_posT[:, 0:nbk], in_=ps_z[:, 0:nbk]), ["ps_z"], ["s_posT"])
        pool(lambda e: e.iota(s_inv, pattern=[[1, 32]], base=0, channel_multiplier=0, allow_small_or_imprecise_dtypes=True), [], ["s_inv"])
        act(lambda e: e.activation(out=s_inv, in_=s_inv, func=AF.Exp, scale=float(-math.log(10000.0) / 32.0)), ["s_inv"], ["s_inv"])
        for b0 in range(0, NB, 8):
            nb8 = min(8, NB - b0)
            ang3 = s_ang[:, 0:nb8 * 32].rearrange("p (b j) -> p b j", j=32)
            dve(lambda e: e.tensor_tensor(out=ang3, in0=s_posT[:, b0:b0 + nb8, None].to_broadcast([128, nb8, 32]),
                                          in1=s_inv[:, None, :].to_broadcast([128, nb8, 32]), op=ALU.mult), ["s_posT", "s_inv"], ["s_ang"])
            sin_of(sinT[:, b0:b0 + nb8, :].rearrange("p b j -> p (b j)"), s_ang[:, 0:nb8 * 32], nb8 * 32, 0.0, ["s_ang"], ["sinT"])
            sin_of(cosT[:, b0:b0 + nb8, :].rearrange("p b j -> p (b j)"), s_ang[:, 0:nb8 * 32], nb8 * 32, math.pi / 2, ["s_ang"], ["cosT"])

        def mark(name):
            if stop == name:
                raise StopBuild()

        def layer_setup(l):
            join(ALLK)
            load_T(g_col[:], norm_g_d[l].rearrange("(k p) -> k p", p=128), 8, "g_col")
            load_T(gm_col[:], mng_d[l].rearrange("(k p) -> k p", p=128), 8, "gm_col")
            load_T(bglu_col[:], bglu_d[l].rearrange("(k p) -> k p", p=128), 4, "bglu_col")
            load_T(d_col[:], dskip_d[l].rearrange("(k p) -> k p", p=128), 4, "d_col")
            load_T(gxq_col[:], xqg_d[l].rearrange("(k p) -> k p", p=128), 1, "gxq_col")
            dve(lambda e: e.tensor_scalar(out=gxq_col[:], in0=gxq_col[:], scalar1=float(1.0 / math.sqrt(128.0)), scalar2=None, op0=ALU.mult),
                ["gxq_col"], ["gxq_col"])
            T.dma(gq_bc[:], qg_d[l].partition_broadcast(128), w=["gq_bc"])
            T.dma(gk_bc[:], kg_d[l].partition_broadcast(128), w=["gk_bc"])
            T.dma(gxk_bc[:], xkg_d[l].partition_broadcast(128), w=["gxk_bc"])
            T.dma(esink[:], sinks_d[l].partition_broadcast(128), w=["esink"])
            act(lambda e: e.activation(out=esink[:], in_=esink[:], func=AF.Exp), ["esink"], ["esink"])
            dve(lambda e: e.tensor_copy(out=g10[:, 0:8, :], in_=gq_bc[:, None, :].to_broadcast([128, 8, 64])), ["gq_bc"], ["g10"])
            dve(lambda e: e.tensor_copy(out=g10[:, 8:10, :], in_=gk_bc[:, None, :].to_broadcast([128, 2, 64])), ["gk_bc", "g10"], ["g10"])

            mark('vecs')
            for mt in range(2):
                T.dma(s_memx, mem_d[mt * 128:(mt + 1) * 128, :], w=["s_memx"])
                act(lambda e: e.activation(out=s_memn, in_=s_memx, func=AF.Square, accum_out=s_smA[:, 0:1]), ["s_memx"], ["s_memn", "s_smA"])
                rsqrt_of(s_smA[:, 1:2], s_smA[:, 0:1], 1.0 / D, ["s_smA"], ["s_smA"])
                act(lambda e: e.activation(out=s_memn, in_=s_memx, func=AF.Copy, scale=s_smA[:, 1:2]), ["s_memx", "s_smA"], ["s_memn"])
                for kc in range(8):
                    tr(ps_t[:, kc, :], s_memn[:, kc * 128:(kc + 1) * 128], ident_b[:], ["s_memn", "ident_b"], ["ps_t"], sig=(kc == 7))
                dve(lambda e: e.tensor_copy(out=s_memT[:, :, mt * 128:(mt + 1) * 128], in_=ps_t[:]), ["ps_t"], ["s_memT"])
            Wkv = Wsb[:, :, 0:D]
            for kc in range(8):
                sg_ = stage[kc % 2]
                T.dma(sg_, wkv_d[l, kc * 128:(kc + 1) * 128, :], w=["stage%d" % (kc % 2)])
                eng = "act" if kc % 2 == 0 else "pool"
                if eng == "act":
                    act(lambda e: e.activation(out=Wkv[:, kc, :], in_=sg_, func=AF.Copy, scale=gm_col[:, kc:kc + 1]),
                        ["stage%d" % (kc % 2), "gm_col"], ["Wsb"])
                else:
                    pool(lambda e: e.tensor_scalar(out=Wkv[:, kc, :], in0=sg_, scalar1=gm_col[:, kc:kc + 1], scalar2=None, op0=ALU.mult),
                         ["stage%d" % (kc % 2), "gm_col"], ["Wsb"])
            for mt in range(2):
                for n in range(2):
                    for kc in range(8):
                        mm(ps_z[:], s_memT[:, kc, mt * 128:(mt + 1) * 128], Wkv[:, kc, n * 512:(n + 1) * 512], kc == 0, kc == 7,
                           ["s_memT", "Wsb"], ["ps_z"])
                    if n == 0:
                        act(lambda e: e.copy(out=s_mkf.rearrange("p h d -> p (h d)"), in_=ps_z[:]), ["ps_z"], ["s_mkf"])
                        dve(lambda e: e.tensor_tensor(out=s_memx[:, 0:512], in0=s_mkf.rearrange("p h d -> p (h d)"),
                                                      in1=s_mkf.rearrange("p h d -> p (h d)"), op=ALU.mult), ["s_mkf"], ["s_memx"])
                        dve(lambda e: e.tensor_reduce(out=s_smA[:, 4:8], in_=s_memx[:, 0:512].rearrange("p (h d) -> p h d", h=4),
                                                      axis=AX.X, op=ALU.add), ["s_memx"], ["s_smA"])
                        rsqrt_of(s_smA[:, 8:12], s_smA[:, 4:8], 1.0 / 128, ["s_smA"], ["s_smA"])
                        dve(lambda e: e.tensor_tensor(out=s_mkf, in0=s_mkf, in1=s_smA[:, 8:12, None].to_broadcast([128, 4, 128]), op=ALU.mult),
                            ["s_mkf", "s_smA"], ["s_mkf"])
                        dve(lambda e: e.tensor_tensor(out=s_mkb, in0=s_mkf, in1=gxk_bc[:, None, :].to_broadcast([128, 4, 128]), op=ALU.mult),
                            ["s_mkf", "gxk_bc"], ["s_mkb"])
                        for h in range(4):
                            tr(ps_t[:, h, :], s_mkb[:, h, :], ident_b[:], ["s_mkb", "ident_b"], ["ps_t"], sig=(h == 3))
                        dve(lambda e: e.tensor_scalar(out=mkT[:, :, mt * 128:(mt + 1) * 128], in0=ps_t[:, 0:4, :], scalar1=gxq_col[:, 0:1],
                                                      scalar2=None, op0=ALU.mult), ["ps_t", "gxq_col"], ["mkT"])
                    else:
                        act(lambda e: e.copy(out=mv_aug[:, mt, :, 0:128], in_=ps_z[:].rearrange("p (h d) -> p h d", h=4)), ["ps_z"], ["mv_aug"])

            mark('memkv')
            cnt = 0
            for kc in range(8):
                for (c0, c1) in ((0, 1024), (1024, 2048), (2048, 3072), (3072, INW)):
                    sg_ = stage[cnt % 2]; sk = "stage%d" % (cnt % 2)
                    T.dma(sg_[:, 0:c1 - c0], w_in_d[l, kc * 128:(kc + 1) * 128, c0:c1], w=[sk])
                    if cnt % 2 == 0:
                        act(lambda e: e.activation(out=Wsb[:, kc, c0:c1], in_=sg_[:, 0:c1 - c0], func=AF.Copy, scale=g_col[:, kc:kc + 1]),
                            [sk, "g_col"], ["Wsb"])
                    else:
                        pool(lambda e: e.tensor_scalar(out=Wsb[:, kc, c0:c1], in0=sg_[:, 0:c1 - c0], scalar1=g_col[:, kc:kc + 1], scalar2=None,
                                                       op0=ALU.mult), [sk, "g_col"], ["Wsb"])
                    cnt += 1
            for kc in range(12):
                sg_ = stage[cnt % 2]; sk = "stage%d" % (cnt % 2)
                T.dma(sg_, wout_d[l, kc * 128:(kc + 1) * 128, :], w=[sk])
                if cnt % 2 == 0:
                    act(lambda e: e.copy(out=Wout[:, kc, :], in_=sg_), [sk], ["Wout"])
                else:
                    pool(lambda e: e.tensor_copy(out=Wout[:, kc, :], in_=sg_), [sk], ["Wout"])
                cnt += 1
            for kc in range(4):
                sg_ = stage[cnt % 2]; sk = "stage%d" % (cnt % 2)
                T.dma(sg_[:, 0:512], wglu_d[l, kc * 128:(kc + 1) * 128, :], w=[sk])
                dve(lambda e: e.tensor_copy(out=Wglu[:, kc, :], in_=sg_[:, 0:512]), [sk], ["Wglu"])
                cnt += 1

            mark('weights')
            join(ALLK)
            lr = s_sm[:, :, 2]; li = s_sm[:, :, 3]; dtv = s_sm[:, :, 4]; lrdt = s_sm[:, :, 5]; lidt = s_sm[:, :, 6]
            t7 = s_sm[:, :, 7]; t8 = s_sm[:, :, 8]; fr = s_sm[:, :, 9]; fi = s_sm[:, :, 10]; t11 = s_sm[:, :, 11]
            load_T(lr, lre_d[l].rearrange("(pr two) p -> pr (two p)", two=2), 16, "s_sm")
            load_T(li, lim_d[l].rearrange("(pr two) p -> pr (two p)", two=2), 16, "s_sm")
            T.dma(s_lg2[0:16, :], ldt_d[l].rearrange("(pr two) -> pr two", two=2), w=["s_lg2"])
            dve(lambda e: e.tensor_copy(out=ldT[0:16, :].rearrange("q (two p) -> q two p", two=2),
                                        in_=s_lg2[0:16, :, None].to_broadcast([16, 2, 64])), ["s_lg2"], ["ldT"])
            tr(ps_z[:, 0:16], ldT[0:16, :], ident_f[0:16, 0:16], ["ldT", "ident_f"], ["ps_z"])
            act(lambda e: e.activation(out=dtv, in_=ps_z[:, 0:16], func=AF.Exp), ["ps_z"], ["s_sm"])
            dve(lambda e: e.tensor_tensor(out=lrdt, in0=lr, in1=dtv, op=ALU.mult), ["s_sm"], ["s_sm"])
            dve(lambda e: e.tensor_tensor(out=lidt, in0=li, in1=dtv, op=ALU.mult), ["s_sm"], ["s_sm"])
            dve(lambda e: e.tensor_tensor(out=s_e9, in0=lrdt[:, :, None].to_broadcast([128, 16, 9]), in1=nvec[:, None, :].to_broadcast([128, 16, 9]),
                                          op=ALU.mult), ["s_sm", "nvec"], ["s_e9"])
            dve(lambda e: e.tensor_tensor(out=s_a9, in0=lidt[:, :, None].to_broadcast([128, 16, 9]), in1=nvec[:, None, :].to_broadcast([128, 16, 9]),
                                          op=ALU.mult), ["s_sm", "nvec"], ["s_a9"])
            act(lambda e: e.activation(out=s_mag9, in_=s_e9, func=AF.Exp), ["s_e9"], ["s_mag9"])
            f2 = lambda v: v.rearrange("p a b -> p (a b)")
            sin_of(f2(s_sin9), f2(s_a9), 144, 0.0, ["s_a9"], ["s_sin9"])
            sin_of(f2(s_cos9), f2(s_a9), 144, math.pi / 2, ["s_a9"], ["s_cos9"])
            dve(lambda e: e.tensor_tensor(out=s_ar9, in0=s_mag9, in1=s_cos9, op=ALU.mult), ["s_mag9", "s_cos9"], ["s_ar9"])
            dve(lambda e: e.tensor_tensor(out=s_ai9, in0=s_mag9, in1=s_sin9, op=ALU.mult), ["s_mag9", "s_sin9"], ["s_ai9"])
            dve(lambda e: e.tensor_copy(out=magA[:], in_=s_mag9[:, :, 8]), ["s_mag9"], ["magA"])
            tmp16 = (s_ki[:, 0:16], s_kf[:, 0:16])
            dve(lambda e: e.tensor_scalar(out=tmp16[0], in0=lidt, scalar1=float(8.0 / TWO_PI), scalar2=None, op0=ALU.mult), ["s_sm"], ["sr_ki"])
            dve(lambda e: e.tensor_copy(out=tmp16[1], in_=tmp16[0]), ["sr_ki"], ["sr_kf"])
            dve(lambda e: e.tensor_scalar(out=t7, in0=lidt, scalar1=8.0, scalar2=None, op0=ALU.mult), ["s_sm"], ["s_sm"])
            dve(lambda e: e.scalar_tensor_tensor(out=t7, in0=tmp16[1], scalar=float(-CW1), in1=t7, op0=ALU.mult, op1=ALU.add), ["sr_kf", "s_sm"], ["s_sm"])
            dve(lambda e: e.scalar_tensor_tensor(out=t7, in0=tmp16[1], scalar=float(-CW2), in1=t7, op0=ALU.mult, op1=ALU.add), ["sr_kf", "s_sm"], ["s_sm"])
            for p0 in range(0, 16, 4):
                ta3 = s_ang[:, 0:4 * NCH].rearrange("p (a k) -> p a k", k=NCH)
                dve(lambda e: e.tensor_tensor(out=ta3, in0=t7[:, p0:p0 + 4, None].to_broadcast([128, 4, NCH]),
                                              in1=kvec[:, None, :].to_broadcast([128, 4, NCH]), op=ALU.mult), ["s_sm", "kvec"], ["s_ang"])
                sin_of(msT[:, p0:p0 + 4, :].rearrange("p a k -> p (a k)"), s_ang[:, 0:4 * NCH], 4 * NCH, 0.0, ["s_ang"], ["msT"])
                sin_of(mcT[:, p0:p0 + 4, :].rearrange("p a k -> p (a k)"), s_ang[:, 0:4 * NCH], 4 * NCH, math.pi / 2, ["s_ang"], ["mcT"])
            dve(lambda e: e.tensor_scalar(out=nms1[:], in0=msT[:, :, 1], scalar1=-1.0, scalar2=None, op0=ALU.mult), ["msT"], ["nms1"])
            dve(lambda e: e.tensor_tensor(out=t8, in0=lr, in1=lr, op=ALU.mult), ["s_sm"], ["s_sm"])
            dve(lambda e: e.tensor_tensor(out=t11, in0=li, in1=li, op=ALU.mult), ["s_sm"], ["s_sm"])
            dve(lambda e: e.tensor_tensor(out=t8, in0=t8, in1=t11, op=ALU.add), ["s_sm"], ["s_sm"])
            dve(lambda e: e.reciprocal(out=t8, in_=t8), ["s_sm"], ["s_sm"])
            dve(lambda e: e.tensor_scalar(out=t11, in0=s_ar9[:, :, 1], scalar1=-1.0, scalar2=None, op0=ALU.add), ["s_ar9"], ["s_sm"])
            dve(lambda e: e.tensor_tensor(out=fr, in0=t11, in1=lr, op=ALU.mult), ["s_sm"], ["s_sm"])
            dve(lambda e: e.tensor_tensor(out=t7, in0=s_ai9[:, :, 1], in1=li, op=ALU.mult), ["s_sm", "s_ai9"], ["s_sm"])
            dve(lambda e: e.tensor_tensor(out=fr, in0=fr, in1=t7, op=ALU.add), ["s_sm"], ["s_sm"])
            dve(lambda e: e.tensor_tensor(out=fr, in0=fr, in1=t8, op=ALU.mult), ["s_sm"], ["s_sm"])
            dve(lambda e: e.tensor_tensor(out=fi, in0=s_ai9[:, :, 1], in1=lr, op=ALU.mult), ["s_sm", "s_ai9"], ["s_sm"])
            dve(lambda e: e.tensor_tensor(out=t7, in0=t11, in1=li, op=ALU.mult), ["s_sm"], ["s_sm"])
            dve(lambda e: e.tensor_tensor(out=fi, in0=fi, in1=t7, op=ALU.subtract), ["s_sm"], ["s_sm"])
            dve(lambda e: e.tensor_tensor(out=fi, in0=fi, in1=t8, op=ALU.mult), ["s_sm"], ["s_sm"])
            mark('ssm_tabs')
            T.dma(s_Br, bre_d[l].rearrange("(pr two) p c -> (two p) pr c", two=2), w=["s_Br"])
            T.dma(s_Bi, bim_d[l].rearrange("(pr two) p c -> (two p) pr c", two=2), w=["s_Bi"])
            frb = fr[:, :, None].to_broadcast([128, 16, 16]); fib = fi[:, :, None].to_broadcast([128, 16, 16])
            dve(lambda e: e.tensor_tensor(out=s_bbr, in0=s_Br, in1=frb, op=ALU.mult), ["s_Br", "s_sm"], ["s_bbr"])
            dve(lambda e: e.tensor_tensor(out=s_bt, in0=s_Bi, in1=fib, op=ALU.mult), ["s_Bi", "s_sm"], ["s_bt"])
            dve(lambda e: e.tensor_tensor(out=s_bbr, in0=s_bbr, in1=s_bt, op=ALU.subtract), ["s_bbr", "s_bt"], ["s_bbr"])
            dve(lambda e: e.tensor_tensor(out=s_bbi, in0=s_Bi, in1=frb, op=ALU.mult), ["s_Bi", "s_sm"], ["s_bbi"])
            dve(lambda e: e.tensor_tensor(out=s_bt, in0=s_Br, in1=fib, op=ALU.mult), ["s_Br", "s_sm", "s_bbr"], ["s_bt"])
            dve(lambda e: e.tensor_tensor(out=s_bbi, in0=s_bbi, in1=s_bt, op=ALU.add), ["s_bbi", "s_bt"], ["s_bbi"])
            for (cd, dst, dk) in ((cre_d, s_Cr, "s_Cr"), (cim_d, s_Ci, "s_Ci")):
                for pr in range(16):
                    T.dma(s_Cst[(pr % 8) * 16:(pr % 8) * 16 + 16, pr // 8, :].rearrange("c (two p) -> c two p", two=2),
                          cd[l, 2 * pr:2 * pr + 2].rearrange("two c p -> c two p"), w=["s_Cst"])
                for hh in range(2):
                    tr(ps_z[:, hh * 128:(hh + 1) * 128], s_Cst[:, hh, :], ident_f[:], ["s_Cst", "ident_f"], ["ps_z"], sig=(hh == 1))
                dve(lambda e: e.tensor_copy(out=dst.rearrange("p a c -> p (a c)"), in_=ps_z[:, 0:256]), ["ps_z"], [dk])
            mark('ssm_BC')
            for q in range(4):
                prs = slice(4 * q, 4 * q + 4)
                arb = s_ar9[:, prs, :, None].to_broadcast([128, 4, 9, 16]); aib = s_ai9[:, prs, :, None].to_broadcast([128, 4, 9, 16])
                crb = s_Cr[:, prs, None, :].to_broadcast([128, 4, 9, 16]); cib = s_Ci[:, prs, None, :].to_broadcast([128, 4, 9, 16])
                dve(lambda e: e.tensor_tensor(out=s_CAr, in0=crb, in1=arb, op=ALU.mult), ["s_Cr", "s_ar9"], ["s_CAr"])
                dve(lambda e: e.tensor_tensor(out=s_CAt, in0=cib, in1=aib, op=ALU.mult), ["s_Ci", "s_ai9"], ["s_CAt"])
                dve(lambda e: e.tensor_tensor(out=s_CAr, in0=s_CAr, in1=s_CAt, op=ALU.subtract), ["s_CAr", "s_CAt"], ["s_CAr"])
                dve(lambda e: e.tensor_tensor(out=s_CAi, in0=crb, in1=aib, op=ALU.mult), ["s_Cr", "s_ai9"], ["s_CAi"])
                dve(lambda e: e.tensor_tensor(out=s_CAt, in0=cib, in1=arb, op=ALU.mult), ["s_Ci", "s_ar9", "s_CAr"], ["s_CAt"])
                dve(lambda e: e.tensor_tensor(out=s_CAi, in0=s_CAi, in1=s_CAt, op=ALU.add), ["s_CAi", "s_CAt"], ["s_CAi"])
                pool(lambda e: e.memset(Vf[:, prs, :, :], 0.0), [], ["Vf"])
                for two in range(2):
                    rows = slice(64 * two, 64 * two + 64)
                    dve(lambda e: e.tensor_copy(out=Vf[rows, prs, 0, two * 128:(two + 1) * 128].rearrange("p a (t c) -> p a t c", c=16),
                                                in_=s_CAr[rows, :, 1:9, :]), ["s_CAr", "Vf"], ["Vf"])
                    dve(lambda e: e.tensor_scalar(out=Vf[rows, prs, 1, two * 128:(two + 1) * 128].rearrange("p a (t c) -> p a t c", c=16),
                                                  in0=s_CAi[rows, :, 1:9, :], scalar1=-1.0, scalar2=None, op0=ALU.mult), ["s_CAi", "Vf"], ["Vf"])
                pool(lambda e: e.memset(s_CAzr, 0.0), [], ["s_CAzr"])
                pool(lambda e: e.memset(s_CAzi, 0.0), [], ["s_CAzi"])
                pool(lambda e: e.memset(s_Bzr, 0.0), [], ["s_Bzr"])
                pool(lambda e: e.memset(s_Bzi, 0.0), [], ["s_Bzi"])
                for two in range(2):
                    rows = slice(64 * two, 64 * two + 64)
                    dve(lambda e: e.tensor_copy(out=s_CAzr[rows, :, two, :, :], in_=s_CAr[rows, :, 0:8, :]), ["s_CAr", "s_CAzr"], ["s_CAzr"])
                    dve(lambda e: e.tensor_scalar(out=s_CAzi[rows, :, two, :, :], in0=s_CAi[rows, :, 0:8, :], scalar1=-1.0, scalar2=None, op0=ALU.mult),
                        ["s_CAi", "s_CAzi"], ["s_CAzi"])
                    for j in range(4):
                        c0 = 32 * j + 16 * two
                        dve(lambda e: e.tensor_copy(out=s_Bzr[rows, j, c0:c0 + 16], in_=s_bbr[rows, 4 * q + j, :]), ["s_bbr", "s_Bzr"], ["s_Bzr"])
                        dve(lambda e: e.tensor_copy(out=s_Bzi[rows, j, c0:c0 + 16], in_=s_bbi[rows, 4 * q + j, :]), ["s_bbi", "s_Bzi"], ["s_Bzi"])
                for j in range(4):
                    mm(ps_z[:, 0:256], s_Bzr[:, j, :], s_CAzr[:, j].rearrange("p a t c -> p (a t c)"), j == 0, False, ["s_Bzr", "s_CAzr"], ["ps_z"])
                    mm(ps_z[:, 0:256], s_Bzi[:, j, :], s_CAzi[:, j].rearrange("p a t c -> p (a t c)"), False, j == 3, ["s_Bzi", "s_CAzi"], ["ps_z"])
                dve(lambda e: e.tensor_copy(out=s_Kt.rearrange("p a t c -> p (a t c)"), in_=ps_z[:, 0:256]), ["ps_z"], ["s_Kt"])
                dve(lambda e: e.tensor_scalar(out=s_Kd, in0=dmask[:], scalar1=d_col[:, q:q + 1], scalar2=None, op0=ALU.mult), ["dmask", "d_col"], ["s_Kd"])
                dve(lambda e: e.tensor_tensor(out=s_Kt[:, :, 0, :], in0=s_Kt[:, :, 0, :], in1=s_Kd.rearrange("p (a c) -> p a c", a=2), op=ALU.add),
                    ["s_Kt", "s_Kd"], ["s_Kt"])
                dve(lambda e: e.tensor_copy(out=Kstrip[:, q, :, 7:15, :], in_=s_Kt), ["s_Kt"], ["Kstrip"])
                brb = s_bbr[:, prs, None, :].to_broadcast([128, 4, 8, 16]); bib = s_bbi[:, prs, None, :].to_broadcast([128, 4, 8, 16])
                ar8 = s_ar9[:, prs, 0:8, None].to_broadcast([128, 4, 8, 16]); ai8 = s_ai9[:, prs, 0:8, None].to_broadcast([128, 4, 8, 16])
                dve(lambda e: e.tensor_tensor(out=s_Xr, in0=brb, in1=ar8, op=ALU.mult), ["s_bbr", "s_ar9"], ["s_Xr"])
                dve(lambda e: e.tensor_tensor(out=s_Xt, in0=bib, in1=ai8, op=ALU.mult), ["s_bbi", "s_ai9"], ["s_Xt"])
                dve(lambda e: e.tensor_tensor(out=s_Xr, in0=s_Xr, in1=s_Xt, op=ALU.subtract), ["s_Xr", "s_Xt"], ["s_Xr"])
                dve(lambda e: e.tensor_tensor(out=s_Xi, in0=bib, in1=ar8, op=ALU.mult), ["s_bbi", "s_ar9"], ["s_Xi"])
                dve(lambda e: e.tensor_tensor(out=s_Xt, in0=brb, in1=ai8, op=ALU.mult), ["s_bbr", "s_ai9", "s_Xr"], ["s_Xt"])
                dve(lambda e: e.tensor_tensor(out=s_Xi, in0=s_Xi, in1=s_Xt, op=ALU.add), ["s_Xi", "s_Xt"], ["s_Xi"])
                pool(lambda e: e.memset(s_Xzr, 0.0), [], ["s_Xzr"])
                pool(lambda e: e.memset(s_Xzi, 0.0), [], ["s_Xzi"])
                for two in range(2):
                    rows = slice(64 * two, 64 * two + 64)
                    dve(lambda e: e.tensor_copy(out=s_Xzr[rows, :, :, 16 * two:16 * two + 16].rearrange("p n a c -> p a n c"), in_=s_Xr[rows]), ["s_Xr", "s_Xzr"], ["s_Xzr"])
                    dve(lambda e: e.tensor_copy(out=s_Xzi[rows, :, :, 16 * two:16 * two + 16].rearrange("p n a c -> p a n c"), in_=s_Xi[rows]), ["s_Xi", "s_Xzi"], ["s_Xzi"])
                for hh in range(2):
                    for s4 in range(4):
                        s_ = 4 * hh + s4
                        for ri, xz in ((0, s_Xzr), (1, s_Xzi)):
                            tr(ps_t[:, 2 * s4 + ri, :], xz[:, 7 - s_, :, :].rearrange("p a c -> p (a c)"), ident_b[:], ["s_Xzr", "s_Xzi", "ident_b"], ["ps_t"],
                               sig=(s4 == 3 and ri == 1))
                    dve(lambda e: e.tensor_copy(out=Wssm[:, q, 4 * hh:4 * hh + 4, :, :].rearrange("p s r m -> p (s r) m"), in_=ps_t[:]),
                        ["ps_t"], ["Wssm"])
            pool(lambda e: e.memset(carry[:], 0.0), [], ["carry"])
            pool(lambda e: e.memset(Hb[:], 0.0), [], ["Hb"])
            join(ALLK)
            pool(lambda e: e.memset(vaug[0][:, :, 64:65], 1.0), [], ["vaug0"])
            pool(lambda e: e.memset(vaug[1][:, :, 64:65], 1.0), [], ["vaug1"])

        def block_main(l, b, src):
            slot = b % 2
            bl = b % BPS
            cols = slice(bl * 128, (bl + 1) * 128)
            xk, kTk, vk, pk = "xt%d" % slot, "kT%d" % slot, "vaug%d" % slot, None
            x_t = xt[slot]
            T.dma(x_t, src[b * 128:(b + 1) * 128, :], r=[("res", b)], w=[xk])
            act(lambda e: e.activation(out=junk, in_=x_t, func=AF.Square, accum_out=st1[:, 0:1]), [xk], ["junk", "st1"])
            rsqrt_of(st1[:, 1:2], st1[:, 0:1], 1.0 / D, ["st1"], ["st1"])
            act(lambda e: e.activation(out=hb, in_=x_t, func=AF.Copy, scale=st1[:, 1:2]), [xk, "st1"], ["hb"])
            for kc in range(8):
                tr(ps_t[:, kc, :], hb[:, kc * 128:(kc + 1) * 128], ident_b[:], ["hb", "ident_b"], ["ps_t"], sig=(kc == 7))
            dve(lambda e: e.tensor_copy(out=hT, in_=ps_t[:]), ["ps_t"], ["hT"])

            def zgroup(c0, n):
                for kc in range(8):
                    mm(ps_z[:, 0:n], hT[:, kc, :], Wsb[:, kc, c0:c0 + n], kc == 0, kc == 7, ["hT", "Wsb"], ["ps_z"])

            if b == 0: mark('bm_norm')
            zgroup(C_AQ, 512)
            act(lambda e: e.copy(out=qk[:, 0:8, :].rearrange("p h d -> p (h d)"), in_=ps_z[:]), ["ps_z"], ["qk"])
            zgroup(C_AK, 256)
            act(lambda e: e.copy(out=qk[:, 8:10, :].rearrange("p h d -> p (h d)"), in_=ps_z[:, 0:128]), ["ps_z", "qk"], ["qk"])
            act(lambda e: e.copy(out=vaug[slot][:, :, 0:64], in_=ps_z[:, 128:256].rearrange("p (h d) -> p h d", h=2)), ["ps_z"], [vk])
            zgroup(C_AG, 512)
            act(lambda e: e.activation(out=gate_a, in_=ps_z[:], func=AF.Silu), ["ps_z"], ["gate_a"])
            if b == 0: mark('bm_z')
            dve(lambda e: e.tensor_tensor(out=sq, in0=qk, in1=qk, op=ALU.mult), ["qk"], ["sq"])
            dve(lambda e: e.tensor_reduce(out=st1[:, 2:12], in_=sq, axis=AX.X, op=ALU.add), ["sq"], ["st1"])
            if b == 0: mark("r1")
            rsqrt_of(st1[:, 2:12], st1[:, 2:12], 1.0 / 64, ["st1"], ["st1"])
            if b == 0: mark("r2")
            dve(lambda e: e.tensor_tensor(out=qk, in0=qk, in1=st1[:, 2:12, None].to_broadcast([128, 10, 64]), op=ALU.mult), ["qk", "st1"], ["qk"])
            dve(lambda e: e.tensor_tensor(out=qk, in0=qk, in1=g10[:], op=ALU.mult), ["qk", "g10"], ["qk"])
            if b == 0: mark("r3")
            qk4 = qk.rearrange("p h (a j) -> p h a j", a=2)
            r14 = rt1.rearrange("p h (a j) -> p h a j", a=2)
            r24 = rt2.rearrange("p h (a j) -> p h a j", a=2)
            qr4 = qr.rearrange("p h (a j) -> p h a j", a=2)
            cb = cosT[:, b, None, None, :].to_broadcast([128, 10, 2, 32])
            sb_ = sinT[:, b, None, :].to_broadcast([128, 10, 32])
            dve(lambda e: e.tensor_tensor(out=r14, in0=qk4, in1=cb, op=ALU.mult), ["qk", "cosT"], ["rt1"])
            if b == 0: mark("r4")
            dve(lambda e: e.tensor_tensor(out=r24[:, :, 0, :], in0=qk4[:, :, 1, :], in1=sb_, op=ALU.mult), ["qk", "sinT"], ["rt2"])
            dve(lambda e: e.tensor_tensor(out=r24[:, :, 1, :], in0=qk4[:, :, 0, :], in1=sb_, op=ALU.mult), ["qk", "sinT", "rt2"], ["rt2"])
            if b == 0: mark("r5")
            dve(lambda e: e.tensor_tensor(out=qr4[:, :, 0, :], in0=r14[:, :, 0, :], in1=r24[:, :, 0, :], op=ALU.subtract), ["rt1", "rt2"], ["qr"])
            dve(lambda e: e.tensor_tensor(out=qr4[:, :, 1, :], in0=r14[:, :, 1, :], in1=r24[:, :, 1, :], op=ALU.add), ["rt1", "rt2", "qr"], ["qr"])
            if b == 0: mark("r6")
            for j in range(5):
                tr(ps_t[:, j, :], qr[:, 2 * j:2 * j + 2, :].rearrange("p h d -> p (h d)"), ident_b[:], ["qr", "ident_b"], ["ps_t"], sig=(j == 4))
            if b == 0: mark("r7")
            dve(lambda e: e.tensor_copy(out=qT, in_=ps_t[:, 0:4, :]), ["ps_t"], ["qT"])
            dve(lambda e: e.tensor_copy(out=kT[slot], in_=ps_t[:, 4, :]), ["ps_t"], [kTk])
            if b == 0: mark('bm_rope')
            tiles = ([] if b == 0 else [(1 - slot, mask_prev, "mask_prev")]) + [(slot, mask_cur, "mask_cur")]
            for kvh in range(2):
                rows = slice(64 * kvh, 64 * kvh + 64)
                for ti, (sl, mk, mkk) in enumerate(tiles):
                    mm(ps_s[ti][:], kT[sl][rows, :], qT[rows, :, :].rearrange("p j q -> p (j q)"), True, True,
                       ["kT%d" % sl, "qT"], ["ps_s%d" % ti])
                    act(lambda e: e.activation(out=pTm[ti], in_=ps_s[ti][:], func=AF.Exp, scale=0.125), ["ps_s%d" % ti], ["pTm%d" % ti])
                    pool(lambda e: e.tensor_tensor(out=pTm[ti], in0=pTm[ti], in1=mk[:].rearrange("p j q -> p (j q)"), op=ALU.mult),
                         ["pTm%d" % ti, mkk], ["pTm%d" % ti])
                for j in range(4):
                    for ti, (sl, mk, mkk) in enumerate(tiles):
                        mm(ps_f[:, j * 65:(j + 1) * 65], pTm[ti][:, j * 128:(j + 1) * 128], vaug[sl][:, kvh, :], ti == 0, ti == len(tiles) - 1,
                           ["pTm%d" % ti, "vaug%d" % sl], ["ps_f"], sig=(j == 3 and ti == len(tiles) - 1))
                o4 = ps_f[:, 0:260].rearrange("p (j d) -> p j d", d=65)
                dve(lambda e: e.tensor_tensor(out=den[:, 0:4], in0=o4[:, :, 64], in1=esink[:, 4 * kvh:4 * kvh + 4], op=ALU.add), ["ps_f", "esink"], ["den"])
                dve(lambda e: e.reciprocal(out=den[:, 0:4], in_=den[:, 0:4]), ["den"], ["den"])
                for j in range(4):
                    h = 4 * kvh + j
                    dve(lambda e: e.scalar_tensor_tensor(out=mix_a[:, h * 64:(h + 1) * 64], in0=o4[:, j, 0:64], scalar=den[:, j:j + 1],
                                                         in1=gate_a[:, h * 64:(h + 1) * 64], op0=ALU.mult, op1=ALU.mult),
                        ["ps_f", "den", "gate_a"], ["mix_a"])
            for j in range(4):
                tr(ps_t[:, j, :], mix_a[:, j * 128:(j + 1) * 128], ident_b[:], ["mix_a", "ident_b"], ["ps_t"], sig=(j == 3))
            dve(lambda e: e.tensor_copy(out=mixT[:, 0:4, cols], in_=ps_t[:, 0:4, :]), ["ps_t"], ["mixT"])
            if b == 0: mark('bm_attn')
            zgroup(C_XQ, 512)
            act(lambda e: e.copy(out=xq_f.rearrange("p h d -> p (h d)"), in_=ps_z[:]), ["ps_z"], ["xq_f"])
            zgroup(C_XG, 512)
            act(lambda e: e.activation(out=gate_x, in_=ps_z[:], func=AF.Silu), ["ps_z"], ["gate_x"])
            sq4 = sq.rearrange("p h d -> p (h d)")[:, 0:512].rearrange("p (h d) -> p h d", h=4)
            dve(lambda e: e.tensor_tensor(out=sq4, in0=xq_f, in1=xq_f, op=ALU.mult), ["xq_f"], ["sq"])
            dve(lambda e: e.tensor_reduce(out=st1[:, 12:16], in_=sq4, axis=AX.X, op=ALU.add), ["sq"], ["st1"])
            rsqrt_of(st1[:, 12:16], st1[:, 12:16], 1.0 / 128, ["st1"], ["st1"])
            dve(lambda e: e.tensor_tensor(out=xq_b, in0=xq_f, in1=st1[:, 12:16, None].to_broadcast([128, 4, 128]), op=ALU.mult), ["xq_f", "st1"], ["xq_b"])
            for h in range(4):
                tr(ps_t[:, h, :], xq_b[:, h, :], ident_b[:], ["xq_b", "ident_b"], ["ps_t"], sig=(h == 3))
            dve(lambda e: e.tensor_copy(out=xqT, in_=ps_t[:, 0:4, :]), ["ps_t"], ["xqT"])
            for mt in range(2):
                for h in range(4):
                    mm(ps_s[mt][:, h * 128:(h + 1) * 128], mkT[:, h, mt * 128:(mt + 1) * 128], xqT[:, h, :], True, True, ["mkT", "xqT"],
                       ["ps_s%d" % mt], sig=(h == 3))
                act(lambda e: e.activation(out=pX[mt], in_=ps_s[mt][:], func=AF.Exp), ["ps_s%d" % mt], ["pX%d" % mt])
            for hp in range(2):
                for i in range(2):
                    h = 2 * hp + i
                    for mt in range(2):
                        mm(ps_f[:, i * 129:(i + 1) * 129], pX[mt][:, h * 128:(h + 1) * 128], mv_aug[:, mt, h, :], mt == 0, mt == 1,
                           ["pX%d" % mt, "mv_aug"], ["ps_f"], sig=(i == 1 and mt == 1))
                o2 = ps_f[:, 0:258].rearrange("p (i d) -> p i d", d=129)
                dve(lambda e: e.reciprocal(out=den[:, 4:6], in_=o2[:, :, 128]), ["ps_f"], ["den"])
                for i in range(2):
                    h = 2 * hp + i
                    dve(lambda e: e.scalar_tensor_tensor(out=mix_c[:, h * 128:(h + 1) * 128], in0=o2[:, i, 0:128], scalar=den[:, 4 + i:5 + i],
                                                         in1=gate_x[:, h * 128:(h + 1) * 128], op0=ALU.mult, op1=ALU.mult),
                        ["ps_f", "den", "gate_x"], ["mix_c"])
            for j in range(4):
                tr(ps_t[:, j, :], mix_c[:, j * 128:(j + 1) * 128], ident_b[:], ["mix_c", "ident_b"], ["ps_t"], sig=(j == 3))
            dve(lambda e: e.tensor_copy(out=mixT[:, 8:12, cols], in_=ps_t[:, 0:4, :]), ["ps_t"], ["mixT"])
            if b == 0: mark('bm_xattn')
            for (c0, dst, dk, fn) in ((C_SU, uT, "uT", None), (C_SG, gT, "gT", AF.Silu)):
                for ct in range(4):
                    for kc in range(8):
                        mm(ps_z[:, ct * 128:(ct + 1) * 128], Wsb[:, kc, c0 + ct * 128:c0 + (ct + 1) * 128], hT[:, kc, :], kc == 0, kc == 7,
                           ["hT", "Wsb"], ["ps_z"], sig=(ct == 3 and kc == 7))
                pz3 = ps_z[:].rearrange("p (c t) -> p c t", c=4)
                if fn is None:
                    act(lambda e: e.copy(out=dst[:, :, cols], in_=pz3), ["ps_z"], [dk])
                else:
                    act(lambda e: e.activation(out=dst[:, :, cols], in_=pz3, func=fn), ["ps_z"], [dk])

        def ssm_superblock(l, sbi):
            for q in range(4):
                for prl in range(4):
                    pr = 4 * q + prl
                    rows = slice(32 * prl, 32 * prl + 32)
                    kw = {"tile_position": (96, 0)} if prl == 3 else {}
                    u3 = uT[rows, q, :].rearrange("p (k s) -> p s k", s=8)
                    for ri in range(2):
                        for s_ in range(8):
                            mm(ps_zs[:, ri * NCH:(ri + 1) * NCH], Wssm[rows, q, s_, ri, :], u3[:, s_, :], s_ == 0, s_ == 7, ["Wssm", "uT"], ["ps_zs"],
                               sig=(ri == 1 and s_ == 7), **kw)
                    zr = ps_zs[:, 0:NCH]; zi = ps_zs[:, NCH:2 * NCH]
                    mc = mcT[:, pr, :]; ms = msT[:, pr, :]
                    dve(lambda e: e.tensor_tensor(out=sct[:, 0, :], in0=zr, in1=mc, op=ALU.mult), ["ps_zs", "mcT"], ["sct"])
                    dve(lambda e: e.tensor_tensor(out=sct[:, 1, :], in0=zi, in1=ms, op=ALU.mult), ["ps_zs", "msT", "sct"], ["sct"])
                    dve(lambda e: e.tensor_tensor(out=sct[:, 2, :], in0=zi, in1=mc, op=ALU.mult), ["ps_zs", "mcT", "sct"], ["sct"])
                    dve(lambda e: e.tensor_tensor(out=sct[:, 3, :], in0=zr, in1=ms, op=ALU.mult), ["ps_zs", "msT", "sct"], ["sct"])
                    pool(lambda e: e.tensor_tensor(out=zt[:, 0, :], in0=sct[:, 0, :], in1=sct[:, 1, :], op=ALU.add), ["sct"], ["zt"])
                    pool(lambda e: e.tensor_tensor(out=zt[:, 1, :], in0=sct[:, 2, :], in1=sct[:, 3, :], op=ALU.subtract), ["sct", "zt"], ["zt"])
                    if sbi == 0:
                        ini = (0.0, 0.0)
                        inik = []
                    else:
                        hr_ = carry[:, pr, 0:1]; hi_ = carry[:, pr, 1:2]
                        dve(lambda e: e.tensor_scalar(out=init2[:, 0:1], in0=hr_, scalar1=mcT[:, pr, 1:2], scalar2=None, op0=ALU.mult), ["carry", "mcT"], ["init2"])
                        dve(lambda e: e.scalar_tensor_tensor(out=init2[:, 0:1], in0=hi_, scalar=nms1[:, pr:pr + 1], in1=init2[:, 0:1], op0=ALU.mult, op1=ALU.add),
                            ["carry", "nms1", "init2"], ["init2"])
                        dve(lambda e: e.tensor_scalar(out=init2[:, 1:2], in0=hi_, scalar1=mcT[:, pr, 1:2], scalar2=None, op0=ALU.mult), ["carry", "mcT", "init2"], ["init2"])
                        dve(lambda e: e.scalar_tensor_tensor(out=init2[:, 1:2], in0=hr_, scalar=msT[:, pr, 1:2], in1=init2[:, 1:2], op0=ALU.mult, op1=ALU.add),
                            ["carry", "msT", "init2"], ["init2"])
                        ini = (init2[:, 0:1], init2[:, 1:2])
                        inik = ["init2"]
                    mg = magA[:, pr:pr + 1].to_broadcast([128, NCH])
                    for ri in range(2):
                        dve(lambda e: e.tensor_tensor_scan(out=Gs[:, ri, :], data0=mg, data1=zt[:, ri, :], initial=ini[ri], op0=ALU.mult, op1=ALU.add),
                            ["zt", "magA", "Gs"] + inik, ["Gs"])
                    pool(lambda e: e.tensor_tensor(out=sct[:, 0, :], in0=Gs[:, 0, :], in1=mc, op=ALU.mult), ["Gs", "mcT"], ["sct"])
                    pool(lambda e: e.tensor_tensor(out=sct[:, 1, :], in0=Gs[:, 1, :], in1=ms, op=ALU.mult), ["Gs", "msT", "sct"], ["sct"])
                    pool(lambda e: e.tensor_tensor(out=sct[:, 2, :], in0=Gs[:, 1, :], in1=mc, op=ALU.mult), ["Gs", "mcT", "sct"], ["sct"])
                    pool(lambda e: e.tensor_tensor(out=sct[:, 3, :], in0=Gs[:, 0, :], in1=ms, op=ALU.mult), ["Gs", "msT", "sct"], ["sct"])
                    pool(lambda e: e.tensor_tensor(out=Hf[:, 0, :], in0=sct[:, 0, :], in1=sct[:, 1, :], op=ALU.subtract), ["sct"], ["Hf"])
                    pool(lambda e: e.tensor_tensor(out=Hf[:, 1, :], in0=sct[:, 2, :], in1=sct[:, 3, :], op=ALU.add), ["sct", "Hf"], ["Hf"])
                    act(lambda e: e.copy(out=Hb[:, pr, :, 0:1], in_=carry[:, pr, :, None]), ["carry"], ["Hb"])
                    act(lambda e: e.copy(out=Hb[:, pr, :, 1:NCH], in_=Hf[:, :, 0:NCH - 1]), ["Hf", "Hb"], ["Hb"])
                    act(lambda e: e.copy(out=carry[:, pr, :], in_=Hf[:, :, NCH - 1]), ["Hf", "Hb"], ["carry"])
                    for s_ in range(8):
                        mm(ps_y[0:NCH, 0:256].rearrange("k (a t c) -> k a t c", a=2, t=8), u3[:, s_, :], Kstrip[rows, q, :, 7 - s_:15 - s_, :], s_ == 0, s_ == 7,
                           ["uT", "Kstrip"], ["ps_y"], **kw)
                    for ri in range(2):
                        mm(ps_y2[0:NCH, 0:256], Hb[:, pr, ri, :], Vf[:, pr, ri, :], ri == 0, ri == 1, ["Hb", "Vf"], ["ps_y2"])
                    act(lambda e: e.copy(out=y2s[0:NCH, :], in_=ps_y2[0:NCH, 0:256]), ["ps_y2"], ["y2s"])
                    dve(lambda e: e.tensor_tensor(out=ysum[0:NCH, :], in0=ps_y[0:NCH, 0:256], in1=y2s[0:NCH, :], op=ALU.add), ["ps_y", "y2s"], ["ysum"])
                    act(lambda e: e.activation(out=y2k[0:NCH, :, 32 * prl:32 * prl + 32].rearrange("k t (a c) -> k t a c", a=2),
                                               in_=ysum[0:NCH, :].rearrange("k (a t c) -> k t a c", a=2, t=8), func=AF.Gelu_apprx_tanh),
                        ["ysum"], ["y2k"])
                for t in range(8):
                    tr(ps_t[:, t, 0:NCH], y2k[0:NCH, t, :], ident_b[0:NCH, 0:NCH], ["y2k", "ident_b"], ["ps_t"], sig=(t == 7))
                dve(lambda e: e.tensor_copy(out=uT[:, q, :].rearrange("c (k t) -> c t k", t=8), in_=ps_t[:, :, 0:NCH]), ["ps_t", "uT"], ["uT"])
            for oc in range(4):
                for kc in range(4):
                    mm(ps_y[:, 0:SBT], Wglu[:, kc, oc * 128:(oc + 1) * 128], uT[:, kc, :], kc == 0, kc == 3, ["Wglu", "uT"], ["ps_y"])
                act(lambda e: e.activation(out=sig_t, in_=ps_y[:, 0:SBT], func=AF.Sigmoid, bias=bglu_col[:, oc:oc + 1]), ["ps_y", "bglu_col"], ["sig_t"])
                pool(lambda e: e.tensor_tensor(out=sig_t, in0=sig_t, in1=gT[:, oc, :], op=ALU.mult), ["sig_t", "gT"], ["sig_t"])
                dve(lambda e: e.tensor_tensor(out=mixT[:, 4 + oc, :], in0=sig_t, in1=uT[:, oc, :], op=ALU.mult), ["sig_t", "uT"], ["mixT"])

        def block_out(l, b, src):
            slot = b % 2
            bl = b % BPS
            cols = slice(bl * 128, (bl + 1) * 128)
            xk = "xr0"
            slot = 0
            T.dma(xr[slot], src[b * 128:(b + 1) * 128, :], r=[("res", b)], w=[xk], q="sp")
            for half in range(2):
                for kc in range(12):
                    mm(ps_f[:], mixT[:, kc, cols], Wout[:, kc, half * 512:(half + 1) * 512], kc == 0, kc == 11, ["mixT", "Wout"], ["ps_f"])
                dve(lambda e: e.tensor_tensor(out=xr[slot][:, half * 512:(half + 1) * 512], in0=ps_f[:], in1=xr[slot][:, half * 512:(half + 1) * 512],
                                              op=ALU.add), ["ps_f", xk], [xk])
            T.dma(out_d[b * 128:(b + 1) * 128, :], xr[slot], r=[xk], w=[("res", b)], q="sp")

        try:
            mark('consts')
            for l in range(n_layers):
                src = x_d if l == 0 else out_d
                layer_setup(l)
                mark('setup')
                for sbi in range(n_sb):
                    for bl in range(BPS):
                        block_main(l, sbi * BPS + bl, src)
                        mark('main%d' % bl)
                    ssm_superblock(l, sbi)
                    mark('ssm')
                    for bl in range(BPS):
                        block_out(l, sbi * BPS + bl, src)
        except StopBuild:
            print("build stopped at", stop)
        T.drain("sp")
        print("ops", T.nops, "waits", T.nwaits)
    return nc


Q_PERM = [0, 4, 1, 5, 2, 6, 3, 7]


def prep_inputs(inputs, n_sb=S // SBT, layers=None, x_override=None):
    SEQ = n_sb * SBT
    lsl = slice(None) if layers is None else slice(layers[0], layers[1])
    w_in = np.asarray(inputs["w_in"], dtype=np.float32)[lsl]
    qcols = np.concatenate([np.arange(h * 64, (h + 1) * 64) for h in Q_PERM])
    perm = np.concatenate([qcols, np.arange(512, INW)])
    w_in_p = np.ascontiguousarray(w_in[:, :, perm])
    shared = {k: np.ascontiguousarray(np.asarray(v)[lsl]) for k, v in inputs.items() if k not in ("x", "mem", "positions", "w_in")}
    shared["w_in"] = w_in_p
    xs = np.asarray(inputs["x"]) if x_override is None else x_override
    maps = []
    for c in range(8):
        m = dict(shared)
        m["x"] = np.ascontiguousarray(xs[c, :SEQ])
        m["mem"] = np.ascontiguousarray(np.asarray(inputs["mem"])[c])
        m["positions"] = np.ascontiguousarray(np.asarray(inputs["positions"])[c, :SEQ]).astype(np.int32)
        maps.append(m)
    return maps


LAYERS_PER_LAUNCH = 4


def kernel(**inputs):
    nc = build_nc(n_layers=LAYERS_PER_LAUNCH, wd=LAYERS_PER_LAUNCH)
    x = np.asarray(inputs["x"], dtype=np.float32)
    for l0 in range(0, DEPTH, LAYERS_PER_LAUNCH):
        maps = prep_inputs(inputs, layers=(l0, l0 + LAYERS_PER_LAUNCH), x_override=x)
        res = run_bass_kernel_spmd(nc, maps, core_ids=list(range(8)))
        x = np.stack([np.asarray(r["out"]) for r in res.results], axis=0).astype(np.float32)
    return x
```

```python
import math
import numpy as np
from contextlib import ExitStack
import concourse.bass as bass
import concourse.mybir as mybir
from concourse.bass_utils import run_bass_kernel_spmd

F32 = mybir.dt.float32
BF16 = mybir.dt.bfloat16
I32 = mybir.dt.int32
ALU = mybir.AluOpType
AF = mybir.ActivationFunctionType
AX = mybir.AxisListType

D = 1024
S = 4096
NMEM = 256
INW = 3328
DEPTH = 4
C_AQ, C_AK, C_AV, C_AG, C_SU, C_SG, C_XQ, C_XG = 0, 512, 640, 768, 1280, 1792, 2304, 2816
EPS = 1e-6
SBT = 512
BPS = SBT // 128
NCH = SBT // 8
N_DMA_SEMS = 40
TWO_PI = 2.0 * math.pi
CW1 = 6.28125
CW2 = TWO_PI - 6.28125


class Trk:
    def __init__(self, nc, stack):
        self.nc = nc
        self.eng = {"pe": nc.tensor, "act": nc.scalar, "dve": nc.vector, "pool": nc.gpsimd, "sp": nc.sync}
        self.sem = {k: stack.enter_context(nc.semaphore("s_" + k)) for k in ("pe", "act", "dve", "pool")}
        self.cnt = {k: 0 for k in self.sem}
        self.dsem = [stack.enter_context(nc.semaphore("d%d" % i)) for i in range(N_DMA_SEMS)]
        self.dcnt = [0] * N_DMA_SEMS
        self.dnext = 0
        self.known = {k: {} for k in self.eng}
        self.lastw = {}
        self.reads = {}
        self.nwaits = 0
        self.nops = 0

    def _wait(self, e, tok):
        kind, key, val = tok
        if kind == "E" and key == e and val > self.cnt[e]:
            return
        kn = self.known[e]
        if kn.get((kind, key), 0) >= val:
            return
        kn[(kind, key)] = val
        sem = self.sem[key] if kind == "E" else self.dsem[key]
        self.eng[e].wait_ge(sem, val)
        self.nwaits += 1

    ALIAS = {"junk": "hb", "sq": "rt1", "xq_f": "rt2", "y2s": "rt1", "ysum": "rt2", "y2k": "qk", "sig_t": "qr",
             "zt": "xt0", "Gs": "xt0", "sct": "xt1", "Hf": "hb",
             "pX0": "pTm0", "pX1": "pTm1", "mix_c": "mix_a"}

    def _deps(self, e, r, w):
        r = [self.ALIAS.get(k, k) for k in r]
        w = [self.ALIAS.get(k, k) for k in w]
        for k in r:
            t = self.lastw.get(k)
            if t is not None:
                self._wait(e, t)
        for k in w:
            t = self.lastw.get(k)
            if t is not None:
                self._wait(e, t)
            for t in self.reads.get(k, ()):
                self._wait(e, t)

    def _commit(self, tok, r, w):
        r = [self.ALIAS.get(k, k) for k in r]
        w = [self.ALIAS.get(k, k) for k in w]
        for k in r:
            self.reads.setdefault(k, []).append(tok)
        for k in w:
            self.lastw[k] = tok
            self.reads[k] = []

    def op(self, e, fn, r=(), w=(), sig=True):
        self._deps(e, r, w)
        inst = fn(self.eng[e])
        self.nops += 1
        if sig:
            self.cnt[e] += 1
            inst.then_inc(self.sem[e], 1)
            tok = ("E", e, self.cnt[e])
        else:
            tok = ("E", e, self.cnt[e] + 1)
        self._commit(tok, r, w)

    def dma(self, out, in_, r=(), w=(), q="sp"):
        self._deps(q, r, w)
        i = self.dnext
        self.dnext = (self.dnext + 1) % N_DMA_SEMS
        if self.dcnt[i] > 0:
            self._wait(q, ("D", i, self.dcnt[i]))
        self.dcnt[i] += 16
        self.eng[q].dma_start(out=out, in_=in_).then_inc(self.dsem[i], 16)
        tok = ("D", i, self.dcnt[i])
        self.nops += 1
        self._commit(tok, r, w)

    def drain(self, e="sp"):
        for k in self.sem:
            if self.cnt[k] > 0:
                self._wait(e, ("E", k, self.cnt[k]))
        for i in range(N_DMA_SEMS):
            if self.dcnt[i] > 0:
                self._wait(e, ("D", i, self.dcnt[i]))

    def finish(self, keys, e="sp"):
        for k in keys:
            t = self.lastw.get(k)
            if t is not None:
                self._wait(e, t)


class StopBuild(Exception):
    pass


def build_nc(n_layers=DEPTH, n_sb=S // SBT, dbg=False, stop=None, wd=DEPTH):
    nc = bass.Bass("TRN2", target_bir_lowering=False)
    SEQ = n_sb * SBT
    di = lambda n, s, dt=F32: nc.dram_tensor(n, s, dt, kind="ExternalInput").ap()
    x_d = di("x", [SEQ, D]); mem_d = di("mem", [NMEM, D]); pos_d = di("positions", [SEQ], I32)
    norm_g_d = di("norm_g", [wd, D]); w_in_d = di("w_in", [wd, D, INW])
    qg_d = di("q_norm_g", [wd, 64]); kg_d = di("k_norm_g", [wd, 64]); sinks_d = di("sinks", [wd, 8])
    lre_d = di("lam_re", [wd, 32, 64]); lim_d = di("lam_im", [wd, 32, 64]); ldt_d = di("log_dt", [wd, 32])
    bre_d = di("b_re", [wd, 32, 64, 16]); bim_d = di("b_im", [wd, 32, 64, 16])
    cre_d = di("c_re", [wd, 32, 16, 64]); cim_d = di("c_im", [wd, 32, 16, 64])
    dskip_d = di("d_skip", [wd, 512]); wglu_d = di("w_glu", [wd, 512, 512]); bglu_d = di("b_glu", [wd, 512])
    mng_d = di("mem_norm_g", [wd, D]); wkv_d = di("w_mem_kv", [wd, D, D])
    xqg_d = di("xq_norm_g", [wd, 128]); xkg_d = di("xk_norm_g", [wd, 128]); wout_d = di("w_out", [wd, 1536, D])
    out_d = nc.dram_tensor("out", [SEQ, D], F32, kind="ExternalOutput").ap()
    NB = SEQ // 128

    with ExitStack() as st:
        T = Trk(nc, st)
        sbt = lambda name, shape, dt: st.enter_context(nc.sbuf_tensor(name, shape, dt))
        pst = lambda name, shape, dt: st.enter_context(nc.psum_tensor(name, shape, dt))
        dve = lambda fn, r, w: T.op("dve", fn, r=r, w=w)
        act = lambda fn, r, w: T.op("act", fn, r=r, w=w)
        pool = lambda fn, r, w: T.op("pool", fn, r=r, w=w)

        def mm(out, lhsT, rhs, start, stop, r, w, sig=None, **kw):
            T.op("pe", lambda e: e.matmul(out, lhsT=lhsT, rhs=rhs, start=start, stop=stop, **kw), r=r, w=w,
                 sig=(stop if sig is None else sig))

        def tr(out, in_, ident, r, w, sig=True):
            T.op("pe", lambda e: e.transpose(out=out, in_=in_, identity=ident), r=r, w=w, sig=sig)

        ps_t = pst("ps_t", [128, 8, 128], BF16)
        ps_z = pst("ps_z", [128, 512], F32)
        ps_s = [pst("ps_s0", [128, 512], F32), pst("ps_s1", [128, 512], F32)]
        ps_f = pst("ps_f", [128, 512], F32)
        ps_y = pst("ps_y", [128, 512], F32)
        ps_y2 = pst("ps_y2", [128, 512], F32)
        ps_zs = pst("ps_zs", [128, 512], F32)

        Wsb = sbt("Wsb", [128, 8, INW], BF16)
        Wout = sbt("Wout", [128, 12, D], BF16)
        Wglu = sbt("Wglu", [128, 4, 512], BF16)
        Kstrip = sbt("Kstrip", [128, 4, 2, 15, 16], BF16)
        Wssm = sbt("Wssm", [128, 4, 8, 2, 128], BF16)
        Vf = sbt("Vf", [128, 16, 2, 256], BF16)
        mcT = sbt("mcT", [128, 16, NCH], F32)
        msT = sbt("msT", [128, 16, NCH], F32)
        Hb = sbt("Hb", [128, 16, 2, NCH], BF16)
        carry = sbt("carry", [128, 16, 2], F32)
        magA = sbt("magA", [128, 16], F32)
        mcA = sbt("mcA", [128, 16], F32)
        msA = sbt("msA", [128, 16], F32)
        nmsA = sbt("nmsA", [128, 16], F32)
        magTab = sbt("magTab", [128, 16, NCH], F32)
        mkT = sbt("mkT", [128, 4, NMEM], BF16)
        mv_aug = sbt("mv_aug", [128, 2, 4, 129], BF16)
        ident_b = sbt("ident_b", [128, 128], BF16)
        ident_f = sbt("ident_f", [128, 128], F32)
        mask_cur = sbt("mask_cur", [128, 4, 128], BF16)
        mask_prev = sbt("mask_prev", [128, 4, 128], BF16)
        dmask = sbt("dmask", [128, 32], F32)
        cosT = sbt("cosT", [128, NB, 32], F32)
        sinT = sbt("sinT", [128, NB, 32], F32)
        g10 = sbt("g10", [128, 10, 64], F32)
        gq_bc = sbt("gq_bc", [128, 64], F32)
        gk_bc = sbt("gk_bc", [128, 64], F32)
        esink = sbt("esink", [128, 8], F32)
        gxk_bc = sbt("gxk_bc", [128, 128], F32)
        gxq_col = sbt("gxq_col", [128, 1], F32)
        bglu_col = sbt("bglu_col", [128, 4], F32)
        d_col = sbt("d_col", [128, 4], F32)
        g_col = sbt("g_col", [128, 8], F32)
        gm_col = sbt("gm_col", [128, 8], F32)
        ldT = sbt("ldT", [32, 128], F32)
        nvec = sbt("nvec", [128, 9], F32)
        kvec = sbt("kvec", [128, NCH], F32)
        dummy = sbt("dummy_t", [128, 4], F32)

        ARW = 14330
        arena = sbt("arena", [128, ARW], F32)
        aoff = {"main": 0, "setup": 0, "setupB": 0}
        akeys = {"main": [], "setup": [], "setupB": []}

        def carve(phase, name, shape, dt):
            n = int(np.prod(shape))
            words = n if dt in (F32, I32) else (n + 1) // 2
            o = aoff[phase]
            aoff[phase] = o + words
            assert aoff[phase] <= ARW, (phase, name, aoff[phase])
            v = arena[:, o:o + words]
            if dt != F32:
                v = v.bitcast(dt)
                if dt == BF16 and n % 2:
                    v = v[:, 0:n]
            if len(shape) > 1:
                names = " ".join("d%d" % i for i in range(len(shape)))
                v = v.rearrange("p (%s) -> p %s" % (names, names), **{"d%d" % i: shape[i] for i in range(1, len(shape))})
            if name not in akeys[phase]:
                akeys[phase].append(name)
            return v

        M = lambda name, shape, dt: carve("main", name, shape, dt)
        U = lambda name, shape, dt: carve("setup", name, shape, dt)
        UB = lambda name, shape, dt: carve("setupB", name, shape, dt)

        uT = M("uT", [4, SBT], BF16)
        gT = M("gT", [4, SBT], BF16)
        mixT = M("mixT", [12, SBT], BF16)
        xt = [M("xt0", [D], F32), M("xt1", [D], F32)]
        xr = [M("xr0", [D], F32)]
        hb = M("hb", [D], BF16)
        junk = hb
        hTs = [M("hT0", [8, 128], BF16), M("hT1", [8, 128], BF16)]
        st1 = M("st1", [16], F32)
        qk = M("qk", [10, 64], F32)
        rt1 = M("rt1", [10, 64], F32)
        rt2 = M("rt2", [10, 64], F32)
        sq = rt1
        qr = M("qr", [10, 64], BF16)
        qT = M("qT", [4, 128], BF16)
        kT = [M("kT0", [128], BF16), M("kT1", [128], BF16)]
        vaug = [M("vaug0", [2, 65], BF16), M("vaug1", [2, 65], BF16)]
        pTm = [M("pTm0", [512], BF16), M("pTm1", [512], BF16)]
        pX = pTm
        gate_a = M("gate_a", [512], BF16)
        gate_x = M("gate_x", [512], BF16)
        mix_a = M("mix_a", [512], BF16)
        mix_c = mix_a
        xq_f = rt2.rearrange("p h d -> p (h d)")[:, 0:512].rearrange("p (h d) -> p h d", h=4)
        xq_b = M("xq_b", [4, 128], BF16)
        xqT = M("xqT", [4, 128], BF16)
        den = M("den", [8], F32)
        zt = xt[0][:, 0:512].rearrange("p (r a k) -> p r a k", r=2, a=4)
        Gs = xt[0][:, 512:1024].rearrange("p (r a k) -> p r a k", r=2, a=4)
        sct = xt[1].rearrange("p (j a k) -> p j a k", j=4, a=4)
        Hf = hb.bitcast(F32).rearrange("p (r a k) -> p r a k", r=2, a=4)
        init4 = M("init4", [4, 4], F32)
        y2s = rt1.rearrange("p h d -> p (h d)")[:, 0:512]
        ysum = rt2.rearrange("p h d -> p (h d)")[:, 0:512]
        y2k = qk.rearrange("p h d -> p (h d)")[:, 0:512].bitcast(BF16).rearrange("p (t c) -> p t c", t=8)
        sig_t = qr.rearrange("p h d -> p (h d)")[:, 0:512]
        print("arena main words", aoff["main"])

        def load_T(dst, src, n, wkey):
            T.dma(ldT[0:n, :], src, w=["ldT"])
            tr(ps_z[:, 0:n], ldT[0:n, :], ident_f[0:n, 0:n], ["ldT", "ident_f"], ["ps_z"])
            dve(lambda e: e.tensor_copy(out=dst, in_=ps_z[:, 0:n]), ["ps_z"], [wkey])

        def sin_of(dst, src, n, shift, rk, wk, tmp=None):
            for c0 in range(0, n, 256):
                c1 = min(n, c0 + 256)
                m = c1 - c0
                ki, kf, yy = s_ki[:, 0:m], s_kf[:, 0:m], s_y[:, 0:m]
                sr = src[:, c0:c1]
                dve(lambda e: e.tensor_scalar(out=ki, in0=sr, scalar1=float(shift), scalar2=float(1.0 / TWO_PI), op0=ALU.add, op1=ALU.mult),
                    rk, ["sr_ki"])
                dve(lambda e: e.tensor_copy(out=kf, in_=ki), ["sr_ki"], ["sr_kf"])
                dve(lambda e: e.tensor_scalar(out=yy, in0=sr, scalar1=float(shift), scalar2=None, op0=ALU.add), rk, ["sr_y"])
                dve(lambda e: e.scalar_tensor_tensor(out=yy, in0=kf, scalar=float(-CW1), in1=yy, op0=ALU.mult, op1=ALU.add),
                    ["sr_kf", "sr_y"], ["sr_y"])
                dve(lambda e: e.scalar_tensor_tensor(out=yy, in0=kf, scalar=float(-CW2), in1=yy, op0=ALU.mult, op1=ALU.add),
                    ["sr_kf", "sr_y"], ["sr_y"])
                dve(lambda e: e.tensor_scalar(out=yy, in0=yy, scalar1=float(math.pi), scalar2=float(-math.pi), op0=ALU.min, op1=ALU.max),
                    ["sr_y"], ["sr_y"])
                act(lambda e: e.activation(out=dst[:, c0:c1], in_=yy, func=AF.Sin), ["sr_y"], wk)

        def rsqrt_of(dst, src, scale, rk, wk):
            dve(lambda e: e.tensor_scalar(out=dst, in0=src, scalar1=float(scale), scalar2=float(EPS), op0=ALU.mult, op1=ALU.add), rk, wk)
            act(lambda e: e.activation(out=dst, in_=dst, func=AF.Sqrt), wk, wk)
            dve(lambda e: e.reciprocal(out=dst, in_=dst), wk, wk)

        def join(keys, e="dve"):
            T.op(e, lambda en: en.memset(dummy[:, 0:1], 0.0), r=[], w=list(keys) + ["dummy"])

        s_ki = U("sr_ki", [256], I32); s_kf = U("sr_kf", [256], F32); s_y = U("sr_y", [256], F32)
        s_ang = U("s_ang", [256], F32)
        UB("sr_ki", [256], I32); UB("sr_kf", [256], F32); UB("sr_y", [256], F32); UB("s_ang", [256], F32)
        ones_f = U("ones_f", [128], F32)
        stage = [U("stage0", [1024], F32), U("stage1", [1024], F32)]
        s_posi = U("s_posi", [128], I32); s_posf = U("s_posf", [128], F32)
        s_posT = U("s_posT", [32], F32); s_inv = U("s_inv", [32], F32)
        s_memx = U("s_memx", [D], F32); s_memn = U("s_memn", [D], BF16); s_memT = U("s_memT", [8, NMEM], BF16)
        s_mkf = U("s_mkf", [4, 128], F32); s_mkb = U("s_mkb", [4, 128], BF16)
        s_smA = U("s_smA", [16], F32)
        s_sm = UB("s_sm", [16, 12], F32)
        s_e9 = UB("s_e9", [16, 9], F32); s_a9 = UB("s_a9", [16, 9], F32); s_mag9 = UB("s_mag9", [16, 9], F32)
        s_cos9 = UB("s_cos9", [16, 9], F32); s_sin9 = UB("s_sin9", [16, 9], F32)
        s_ar9 = UB("s_ar9", [16, 9], F32); s_ai9 = UB("s_ai9", [16, 9], F32)
        s_Br = UB("s_Br", [16, 16], F32); s_Bi = UB("s_Bi", [16, 16], F32)
        s_bbr = UB("s_bbr", [16, 16], F32); s_bbi = UB("s_bbi", [16, 16], F32); s_bt = UB("s_bt", [16, 16], F32)
        s_Cst = UB("s_Cst", [2, 128], F32)
        s_Cr = UB("s_Cr", [16, 16], F32); s_Ci = UB("s_Ci", [16, 16], F32)
        s_CAr = UB("s_CAr", [4, 9, 16], F32); s_CAi = UB("s_CAi", [4, 9, 16], F32); s_CAt = UB("s_CAt", [4, 9, 16], F32)
        s_CAzr = UB("s_CAzr", [4, 2, 8, 16], F32); s_CAzi = UB("s_CAzi", [4, 2, 8, 16], F32)
        s_Bzr = UB("s_Bzr", [4, 128], F32); s_Bzi = UB("s_Bzi", [4, 128], F32)
        s_Kt = UB("s_Kt", [2, 8, 16], F32); s_Kd = UB("s_Kd", [32], F32)
        s_Xr = UB("s_Xr", [4, 8, 16], F32); s_Xi = UB("s_Xi", [4, 8, 16], F32); s_Xt = UB("s_Xt", [4, 8, 16], F32)
        s_Xzr = UB("s_Xzr", [8, 4, 32], BF16); s_Xzi = UB("s_Xzi", [8, 4, 32], BF16)
        s_lg2 = UB("s_lg2", [2], F32)
        print("arena setup words", aoff["setup"], aoff["setupB"])
        ALLK = list(dict.fromkeys(akeys["main"] + akeys["setup"] + akeys["setupB"]))

        pool(lambda e: e.memset(ones_f, 1.0), [], ["ones_f"])
        pool(lambda e: e.memset(dummy[:], 0.0), [], ["dummy"])
        pool(lambda e: e.affine_select(out=ident_f[:], in_=ones_f, pattern=[[-1, 128]], compare_op=ALU.is_equal, fill=0.0,
                                       base=0, channel_multiplier=1), ["ones_f"], ["ident_f"])
        pool(lambda e: e.affine_select(out=ident_b[:], in_=ones_f, pattern=[[-1, 128]], compare_op=ALU.is_equal, fill=0.0,
                                       base=0, channel_multiplier=1), ["ones_f"], ["ident_b"])
        for j in range(4):
            pool(lambda e: e.affine_select(out=mask_cur[:, j, :], in_=ones_f, pattern=[[1, 128]], compare_op=ALU.is_ge, fill=0.0,
                                           base=0, channel_multiplier=-1), ["ones_f"], ["mask_cur"])
            pool(lambda e: e.affine_select(out=mask_prev[:, j, :], in_=ones_f, pattern=[[-1, 128]], compare_op=ALU.is_ge, fill=0.0,
                                           base=-1, channel_multiplier=1), ["ones_f"], ["mask_prev"])
        pool(lambda e: e.tensor_tensor(out=dmask[:], in0=ident_f[:, 0:32], in1=ident_f[:, 32:64], op=ALU.add), ["ident_f"], ["dmask"])
        pool(lambda e: e.tensor_tensor(out=dmask[:], in0=dmask[:], in1=ident_f[:, 64:96], op=ALU.add), ["ident_f", "dmask"], ["dmask"])
        pool(lambda e: e.tensor_tensor(out=dmask[:], in0=dmask[:], in1=ident_f[:, 96:128], op=ALU.add), ["ident_f", "dmask"], ["dmask"])
        pool(lambda e: e.iota(nvec[:], pattern=[[1, 9]], base=0, channel_multiplier=0, allow_small_or_imprecise_dtypes=True), [], ["nvec"])
        pool(lambda e: e.iota(kvec[:], pattern=[[1, NCH]], base=0, channel_multiplier=0, allow_small_or_imprecise_dtypes=True), [], ["kvec"])
        pool(lambda e: e.memset(mv_aug[:], 1.0), [], ["mv_aug"])
        pool(lambda e: e.memset(Kstrip[:], 0.0), [], ["Kstrip"])

        for bb in range(0, NB, 32):
            nbk = min(32, NB - bb)
            T.dma(s_posi[0:nbk, :], pos_d[bb * 128:(bb + nbk) * 128].rearrange("(b p) -> b p", p=128), w=["s_posi"])
            dve(lambda e: e.tensor_copy(out=s_posf[0:nbk, :], in_=s_posi[0:nbk, :]), ["s_posi"], ["s_posf"])
            tr(ps_z[:, 0:nbk], s_posf[0:nbk, :], ident_f[0:nbk, 0:nbk], ["s_posf", "ident_f"], ["ps_z"])
            dve(lambda e: e.tensor_copy(out=s_posT[:, 0:nbk], in_=ps_z[:, 0:nbk]), ["ps_z"], ["s_posT"])
        pool(lambda e: e.iota(s_inv, pattern=[[1, 32]], base=0, channel_multiplier=0, allow_small_or_imprecise_dtypes=True), [], ["s_inv"])
        act(lambda e: e.activation(out=s_inv, in_=s_inv, func=AF.Exp, scale=float(-math.log(10000.0) / 32.0)), ["s_inv"], ["s_inv"])
        for b0 in range(0, NB, 8):
            nb8 = min(8, NB - b0)
            ang3 = s_ang[:, 0:nb8 * 32].rearrange("p (b j) -> p b j", j=32)
            dve(lambda e: e.tensor_tensor(out=ang3, in0=s_posT[:, b0:b0 + nb8, None].to_broadcast([128, nb8, 32]),
                                          in1=s_inv[:, None, :].to_broadcast([128, nb8, 32]), op=ALU.mult), ["s_posT", "s_inv"], ["s_ang"])
            sin_of(sinT[:, b0:b0 + nb8, :].rearrange("p b j -> p (b j)"), s_ang[:, 0:nb8 * 32], nb8 * 32, 0.0, ["s_ang"], ["sinT"])
            sin_of(cosT[:, b0:b0 + nb8, :].rearrange("p b j -> p (b j)"), s_ang[:, 0:nb8 * 32], nb8 * 32, math.pi / 2, ["s_ang"], ["cosT"])

        def mark(name):
            if stop == name:
                raise StopBuild()

        def layer_setup(l):
            join(ALLK)
            load_T(g_col[:], norm_g_d[l].rearrange("(k p) -> k p", p=128), 8, "g_col")
            load_T(gm_col[:], mng_d[l].rearrange("(k p) -> k p", p=128), 8, "gm_col")
            load_T(bglu_col[:], bglu_d[l].rearrange("(k p) -> k p", p=128), 4, "bglu_col")
            load_T(d_col[:], dskip_d[l].rearrange("(k p) -> k p", p=128), 4, "d_col")
            load_T(gxq_col[:], xqg_d[l].rearrange("(k p) -> k p", p=128), 1, "gxq_col")
            dve(lambda e: e.tensor_scalar(out=gxq_col[:], in0=gxq_col[:], scalar1=float(1.0 / math.sqrt(128.0)), scalar2=None, op0=ALU.mult),
                ["gxq_col"], ["gxq_col"])
            T.dma(gq_bc[:], qg_d[l].partition_broadcast(128), w=["gq_bc"])
            T.dma(gk_bc[:], kg_d[l].partition_broadcast(128), w=["gk_bc"])
            T.dma(gxk_bc[:], xkg_d[l].partition_broadcast(128), w=["gxk_bc"])
            T.dma(esink[:], sinks_d[l].partition_broadcast(128), w=["esink"])
            act(lambda e: e.activation(out=esink[:], in_=esink[:], func=AF.Exp), ["esink"], ["esink"])
            dve(lambda e: e.tensor_copy(out=g10[:, 0:8, :], in_=gq_bc[:, None, :].to_broadcast([128, 8, 64])), ["gq_bc"], ["g10"])
            dve(lambda e: e.tensor_copy(out=g10[:, 8:10, :], in_=gk_bc[:, None, :].to_broadcast([128, 2, 64])), ["gk_bc", "g10"], ["g10"])

            mark('vecs')
            for mt in range(2):
                T.dma(s_memx, mem_d[mt * 128:(mt + 1) * 128, :], w=["s_memx"])
                act(lambda e: e.activation(out=s_memn, in_=s_memx, func=AF.Square, accum_out=s_smA[:, 0:1]), ["s_memx"], ["s_memn", "s_smA"])
                rsqrt_of(s_smA[:, 1:2], s_smA[:, 0:1], 1.0 / D, ["s_smA"], ["s_smA"])
                act(lambda e: e.activation(out=s_memn, in_=s_memx, func=AF.Copy, scale=s_smA[:, 1:2]), ["s_memx", "s_smA"], ["s_memn"])
                for kc in range(8):
                    tr(ps_t[:, kc, :], s_memn[:, kc * 128:(kc + 1) * 128], ident_b[:], ["s_memn", "ident_b"], ["ps_t"], sig=(kc == 7))
                dve(lambda e: e.tensor_copy(out=s_memT[:, :, mt * 128:(mt + 1) * 128], in_=ps_t[:]), ["ps_t"], ["s_memT"])
            Wkv = Wsb[:, :, 0:D]
            for kc in range(8):
                sg_ = stage[kc % 2]
                T.dma(sg_, wkv_d[l, kc * 128:(kc + 1) * 128, :], w=["stage%d" % (kc % 2)])
                eng = "act" if kc % 2 == 0 else "pool"
                if eng == "act":
                    act(lambda e: e.activation(out=Wkv[:, kc, :], in_=sg_, func=AF.Copy, scale=gm_col[:, kc:kc + 1]),
                        ["stage%d" % (kc % 2), "gm_col"], ["Wsb"])
                else:
                    pool(lambda e: e.tensor_scalar(out=Wkv[:, kc, :], in0=sg_, scalar1=gm_col[:, kc:kc + 1], scalar2=None, op0=ALU.mult),
                         ["stage%d" % (kc % 2), "gm_col"], ["Wsb"])
            for mt in range(2):
                for n in range(2):
                    for kc in range(8):
                        mm(ps_z[:], s_memT[:, kc, mt * 128:(mt + 1) * 128], Wkv[:, kc, n * 512:(n + 1) * 512], kc == 0, kc == 7,
                           ["s_memT", "Wsb"], ["ps_z"])
                    if n == 0:
                        act(lambda e: e.copy(out=s_mkf.rearrange("p h d -> p (h d)"), in_=ps_z[:]), ["ps_z"], ["s_mkf"])
                        dve(lambda e: e.tensor_tensor(out=s_memx[:, 0:512], in0=s_mkf.rearrange("p h d -> p (h d)"),
                                                      in1=s_mkf.rearrange("p h d -> p (h d)"), op=ALU.mult), ["s_mkf"], ["s_memx"])
                        dve(lambda e: e.tensor_reduce(out=s_smA[:, 4:8], in_=s_memx[:, 0:512].rearrange("p (h d) -> p h d", h=4),
                                                      axis=AX.X, op=ALU.add), ["s_memx"], ["s_smA"])
                        rsqrt_of(s_smA[:, 8:12], s_smA[:, 4:8], 1.0 / 128, ["s_smA"], ["s_smA"])
                        dve(lambda e: e.tensor_tensor(out=s_mkf, in0=s_mkf, in1=s_smA[:, 8:12, None].to_broadcast([128, 4, 128]), op=ALU.mult),
                            ["s_mkf", "s_smA"], ["s_mkf"])
                        dve(lambda e: e.tensor_tensor(out=s_mkb, in0=s_mkf, in1=gxk_bc[:, None, :].to_broadcast([128, 4, 128]), op=ALU.mult),
                            ["s_mkf", "gxk_bc"], ["s_mkb"])
                        for h in range(4):
                            tr(ps_t[:, h, :], s_mkb[:, h, :], ident_b[:], ["s_mkb", "ident_b"], ["ps_t"], sig=(h == 3))
                        dve(lambda e: e.tensor_scalar(out=mkT[:, :, mt * 128:(mt + 1) * 128], in0=ps_t[:, 0:4, :], scalar1=gxq_col[:, 0:1],
                                                      scalar2=None, op0=ALU.mult), ["ps_t", "gxq_col"], ["mkT"])
                    else:
                        act(lambda e: e.copy(out=mv_aug[:, mt, :, 0:128], in_=ps_z[:].rearrange("p (h d) -> p h d", h=4)), ["ps_z"], ["mv_aug"])

            mark('memkv')
            cnt = 0
            for kc in range(8):
                for (c0, c1) in ((0, 1024), (1024, 2048), (2048, 3072), (3072, INW)):
                    sg_ = stage[cnt % 2]; sk = "stage%d" % (cnt % 2)
                    T.dma(sg_[:, 0:c1 - c0], w_in_d[l, kc * 128:(kc + 1) * 128, c0:c1], w=[sk])
                    if cnt % 2 == 0:
                        act(lambda e: e.activation(out=Wsb[:, kc, c0:c1], in_=sg_[:, 0:c1 - c0], func=AF.Copy, scale=g_col[:, kc:kc + 1]),
                            [sk, "g_col"], ["Wsb"])
                    else:
                        pool(lambda e: e.tensor_scalar(out=Wsb[:, kc, c0:c1], in0=sg_[:, 0:c1 - c0], scalar1=g_col[:, kc:kc + 1], scalar2=None,
                                                       op0=ALU.mult), [sk, "g_col"], ["Wsb"])
                    cnt += 1
            for kc in range(12):
                sg_ = stage[cnt % 2]; sk = "stage%d" % (cnt % 2)
                T.dma(sg_, wout_d[l, kc * 128:(kc + 1) * 128, :], w=[sk])
                if cnt % 2 == 0:
                    act(lambda e: e.copy(out=Wout[:, kc, :], in_=sg_), [sk], ["Wout"])
                else:
                    pool(lambda e: e.tensor_copy(out=Wout[:, kc, :], in_=sg_), [sk], ["Wout"])
                cnt += 1
            for kc in range(4):
                sg_ = stage[cnt % 2]; sk = "stage%d" % (cnt % 2)
                T.dma(sg_[:, 0:512], wglu_d[l, kc * 128:(kc + 1) * 128, :], w=[sk])
                dve(lambda e: e.tensor_copy(out=Wglu[:, kc, :], in_=sg_[:, 0:512]), [sk], ["Wglu"])
                cnt += 1

            mark('weights')
            join(ALLK)
            lr = s_sm[:, :, 2]; li = s_sm[:, :, 3]; dtv = s_sm[:, :, 4]; lrdt = s_sm[:, :, 5]; lidt = s_sm[:, :, 6]
            t7 = s_sm[:, :, 7]; t8 = s_sm[:, :, 8]; fr = s_sm[:, :, 9]; fi = s_sm[:, :, 10]; t11 = s_sm[:, :, 11]
            load_T(lr, lre_d[l].rearrange("(pr two) p -> pr (two p)", two=2), 16, "s_sm")
            load_T(li, lim_d[l].rearrange("(pr two) p -> pr (two p)", two=2), 16, "s_sm")
            T.dma(s_lg2[0:16, :], ldt_d[l].rearrange("(pr two) -> pr two", two=2), w=["s_lg2"])
            dve(lambda e: e.tensor_copy(out=ldT[0:16, :].rearrange("q (two p) -> q two p", two=2),
                                        in_=s_lg2[0:16, :, None].to_broadcast([16, 2, 64])), ["s_lg2"], ["ldT"])
            tr(ps_z[:, 0:16], ldT[0:16, :], ident_f[0:16, 0:16], ["ldT", "ident_f"], ["ps_z"])
            act(lambda e: e.activation(out=dtv, in_=ps_z[:, 0:16], func=AF.Exp), ["ps_z"], ["s_sm"])
            dve(lambda e: e.tensor_tensor(out=lrdt, in0=lr, in1=dtv, op=ALU.mult), ["s_sm"], ["s_sm"])
            dve(lambda e: e.tensor_tensor(out=lidt, in0=li, in1=dtv, op=ALU.mult), ["s_sm"], ["s_sm"])
            dve(lambda e: e.tensor_tensor(out=s_e9, in0=lrdt[:, :, None].to_broadcast([128, 16, 9]), in1=nvec[:, None, :].to_broadcast([128, 16, 9]),
                                          op=ALU.mult), ["s_sm", "nvec"], ["s_e9"])
            dve(lambda e: e.tensor_tensor(out=s_a9, in0=lidt[:, :, None].to_broadcast([128, 16, 9]), in1=nvec[:, None, :].to_broadcast([128, 16, 9]),
                                          op=ALU.mult), ["s_sm", "nvec"], ["s_a9"])
            act(lambda e: e.activation(out=s_mag9, in_=s_e9, func=AF.Exp), ["s_e9"], ["s_mag9"])
            f2 = lambda v: v.rearrange("p a b -> p (a b)")
            sin_of(f2(s_sin9), f2(s_a9), 144, 0.0, ["s_a9"], ["s_sin9"])
            sin_of(f2(s_cos9), f2(s_a9), 144, math.pi / 2, ["s_a9"], ["s_cos9"])
            dve(lambda e: e.tensor_tensor(out=s_ar9, in0=s_mag9, in1=s_cos9, op=ALU.mult), ["s_mag9", "s_cos9"], ["s_ar9"])
            dve(lambda e: e.tensor_tensor(out=s_ai9, in0=s_mag9, in1=s_sin9, op=ALU.mult), ["s_mag9", "s_sin9"], ["s_ai9"])
            dve(lambda e: e.tensor_copy(out=magA[:], in_=s_mag9[:, :, 8]), ["s_mag9"], ["magA"])
            tmp16 = (s_ki[:, 0:16], s_kf[:, 0:16])
            dve(lambda e: e.tensor_scalar(out=tmp16[0], in0=lidt, scalar1=float(8.0 / TWO_PI), scalar2=None, op0=ALU.mult), ["s_sm"], ["sr_ki"])
            dve(lambda e: e.tensor_copy(out=tmp16[1], in_=tmp16[0]), ["sr_ki"], ["sr_kf"])
            dve(lambda e: e.tensor_scalar(out=t7, in0=lidt, scalar1=8.0, scalar2=None, op0=ALU.mult), ["s_sm"], ["s_sm"])
            dve(lambda e: e.scalar_tensor_tensor(out=t7, in0=tmp16[1], scalar=float(-CW1), in1=t7, op0=ALU.mult, op1=ALU.add), ["sr_kf", "s_sm"], ["s_sm"])
            dve(lambda e: e.scalar_tensor_tensor(out=t7, in0=tmp16[1], scalar=float(-CW2), in1=t7, op0=ALU.mult, op1=ALU.add), ["sr_kf", "s_sm"], ["s_sm"])
            for p0 in range(0, 16, 4):
                ta3 = s_ang[:, 0:4 * NCH].rearrange("p (a k) -> p a k", k=NCH)
                dve(lambda e: e.tensor_tensor(out=ta3, in0=t7[:, p0:p0 + 4, None].to_broadcast([128, 4, NCH]),
                                              in1=kvec[:, None, :].to_broadcast([128, 4, NCH]), op=ALU.mult), ["s_sm", "kvec"], ["s_ang"])
                sin_of(msT[:, p0:p0 + 4, :].rearrange("p a k -> p (a k)"), s_ang[:, 0:4 * NCH], 4 * NCH, 0.0, ["s_ang"], ["msT"])
                sin_of(mcT[:, p0:p0 + 4, :].rearrange("p a k -> p (a k)"), s_ang[:, 0:4 * NCH], 4 * NCH, math.pi / 2, ["s_ang"], ["mcT"])
            dve(lambda e: e.tensor_tensor(out=mcA[:], in0=mcT[:, :, 1], in1=magA[:], op=ALU.mult), ["mcT", "magA"], ["mcA"])
            dve(lambda e: e.tensor_tensor(out=msA[:], in0=msT[:, :, 1], in1=magA[:], op=ALU.mult), ["msT", "magA"], ["msA"])
            dve(lambda e: e.tensor_scalar(out=nmsA[:], in0=msA[:], scalar1=-1.0, scalar2=None, op0=ALU.mult), ["msA"], ["nmsA"])
            dve(lambda e: e.tensor_copy(out=magTab[:], in_=magA[:, :, None].to_broadcast([128, 16, NCH])), ["magA"], ["magTab"])
            dve(lambda e: e.memset(magTab[:, :, 0:1], 0.0), ["magTab"], ["magTab"])
            dve(lambda e: e.tensor_tensor(out=t8, in0=lr, in1=lr, op=ALU.mult), ["s_sm"], ["s_sm"])
            dve(lambda e: e.tensor_tensor(out=t11, in0=li, in1=li, op=ALU.mult), ["s_sm"], ["s_sm"])
            dve(lambda e: e.tensor_tensor(out=t8, in0=t8, in1=t11, op=ALU.add), ["s_sm"], ["s_sm"])
            dve(lambda e: e.reciprocal(out=t8, in_=t8), ["s_sm"], ["s_sm"])
            dve(lambda e: e.tensor_scalar(out=t11, in0=s_ar9[:, :, 1], scalar1=-1.0, scalar2=None, op0=ALU.add), ["s_ar9"], ["s_sm"])
            dve(lambda e: e.tensor_tensor(out=fr, in0=t11, in1=lr, op=ALU.mult), ["s_sm"], ["s_sm"])
            dve(lambda e: e.tensor_tensor(out=t7, in0=s_ai9[:, :, 1], in1=li, op=ALU.mult), ["s_sm", "s_ai9"], ["s_sm"])
            dve(lambda e: e.tensor_tensor(out=fr, in0=fr, in1=t7, op=ALU.add), ["s_sm"], ["s_sm"])
            dve(lambda e: e.tensor_tensor(out=fr, in0=fr, in1=t8, op=ALU.mult), ["s_sm"], ["s_sm"])
            dve(lambda e: e.tensor_tensor(out=fi, in0=s_ai9[:, :, 1], in1=lr, op=ALU.mult), ["s_sm", "s_ai9"], ["s_sm"])
            dve(lambda e: e.tensor_tensor(out=t7, in0=t11, in1=li, op=ALU.mult), ["s_sm"], ["s_sm"])
            dve(lambda e: e.tensor_tensor(out=fi, in0=fi, in1=t7, op=ALU.subtract), ["s_sm"], ["s_sm"])
            dve(lambda e: e.tensor_tensor(out=fi, in0=fi, in1=t8, op=ALU.mult), ["s_sm"], ["s_sm"])
            mark('ssm_tabs')
            T.dma(s_Br, bre_d[l].rearrange("(pr two) p c -> (two p) pr c", two=2), w=["s_Br"])
            T.dma(s_Bi, bim_d[l].rearrange("(pr two) p c -> (two p) pr c", two=2), w=["s_Bi"])
            frb = fr[:, :, None].to_broadcast([128, 16, 16]); fib = fi[:, :, None].to_broadcast([128, 16, 16])
            dve(lambda e: e.tensor_tensor(out=s_bbr, in0=s_Br, in1=frb, op=ALU.mult), ["s_Br", "s_sm"], ["s_bbr"])
            dve(lambda e: e.tensor_tensor(out=s_bt, in0=s_Bi, in1=fib, op=ALU.mult), ["s_Bi", "s_sm"], ["s_bt"])
            dve(lambda e: e.tensor_tensor(out=s_bbr, in0=s_bbr, in1=s_bt, op=ALU.subtract), ["s_bbr", "s_bt"], ["s_bbr"])
            dve(lambda e: e.tensor_tensor(out=s_bbi, in0=s_Bi, in1=frb, op=ALU.mult), ["s_Bi", "s_sm"], ["s_bbi"])
            dve(lambda e: e.tensor_tensor(out=s_bt, in0=s_Br, in1=fib, op=ALU.mult), ["s_Br", "s_sm", "s_bbr"], ["s_bt"])
            dve(lambda e: e.tensor_tensor(out=s_bbi, in0=s_bbi, in1=s_bt, op=ALU.add), ["s_bbi", "s_bt"], ["s_bbi"])
            for (cd, dst, dk) in ((cre_d, s_Cr, "s_Cr"), (cim_d, s_Ci, "s_Ci")):
                for pr in range(16):
                    T.dma(s_Cst[(pr % 8) * 16:(pr % 8) * 16 + 16, pr // 8, :].rearrange("c (two p) -> c two p", two=2),
                          cd[l, 2 * pr:2 * pr + 2].rearrange("two c p -> c two p"), w=["s_Cst"])
                for hh in range(2):
                    tr(ps_z[:, hh * 128:(hh + 1) * 128], s_Cst[:, hh, :], ident_f[:], ["s_Cst", "ident_f"], ["ps_z"], sig=(hh == 1))
                dve(lambda e: e.tensor_copy(out=dst.rearrange("p a c -> p (a c)"), in_=ps_z[:, 0:256]), ["ps_z"], [dk])
            mark('ssm_BC')
            for q in range(4):
                prs = slice(4 * q, 4 * q + 4)
                arb = s_ar9[:, prs, :, None].to_broadcast([128, 4, 9, 16]); aib = s_ai9[:, prs, :, None].to_broadcast([128, 4, 9, 16])
                crb = s_Cr[:, prs, None, :].to_broadcast([128, 4, 9, 16]); cib = s_Ci[:, prs, None, :].to_broadcast([128, 4, 9, 16])
                dve(lambda e: e.tensor_tensor(out=s_CAr, in0=crb, in1=arb, op=ALU.mult), ["s_Cr", "s_ar9"], ["s_CAr"])
                dve(lambda e: e.tensor_tensor(out=s_CAt, in0=cib, in1=aib, op=ALU.mult), ["s_Ci", "s_ai9"], ["s_CAt"])
                dve(lambda e: e.tensor_tensor(out=s_CAr, in0=s_CAr, in1=s_CAt, op=ALU.subtract), ["s_CAr", "s_CAt"], ["s_CAr"])
                dve(lambda e: e.tensor_tensor(out=s_CAi, in0=crb, in1=aib, op=ALU.mult), ["s_Cr", "s_ai9"], ["s_CAi"])
                dve(lambda e: e.tensor_tensor(out=s_CAt, in0=cib, in1=arb, op=ALU.mult), ["s_Ci", "s_ar9", "s_CAr"], ["s_CAt"])
                dve(lambda e: e.tensor_tensor(out=s_CAi, in0=s_CAi, in1=s_CAt, op=ALU.add), ["s_CAi", "s_CAt"], ["s_CAi"])
                pool(lambda e: e.memset(Vf[:, prs, :, :], 0.0), [], ["Vf"])
                for two in range(2):
                    rows = slice(64 * two, 64 * two + 64)
                    dve(lambda e: e.tensor_copy(out=Vf[rows, prs, 0, two * 128:(two + 1) * 128].rearrange("p a (t c) -> p a t c", c=16),
                                                in_=s_CAr[rows, :, 1:9, :]), ["s_CAr", "Vf"], ["Vf"])
                    dve(lambda e: e.tensor_scalar(out=Vf[rows, prs, 1, two * 128:(two + 1) * 128].rearrange("p a (t c) -> p a t c", c=16),
                                                  in0=s_CAi[rows, :, 1:9, :], scalar1=-1.0, scalar2=None, op0=ALU.mult), ["s_CAi", "Vf"], ["Vf"])
                pool(lambda e: e.memset(s_CAzr, 0.0), [], ["s_CAzr"])
                pool(lambda e: e.memset(s_CAzi, 0.0), [], ["s_CAzi"])
                pool(lambda e: e.memset(s_Bzr, 0.0), [], ["s_Bzr"])
                pool(lambda e: e.memset(s_Bzi, 0.0), [], ["s_Bzi"])
                for two in range(2):
                    rows = slice(64 * two, 64 * two + 64)
                    dve(lambda e: e.tensor_copy(out=s_CAzr[rows, :, two, :, :], in_=s_CAr[rows, :, 0:8, :]), ["s_CAr", "s_CAzr"], ["s_CAzr"])
                    dve(lambda e: e.tensor_scalar(out=s_CAzi[rows, :, two, :, :], in0=s_CAi[rows, :, 0:8, :], scalar1=-1.0, scalar2=None, op0=ALU.mult),
                        ["s_CAi", "s_CAzi"], ["s_CAzi"])
                    for j in range(4):
                        c0 = 32 * j + 16 * two
                        dve(lambda e: e.tensor_copy(out=s_Bzr[rows, j, c0:c0 + 16], in_=s_bbr[rows, 4 * q + j, :]), ["s_bbr", "s_Bzr"], ["s_Bzr"])
                        dve(lambda e: e.tensor_copy(out=s_Bzi[rows, j, c0:c0 + 16], in_=s_bbi[rows, 4 * q + j, :]), ["s_bbi", "s_Bzi"], ["s_Bzi"])
                for j in range(4):
                    mm(ps_z[:, 0:256], s_Bzr[:, j, :], s_CAzr[:, j].rearrange("p a t c -> p (a t c)"), j == 0, False, ["s_Bzr", "s_CAzr"], ["ps_z"])
                    mm(ps_z[:, 0:256], s_Bzi[:, j, :], s_CAzi[:, j].rearrange("p a t c -> p (a t c)"), False, j == 3, ["s_Bzi", "s_CAzi"], ["ps_z"])
                dve(lambda e: e.tensor_copy(out=s_Kt.rearrange("p a t c -> p (a t c)"), in_=ps_z[:, 0:256]), ["ps_z"], ["s_Kt"])
                dve(lambda e: e.tensor_scalar(out=s_Kd, in0=dmask[:], scalar1=d_col[:, q:q + 1], scalar2=None, op0=ALU.mult), ["dmask", "d_col"], ["s_Kd"])
                dve(lambda e: e.tensor_tensor(out=s_Kt[:, :, 0, :], in0=s_Kt[:, :, 0, :], in1=s_Kd.rearrange("p (a c) -> p a c", a=2), op=ALU.add),
                    ["s_Kt", "s_Kd"], ["s_Kt"])
                dve(lambda e: e.tensor_copy(out=Kstrip[:, q, :, 7:15, :], in_=s_Kt), ["s_Kt"], ["Kstrip"])
                brb = s_bbr[:, prs, None, :].to_broadcast([128, 4, 8, 16]); bib = s_bbi[:, prs, None, :].to_broadcast([128, 4, 8, 16])
                ar8 = s_ar9[:, prs, 0:8, None].to_broadcast([128, 4, 8, 16]); ai8 = s_ai9[:, prs, 0:8, None].to_broadcast([128, 4, 8, 16])
                dve(lambda e: e.tensor_tensor(out=s_Xr, in0=brb, in1=ar8, op=ALU.mult), ["s_bbr", "s_ar9"], ["s_Xr"])
                dve(lambda e: e.tensor_tensor(out=s_Xt, in0=bib, in1=ai8, op=ALU.mult), ["s_bbi", "s_ai9"], ["s_Xt"])
                dve(lambda e: e.tensor_tensor(out=s_Xr, in0=s_Xr, in1=s_Xt, op=ALU.subtract), ["s_Xr", "s_Xt"], ["s_Xr"])
                dve(lambda e: e.tensor_tensor(out=s_Xi, in0=bib, in1=ar8, op=ALU.mult), ["s_bbi", "s_ar9"], ["s_Xi"])
                dve(lambda e: e.tensor_tensor(out=s_Xt, in0=brb, in1=ai8, op=ALU.mult), ["s_bbr", "s_ai9", "s_Xr"], ["s_Xt"])
                dve(lambda e: e.tensor_tensor(out=s_Xi, in0=s_Xi, in1=s_Xt, op=ALU.add), ["s_Xi", "s_Xt"], ["s_Xi"])
                pool(lambda e: e.memset(s_Xzr, 0.0), [], ["s_Xzr"])
                pool(lambda e: e.memset(s_Xzi, 0.0), [], ["s_Xzi"])
                for two in range(2):
                    rows = slice(64 * two, 64 * two + 64)
                    dve(lambda e: e.tensor_copy(out=s_Xzr[rows, :, :, 16 * two:16 * two + 16].rearrange("p n a c -> p a n c"), in_=s_Xr[rows]), ["s_Xr", "s_Xzr"], ["s_Xzr"])
                    dve(lambda e: e.tensor_copy(out=s_Xzi[rows, :, :, 16 * two:16 * two + 16].rearrange("p n a c -> p a n c"), in_=s_Xi[rows]), ["s_Xi", "s_Xzi"], ["s_Xzi"])
                for hh in range(2):
                    for s4 in range(4):
                        s_ = 4 * hh + s4
                        for ri, xz in ((0, s_Xzr), (1, s_Xzi)):
                            tr(ps_t[:, 2 * s4 + ri, :], xz[:, 7 - s_, :, :].rearrange("p a c -> p (a c)"), ident_b[:], ["s_Xzr", "s_Xzi", "ident_b"], ["ps_t"],
                               sig=(s4 == 3 and ri == 1))
                    dve(lambda e: e.tensor_copy(out=Wssm[:, q, 4 * hh:4 * hh + 4, :, :].rearrange("p s r m -> p (s r) m"), in_=ps_t[:]),
                        ["ps_t"], ["Wssm"])
            pool(lambda e: e.memset(carry[:], 0.0), [], ["carry"])
            pool(lambda e: e.memset(Hb[:], 0.0), [], ["Hb"])
            join(ALLK)
            pool(lambda e: e.memset(vaug[0][:, :, 64:65], 1.0), [], ["vaug0"])
            pool(lambda e: e.memset(vaug[1][:, :, 64:65], 1.0), [], ["vaug1"])

        def block_front(l, b, src):
            slot = b % 2
            bl = b % BPS
            cols = slice(bl * 128, (bl + 1) * 128)
            xk, kTk, vk, pk = "xt%d" % slot, "kT%d" % slot, "vaug%d" % slot, None
            x_t = xt[slot]
            T.dma(x_t, src[b * 128:(b + 1) * 128, :], r=[("res", b)], w=[xk])
            act(lambda e: e.activation(out=junk, in_=x_t, func=AF.Square, accum_out=st1[:, 0:1]), [xk], ["junk", "st1"])
            rsqrt_of(st1[:, 1:2], st1[:, 0:1], 1.0 / D, ["st1"], ["st1"])
            act(lambda e: e.activation(out=hb, in_=x_t, func=AF.Copy, scale=st1[:, 1:2]), [xk, "st1"], ["hb"])
            for kc in range(8):
                tr(ps_t[:, kc, :], hb[:, kc * 128:(kc + 1) * 128], ident_b[:], ["hb", "ident_b"], ["ps_t"], sig=(kc == 7))
            dve(lambda e: e.tensor_copy(out=hTs[slot], in_=ps_t[:]), ["ps_t"], ["hT%d" % slot])


        def block_rest(l, b):
            slot = b % 2
            bl = b % BPS
            cols = slice(bl * 128, (bl + 1) * 128)
            kTk, vk = "kT%d" % slot, "vaug%d" % slot
            hT = hTs[slot]
            hTk = "hT%d" % slot

            def zgroup(c0, n):
                for kc in range(8):
                    mm(ps_z[:, 0:n], hT[:, kc, :], Wsb[:, kc, c0:c0 + n], kc == 0, kc == 7, [hTk, "Wsb"], ["ps_z"])

            zgroup(C_AQ, 512)
            act(lambda e: e.copy(out=qk[:, 0:8, :].rearrange("p h d -> p (h d)"), in_=ps_z[:]), ["ps_z"], ["qk"])
            zgroup(C_AK, 256)
            act(lambda e: e.copy(out=qk[:, 8:10, :].rearrange("p h d -> p (h d)"), in_=ps_z[:, 0:128]), ["ps_z", "qk"], ["qk"])
            act(lambda e: e.copy(out=vaug[slot][:, :, 0:64], in_=ps_z[:, 128:256].rearrange("p (h d) -> p h d", h=2)), ["ps_z"], [vk])
            zgroup(C_AG, 512)
            act(lambda e: e.activation(out=gate_a, in_=ps_z[:], func=AF.Silu), ["ps_z"], ["gate_a"])
            for (c0, dst, dk, fn) in ((C_SU, uT, "uT", None), (C_SG, gT, "gT", AF.Silu)):
                for ct in range(4):
                    for kc in range(8):
                        mm(ps_z[:, ct * 128:(ct + 1) * 128], Wsb[:, kc, c0 + ct * 128:c0 + (ct + 1) * 128], hT[:, kc, :], kc == 0, kc == 7,
                           [hTk, "Wsb"], ["ps_z"], sig=(ct == 3 and kc == 7))
                pz3 = ps_z[:].rearrange("p (c t) -> p c t", c=4)
                if fn is None:
                    act(lambda e: e.copy(out=dst[:, :, cols], in_=pz3), ["ps_z"], [dk])
                else:
                    act(lambda e: e.activation(out=dst[:, :, cols], in_=pz3, func=fn), ["ps_z"], [dk])

            if b == 0: mark('bm_z')
            dve(lambda e: e.tensor_tensor(out=sq, in0=qk, in1=qk, op=ALU.mult), ["qk"], ["sq"])
            dve(lambda e: e.tensor_reduce(out=st1[:, 2:12], in_=sq, axis=AX.X, op=ALU.add), ["sq"], ["st1"])
            if b == 0: mark("r1")
            rsqrt_of(st1[:, 2:12], st1[:, 2:12], 1.0 / 64, ["st1"], ["st1"])
            if b == 0: mark("r2")
            dve(lambda e: e.tensor_tensor(out=qk, in0=qk, in1=st1[:, 2:12, None].to_broadcast([128, 10, 64]), op=ALU.mult), ["qk", "st1"], ["qk"])
            dve(lambda e: e.tensor_tensor(out=qk, in0=qk, in1=g10[:], op=ALU.mult), ["qk", "g10"], ["qk"])
            if b == 0: mark("r3")
            qk4 = qk.rearrange("p h (a j) -> p h a j", a=2)
            r14 = rt1.rearrange("p h (a j) -> p h a j", a=2)
            r24 = rt2.rearrange("p h (a j) -> p h a j", a=2)
            qr4 = qr.rearrange("p h (a j) -> p h a j", a=2)
            cb = cosT[:, b, None, None, :].to_broadcast([128, 10, 2, 32])
            sb_ = sinT[:, b, None, :].to_broadcast([128, 10, 32])
            dve(lambda e: e.tensor_tensor(out=r14, in0=qk4, in1=cb, op=ALU.mult), ["qk", "cosT"], ["rt1"])
            if b == 0: mark("r4")
            dve(lambda e: e.tensor_tensor(out=r24[:, :, 0, :], in0=qk4[:, :, 1, :], in1=sb_, op=ALU.mult), ["qk", "sinT"], ["rt2"])
            dve(lambda e: e.tensor_tensor(out=r24[:, :, 1, :], in0=qk4[:, :, 0, :], in1=sb_, op=ALU.mult), ["qk", "sinT", "rt2"], ["rt2"])
            if b == 0: mark("r5")
            dve(lambda e: e.tensor_tensor(out=qr4[:, :, 0, :], in0=r14[:, :, 0, :], in1=r24[:, :, 0, :], op=ALU.subtract), ["rt1", "rt2"], ["qr"])
            dve(lambda e: e.tensor_tensor(out=qr4[:, :, 1, :], in0=r14[:, :, 1, :], in1=r24[:, :, 1, :], op=ALU.add), ["rt1", "rt2", "qr"], ["qr"])
            if b == 0: mark("r6")
            for j in range(5):
                tr(ps_t[:, j, :], qr[:, 2 * j:2 * j + 2, :].rearrange("p h d -> p (h d)"), ident_b[:], ["qr", "ident_b"], ["ps_t"], sig=(j == 4))
            if b == 0: mark("r7")
            dve(lambda e: e.tensor_copy(out=qT, in_=ps_t[:, 0:4, :]), ["ps_t"], ["qT"])
            dve(lambda e: e.tensor_copy(out=kT[slot], in_=ps_t[:, 4, :]), ["ps_t"], [kTk])
            if b == 0: mark('bm_rope')
            tiles = ([] if b == 0 else [(1 - slot, mask_prev, "mask_prev")]) + [(slot, mask_cur, "mask_cur")]
            for kvh in range(2):
                rows = slice(64 * kvh, 64 * kvh + 64)
                for ti, (sl, mk, mkk) in enumerate(tiles):
                    mm(ps_s[ti][:], kT[sl][rows, :], qT[rows, :, :].rearrange("p j q -> p (j q)"), True, True,
                       ["kT%d" % sl, "qT"], ["ps_s%d" % ti])
                    act(lambda e: e.activation(out=pTm[ti], in_=ps_s[ti][:], func=AF.Exp, scale=0.125), ["ps_s%d" % ti], ["pTm%d" % ti])
                    pool(lambda e: e.tensor_tensor(out=pTm[ti], in0=pTm[ti], in1=mk[:].rearrange("p j q -> p (j q)"), op=ALU.mult),
                         ["pTm%d" % ti, mkk], ["pTm%d" % ti])
                for j in range(4):
                    for ti, (sl, mk, mkk) in enumerate(tiles):
                        mm(ps_f[:, j * 65:(j + 1) * 65], pTm[ti][:, j * 128:(j + 1) * 128], vaug[sl][:, kvh, :], ti == 0, ti == len(tiles) - 1,
                           ["pTm%d" % ti, "vaug%d" % sl], ["ps_f"], sig=(j == 3 and ti == len(tiles) - 1))
                o4 = ps_f[:, 0:260].rearrange("p (j d) -> p j d", d=65)
                dve(lambda e: e.tensor_tensor(out=den[:, 0:4], in0=o4[:, :, 64], in1=esink[:, 4 * kvh:4 * kvh + 4], op=ALU.add), ["ps_f", "esink"], ["den"])
                dve(lambda e: e.reciprocal(out=den[:, 0:4], in_=den[:, 0:4]), ["den"], ["den"])
                for j in range(4):
                    h = 4 * kvh + j
                    dve(lambda e: e.scalar_tensor_tensor(out=mix_a[:, h * 64:(h + 1) * 64], in0=o4[:, j, 0:64], scalar=den[:, j:j + 1],
                                                         in1=gate_a[:, h * 64:(h + 1) * 64], op0=ALU.mult, op1=ALU.mult),
                        ["ps_f", "den", "gate_a"], ["mix_a"])
            for j in range(4):
                tr(ps_t[:, j, :], mix_a[:, j * 128:(j + 1) * 128], ident_b[:], ["mix_a", "ident_b"], ["ps_t"], sig=(j == 3))
            dve(lambda e: e.tensor_copy(out=mixT[:, 0:4, cols], in_=ps_t[:, 0:4, :]), ["ps_t"], ["mixT"])
            if b == 0: mark('bm_attn')
            zgroup(C_XQ, 512)
            act(lambda e: e.copy(out=xq_f.rearrange("p h d -> p (h d)"), in_=ps_z[:]), ["ps_z"], ["xq_f"])
            zgroup(C_XG, 512)
            act(lambda e: e.activation(out=gate_x, in_=ps_z[:], func=AF.Silu), ["ps_z"], ["gate_x"])
            sq4 = sq.rearrange("p h d -> p (h d)")[:, 0:512].rearrange("p (h d) -> p h d", h=4)
            dve(lambda e: e.tensor_tensor(out=sq4, in0=xq_f, in1=xq_f, op=ALU.mult), ["xq_f"], ["sq"])
            dve(lambda e: e.tensor_reduce(out=st1[:, 12:16], in_=sq4, axis=AX.X, op=ALU.add), ["sq"], ["st1"])
            rsqrt_of(st1[:, 12:16], st1[:, 12:16], 1.0 / 128, ["st1"], ["st1"])
            dve(lambda e: e.tensor_tensor(out=xq_b, in0=xq_f, in1=st1[:, 12:16, None].to_broadcast([128, 4, 128]), op=ALU.mult), ["xq_f", "st1"], ["xq_b"])
            for h in range(4):
                tr(ps_t[:, h, :], xq_b[:, h, :], ident_b[:], ["xq_b", "ident_b"], ["ps_t"], sig=(h == 3))
            dve(lambda e: e.tensor_copy(out=xqT, in_=ps_t[:, 0:4, :]), ["ps_t"], ["xqT"])
            for mt in range(2):
                for h in range(4):
                    mm(ps_s[mt][:, h * 128:(h + 1) * 128], mkT[:, h, mt * 128:(mt + 1) * 128], xqT[:, h, :], True, True, ["mkT", "xqT"],
                       ["ps_s%d" % mt], sig=(h == 3))
                act(lambda e: e.activation(out=pX[mt], in_=ps_s[mt][:], func=AF.Exp), ["ps_s%d" % mt], ["pX%d" % mt])
            for hp in range(2):
                for i in range(2):
                    h = 2 * hp + i
                    for mt in range(2):
                        mm(ps_f[:, i * 129:(i + 1) * 129], pX[mt][:, h * 128:(h + 1) * 128], mv_aug[:, mt, h, :], mt == 0, mt == 1,
                           ["pX%d" % mt, "mv_aug"], ["ps_f"], sig=(i == 1 and mt == 1))
                o2 = ps_f[:, 0:258].rearrange("p (i d) -> p i d", d=129)
                dve(lambda e: e.reciprocal(out=den[:, 4:6], in_=o2[:, :, 128]), ["ps_f"], ["den"])
                for i in range(2):
                    h = 2 * hp + i
                    dve(lambda e: e.scalar_tensor_tensor(out=mix_c[:, h * 128:(h + 1) * 128], in0=o2[:, i, 0:128], scalar=den[:, 4 + i:5 + i],
                                                         in1=gate_x[:, h * 128:(h + 1) * 128], op0=ALU.mult, op1=ALU.mult),
                        ["ps_f", "den", "gate_x"], ["mix_c"])
            for j in range(4):
                tr(ps_t[:, j, :], mix_c[:, j * 128:(j + 1) * 128], ident_b[:], ["mix_c", "ident_b"], ["ps_t"], sig=(j == 3))
            dve(lambda e: e.tensor_copy(out=mixT[:, 8:12, cols], in_=ps_t[:, 0:4, :]), ["ps_t"], ["mixT"])

        def ssm_superblock(l, sbi):
            ps_zs4 = ps_zs[:, 0:8 * NCH].rearrange("p (a r k) -> p a r k", a=4, r=2)
            for q in range(4):
                prs = slice(4 * q, 4 * q + 4)
                u3s = []
                for prl in range(4):
                    rows = slice(32 * prl, 32 * prl + 32)
                    kw = {"tile_position": (96, 0)} if prl == 3 else {}
                    u3 = uT[rows, q, :].rearrange("p (k s) -> p s k", s=8)
                    u3s.append((rows, kw, u3))
                    for ri in range(2):
                        for s_ in range(8):
                            mm(ps_zs4[:, prl, ri, :], Wssm[rows, q, s_, ri, :], u3[:, s_, :], s_ == 0, s_ == 7, ["Wssm", "uT"], ["ps_zs"],
                               sig=(ri == 1 and s_ == 7), **kw)
                zr = ps_zs4[:, :, 0, :]; zi = ps_zs4[:, :, 1, :]
                mc = mcT[:, prs, :]; ms = msT[:, prs, :]
                dve(lambda e: e.tensor_tensor(out=sct[:, 0], in0=zr, in1=mc, op=ALU.mult), ["ps_zs", "mcT"], ["sct"])
                dve(lambda e: e.tensor_tensor(out=sct[:, 1], in0=zi, in1=ms, op=ALU.mult), ["ps_zs", "msT", "sct"], ["sct"])
                dve(lambda e: e.tensor_tensor(out=sct[:, 2], in0=zi, in1=mc, op=ALU.mult), ["ps_zs", "mcT", "sct"], ["sct"])
                dve(lambda e: e.tensor_tensor(out=sct[:, 3], in0=zr, in1=ms, op=ALU.mult), ["ps_zs", "msT", "sct"], ["sct"])
                dve(lambda e: e.tensor_tensor(out=zt[:, 0], in0=sct[:, 0], in1=sct[:, 1], op=ALU.add), ["sct"], ["zt"])
                dve(lambda e: e.tensor_tensor(out=zt[:, 1], in0=sct[:, 2], in1=sct[:, 3], op=ALU.subtract), ["sct", "zt"], ["zt"])
                if sbi > 0:
                    hr_ = carry[:, prs, 0]; hi_ = carry[:, prs, 1]
                    dve(lambda e: e.tensor_tensor(out=init4[:, 0, :], in0=hr_, in1=mcA[:, prs], op=ALU.mult), ["carry", "mcA"], ["init4"])
                    dve(lambda e: e.tensor_tensor(out=init4[:, 1, :], in0=hi_, in1=nmsA[:, prs], op=ALU.mult), ["carry", "nmsA", "init4"], ["init4"])
                    dve(lambda e: e.tensor_tensor(out=init4[:, 2, :], in0=hi_, in1=mcA[:, prs], op=ALU.mult), ["carry", "mcA", "init4"], ["init4"])
                    dve(lambda e: e.tensor_tensor(out=init4[:, 3, :], in0=hr_, in1=msA[:, prs], op=ALU.mult), ["carry", "msA", "init4"], ["init4"])
                    for ri in range(2):
                        for jj in range(2):
                            dve(lambda e: e.tensor_tensor(out=zt[:, ri, :, 0], in0=zt[:, ri, :, 0], in1=init4[:, 2 * ri + jj, :], op=ALU.add),
                                ["zt", "init4"], ["zt"])
                mg = magTab[:, prs, :].rearrange("p a k -> p (a k)")
                for ri in range(2):
                    dve(lambda e: e.tensor_tensor_scan(out=Gs[:, ri].rearrange("p a k -> p (a k)"), data0=mg,
                                                      data1=zt[:, ri].rearrange("p a k -> p (a k)"), initial=0.0, op0=ALU.mult, op1=ALU.add),
                        ["zt", "magTab", "Gs"], ["Gs"])
                f2 = lambda v: v.rearrange("p a k -> p (a k)")
                pool(lambda e: e.tensor_tensor(out=f2(sct[:, 0]), in0=f2(Gs[:, 0]), in1=f2(mc), op=ALU.mult), ["Gs", "mcT"], ["sct"])
                pool(lambda e: e.tensor_tensor(out=f2(sct[:, 1]), in0=f2(Gs[:, 1]), in1=f2(ms), op=ALU.mult), ["Gs", "msT", "sct"], ["sct"])
                pool(lambda e: e.tensor_tensor(out=f2(sct[:, 2]), in0=f2(Gs[:, 1]), in1=f2(mc), op=ALU.mult), ["Gs", "mcT", "sct"], ["sct"])
                pool(lambda e: e.tensor_tensor(out=f2(sct[:, 3]), in0=f2(Gs[:, 0]), in1=f2(ms), op=ALU.mult), ["Gs", "msT", "sct"], ["sct"])
                pool(lambda e: e.tensor_tensor(out=f2(Hf[:, 0]), in0=f2(sct[:, 0]), in1=f2(sct[:, 1]), op=ALU.subtract), ["sct"], ["Hf"])
                pool(lambda e: e.tensor_tensor(out=f2(Hf[:, 1]), in0=f2(sct[:, 2]), in1=f2(sct[:, 3]), op=ALU.add), ["sct", "Hf"], ["Hf"])
                Hfp = Hf.rearrange("p r a k -> p a r k")
                act(lambda e: e.copy(out=Hb[:, prs, :, 0:1], in_=carry[:, prs, :, None]), ["carry"], ["Hb"])
                act(lambda e: e.copy(out=Hb[:, prs, :, 1:NCH], in_=Hfp[:, :, :, 0:NCH - 1]), ["Hf", "Hb"], ["Hb"])
                act(lambda e: e.copy(out=carry[:, prs, :], in_=Hfp[:, :, :, NCH - 1]), ["Hf", "Hb"], ["carry"])
                for h2 in range(2):
                    for i2 in range(2):
                        prl = 2 * h2 + i2
                        rows, kw, u3 = u3s[prl]
                        for s_ in range(8):
                            mm(ps_y[0:NCH, i2 * 256:(i2 + 1) * 256].rearrange("k (a t c) -> k a t c", a=2, t=8), u3[:, s_, :],
                               Kstrip[rows, q, :, 7 - s_:15 - s_, :], s_ == 0, s_ == 7, ["uT", "Kstrip"], ["ps_y"], **kw)
                    for i2 in range(2):
                        pr = 4 * q + 2 * h2 + i2
                        for ri in range(2):
                            mm(ps_y2[0:NCH, i2 * 256:(i2 + 1) * 256], Hb[:, pr, ri, :], Vf[:, pr, ri, :], ri == 0, ri == 1, ["Hb", "Vf"], ["ps_y2"],
                               sig=(i2 == 1 and ri == 1))
                    act(lambda e: e.copy(out=y2s[0:NCH, :], in_=ps_y2[0:NCH, 0:512]), ["ps_y2"], ["y2s"])
                    dve(lambda e: e.tensor_tensor(out=ysum[0:NCH, :], in0=ps_y[0:NCH, 0:512], in1=y2s[0:NCH, :], op=ALU.add), ["ps_y", "y2s"], ["ysum"])
                    for i2 in range(2):
                        prl = 2 * h2 + i2
                        act(lambda e: e.activation(out=y2k[0:NCH, :, 32 * prl:32 * prl + 32].rearrange("k t (a c) -> k t a c", a=2),
                                                   in_=ysum[0:NCH, i2 * 256:(i2 + 1) * 256].rearrange("k (a t c) -> k t a c", a=2, t=8),
                                                   func=AF.Gelu_apprx_tanh), ["ysum", "y2k"], ["y2k"])
                for t in range(8):
                    tr(ps_t[:, t, 0:NCH], y2k[0:NCH, t, :], ident_b[0:NCH, 0:NCH], ["y2k", "ident_b"], ["ps_t"], sig=(t == 7))
                dve(lambda e: e.tensor_copy(out=uT[:, q, :].rearrange("c (k t) -> c t k", t=8), in_=ps_t[:, :, 0:NCH]), ["ps_t", "uT"], ["uT"])
            for oc in range(4):
                for kc in range(4):
                    mm(ps_y[:, 0:SBT], Wglu[:, kc, oc * 128:(oc + 1) * 128], uT[:, kc, :], kc == 0, kc == 3, ["Wglu", "uT"], ["ps_y"])
                act(lambda e: e.activation(out=sig_t, in_=ps_y[:, 0:SBT], func=AF.Sigmoid, bias=bglu_col[:, oc:oc + 1]), ["ps_y", "bglu_col"], ["sig_t"])
                pool(lambda e: e.tensor_tensor(out=sig_t, in0=sig_t, in1=gT[:, oc, :], op=ALU.mult), ["sig_t", "gT"], ["sig_t"])
                dve(lambda e: e.tensor_tensor(out=mixT[:, 4 + oc, :], in0=sig_t, in1=uT[:, oc, :], op=ALU.mult), ["sig_t", "uT"], ["mixT"])

        def block_out(l, b, src):
            slot = b % 2
            bl = b % BPS
            cols = slice(bl * 128, (bl + 1) * 128)
            xk = "xr0"
            slot = 0
            T.dma(xr[slot], src[b * 128:(b + 1) * 128, :], r=[("res", b)], w=[xk], q="sp")
            for half in range(2):
                for kc in range(12):
                    mm(ps_f[:], mixT[:, kc, cols], Wout[:, kc, half * 512:(half + 1) * 512], kc == 0, kc == 11, ["mixT", "Wout"], ["ps_f"])
                dve(lambda e: e.tensor_tensor(out=xr[slot][:, half * 512:(half + 1) * 512], in0=ps_f[:], in1=xr[slot][:, half * 512:(half + 1) * 512],
                                              op=ALU.add), ["ps_f", xk], [xk])
            T.dma(out_d[b * 128:(b + 1) * 128, :], xr[slot], r=[xk], w=[("res", b)], q="sp")

        try:
            mark('consts')
            for l in range(n_layers):
                src = x_d if l == 0 else out_d
                layer_setup(l)
                mark('setup')
                for sbi in range(n_sb):
                    block_front(l, sbi * BPS, src)
                    for bl in range(BPS):
                        if bl + 1 < BPS:
                            block_front(l, sbi * BPS + bl + 1, src)
                        block_rest(l, sbi * BPS + bl)
                        mark('main%d' % bl)
                    ssm_superblock(l, sbi)
                    mark('ssm')
                    for bl in range(BPS):
                        block_out(l, sbi * BPS + bl, src)
        except StopBuild:
            print("build stopped at", stop)
        T.drain("sp")
        print("ops", T.nops, "waits", T.nwaits)
    return nc


Q_PERM = [0, 4, 1, 5, 2, 6, 3, 7]


def prep_inputs(inputs, n_sb=S // SBT, layers=None, x_override=None):
    SEQ = n_sb * SBT
    lsl = slice(None) if layers is None else slice(layers[0], layers[1])
    w_in = np.asarray(inputs["w_in"], dtype=np.float32)[lsl]
    qcols = np.concatenate([np.arange(h * 64, (h + 1) * 64) for h in Q_PERM])
    perm = np.concatenate([qcols, np.arange(512, INW)])
    w_in_p = np.ascontiguousarray(w_in[:, :, perm])
    shared = {k: np.ascontiguousarray(np.asarray(v)[lsl]) for k, v in inputs.items() if k not in ("x", "mem", "positions", "w_in")}
    shared["w_in"] = w_in_p
    xs = np.asarray(inputs["x"]) if x_override is None else x_override
    maps = []
    for c in range(8):
        m = dict(shared)
        m["x"] = np.ascontiguousarray(xs[c, :SEQ])
        m["mem"] = np.ascontiguousarray(np.asarray(inputs["mem"])[c])
        m["positions"] = np.ascontiguousarray(np.asarray(inputs["positions"])[c, :SEQ]).astype(np.int32)
        maps.append(m)
    return maps


LAYERS_PER_LAUNCH = 4


def kernel(**inputs):
    nc = build_nc(n_layers=LAYERS_PER_LAUNCH, wd=LAYERS_PER_LAUNCH)
    x = np.asarray(inputs["x"], dtype=np.float32)
    for l0 in range(0, DEPTH, LAYERS_PER_LAUNCH):
        maps = prep_inputs(inputs, layers=(l0, l0 + LAYERS_PER_LAUNCH), x_override=x)
        res = run_bass_kernel_spmd(nc, maps, core_ids=list(range(8)))
        x = np.stack([np.asarray(r["out"]) for r in res.results], axis=0).astype(np.float32)
    return x
```

```python
import math
import numpy as np
from contextlib import ExitStack
import concourse.bass as bass
import concourse.mybir as mybir
from concourse.bass_utils import run_bass_kernel_spmd

F32 = mybir.dt.float32
BF16 = mybir.dt.bfloat16
I32 = mybir.dt.int32
ALU = mybir.AluOpType
AF = mybir.ActivationFunctionType
AX = mybir.AxisListType

D = 1024
S = 4096
NMEM = 256
INW = 3328
DEPTH = 4
C_AQ, C_AK, C_AV, C_AG, C_SU, C_SG, C_XQ, C_XG = 0, 512, 640, 768, 1280, 1792, 2304, 2816
EPS = 1e-6
SBT = 512
BPS = SBT // 128
NCH = SBT // 8
N_DMA_SEMS = 40
TWO_PI = 2.0 * math.pi
CW1 = 6.28125
CW2 = TWO_PI - 6.28125


class Trk:
    def __init__(self, nc, stack):
        self.nc = nc
        self.eng = {"pe": nc.tensor, "act": nc.scalar, "dve": nc.vector, "pool": nc.gpsimd, "sp": nc.sync}
        self.sem = {k: stack.enter_context(nc.semaphore("s_" + k)) for k in ("pe", "act", "dve", "pool")}
        self.cnt = {k: 0 for k in self.sem}
        self.dsem = [stack.enter_context(nc.semaphore("d%d" % i)) for i in range(N_DMA_SEMS)]
        self.dcnt = [0] * N_DMA_SEMS
        self.dnext = 0
        self.known = {k: {} for k in self.eng}
        self.lastw = {}
        self.reads = {}
        self.nwaits = 0
        self.nops = 0

    def _wait(self, e, tok):
        kind, key, val = tok
        if kind == "E" and key == e and val > self.cnt[e]:
            return
        kn = self.known[e]
        if kn.get((kind, key), 0) >= val:
            return
        kn[(kind, key)] = val
        sem = self.sem[key] if kind == "E" else self.dsem[key]
        self.eng[e].wait_ge(sem, val)
        self.nwaits += 1

    ALIAS = {"junk": "hb", "sq": "rt1", "xq_f": "rt2", "y2s": "rt1", "ysum": "rt2", "y2k": "qk", "sig_t": "qr",
             "zt": "xt0", "Gs": "xt0", "sct": "xt1", "Hf": "hb",
             "pX0": "pTm0", "pX1": "pTm1", "mix_c": "mix_a"}

    def _deps(self, e, r, w):
        r = [self.ALIAS.get(k, k) for k in r]
        w = [self.ALIAS.get(k, k) for k in w]
        for k in r:
            t = self.lastw.get(k)
            if t is not None:
                self._wait(e, t)
        for k in w:
            t = self.lastw.get(k)
            if t is not None:
                self._wait(e, t)
            for t in self.reads.get(k, ()):
                self._wait(e, t)

    def _commit(self, tok, r, w):
        r = [self.ALIAS.get(k, k) for k in r]
        w = [self.ALIAS.get(k, k) for k in w]
        for k in r:
            self.reads.setdefault(k, []).append(tok)
        for k in w:
            self.lastw[k] = tok
            self.reads[k] = []

    def op(self, e, fn, r=(), w=(), sig=True):
        self._deps(e, r, w)
        inst = fn(self.eng[e])
        self.nops += 1
        if sig:
            self.cnt[e] += 1
            inst.then_inc(self.sem[e], 1)
            tok = ("E", e, self.cnt[e])
        else:
            tok = ("E", e, self.cnt[e] + 1)
        self._commit(tok, r, w)

    def dma(self, out, in_, r=(), w=(), q="sp"):
        self._deps(q, r, w)
        i = self.dnext
        self.dnext = (self.dnext + 1) % N_DMA_SEMS
        if self.dcnt[i] > 0:
            self._wait(q, ("D", i, self.dcnt[i]))
        self.dcnt[i] += 16
        self.eng[q].dma_start(out=out, in_=in_).then_inc(self.dsem[i], 16)
        tok = ("D", i, self.dcnt[i])
        self.nops += 1
        self._commit(tok, r, w)

    def drain(self, e="sp"):
        for k in self.sem:
            if self.cnt[k] > 0:
                self._wait(e, ("E", k, self.cnt[k]))
        for i in range(N_DMA_SEMS):
            if self.dcnt[i] > 0:
                self._wait(e, ("D", i, self.dcnt[i]))

    def finish(self, keys, e="sp"):
        for k in keys:
            t = self.lastw.get(k)
            if t is not None:
                self._wait(e, t)


class StopBuild(Exception):
    pass


def build_nc(n_layers=DEPTH, n_sb=S // SBT, dbg=False, stop=None, wd=DEPTH):
    nc = bass.Bass("TRN2", target_bir_lowering=False)
    SEQ = n_sb * SBT
    di = lambda n, s, dt=F32: nc.dram_tensor(n, s, dt, kind="ExternalInput").ap()
    x_d = di("x", [SEQ, D]); mem_d = di("mem", [NMEM, D]); pos_d = di("positions", [SEQ], I32)
    norm_g_d = di("norm_g", [wd, D]); w_in_d = di("w_in", [wd, D, INW])
    qg_d = di("q_norm_g", [wd, 64]); kg_d = di("k_norm_g", [wd, 64]); sinks_d = di("sinks", [wd, 8])
    lre_d = di("lam_re", [wd, 32, 64]); lim_d = di("lam_im", [wd, 32, 64]); ldt_d = di("log_dt", [wd, 32])
    bre_d = di("b_re", [wd, 32, 64, 16]); bim_d = di("b_im", [wd, 32, 64, 16])
    cre_d = di("c_re", [wd, 32, 16, 64]); cim_d = di("c_im", [wd, 32, 16, 64])
    dskip_d = di("d_skip", [wd, 512]); wglu_d = di("w_glu", [wd, 512, 512]); bglu_d = di("b_glu", [wd, 512])
    mng_d = di("mem_norm_g", [wd, D]); wkv_d = di("w_mem_kv", [wd, D, D])
    xqg_d = di("xq_norm_g", [wd, 128]); xkg_d = di("xk_norm_g", [wd, 128]); wout_d = di("w_out", [wd, 1536, D])
    out_d = nc.dram_tensor("out", [SEQ, D], F32, kind="ExternalOutput").ap()
    NB = SEQ // 128

    with ExitStack() as st:
        T = Trk(nc, st)
        sbt = lambda name, shape, dt: st.enter_context(nc.sbuf_tensor(name, shape, dt))
        pst = lambda name, shape, dt: st.enter_context(nc.psum_tensor(name, shape, dt))
        dve = lambda fn, r, w: T.op("dve", fn, r=r, w=w)
        act = lambda fn, r, w: T.op("act", fn, r=r, w=w)
        pool = lambda fn, r, w: T.op("pool", fn, r=r, w=w)

        def mm(out, lhsT, rhs, start, stop, r, w, sig=None, **kw):
            T.op("pe", lambda e: e.matmul(out, lhsT=lhsT, rhs=rhs, start=start, stop=stop, **kw), r=r, w=w,
                 sig=(stop if sig is None else sig))

        def tr(out, in_, ident, r, w, sig=True):
            T.op("pe", lambda e: e.transpose(out=out, in_=in_, identity=ident), r=r, w=w, sig=sig)

        ps_t = pst("ps_t", [128, 8, 128], BF16)
        ps_z = pst("ps_z", [128, 512], F32)
        ps_s = [pst("ps_s0", [128, 512], F32), pst("ps_s1", [128, 512], F32)]
        ps_f = pst("ps_f", [128, 512], F32)
        ps_y = pst("ps_y", [128, 512], F32)
        ps_y2 = pst("ps_y2", [128, 512], F32)
        ps_zs = pst("ps_zs", [128, 512], F32)

        Wsb = sbt("Wsb", [128, 8, INW], BF16)
        Wout = sbt("Wout", [128, 12, D], BF16)
        Wglu = sbt("Wglu", [128, 4, 512], BF16)
        Kstrip = sbt("Kstrip", [128, 4, 2, 15, 16], BF16)
        Wssm = sbt("Wssm", [128, 4, 8, 2, 128], BF16)
        Vf = sbt("Vf", [128, 16, 2, 256], BF16)
        mcT = sbt("mcT", [128, 16, NCH], F32)
        msT = sbt("msT", [128, 16, NCH], F32)
        Hb = sbt("Hb", [128, 16, 2, NCH], BF16)
        carry = sbt("carry", [128, 16, 2], F32)
        magA = sbt("magA", [128, 16], F32)
        mcA = sbt("mcA", [128, 16], F32)
        msA = sbt("msA", [128, 16], F32)
        nmsA = sbt("nmsA", [128, 16], F32)
        magTab = sbt("magTab", [128, 16, NCH], F32)
        mkT = sbt("mkT", [128, 4, NMEM], BF16)
        mv_aug = sbt("mv_aug", [128, 2, 4, 129], BF16)
        ident_b = sbt("ident_b", [128, 128], BF16)
        ident_f = sbt("ident_f", [128, 128], F32)
        mask_cur = sbt("mask_cur", [128, 4, 128], BF16)
        mask_prev = sbt("mask_prev", [128, 4, 128], BF16)
        dmask = sbt("dmask", [128, 32], F32)
        cosT = sbt("cosT", [128, NB, 32], F32)
        sinT = sbt("sinT", [128, NB, 32], F32)
        g10 = sbt("g10", [128, 10, 64], F32)
        gq_bc = sbt("gq_bc", [128, 64], F32)
        gk_bc = sbt("gk_bc", [128, 64], F32)
        esink = sbt("esink", [128, 8], F32)
        gxk_bc = sbt("gxk_bc", [128, 128], F32)
        gxq_col = sbt("gxq_col", [128, 1], F32)
        bglu_col = sbt("bglu_col", [128, 4], F32)
        d_col = sbt("d_col", [128, 4], F32)
        g_col = sbt("g_col", [128, 8], F32)
        gm_col = sbt("gm_col", [128, 8], F32)
        ldT = sbt("ldT", [32, 128], F32)
        nvec = sbt("nvec", [128, 9], F32)
        kvec = sbt("kvec", [128, NCH], F32)
        dummy = sbt("dummy_t", [128, 4], F32)

        ARW = 14330
        arena = sbt("arena", [128, ARW], F32)
        aoff = {"main": 0, "setup": 0, "setupB": 0}
        akeys = {"main": [], "setup": [], "setupB": []}

        def carve(phase, name, shape, dt):
            n = int(np.prod(shape))
            words = n if dt in (F32, I32) else (n + 1) // 2
            o = aoff[phase]
            aoff[phase] = o + words
            assert aoff[phase] <= ARW, (phase, name, aoff[phase])
            v = arena[:, o:o + words]
            if dt != F32:
                v = v.bitcast(dt)
                if dt == BF16 and n % 2:
                    v = v[:, 0:n]
            if len(shape) > 1:
                names = " ".join("d%d" % i for i in range(len(shape)))
                v = v.rearrange("p (%s) -> p %s" % (names, names), **{"d%d" % i: shape[i] for i in range(1, len(shape))})
            if name not in akeys[phase]:
                akeys[phase].append(name)
            return v

        M = lambda name, shape, dt: carve("main", name, shape, dt)
        U = lambda name, shape, dt: carve("setup", name, shape, dt)
        UB = lambda name, shape, dt: carve("setupB", name, shape, dt)

        uT = M("uT", [4, SBT], BF16)
        gT = M("gT", [4, SBT], BF16)
        mixT = M("mixT", [12, SBT], BF16)
        xt = [M("xt0", [D], F32), M("xt1", [D], F32)]
        xr = [M("xr0", [D], F32)]
        hb = M("hb", [D], BF16)
        junk = hb
        hTs = [M("hT0", [8, 128], BF16), M("hT1", [8, 128], BF16)]
        st1 = M("st1", [16], F32)
        qk = M("qk", [10, 64], F32)
        rt1 = M("rt1", [10, 64], F32)
        rt2 = M("rt2", [10, 64], F32)
        sq = rt1
        qr = M("qr", [10, 64], BF16)
        qT = M("qT", [4, 128], BF16)
        kT = [M("kT0", [128], BF16), M("kT1", [128], BF16)]
        vaug = [M("vaug0", [2, 65], BF16), M("vaug1", [2, 65], BF16)]
        pTm = [M("pTm0", [512], BF16), M("pTm1", [512], BF16)]
        pX = pTm
        gate_a = M("gate_a", [512], BF16)
        gate_x = M("gate_x", [512], BF16)
        mix_a = M("mix_a", [512], BF16)
        mix_c = mix_a
        xq_f = rt2.rearrange("p h d -> p (h d)")[:, 0:512].rearrange("p (h d) -> p h d", h=4)
        xq_b = M("xq_b", [4, 128], BF16)
        xqT = M("xqT", [4, 128], BF16)
        den = M("den", [8], F32)
        zt = xt[0][:, 0:512].rearrange("p (r a k) -> p r a k", r=2, a=4)
        Gs = xt[0][:, 512:1024].rearrange("p (r a k) -> p r a k", r=2, a=4)
        sct = xt[1].rearrange("p (j a k) -> p j a k", j=4, a=4)
        Hf = hb.bitcast(F32).rearrange("p (r a k) -> p r a k", r=2, a=4)
        init4 = M("init4", [4, 4], F32)
        y2s = rt1.rearrange("p h d -> p (h d)")[:, 0:512]
        ysum = rt2.rearrange("p h d -> p (h d)")[:, 0:512]
        y2k = qk.rearrange("p h d -> p (h d)")[:, 0:512].bitcast(BF16).rearrange("p (t c) -> p t c", t=8)
        sig_t = qr.rearrange("p h d -> p (h d)")[:, 0:512]
        print("arena main words", aoff["main"])

        def load_T(dst, src, n, wkey):
            T.dma(ldT[0:n, :], src, w=["ldT"])
            tr(ps_z[:, 0:n], ldT[0:n, :], ident_f[0:n, 0:n], ["ldT", "ident_f"], ["ps_z"])
            dve(lambda e: e.tensor_copy(out=dst, in_=ps_z[:, 0:n]), ["ps_z"], [wkey])

        def sin_of(dst, src, n, shift, rk, wk, tmp=None):
            for c0 in range(0, n, 256):
                c1 = min(n, c0 + 256)
                m = c1 - c0
                ki, kf, yy = s_ki[:, 0:m], s_kf[:, 0:m], s_y[:, 0:m]
                sr = src[:, c0:c1]
                dve(lambda e: e.tensor_scalar(out=ki, in0=sr, scalar1=float(shift), scalar2=float(1.0 / TWO_PI), op0=ALU.add, op1=ALU.mult),
                    rk, ["sr_ki"])
                dve(lambda e: e.tensor_copy(out=kf, in_=ki), ["sr_ki"], ["sr_kf"])
                dve(lambda e: e.tensor_scalar(out=yy, in0=sr, scalar1=float(shift), scalar2=None, op0=ALU.add), rk, ["sr_y"])
                dve(lambda e: e.scalar_tensor_tensor(out=yy, in0=kf, scalar=float(-CW1), in1=yy, op0=ALU.mult, op1=ALU.add),
                    ["sr_kf", "sr_y"], ["sr_y"])
                dve(lambda e: e.scalar_tensor_tensor(out=yy, in0=kf, scalar=float(-CW2), in1=yy, op0=ALU.mult, op1=ALU.add),
                    ["sr_kf", "sr_y"], ["sr_y"])
                dve(lambda e: e.tensor_scalar(out=yy, in0=yy, scalar1=float(math.pi), scalar2=float(-math.pi), op0=ALU.min, op1=ALU.max),
                    ["sr_y"], ["sr_y"])
                act(lambda e: e.activation(out=dst[:, c0:c1], in_=yy, func=AF.Sin), ["sr_y"], wk)

        def rsqrt_of(dst, src, scale, rk, wk):
            dve(lambda e: e.tensor_scalar(out=dst, in0=src, scalar1=float(scale), scalar2=float(EPS), op0=ALU.mult, op1=ALU.add), rk, wk)
            act(lambda e: e.activation(out=dst, in_=dst, func=AF.Sqrt), wk, wk)
            dve(lambda e: e.reciprocal(out=dst, in_=dst), wk, wk)

        def join(keys, e="dve"):
            T.op(e, lambda en: en.memset(dummy[:, 0:1], 0.0), r=[], w=list(keys) + ["dummy"])

        s_ki = U("sr_ki", [256], I32); s_kf = U("sr_kf", [256], F32); s_y = U("sr_y", [256], F32)
        s_ang = U("s_ang", [256], F32)
        UB("sr_ki", [256], I32); UB("sr_kf", [256], F32); UB("sr_y", [256], F32); UB("s_ang", [256], F32)
        ones_f = U("ones_f", [128], F32)
        stage = [U("stage%d" % i, [1024], F32) for i in range(4)]
        s_posi = U("s_posi", [128], I32); s_posf = U("s_posf", [128], F32)
        s_posT = U("s_posT", [32], F32); s_inv = U("s_inv", [32], F32)
        s_memx = U("s_memx", [D], F32); s_memn = U("s_memn", [D], BF16); s_memT = U("s_memT", [8, NMEM], BF16)
        s_mkf = U("s_mkf", [4, 128], F32); s_mkb = U("s_mkb", [4, 128], BF16)
        s_smA = U("s_smA", [16], F32)
        s_sm = UB("s_sm", [16, 12], F32)
        s_e9 = UB("s_e9", [16, 9], F32); s_a9 = UB("s_a9", [16, 9], F32); s_mag9 = UB("s_mag9", [16, 9], F32)
        s_cos9 = UB("s_cos9", [16, 9], F32); s_sin9 = UB("s_sin9", [16, 9], F32)
        s_ar9 = UB("s_ar9", [16, 9], F32); s_ai9 = UB("s_ai9", [16, 9], F32)
        s_Br = UB("s_Br", [16, 16], F32); s_Bi = UB("s_Bi", [16, 16], F32)
        s_bbr = UB("s_bbr", [16, 16], F32); s_bbi = UB("s_bbi", [16, 16], F32); s_bt = UB("s_bt", [16, 16], F32)
        s_Cst = UB("s_Cst", [2, 128], F32)
        s_Cr = UB("s_Cr", [16, 16], F32); s_Ci = UB("s_Ci", [16, 16], F32)
        s_CAr = UB("s_CAr", [4, 9, 16], F32); s_CAi = UB("s_CAi", [4, 9, 16], F32); s_CAt = UB("s_CAt", [4, 9, 16], F32)
        s_CAzr = UB("s_CAzr", [4, 2, 8, 16], F32); s_CAzi = UB("s_CAzi", [4, 2, 8, 16], F32)
        s_Bzr = UB("s_Bzr", [4, 128], F32); s_Bzi = UB("s_Bzi", [4, 128], F32)
        s_Kt = UB("s_Kt", [2, 8, 16], F32); s_Kd = UB("s_Kd", [32], F32)
        s_Xr = UB("s_Xr", [4, 8, 16], F32); s_Xi = UB("s_Xi", [4, 8, 16], F32); s_Xt = UB("s_Xt", [4, 8, 16], F32)
        s_Xzr = UB("s_Xzr", [8, 4, 32], BF16); s_Xzi = UB("s_Xzi", [8, 4, 32], BF16)
        s_lg2 = UB("s_lg2", [2], F32)
        print("arena setup words", aoff["setup"], aoff["setupB"])
        ALLK = list(dict.fromkeys(akeys["main"] + akeys["setup"] + akeys["setupB"]))

        pool(lambda e: e.memset(ones_f, 1.0), [], ["ones_f"])
        pool(lambda e: e.memset(dummy[:], 0.0), [], ["dummy"])
        pool(lambda e: e.affine_select(out=ident_f[:], in_=ones_f, pattern=[[-1, 128]], compare_op=ALU.is_equal, fill=0.0,
                                       base=0, channel_multiplier=1), ["ones_f"], ["ident_f"])
        pool(lambda e: e.affine_select(out=ident_b[:], in_=ones_f, pattern=[[-1, 128]], compare_op=ALU.is_equal, fill=0.0,
                                       base=0, channel_multiplier=1), ["ones_f"], ["ident_b"])
        for j in range(4):
            pool(lambda e: e.affine_select(out=mask_cur[:, j, :], in_=ones_f, pattern=[[1, 128]], compare_op=ALU.is_ge, fill=0.0,
                                           base=0, channel_multiplier=-1), ["ones_f"], ["mask_cur"])
            pool(lambda e: e.affine_select(out=mask_prev[:, j, :], in_=ones_f, pattern=[[-1, 128]], compare_op=ALU.is_ge, fill=0.0,
                                           base=-1, channel_multiplier=1), ["ones_f"], ["mask_prev"])
        pool(lambda e: e.tensor_tensor(out=dmask[:], in0=ident_f[:, 0:32], in1=ident_f[:, 32:64], op=ALU.add), ["ident_f"], ["dmask"])
        pool(lambda e: e.tensor_tensor(out=dmask[:], in0=dmask[:], in1=ident_f[:, 64:96], op=ALU.add), ["ident_f", "dmask"], ["dmask"])
        pool(lambda e: e.tensor_tensor(out=dmask[:], in0=dmask[:], in1=ident_f[:, 96:128], op=ALU.add), ["ident_f", "dmask"], ["dmask"])
        pool(lambda e: e.iota(nvec[:], pattern=[[1, 9]], base=0, channel_multiplier=0, allow_small_or_imprecise_dtypes=True), [], ["nvec"])
        pool(lambda e: e.iota(kvec[:], pattern=[[1, NCH]], base=0, channel_multiplier=0, allow_small_or_imprecise_dtypes=True), [], ["kvec"])
        pool(lambda e: e.memset(mv_aug[:], 1.0), [], ["mv_aug"])
        pool(lambda e: e.memset(Kstrip[:], 0.0), [], ["Kstrip"])

        for bb in range(0, NB, 32):
            nbk = min(32, NB - bb)
            T.dma(s_posi[0:nbk, :], pos_d[bb * 128:(bb + nbk) * 128].rearrange("(b p) -> b p", p=128), w=["s_posi"])
            dve(lambda e: e.tensor_copy(out=s_posf[0:nbk, :], in_=s_posi[0:nbk, :]), ["s_posi"], ["s_posf"])
            tr(ps_z[:, 0:nbk], s_posf[0:nbk, :], ident_f[0:nbk, 0:nbk], ["s_posf", "ident_f"], ["ps_z"])
            dve(lambda e: e.tensor_copy(out=s_posT[:, 0:nbk], in_=ps_z[:, 0:nbk]), ["ps_z"], ["s_posT"])
        pool(lambda e: e.iota(s_inv, pattern=[[1, 32]], base=0, channel_multiplier=0, allow_small_or_imprecise_dtypes=True), [], ["s_inv"])
        act(lambda e: e.activation(out=s_inv, in_=s_inv, func=AF.Exp, scale=float(-math.log(10000.0) / 32.0)), ["s_inv"], ["s_inv"])
        for b0 in range(0, NB, 8):
            nb8 = min(8, NB - b0)
            ang3 = s_ang[:, 0:nb8 * 32].rearrange("p (b j) -> p b j", j=32)
            dve(lambda e: e.tensor_tensor(out=ang3, in0=s_posT[:, b0:b0 + nb8, None].to_broadcast([128, nb8, 32]),
                                          in1=s_inv[:, None, :].to_broadcast([128, nb8, 32]), op=ALU.mult), ["s_posT", "s_inv"], ["s_ang"])
            sin_of(sinT[:, b0:b0 + nb8, :].rearrange("p b j -> p (b j)"), s_ang[:, 0:nb8 * 32], nb8 * 32, 0.0, ["s_ang"], ["sinT"])
            sin_of(cosT[:, b0:b0 + nb8, :].rearrange("p b j -> p (b j)"), s_ang[:, 0:nb8 * 32], nb8 * 32, math.pi / 2, ["s_ang"], ["cosT"])

        def mark(name):
            if stop == name:
                raise StopBuild()

        def layer_setup(l):
            join(ALLK)
            load_T(g_col[:], norm_g_d[l].rearrange("(k p) -> k p", p=128), 8, "g_col")
            load_T(gm_col[:], mng_d[l].rearrange("(k p) -> k p", p=128), 8, "gm_col")
            load_T(bglu_col[:], bglu_d[l].rearrange("(k p) -> k p", p=128), 4, "bglu_col")
            load_T(d_col[:], dskip_d[l].rearrange("(k p) -> k p", p=128), 4, "d_col")
            load_T(gxq_col[:], xqg_d[l].rearrange("(k p) -> k p", p=128), 1, "gxq_col")
            dve(lambda e: e.tensor_scalar(out=gxq_col[:], in0=gxq_col[:], scalar1=float(1.0 / math.sqrt(128.0)), scalar2=None, op0=ALU.mult),
                ["gxq_col"], ["gxq_col"])
            T.dma(gq_bc[:], qg_d[l].partition_broadcast(128), w=["gq_bc"])
            T.dma(gk_bc[:], kg_d[l].partition_broadcast(128), w=["gk_bc"])
            T.dma(gxk_bc[:], xkg_d[l].partition_broadcast(128), w=["gxk_bc"])
            T.dma(esink[:], sinks_d[l].partition_broadcast(128), w=["esink"])
            act(lambda e: e.activation(out=esink[:], in_=esink[:], func=AF.Exp), ["esink"], ["esink"])
            dve(lambda e: e.tensor_copy(out=g10[:, 0:8, :], in_=gq_bc[:, None, :].to_broadcast([128, 8, 64])), ["gq_bc"], ["g10"])
            dve(lambda e: e.tensor_copy(out=g10[:, 8:10, :], in_=gk_bc[:, None, :].to_broadcast([128, 2, 64])), ["gk_bc", "g10"], ["g10"])

            mark('vecs')
            for mt in range(2):
                T.dma(s_memx, mem_d[mt * 128:(mt + 1) * 128, :], w=["s_memx"])
                act(lambda e: e.activation(out=s_memn, in_=s_memx, func=AF.Square, accum_out=s_smA[:, 0:1]), ["s_memx"], ["s_memn", "s_smA"])
                rsqrt_of(s_smA[:, 1:2], s_smA[:, 0:1], 1.0 / D, ["s_smA"], ["s_smA"])
                act(lambda e: e.activation(out=s_memn, in_=s_memx, func=AF.Copy, scale=s_smA[:, 1:2]), ["s_memx", "s_smA"], ["s_memn"])
                for kc in range(8):
                    tr(ps_t[:, kc, :], s_memn[:, kc * 128:(kc + 1) * 128], ident_b[:], ["s_memn", "ident_b"], ["ps_t"], sig=(kc == 7))
                dve(lambda e: e.tensor_copy(out=s_memT[:, :, mt * 128:(mt + 1) * 128], in_=ps_t[:]), ["ps_t"], ["s_memT"])
            Wkv = Wsb[:, :, 0:D]
            for kc in range(8):
                sg_ = stage[kc % 4]
                T.dma(sg_, wkv_d[l, kc * 128:(kc + 1) * 128, :], w=["stage%d" % (kc % 4)])
                eng = "act" if kc % 2 == 0 else "pool"
                if eng == "act":
                    act(lambda e: e.activation(out=Wkv[:, kc, :], in_=sg_, func=AF.Copy, scale=gm_col[:, kc:kc + 1]),
                        ["stage%d" % (kc % 4), "gm_col"], ["Wsb"])
                else:
                    dve(lambda e: e.tensor_scalar(out=Wkv[:, kc, :], in0=sg_, scalar1=gm_col[:, kc:kc + 1], scalar2=None, op0=ALU.mult),
                         ["stage%d" % (kc % 4), "gm_col"], ["Wsb"])
            for mt in range(2):
                for n in range(2):
                    for kc in range(8):
                        mm(ps_z[:], s_memT[:, kc, mt * 128:(mt + 1) * 128], Wkv[:, kc, n * 512:(n + 1) * 512], kc == 0, kc == 7,
                           ["s_memT", "Wsb"], ["ps_z"])
                    if n == 0:
                        act(lambda e: e.copy(out=s_mkf.rearrange("p h d -> p (h d)"), in_=ps_z[:]), ["ps_z"], ["s_mkf"])
                        dve(lambda e: e.tensor_tensor(out=s_memx[:, 0:512], in0=s_mkf.rearrange("p h d -> p (h d)"),
                                                      in1=s_mkf.rearrange("p h d -> p (h d)"), op=ALU.mult), ["s_mkf"], ["s_memx"])
                        dve(lambda e: e.tensor_reduce(out=s_smA[:, 4:8], in_=s_memx[:, 0:512].rearrange("p (h d) -> p h d", h=4),
                                                      axis=AX.X, op=ALU.add), ["s_memx"], ["s_smA"])
                        rsqrt_of(s_smA[:, 8:12], s_smA[:, 4:8], 1.0 / 128, ["s_smA"], ["s_smA"])
                        dve(lambda e: e.tensor_tensor(out=s_mkf, in0=s_mkf, in1=s_smA[:, 8:12, None].to_broadcast([128, 4, 128]), op=ALU.mult),
                            ["s_mkf", "s_smA"], ["s_mkf"])
                        dve(lambda e: e.tensor_tensor(out=s_mkb, in0=s_mkf, in1=gxk_bc[:, None, :].to_broadcast([128, 4, 128]), op=ALU.mult),
                            ["s_mkf", "gxk_bc"], ["s_mkb"])
                        for h in range(4):
                            tr(ps_t[:, h, :], s_mkb[:, h, :], ident_b[:], ["s_mkb", "ident_b"], ["ps_t"], sig=(h == 3))
                        dve(lambda e: e.tensor_scalar(out=mkT[:, :, mt * 128:(mt + 1) * 128], in0=ps_t[:, 0:4, :], scalar1=gxq_col[:, 0:1],
                                                      scalar2=None, op0=ALU.mult), ["ps_t", "gxq_col"], ["mkT"])
                    else:
                        act(lambda e: e.copy(out=mv_aug[:, mt, :, 0:128], in_=ps_z[:].rearrange("p (h d) -> p h d", h=4)), ["ps_z"], ["mv_aug"])

            mark('memkv')
            cnt = 0
            for kc in range(8):
                for (c0, c1) in ((0, 1024), (1024, 2048), (2048, 3072), (3072, INW)):
                    sg_ = stage[cnt % 4]; sk = "stage%d" % (cnt % 4)
                    T.dma(sg_[:, 0:c1 - c0], w_in_d[l, kc * 128:(kc + 1) * 128, c0:c1], w=[sk])
                    if cnt % 2 == 0:
                        act(lambda e: e.activation(out=Wsb[:, kc, c0:c1], in_=sg_[:, 0:c1 - c0], func=AF.Copy, scale=g_col[:, kc:kc + 1]),
                            [sk, "g_col"], ["Wsb"])
                    else:
                        dve(lambda e: e.tensor_scalar(out=Wsb[:, kc, c0:c1], in0=sg_[:, 0:c1 - c0], scalar1=g_col[:, kc:kc + 1], scalar2=None,
                                                       op0=ALU.mult), [sk, "g_col"], ["Wsb"])
                    cnt += 1
            for kc in range(12):
                sg_ = stage[cnt % 4]; sk = "stage%d" % (cnt % 4)
                T.dma(sg_, wout_d[l, kc * 128:(kc + 1) * 128, :], w=[sk])
                if cnt % 2 == 0:
                    act(lambda e: e.copy(out=Wout[:, kc, :], in_=sg_), [sk], ["Wout"])
                else:
                    dve(lambda e: e.tensor_copy(out=Wout[:, kc, :], in_=sg_), [sk], ["Wout"])
                cnt += 1
            for kc in range(4):
                sg_ = stage[cnt % 4]; sk = "stage%d" % (cnt % 4)
                T.dma(sg_[:, 0:512], wglu_d[l, kc * 128:(kc + 1) * 128, :], w=[sk])
                dve(lambda e: e.tensor_copy(out=Wglu[:, kc, :], in_=sg_[:, 0:512]), [sk], ["Wglu"])
                cnt += 1

            mark('weights')
            join(ALLK)
            lr = s_sm[:, :, 2]; li = s_sm[:, :, 3]; dtv = s_sm[:, :, 4]; lrdt = s_sm[:, :, 5]; lidt = s_sm[:, :, 6]
            t7 = s_sm[:, :, 7]; t8 = s_sm[:, :, 8]; fr = s_sm[:, :, 9]; fi = s_sm[:, :, 10]; t11 = s_sm[:, :, 11]
            load_T(lr, lre_d[l].rearrange("(pr two) p -> pr (two p)", two=2), 16, "s_sm")
            load_T(li, lim_d[l].rearrange("(pr two) p -> pr (two p)", two=2), 16, "s_sm")
            T.dma(s_lg2[0:16, :], ldt_d[l].rearrange("(pr two) -> pr two", two=2), w=["s_lg2"])
            dve(lambda e: e.tensor_copy(out=ldT[0:16, :].rearrange("q (two p) -> q two p", two=2),
                                        in_=s_lg2[0:16, :, None].to_broadcast([16, 2, 64])), ["s_lg2"], ["ldT"])
            tr(ps_z[:, 0:16], ldT[0:16, :], ident_f[0:16, 0:16], ["ldT", "ident_f"], ["ps_z"])
            act(lambda e: e.activation(out=dtv, in_=ps_z[:, 0:16], func=AF.Exp), ["ps_z"], ["s_sm"])
            dve(lambda e: e.tensor_tensor(out=lrdt, in0=lr, in1=dtv, op=ALU.mult), ["s_sm"], ["s_sm"])
            dve(lambda e: e.tensor_tensor(out=lidt, in0=li, in1=dtv, op=ALU.mult), ["s_sm"], ["s_sm"])
            dve(lambda e: e.tensor_tensor(out=s_e9, in0=lrdt[:, :, None].to_broadcast([128, 16, 9]), in1=nvec[:, None, :].to_broadcast([128, 16, 9]),
                                          op=ALU.mult), ["s_sm", "nvec"], ["s_e9"])
            dve(lambda e: e.tensor_tensor(out=s_a9, in0=lidt[:, :, None].to_broadcast([128, 16, 9]), in1=nvec[:, None, :].to_broadcast([128, 16, 9]),
                                          op=ALU.mult), ["s_sm", "nvec"], ["s_a9"])
            act(lambda e: e.activation(out=s_mag9, in_=s_e9, func=AF.Exp), ["s_e9"], ["s_mag9"])
            f2 = lambda v: v.rearrange("p a b -> p (a b)")
            sin_of(f2(s_sin9), f2(s_a9), 144, 0.0, ["s_a9"], ["s_sin9"])
            sin_of(f2(s_cos9), f2(s_a9), 144, math.pi / 2, ["s_a9"], ["s_cos9"])
            dve(lambda e: e.tensor_tensor(out=s_ar9, in0=s_mag9, in1=s_cos9, op=ALU.mult), ["s_mag9", "s_cos9"], ["s_ar9"])
            dve(lambda e: e.tensor_tensor(out=s_ai9, in0=s_mag9, in1=s_sin9, op=ALU.mult), ["s_mag9", "s_sin9"], ["s_ai9"])
            dve(lambda e: e.tensor_copy(out=magA[:], in_=s_mag9[:, :, 8]), ["s_mag9"], ["magA"])
            tmp16 = (s_ki[:, 0:16], s_kf[:, 0:16])
            dve(lambda e: e.tensor_scalar(out=tmp16[0], in0=lidt, scalar1=float(8.0 / TWO_PI), scalar2=None, op0=ALU.mult), ["s_sm"], ["sr_ki"])
            dve(lambda e: e.tensor_copy(out=tmp16[1], in_=tmp16[0]), ["sr_ki"], ["sr_kf"])
            dve(lambda e: e.tensor_scalar(out=t7, in0=lidt, scalar1=8.0, scalar2=None, op0=ALU.mult), ["s_sm"], ["s_sm"])
            dve(lambda e: e.scalar_tensor_tensor(out=t7, in0=tmp16[1], scalar=float(-CW1), in1=t7, op0=ALU.mult, op1=ALU.add), ["sr_kf", "s_sm"], ["s_sm"])
            dve(lambda e: e.scalar_tensor_tensor(out=t7, in0=tmp16[1], scalar=float(-CW2), in1=t7, op0=ALU.mult, op1=ALU.add), ["sr_kf", "s_sm"], ["s_sm"])
            for p0 in range(0, 16, 4):
                ta3 = s_ang[:, 0:4 * NCH].rearrange("p (a k) -> p a k", k=NCH)
                dve(lambda e: e.tensor_tensor(out=ta3, in0=t7[:, p0:p0 + 4, None].to_broadcast([128, 4, NCH]),
                                              in1=kvec[:, None, :].to_broadcast([128, 4, NCH]), op=ALU.mult), ["s_sm", "kvec"], ["s_ang"])
                sin_of(msT[:, p0:p0 + 4, :].rearrange("p a k -> p (a k)"), s_ang[:, 0:4 * NCH], 4 * NCH, 0.0, ["s_ang"], ["msT"])
                sin_of(mcT[:, p0:p0 + 4, :].rearrange("p a k -> p (a k)"), s_ang[:, 0:4 * NCH], 4 * NCH, math.pi / 2, ["s_ang"], ["mcT"])
            dve(lambda e: e.tensor_tensor(out=mcA[:], in0=mcT[:, :, 1], in1=magA[:], op=ALU.mult), ["mcT", "magA"], ["mcA"])
            dve(lambda e: e.tensor_tensor(out=msA[:], in0=msT[:, :, 1], in1=magA[:], op=ALU.mult), ["msT", "magA"], ["msA"])
            dve(lambda e: e.tensor_scalar(out=nmsA[:], in0=msA[:], scalar1=-1.0, scalar2=None, op0=ALU.mult), ["msA"], ["nmsA"])
            dve(lambda e: e.tensor_copy(out=magTab[:], in_=magA[:, :, None].to_broadcast([128, 16, NCH])), ["magA"], ["magTab"])
            dve(lambda e: e.memset(magTab[:, :, 0:1], 0.0), ["magTab"], ["magTab"])
            dve(lambda e: e.tensor_tensor(out=t8, in0=lr, in1=lr, op=ALU.mult), ["s_sm"], ["s_sm"])
            dve(lambda e: e.tensor_tensor(out=t11, in0=li, in1=li, op=ALU.mult), ["s_sm"], ["s_sm"])
            dve(lambda e: e.tensor_tensor(out=t8, in0=t8, in1=t11, op=ALU.add), ["s_sm"], ["s_sm"])
            dve(lambda e: e.reciprocal(out=t8, in_=t8), ["s_sm"], ["s_sm"])
            dve(lambda e: e.tensor_scalar(out=t11, in0=s_ar9[:, :, 1], scalar1=-1.0, scalar2=None, op0=ALU.add), ["s_ar9"], ["s_sm"])
            dve(lambda e: e.tensor_tensor(out=fr, in0=t11, in1=lr, op=ALU.mult), ["s_sm"], ["s_sm"])
            dve(lambda e: e.tensor_tensor(out=t7, in0=s_ai9[:, :, 1], in1=li, op=ALU.mult), ["s_sm", "s_ai9"], ["s_sm"])
            dve(lambda e: e.tensor_tensor(out=fr, in0=fr, in1=t7, op=ALU.add), ["s_sm"], ["s_sm"])
            dve(lambda e: e.tensor_tensor(out=fr, in0=fr, in1=t8, op=ALU.mult), ["s_sm"], ["s_sm"])
            dve(lambda e: e.tensor_tensor(out=fi, in0=s_ai9[:, :, 1], in1=lr, op=ALU.mult), ["s_sm", "s_ai9"], ["s_sm"])
            dve(lambda e: e.tensor_tensor(out=t7, in0=t11, in1=li, op=ALU.mult), ["s_sm"], ["s_sm"])
            dve(lambda e: e.tensor_tensor(out=fi, in0=fi, in1=t7, op=ALU.subtract), ["s_sm"], ["s_sm"])
            dve(lambda e: e.tensor_tensor(out=fi, in0=fi, in1=t8, op=ALU.mult), ["s_sm"], ["s_sm"])
            mark('ssm_tabs')
            T.dma(s_Br, bre_d[l].rearrange("(pr two) p c -> (two p) pr c", two=2), w=["s_Br"])
            T.dma(s_Bi, bim_d[l].rearrange("(pr two) p c -> (two p) pr c", two=2), w=["s_Bi"])
            frb = fr[:, :, None].to_broadcast([128, 16, 16]); fib = fi[:, :, None].to_broadcast([128, 16, 16])
            dve(lambda e: e.tensor_tensor(out=s_bbr, in0=s_Br, in1=frb, op=ALU.mult), ["s_Br", "s_sm"], ["s_bbr"])
            dve(lambda e: e.tensor_tensor(out=s_bt, in0=s_Bi, in1=fib, op=ALU.mult), ["s_Bi", "s_sm"], ["s_bt"])
            dve(lambda e: e.tensor_tensor(out=s_bbr, in0=s_bbr, in1=s_bt, op=ALU.subtract), ["s_bbr", "s_bt"], ["s_bbr"])
            dve(lambda e: e.tensor_tensor(out=s_bbi, in0=s_Bi, in1=frb, op=ALU.mult), ["s_Bi", "s_sm"], ["s_bbi"])
            dve(lambda e: e.tensor_tensor(out=s_bt, in0=s_Br, in1=fib, op=ALU.mult), ["s_Br", "s_sm", "s_bbr"], ["s_bt"])
            dve(lambda e: e.tensor_tensor(out=s_bbi, in0=s_bbi, in1=s_bt, op=ALU.add), ["s_bbi", "s_bt"], ["s_bbi"])
            for (cd, dst, dk) in ((cre_d, s_Cr, "s_Cr"), (cim_d, s_Ci, "s_Ci")):
                for pr in range(16):
                    T.dma(s_Cst[(pr % 8) * 16:(pr % 8) * 16 + 16, pr // 8, :].rearrange("c (two p) -> c two p", two=2),
                          cd[l, 2 * pr:2 * pr + 2].rearrange("two c p -> c two p"), w=["s_Cst"])
                for hh in range(2):
                    tr(ps_z[:, hh * 128:(hh + 1) * 128], s_Cst[:, hh, :], ident_f[:], ["s_Cst", "ident_f"], ["ps_z"], sig=(hh == 1))
                dve(lambda e: e.tensor_copy(out=dst.rearrange("p a c -> p (a c)"), in_=ps_z[:, 0:256]), ["ps_z"], [dk])
            mark('ssm_BC')
            for q in range(4):
                prs = slice(4 * q, 4 * q + 4)
                arb = s_ar9[:, prs, :, None].to_broadcast([128, 4, 9, 16]); aib = s_ai9[:, prs, :, None].to_broadcast([128, 4, 9, 16])
                crb = s_Cr[:, prs, None, :].to_broadcast([128, 4, 9, 16]); cib = s_Ci[:, prs, None, :].to_broadcast([128, 4, 9, 16])
                dve(lambda e: e.tensor_tensor(out=s_CAr, in0=crb, in1=arb, op=ALU.mult), ["s_Cr", "s_ar9"], ["s_CAr"])
                dve(lambda e: e.tensor_tensor(out=s_CAt, in0=cib, in1=aib, op=ALU.mult), ["s_Ci", "s_ai9"], ["s_CAt"])
                dve(lambda e: e.tensor_tensor(out=s_CAr, in0=s_CAr, in1=s_CAt, op=ALU.subtract), ["s_CAr", "s_CAt"], ["s_CAr"])
                dve(lambda e: e.tensor_tensor(out=s_CAi, in0=crb, in1=aib, op=ALU.mult), ["s_Cr", "s_ai9"], ["s_CAi"])
                dve(lambda e: e.tensor_tensor(out=s_CAt, in0=cib, in1=arb, op=ALU.mult), ["s_Ci", "s_ar9", "s_CAr"], ["s_CAt"])
                dve(lambda e: e.tensor_tensor(out=s_CAi, in0=s_CAi, in1=s_CAt, op=ALU.add), ["s_CAi", "s_CAt"], ["s_CAi"])
                pool(lambda e: e.memset(Vf[:, prs, :, :], 0.0), [], ["Vf"])
                for two in range(2):
                    rows = slice(64 * two, 64 * two + 64)
                    dve(lambda e: e.tensor_copy(out=Vf[rows, prs, 0, two * 128:(two + 1) * 128].rearrange("p a (t c) -> p a t c", c=16),
                                                in_=s_CAr[rows, :, 1:9, :]), ["s_CAr", "Vf"], ["Vf"])
                    dve(lambda e: e.tensor_scalar(out=Vf[rows, prs, 1, two * 128:(two + 1) * 128].rearrange("p a (t c) -> p a t c", c=16),
                                                  in0=s_CAi[rows, :, 1:9, :], scalar1=-1.0, scalar2=None, op0=ALU.mult), ["s_CAi", "Vf"], ["Vf"])
                pool(lambda e: e.memset(s_CAzr, 0.0), [], ["s_CAzr"])
                pool(lambda e: e.memset(s_CAzi, 0.0), [], ["s_CAzi"])
                pool(lambda e: e.memset(s_Bzr, 0.0), [], ["s_Bzr"])
                pool(lambda e: e.memset(s_Bzi, 0.0), [], ["s_Bzi"])
                for two in range(2):
                    rows = slice(64 * two, 64 * two + 64)
                    dve(lambda e: e.tensor_copy(out=s_CAzr[rows, :, two, :, :], in_=s_CAr[rows, :, 0:8, :]), ["s_CAr", "s_CAzr"], ["s_CAzr"])
                    dve(lambda e: e.tensor_scalar(out=s_CAzi[rows, :, two, :, :], in0=s_CAi[rows, :, 0:8, :], scalar1=-1.0, scalar2=None, op0=ALU.mult),
                        ["s_CAi", "s_CAzi"], ["s_CAzi"])
                    for j in range(4):
                        c0 = 32 * j + 16 * two
                        dve(lambda e: e.tensor_copy(out=s_Bzr[rows, j, c0:c0 + 16], in_=s_bbr[rows, 4 * q + j, :]), ["s_bbr", "s_Bzr"], ["s_Bzr"])
                        dve(lambda e: e.tensor_copy(out=s_Bzi[rows, j, c0:c0 + 16], in_=s_bbi[rows, 4 * q + j, :]), ["s_bbi", "s_Bzi"], ["s_Bzi"])
                for j in range(4):
                    mm(ps_z[:, 0:256], s_Bzr[:, j, :], s_CAzr[:, j].rearrange("p a t c -> p (a t c)"), j == 0, False, ["s_Bzr", "s_CAzr"], ["ps_z"])
                    mm(ps_z[:, 0:256], s_Bzi[:, j, :], s_CAzi[:, j].rearrange("p a t c -> p (a t c)"), False, j == 3, ["s_Bzi", "s_CAzi"], ["ps_z"])
                dve(lambda e: e.tensor_copy(out=s_Kt.rearrange("p a t c -> p (a t c)"), in_=ps_z[:, 0:256]), ["ps_z"], ["s_Kt"])
                dve(lambda e: e.tensor_scalar(out=s_Kd, in0=dmask[:], scalar1=d_col[:, q:q + 1], scalar2=None, op0=ALU.mult), ["dmask", "d_col"], ["s_Kd"])
                dve(lambda e: e.tensor_tensor(out=s_Kt[:, :, 0, :], in0=s_Kt[:, :, 0, :], in1=s_Kd.rearrange("p (a c) -> p a c", a=2), op=ALU.add),
                    ["s_Kt", "s_Kd"], ["s_Kt"])
                dve(lambda e: e.tensor_copy(out=Kstrip[:, q, :, 7:15, :], in_=s_Kt), ["s_Kt"], ["Kstrip"])
                brb = s_bbr[:, prs, None, :].to_broadcast([128, 4, 8, 16]); bib = s_bbi[:, prs, None, :].to_broadcast([128, 4, 8, 16])
                ar8 = s_ar9[:, prs, 0:8, None].to_broadcast([128, 4, 8, 16]); ai8 = s_ai9[:, prs, 0:8, None].to_broadcast([128, 4, 8, 16])
                dve(lambda e: e.tensor_tensor(out=s_Xr, in0=brb, in1=ar8, op=ALU.mult), ["s_bbr", "s_ar9"], ["s_Xr"])
                dve(lambda e: e.tensor_tensor(out=s_Xt, in0=bib, in1=ai8, op=ALU.mult), ["s_bbi", "s_ai9"], ["s_Xt"])
                dve(lambda e: e.tensor_tensor(out=s_Xr, in0=s_Xr, in1=s_Xt, op=ALU.subtract), ["s_Xr", "s_Xt"], ["s_Xr"])
                dve(lambda e: e.tensor_tensor(out=s_Xi, in0=bib, in1=ar8, op=ALU.mult), ["s_bbi", "s_ar9"], ["s_Xi"])
                dve(lambda e: e.tensor_tensor(out=s_Xt, in0=brb, in1=ai8, op=ALU.mult), ["s_bbr", "s_ai9", "s_Xr"], ["s_Xt"])
                dve(lambda e: e.tensor_tensor(out=s_Xi, in0=s_Xi, in1=s_Xt, op=ALU.add), ["s_Xi", "s_Xt"], ["s_Xi"])
                pool(lambda e: e.memset(s_Xzr, 0.0), [], ["s_Xzr"])
                pool(lambda e: e.memset(s_Xzi, 0.0), [], ["s_Xzi"])
                for two in range(2):
                    rows = slice(64 * two, 64 * two + 64)
                    dve(lambda e: e.tensor_copy(out=s_Xzr[rows, :, :, 16 * two:16 * two + 16].rearrange("p n a c -> p a n c"), in_=s_Xr[rows]), ["s_Xr", "s_Xzr"], ["s_Xzr"])
                    dve(lambda e: e.tensor_copy(out=s_Xzi[rows, :, :, 16 * two:16 * two + 16].rearrange("p n a c -> p a n c"), in_=s_Xi[rows]), ["s_Xi", "s_Xzi"], ["s_Xzi"])
                for hh in range(2):
                    for s4 in range(4):
                        s_ = 4 * hh + s4
                        for ri, xz in ((0, s_Xzr), (1, s_Xzi)):
                            tr(ps_t[:, 2 * s4 + ri, :], xz[:, 7 - s_, :, :].rearrange("p a c -> p (a c)"), ident_b[:], ["s_Xzr", "s_Xzi", "ident_b"], ["ps_t"],
                               sig=(s4 == 3 and ri == 1))
                    dve(lambda e: e.tensor_copy(out=Wssm[:, q, 4 * hh:4 * hh + 4, :, :].rearrange("p s r m -> p (s r) m"), in_=ps_t[:]),
                        ["ps_t"], ["Wssm"])
            pool(lambda e: e.memset(carry[:], 0.0), [], ["carry"])
            pool(lambda e: e.memset(Hb[:], 0.0), [], ["Hb"])
            join(ALLK)
            pool(lambda e: e.memset(vaug[0][:, :, 64:65], 1.0), [], ["vaug0"])
            pool(lambda e: e.memset(vaug[1][:, :, 64:65], 1.0), [], ["vaug1"])

        def block_front(l, b, src):
            slot = b % 2
            bl = b % BPS
            cols = slice(bl * 128, (bl + 1) * 128)
            xk, kTk, vk, pk = "xt%d" % slot, "kT%d" % slot, "vaug%d" % slot, None
            x_t = xt[slot]
            T.dma(x_t, src[b * 128:(b + 1) * 128, :], r=[("res", b)], w=[xk])
            act(lambda e: e.activation(out=junk, in_=x_t, func=AF.Square, accum_out=st1[:, 0:1]), [xk], ["junk", "st1"])
            rsqrt_of(st1[:, 1:2], st1[:, 0:1], 1.0 / D, ["st1"], ["st1"])
            act(lambda e: e.activation(out=hb, in_=x_t, func=AF.Copy, scale=st1[:, 1:2]), [xk, "st1"], ["hb"])
            for kc in range(8):
                tr(ps_t[:, kc, :], hb[:, kc * 128:(kc + 1) * 128], ident_b[:], ["hb", "ident_b"], ["ps_t"], sig=(kc == 7))
            dve(lambda e: e.tensor_copy(out=hTs[slot], in_=ps_t[:]), ["ps_t"], ["hT%d" % slot])


        def block_rest(l, b):
            slot = b % 2
            bl = b % BPS
            cols = slice(bl * 128, (bl + 1) * 128)
            kTk, vk = "kT%d" % slot, "vaug%d" % slot
            hT = hTs[slot]
            hTk = "hT%d" % slot

            def zgroup(c0, n):
                for kc in range(8):
                    mm(ps_z[:, 0:n], hT[:, kc, :], Wsb[:, kc, c0:c0 + n], kc == 0, kc == 7, [hTk, "Wsb"], ["ps_z"])

            zgroup(C_AQ, 512)
            act(lambda e: e.copy(out=qk[:, 0:8, :].rearrange("p h d -> p (h d)"), in_=ps_z[:]), ["ps_z"], ["qk"])
            zgroup(C_AK, 256)
            act(lambda e: e.copy(out=qk[:, 8:10, :].rearrange("p h d -> p (h d)"), in_=ps_z[:, 0:128]), ["ps_z", "qk"], ["qk"])
            act(lambda e: e.copy(out=vaug[slot][:, :, 0:64], in_=ps_z[:, 128:256].rearrange("p (h d) -> p h d", h=2)), ["ps_z"], [vk])
            zgroup(C_AG, 512)
            act(lambda e: e.activation(out=gate_a, in_=ps_z[:], func=AF.Silu), ["ps_z"], ["gate_a"])
            for (c0, dst, dk, fn) in ((C_SU, uT, "uT", None), (C_SG, gT, "gT", AF.Silu)):
                for ct in range(4):
                    for kc in range(8):
                        mm(ps_z[:, ct * 128:(ct + 1) * 128], Wsb[:, kc, c0 + ct * 128:c0 + (ct + 1) * 128], hT[:, kc, :], kc == 0, kc == 7,
                           [hTk, "Wsb"], ["ps_z"], sig=(ct == 3 and kc == 7))
                pz3 = ps_z[:].rearrange("p (c t) -> p c t", c=4)
                if fn is None:
                    act(lambda e: e.copy(out=dst[:, :, cols], in_=pz3), ["ps_z"], [dk])
                else:
                    act(lambda e: e.activation(out=dst[:, :, cols], in_=pz3, func=fn), ["ps_z"], [dk])

            if b == 0: mark('bm_z')
            dve(lambda e: e.tensor_tensor(out=sq, in0=qk, in1=qk, op=ALU.mult), ["qk"], ["sq"])
            dve(lambda e: e.tensor_reduce(out=st1[:, 2:12], in_=sq, axis=AX.X, op=ALU.add), ["sq"], ["st1"])
            if b == 0: mark("r1")
            rsqrt_of(st1[:, 2:12], st1[:, 2:12], 1.0 / 64, ["st1"], ["st1"])
            if b == 0: mark("r2")
            dve(lambda e: e.tensor_tensor(out=qk, in0=qk, in1=st1[:, 2:12, None].to_broadcast([128, 10, 64]), op=ALU.mult), ["qk", "st1"], ["qk"])
            dve(lambda e: e.tensor_tensor(out=qk, in0=qk, in1=g10[:], op=ALU.mult), ["qk", "g10"], ["qk"])
            if b == 0: mark("r3")
            qk4 = qk.rearrange("p h (a j) -> p h a j", a=2)
            r14 = rt1.rearrange("p h (a j) -> p h a j", a=2)
            r24 = rt2.rearrange("p h (a j) -> p h a j", a=2)
            qr4 = qr.rearrange("p h (a j) -> p h a j", a=2)
            cb = cosT[:, b, None, None, :].to_broadcast([128, 10, 2, 32])
            sb_ = sinT[:, b, None, :].to_broadcast([128, 10, 32])
            dve(lambda e: e.tensor_tensor(out=r14, in0=qk4, in1=cb, op=ALU.mult), ["qk", "cosT"], ["rt1"])
            if b == 0: mark("r4")
            dve(lambda e: e.tensor_tensor(out=r24[:, :, 0, :], in0=qk4[:, :, 1, :], in1=sb_, op=ALU.mult), ["qk", "sinT"], ["rt2"])
            dve(lambda e: e.tensor_tensor(out=r24[:, :, 1, :], in0=qk4[:, :, 0, :], in1=sb_, op=ALU.mult), ["qk", "sinT", "rt2"], ["rt2"])
            if b == 0: mark("r5")
            dve(lambda e: e.tensor_tensor(out=qr4[:, :, 0, :], in0=r14[:, :, 0, :], in1=r24[:, :, 0, :], op=ALU.subtract), ["rt1", "rt2"], ["qr"])
            dve(lambda e: e.tensor_tensor(out=qr4[:, :, 1, :], in0=r14[:, :, 1, :], in1=r24[:, :, 1, :], op=ALU.add), ["rt1", "rt2", "qr"], ["qr"])
            if b == 0: mark("r6")
            for j in range(5):
                tr(ps_t[:, j, :], qr[:, 2 * j:2 * j + 2, :].rearrange("p h d -> p (h d)"), ident_b[:], ["qr", "ident_b"], ["ps_t"], sig=(j == 4))
            if b == 0: mark("r7")
            dve(lambda e: e.tensor_copy(out=qT, in_=ps_t[:, 0:4, :]), ["ps_t"], ["qT"])
            dve(lambda e: e.tensor_copy(out=kT[slot], in_=ps_t[:, 4, :]), ["ps_t"], [kTk])
            if b == 0: mark('bm_rope')
            tiles = ([] if b == 0 else [(1 - slot, mask_prev, "mask_prev")]) + [(slot, mask_cur, "mask_cur")]
            for kvh in range(2):
                rows = slice(64 * kvh, 64 * kvh + 64)
                for ti, (sl, mk, mkk) in enumerate(tiles):
                    mm(ps_s[ti][:], kT[sl][rows, :], qT[rows, :, :].rearrange("p j q -> p (j q)"), True, True,
                       ["kT%d" % sl, "qT"], ["ps_s%d" % ti])
                    act(lambda e: e.activation(out=pTm[ti], in_=ps_s[ti][:], func=AF.Exp, scale=0.125), ["ps_s%d" % ti], ["pTm%d" % ti])
                    pool(lambda e: e.tensor_tensor(out=pTm[ti], in0=pTm[ti], in1=mk[:].rearrange("p j q -> p (j q)"), op=ALU.mult),
                         ["pTm%d" % ti, mkk], ["pTm%d" % ti])
                for j in range(4):
                    for ti, (sl, mk, mkk) in enumerate(tiles):
                        mm(ps_f[:, j * 65:(j + 1) * 65], pTm[ti][:, j * 128:(j + 1) * 128], vaug[sl][:, kvh, :], ti == 0, ti == len(tiles) - 1,
                           ["pTm%d" % ti, "vaug%d" % sl], ["ps_f"], sig=(j == 3 and ti == len(tiles) - 1))
                o4 = ps_f[:, 0:260].rearrange("p (j d) -> p j d", d=65)
                dve(lambda e: e.tensor_tensor(out=den[:, 0:4], in0=o4[:, :, 64], in1=esink[:, 4 * kvh:4 * kvh + 4], op=ALU.add), ["ps_f", "esink"], ["den"])
                dve(lambda e: e.reciprocal(out=den[:, 0:4], in_=den[:, 0:4]), ["den"], ["den"])
                for j in range(4):
                    h = 4 * kvh + j
                    dve(lambda e: e.scalar_tensor_tensor(out=mix_a[:, h * 64:(h + 1) * 64], in0=o4[:, j, 0:64], scalar=den[:, j:j + 1],
                                                         in1=gate_a[:, h * 64:(h + 1) * 64], op0=ALU.mult, op1=ALU.mult),
                        ["ps_f", "den", "gate_a"], ["mix_a"])
            for j in range(4):
                tr(ps_t[:, j, :], mix_a[:, j * 128:(j + 1) * 128], ident_b[:], ["mix_a", "ident_b"], ["ps_t"], sig=(j == 3))
            dve(lambda e: e.tensor_copy(out=mixT[:, 0:4, cols], in_=ps_t[:, 0:4, :]), ["ps_t"], ["mixT"])
            if b == 0: mark('bm_attn')
            zgroup(C_XQ, 512)
            act(lambda e: e.copy(out=xq_f.rearrange("p h d -> p (h d)"), in_=ps_z[:]), ["ps_z"], ["xq_f"])
            zgroup(C_XG, 512)
            act(lambda e: e.activation(out=gate_x, in_=ps_z[:], func=AF.Silu), ["ps_z"], ["gate_x"])
            sq4 = sq.rearrange("p h d -> p (h d)")[:, 0:512].rearrange("p (h d) -> p h d", h=4)
            dve(lambda e: e.tensor_tensor(out=sq4, in0=xq_f, in1=xq_f, op=ALU.mult), ["xq_f"], ["sq"])
            dve(lambda e: e.tensor_reduce(out=st1[:, 12:16], in_=sq4, axis=AX.X, op=ALU.add), ["sq"], ["st1"])
            rsqrt_of(st1[:, 12:16], st1[:, 12:16], 1.0 / 128, ["st1"], ["st1"])
            dve(lambda e: e.tensor_tensor(out=xq_b, in0=xq_f, in1=st1[:, 12:16, None].to_broadcast([128, 4, 128]), op=ALU.mult), ["xq_f", "st1"], ["xq_b"])
            for h in range(4):
                tr(ps_t[:, h, :], xq_b[:, h, :], ident_b[:], ["xq_b", "ident_b"], ["ps_t"], sig=(h == 3))
            dve(lambda e: e.tensor_copy(out=xqT, in_=ps_t[:, 0:4, :]), ["ps_t"], ["xqT"])
            for mt in range(2):
                for h in range(4):
                    mm(ps_s[mt][:, h * 128:(h + 1) * 128], mkT[:, h, mt * 128:(mt + 1) * 128], xqT[:, h, :], True, True, ["mkT", "xqT"],
                       ["ps_s%d" % mt], sig=(h == 3))
                act(lambda e: e.activation(out=pX[mt], in_=ps_s[mt][:], func=AF.Exp), ["ps_s%d" % mt], ["pX%d" % mt])
            for hp in range(2):
                for i in range(2):
                    h = 2 * hp + i
                    for mt in range(2):
                        mm(ps_f[:, i * 129:(i + 1) * 129], pX[mt][:, h * 128:(h + 1) * 128], mv_aug[:, mt, h, :], mt == 0, mt == 1,
                           ["pX%d" % mt, "mv_aug"], ["ps_f"], sig=(i == 1 and mt == 1))
                o2 = ps_f[:, 0:258].rearrange("p (i d) -> p i d", d=129)
                dve(lambda e: e.reciprocal(out=den[:, 4:6], in_=o2[:, :, 128]), ["ps_f"], ["den"])
                for i in range(2):
                    h = 2 * hp + i
                    dve(lambda e: e.scalar_tensor_tensor(out=mix_c[:, h * 128:(h + 1) * 128], in0=o2[:, i, 0:128], scalar=den[:, 4 + i:5 + i],
                                                         in1=gate_x[:, h * 128:(h + 1) * 128], op0=ALU.mult, op1=ALU.mult),
                        ["ps_f", "den", "gate_x"], ["mix_c"])
            for j in range(4):
                tr(ps_t[:, j, :], mix_c[:, j * 128:(j + 1) * 128], ident_b[:], ["mix_c", "ident_b"], ["ps_t"], sig=(j == 3))
            dve(lambda e: e.tensor_copy(out=mixT[:, 8:12, cols], in_=ps_t[:, 0:4, :]), ["ps_t"], ["mixT"])

        def ssm_superblock(l, sbi):
            ps_zs4 = ps_zs[:, 0:8 * NCH].rearrange("p (a r k) -> p a r k", a=4, r=2)
            for q in range(4):
                prs = slice(4 * q, 4 * q + 4)
                u3s = []
                for prl in range(4):
                    rows = slice(32 * prl, 32 * prl + 32)
                    kw = {"tile_position": (96, 0)} if prl == 3 else {}
                    u3 = uT[rows, q, :].rearrange("p (k s) -> p s k", s=8)
                    u3s.append((rows, kw, u3))
                    for ri in range(2):
                        for s_ in range(8):
                            mm(ps_zs4[:, prl, ri, :], Wssm[rows, q, s_, ri, :], u3[:, s_, :], s_ == 0, s_ == 7, ["Wssm", "uT"], ["ps_zs"],
                               sig=(ri == 1 and s_ == 7), **kw)
                zr = ps_zs4[:, :, 0, :]; zi = ps_zs4[:, :, 1, :]
                mc = mcT[:, prs, :]; ms = msT[:, prs, :]
                dve(lambda e: e.tensor_tensor(out=sct[:, 0], in0=zr, in1=mc, op=ALU.mult), ["ps_zs", "mcT"], ["sct"])
                dve(lambda e: e.tensor_tensor(out=sct[:, 1], in0=zi, in1=ms, op=ALU.mult), ["ps_zs", "msT", "sct"], ["sct"])
                dve(lambda e: e.tensor_tensor(out=sct[:, 2], in0=zi, in1=mc, op=ALU.mult), ["ps_zs", "mcT", "sct"], ["sct"])
                dve(lambda e: e.tensor_tensor(out=sct[:, 3], in0=zr, in1=ms, op=ALU.mult), ["ps_zs", "msT", "sct"], ["sct"])
                dve(lambda e: e.tensor_tensor(out=zt[:, 0], in0=sct[:, 0], in1=sct[:, 1], op=ALU.add), ["sct"], ["zt"])
                dve(lambda e: e.tensor_tensor(out=zt[:, 1], in0=sct[:, 2], in1=sct[:, 3], op=ALU.subtract), ["sct", "zt"], ["zt"])
                if sbi > 0:
                    hr_ = carry[:, prs, 0]; hi_ = carry[:, prs, 1]
                    dve(lambda e: e.tensor_tensor(out=init4[:, 0, :], in0=hr_, in1=mcA[:, prs], op=ALU.mult), ["carry", "mcA"], ["init4"])
                    dve(lambda e: e.tensor_tensor(out=init4[:, 1, :], in0=hi_, in1=nmsA[:, prs], op=ALU.mult), ["carry", "nmsA", "init4"], ["init4"])
                    dve(lambda e: e.tensor_tensor(out=init4[:, 2, :], in0=hi_, in1=mcA[:, prs], op=ALU.mult), ["carry", "mcA", "init4"], ["init4"])
                    dve(lambda e: e.tensor_tensor(out=init4[:, 3, :], in0=hr_, in1=msA[:, prs], op=ALU.mult), ["carry", "msA", "init4"], ["init4"])
                    for ri in range(2):
                        for jj in range(2):
                            dve(lambda e: e.tensor_tensor(out=zt[:, ri, :, 0], in0=zt[:, ri, :, 0], in1=init4[:, 2 * ri + jj, :], op=ALU.add),
                                ["zt", "init4"], ["zt"])
                mg = magTab[:, prs, :].rearrange("p a k -> p (a k)")
                for ri in range(2):
                    dve(lambda e: e.tensor_tensor_scan(out=Gs[:, ri].rearrange("p a k -> p (a k)"), data0=mg,
                                                      data1=zt[:, ri].rearrange("p a k -> p (a k)"), initial=0.0, op0=ALU.mult, op1=ALU.add),
                        ["zt", "magTab", "Gs"], ["Gs"])
                f2 = lambda v: v.rearrange("p a k -> p (a k)")
                pool(lambda e: e.tensor_tensor(out=f2(sct[:, 0]), in0=f2(Gs[:, 0]), in1=f2(mc), op=ALU.mult), ["Gs", "mcT"], ["sct"])
                pool(lambda e: e.tensor_tensor(out=f2(sct[:, 1]), in0=f2(Gs[:, 1]), in1=f2(ms), op=ALU.mult), ["Gs", "msT", "sct"], ["sct"])
                pool(lambda e: e.tensor_tensor(out=f2(sct[:, 2]), in0=f2(Gs[:, 1]), in1=f2(mc), op=ALU.mult), ["Gs", "mcT", "sct"], ["sct"])
                pool(lambda e: e.tensor_tensor(out=f2(sct[:, 3]), in0=f2(Gs[:, 0]), in1=f2(ms), op=ALU.mult), ["Gs", "msT", "sct"], ["sct"])
                pool(lambda e: e.tensor_tensor(out=f2(Hf[:, 0]), in0=f2(sct[:, 0]), in1=f2(sct[:, 1]), op=ALU.subtract), ["sct"], ["Hf"])
                pool(lambda e: e.tensor_tensor(out=f2(Hf[:, 1]), in0=f2(sct[:, 2]), in1=f2(sct[:, 3]), op=ALU.add), ["sct", "Hf"], ["Hf"])
                Hfp = Hf.rearrange("p r a k -> p a r k")
                act(lambda e: e.copy(out=Hb[:, prs, :, 0:1], in_=carry[:, prs, :, None]), ["carry"], ["Hb"])
                act(lambda e: e.copy(out=Hb[:, prs, :, 1:NCH], in_=Hfp[:, :, :, 0:NCH - 1]), ["Hf", "Hb"], ["Hb"])
                act(lambda e: e.copy(out=carry[:, prs, :], in_=Hfp[:, :, :, NCH - 1]), ["Hf", "Hb"], ["carry"])
                for h2 in range(2):
                    for i2 in range(2):
                        prl = 2 * h2 + i2
                        rows, kw, u3 = u3s[prl]
                        for s_ in range(8):
                            mm(ps_y[0:NCH, i2 * 256:(i2 + 1) * 256].rearrange("k (a t c) -> k a t c", a=2, t=8), u3[:, s_, :],
                               Kstrip[rows, q, :, 7 - s_:15 - s_, :], s_ == 0, s_ == 7, ["uT", "Kstrip"], ["ps_y"], **kw)
                    for i2 in range(2):
                        pr = 4 * q + 2 * h2 + i2
                        for ri in range(2):
                            mm(ps_y2[0:NCH, i2 * 256:(i2 + 1) * 256], Hb[:, pr, ri, :], Vf[:, pr, ri, :], ri == 0, ri == 1, ["Hb", "Vf"], ["ps_y2"],
                               sig=(i2 == 1 and ri == 1))
                    act(lambda e: e.copy(out=y2s[0:NCH, :], in_=ps_y2[0:NCH, 0:512]), ["ps_y2"], ["y2s"])
                    dve(lambda e: e.tensor_tensor(out=ysum[0:NCH, :], in0=ps_y[0:NCH, 0:512], in1=y2s[0:NCH, :], op=ALU.add), ["ps_y", "y2s"], ["ysum"])
                    for i2 in range(2):
                        prl = 2 * h2 + i2
                        act(lambda e: e.activation(out=y2k[0:NCH, :, 32 * prl:32 * prl + 32].rearrange("k t (a c) -> k t a c", a=2),
                                                   in_=ysum[0:NCH, i2 * 256:(i2 + 1) * 256].rearrange("k (a t c) -> k t a c", a=2, t=8),
                                                   func=AF.Gelu_apprx_tanh), ["ysum", "y2k"], ["y2k"])
                for t in range(8):
                    tr(ps_t[:, t, 0:NCH], y2k[0:NCH, t, :], ident_b[0:NCH, 0:NCH], ["y2k", "ident_b"], ["ps_t"], sig=(t == 7))
                dve(lambda e: e.tensor_copy(out=uT[:, q, :].rearrange("c (k t) -> c t k", t=8), in_=ps_t[:, :, 0:NCH]), ["ps_t", "uT"], ["uT"])
            for oc in range(4):
                for kc in range(4):
                    mm(ps_y[:, 0:SBT], Wglu[:, kc, oc * 128:(oc + 1) * 128], uT[:, kc, :], kc == 0, kc == 3, ["Wglu", "uT"], ["ps_y"])
                act(lambda e: e.activation(out=sig_t, in_=ps_y[:, 0:SBT], func=AF.Sigmoid, bias=bglu_col[:, oc:oc + 1]), ["ps_y", "bglu_col"], ["sig_t"])
                pool(lambda e: e.tensor_tensor(out=sig_t, in0=sig_t, in1=gT[:, oc, :], op=ALU.mult), ["sig_t", "gT"], ["sig_t"])
                dve(lambda e: e.tensor_tensor(out=mixT[:, 4 + oc, :], in0=sig_t, in1=uT[:, oc, :], op=ALU.mult), ["sig_t", "uT"], ["mixT"])

        def block_out(l, b, src):
            slot = b % 2
            bl = b % BPS
            cols = slice(bl * 128, (bl + 1) * 128)
            xk = "xr0"
            slot = 0
            T.dma(xr[slot], src[b * 128:(b + 1) * 128, :], r=[("res", b)], w=[xk], q="sp")
            for half in range(2):
                for kc in range(12):
                    mm(ps_f[:], mixT[:, kc, cols], Wout[:, kc, half * 512:(half + 1) * 512], kc == 0, kc == 11, ["mixT", "Wout"], ["ps_f"])
                dve(lambda e: e.tensor_tensor(out=xr[slot][:, half * 512:(half + 1) * 512], in0=ps_f[:], in1=xr[slot][:, half * 512:(half + 1) * 512],
                                              op=ALU.add), ["ps_f", xk], [xk])
            T.dma(out_d[b * 128:(b + 1) * 128, :], xr[slot], r=[xk], w=[("res", b)], q="sp")

        try:
            mark('consts')
            for l in range(n_layers):
                src = x_d if l == 0 else out_d
                layer_setup(l)
                mark('setup')
                for sbi in range(n_sb):
                    block_front(l, sbi * BPS, src)
                    for bl in range(BPS):
                        if bl + 1 < BPS:
                            block_front(l, sbi * BPS + bl + 1, src)
                        block_rest(l, sbi * BPS + bl)
                        mark('main%d' % bl)
                    ssm_superblock(l, sbi)
                    mark('ssm')
                    for bl in range(BPS):
                        block_out(l, sbi * BPS + bl, src)
        except StopBuild:
            print("build stopped at", stop)
        T.drain("sp")
        print("ops", T.nops, "waits", T.nwaits)
    return nc


Q_PERM = [0, 4, 1, 5, 2, 6, 3, 7]


def prep_inputs(inputs, n_sb=S // SBT, layers=None, x_override=None):
    SEQ = n_sb * SBT
    lsl = slice(None) if layers is None else slice(layers[0], layers[1])
    w_in = np.asarray(inputs["w_in"], dtype=np.float32)[lsl]
    qcols = np.concatenate([np.arange(h * 64, (h + 1) * 64) for h in Q_PERM])
    perm = np.concatenate([qcols, np.arange(512, INW)])
    w_in_p = np.ascontiguousarray(w_in[:, :, perm])
    shared = {k: np.ascontiguousarray(np.asarray(v)[lsl]) for k, v in inputs.items() if k not in ("x", "mem", "positions", "w_in")}
    shared["w_in"] = w_in_p
    xs = np.asarray(inputs["x"]) if x_override is None else x_override
    maps = []
    for c in range(8):
        m = dict(shared)
        m["x"] = np.ascontiguousarray(xs[c, :SEQ])
        m["mem"] = np.ascontiguousarray(np.asarray(inputs["mem"])[c])
        m["positions"] = np.ascontiguousarray(np.asarray(inputs["positions"])[c, :SEQ]).astype(np.int32)
        maps.append(m)
    return maps


LAYERS_PER_LAUNCH = 4


def kernel(**inputs):
    nc = build_nc(n_layers=LAYERS_PER_LAUNCH, wd=LAYERS_PER_LAUNCH)
    x = np.asarray(inputs["x"], dtype=np.float32)
    for l0 in range(0, DEPTH, LAYERS_PER_LAUNCH):
        maps = prep_inputs(inputs, layers=(l0, l0 + LAYERS_PER_LAUNCH), x_override=x)
        res = run_bass_kernel_spmd(nc, maps, core_ids=list(range(8)))
        x = np.stack([np.asarray(r["out"]) for r in res.results], axis=0).astype(np.float32)
    return x
```

```python
import math
import numpy as np
from contextlib import ExitStack
import concourse.bass as bass
import concourse.mybir as mybir
from concourse.bass_utils import run_bass_kernel_spmd

F32 = mybir.dt.float32
BF16 = mybir.dt.bfloat16
I32 = mybir.dt.int32
ALU = mybir.AluOpType
AF = mybir.ActivationFunctionType
AX = mybir.AxisListType

D = 1024
S = 4096
NMEM = 256
INW = 3328
DEPTH = 4
C_AQ, C_AK, C_AV, C_AG, C_SU, C_SG, C_XQ, C_XG = 0, 512, 640, 768, 1280, 1792, 2304, 2816
EPS = 1e-6
SBT = 512
BPS = SBT // 128
NCH = SBT // 8
N_DMA_SEMS = 40
TWO_PI = 2.0 * math.pi
CW1 = 6.28125
CW2 = TWO_PI - 6.28125


class Trk:
    def __init__(self, nc, stack):
        self.nc = nc
        self.eng = {"pe": nc.tensor, "act": nc.scalar, "dve": nc.vector, "pool": nc.gpsimd, "sp": nc.sync}
        self.sem = {k: stack.enter_context(nc.semaphore("s_" + k)) for k in ("pe", "act", "dve", "pool")}
        self.cnt = {k: 0 for k in self.sem}
        self.dsem = [stack.enter_context(nc.semaphore("d%d" % i)) for i in range(N_DMA_SEMS)]
        self.dcnt = [0] * N_DMA_SEMS
        self.dnext = 0
        self.known = {k: {} for k in self.eng}
        self.lastw = {}
        self.reads = {}
        self.nwaits = 0
        self.nops = 0

    def _wait(self, e, tok):
        kind, key, val = tok
        if kind == "E" and key == e and val > self.cnt[e]:
            return
        kn = self.known[e]
        if kn.get((kind, key), 0) >= val:
            return
        kn[(kind, key)] = val
        sem = self.sem[key] if kind == "E" else self.dsem[key]
        self.eng[e].wait_ge(sem, val)
        self.nwaits += 1

    ALIAS = {"junk": "hb", "sq": "rt1", "xq_f": "rt2", "y2s": "rt1", "ysum": "rt2", "y2k": "qk", "sig_t": "qr",
             "zt": "xt0", "Gs": "xt0", "sct": "xt1", "Hf": "hb",
             "pX0": "pTm0", "pX1": "pTm1", "mix_c": "mix_a"}

    def _deps(self, e, r, w):
        r = [self.ALIAS.get(k, k) for k in r]
        w = [self.ALIAS.get(k, k) for k in w]
        for k in r:
            t = self.lastw.get(k)
            if t is not None:
                self._wait(e, t)
        for k in w:
            t = self.lastw.get(k)
            if t is not None:
                self._wait(e, t)
            for t in self.reads.get(k, ()):
                self._wait(e, t)

    def _commit(self, tok, r, w):
        r = [self.ALIAS.get(k, k) for k in r]
        w = [self.ALIAS.get(k, k) for k in w]
        for k in r:
            self.reads.setdefault(k, []).append(tok)
        for k in w:
            self.lastw[k] = tok
            self.reads[k] = []

    def op(self, e, fn, r=(), w=(), sig=True):
        self._deps(e, r, w)
        inst = fn(self.eng[e])
        self.nops += 1
        if sig:
            self.cnt[e] += 1
            inst.then_inc(self.sem[e], 1)
            tok = ("E", e, self.cnt[e])
        else:
            tok = ("E", e, self.cnt[e] + 1)
        self._commit(tok, r, w)

    def dma(self, out, in_, r=(), w=(), q="sp"):
        self._deps(q, r, w)
        i = self.dnext
        self.dnext = (self.dnext + 1) % N_DMA_SEMS
        if self.dcnt[i] > 0:
            self._wait(q, ("D", i, self.dcnt[i]))
        self.dcnt[i] += 16
        self.eng[q].dma_start(out=out, in_=in_).then_inc(self.dsem[i], 16)
        tok = ("D", i, self.dcnt[i])
        self.nops += 1
        self._commit(tok, r, w)

    def drain(self, e="sp"):
        for k in self.sem:
            if self.cnt[k] > 0:
                self._wait(e, ("E", k, self.cnt[k]))
        for i in range(N_DMA_SEMS):
            if self.dcnt[i] > 0:
                self._wait(e, ("D", i, self.dcnt[i]))

    def finish(self, keys, e="sp"):
        for k in keys:
            t = self.lastw.get(k)
            if t is not None:
                self._wait(e, t)


class StopBuild(Exception):
    pass


def build_nc(n_layers=DEPTH, n_sb=S // SBT, dbg=False, stop=None, wd=DEPTH):
    nc = bass.Bass("TRN2", target_bir_lowering=False)
    SEQ = n_sb * SBT
    di = lambda n, s, dt=F32: nc.dram_tensor(n, s, dt, kind="ExternalInput").ap()
    x_d = di("x", [SEQ, D]); mem_d = di("mem", [NMEM, D]); pos_d = di("positions", [SEQ], I32)
    norm_g_d = di("norm_g", [wd, D]); w_in_d = di("w_in", [wd, D, INW])
    qg_d = di("q_norm_g", [wd, 64]); kg_d = di("k_norm_g", [wd, 64]); sinks_d = di("sinks", [wd, 8])
    lre_d = di("lam_re", [wd, 32, 64]); lim_d = di("lam_im", [wd, 32, 64]); ldt_d = di("log_dt", [wd, 32])
    bre_d = di("b_re", [wd, 32, 64, 16]); bim_d = di("b_im", [wd, 32, 64, 16])
    cre_d = di("c_re", [wd, 32, 16, 64]); cim_d = di("c_im", [wd, 32, 16, 64])
    dskip_d = di("d_skip", [wd, 512]); wglu_d = di("w_glu", [wd, 512, 512]); bglu_d = di("b_glu", [wd, 512])
    mng_d = di("mem_norm_g", [wd, D]); wkv_d = di("w_mem_kv", [wd, D, D])
    xqg_d = di("xq_norm_g", [wd, 128]); xkg_d = di("xk_norm_g", [wd, 128]); wout_d = di("w_out", [wd, 1536, D])
    out_d = nc.dram_tensor("out", [SEQ, D], F32, kind="ExternalOutput").ap()
    NB = SEQ // 128

    with ExitStack() as st:
        T = Trk(nc, st)
        sbt = lambda name, shape, dt: st.enter_context(nc.sbuf_tensor(name, shape, dt))
        pst = lambda name, shape, dt: st.enter_context(nc.psum_tensor(name, shape, dt))
        dve = lambda fn, r, w: T.op("dve", fn, r=r, w=w)
        act = lambda fn, r, w: T.op("act", fn, r=r, w=w)
        pool = lambda fn, r, w: T.op("pool", fn, r=r, w=w)

        def mm(out, lhsT, rhs, start, stop, r, w, sig=None, **kw):
            T.op("pe", lambda e: e.matmul(out, lhsT=lhsT, rhs=rhs, start=start, stop=stop, **kw), r=r, w=w,
                 sig=(stop if sig is None else sig))

        def tr(out, in_, ident, r, w, sig=True):
            T.op("pe", lambda e: e.transpose(out=out, in_=in_, identity=ident), r=r, w=w, sig=sig)

        ps_t = pst("ps_t", [128, 8, 128], BF16)
        ps_z = pst("ps_z", [128, 512], F32)
        ps_s = [pst("ps_s0", [128, 512], F32), pst("ps_s1", [128, 512], F32)]
        ps_f = pst("ps_f", [128, 512], F32)
        ps_y = pst("ps_y", [128, 512], F32)
        ps_y2 = pst("ps_y2", [128, 512], F32)
        ps_zs = pst("ps_zs", [128, 512], F32)

        Wsb = sbt("Wsb", [128, 8, INW], BF16)
        Wout = sbt("Wout", [128, 12, D], BF16)
        Wglu = sbt("Wglu", [128, 4, 512], BF16)
        Kstrip = sbt("Kstrip", [128, 4, 2, 15, 16], BF16)
        Wssm = sbt("Wssm", [128, 4, 8, 2, 128], BF16)
        Vf = sbt("Vf", [128, 16, 2, 256], BF16)
        mcT = sbt("mcT", [128, 16, NCH], F32)
        msT = sbt("msT", [128, 16, NCH], F32)
        Hb = sbt("Hb", [128, 16, 2, NCH], BF16)
        carry = sbt("carry", [128, 16, 2], F32)
        magA = sbt("magA", [128, 16], F32)
        mcA = sbt("mcA", [128, 16], F32)
        msA = sbt("msA", [128, 16], F32)
        nmsA = sbt("nmsA", [128, 16], F32)
        magTab = sbt("magTab", [128, 16, NCH], F32)
        mkT = sbt("mkT", [128, 4, NMEM], BF16)
        mv_aug = sbt("mv_aug", [128, 2, 4, 129], BF16)
        ident_b = sbt("ident_b", [128, 128], BF16)
        ident_f = sbt("ident_f", [128, 128], F32)
        mask_cur = sbt("mask_cur", [128, 4, 128], BF16)
        mask_prev = sbt("mask_prev", [128, 4, 128], BF16)
        dmask = sbt("dmask", [128, 32], F32)
        cosT = sbt("cosT", [128, NB, 32], F32)
        sinT = sbt("sinT", [128, NB, 32], F32)
        g10 = sbt("g10", [128, 10, 64], F32)
        gq_bc = sbt("gq_bc", [128, 64], F32)
        gk_bc = sbt("gk_bc", [128, 64], F32)
        esink = sbt("esink", [128, 8], F32)
        gxk_bc = sbt("gxk_bc", [128, 128], F32)
        gxq_col = sbt("gxq_col", [128, 1], F32)
        bglu_col = sbt("bglu_col", [128, 4], F32)
        d_col = sbt("d_col", [128, 4], F32)
        g_col = sbt("g_col", [128, 8], F32)
        gm_col = sbt("gm_col", [128, 8], F32)
        ldT = sbt("ldT", [32, 128], F32)
        nvec = sbt("nvec", [128, 9], F32)
        kvec = sbt("kvec", [128, NCH], F32)
        dummy = sbt("dummy_t", [128, 4], F32)

        ARW = 14330
        arena = sbt("arena", [128, ARW], F32)
        aoff = {"main": 0, "setup": 0, "setupB": 0}
        akeys = {"main": [], "setup": [], "setupB": []}

        def carve(phase, name, shape, dt):
            n = int(np.prod(shape))
            words = n if dt in (F32, I32) else (n + 1) // 2
            o = aoff[phase]
            aoff[phase] = o + words
            assert aoff[phase] <= ARW, (phase, name, aoff[phase])
            v = arena[:, o:o + words]
            if dt != F32:
                v = v.bitcast(dt)
                if dt == BF16 and n % 2:
                    v = v[:, 0:n]
            if len(shape) > 1:
                names = " ".join("d%d" % i for i in range(len(shape)))
                v = v.rearrange("p (%s) -> p %s" % (names, names), **{"d%d" % i: shape[i] for i in range(1, len(shape))})
            if name not in akeys[phase]:
                akeys[phase].append(name)
            return v

        M = lambda name, shape, dt: carve("main", name, shape, dt)
        U = lambda name, shape, dt: carve("setup", name, shape, dt)
        UB = lambda name, shape, dt: carve("setupB", name, shape, dt)

        uT = M("uT", [4, SBT], BF16)
        gT = M("gT", [4, SBT], BF16)
        mixT = M("mixT", [12, SBT], BF16)
        xt = [M("xt0", [D], F32), M("xt1", [D], F32)]
        xr = [M("xr0", [D], F32)]
        hb = M("hb", [D], BF16)
        junk = hb
        hTs = [M("hT0", [8, 128], BF16), M("hT1", [8, 128], BF16)]
        st1 = M("st1", [16], F32)
        qk = M("qk", [10, 64], F32)
        rt1 = M("rt1", [10, 64], F32)
        rt2 = M("rt2", [10, 64], F32)
        sq = rt1
        qr = M("qr", [10, 64], BF16)
        qT = M("qT", [4, 128], BF16)
        kT = [M("kT0", [128], BF16), M("kT1", [128], BF16)]
        vaug = [M("vaug0", [2, 65], BF16), M("vaug1", [2, 65], BF16)]
        pTm = [M("pTm0", [512], BF16), M("pTm1", [512], BF16)]
        pX = pTm
        gate_a = M("gate_a", [512], BF16)
        gate_x = M("gate_x", [512], BF16)
        mix_a = M("mix_a", [512], BF16)
        mix_c = mix_a
        xq_f = rt2.rearrange("p h d -> p (h d)")[:, 0:512].rearrange("p (h d) -> p h d", h=4)
        xq_b = M("xq_b", [4, 128], BF16)
        xqT = M("xqT", [4, 128], BF16)
        den = M("den", [8], F32)
        zt = xt[0][:, 0:512].rearrange("p (r a k) -> p r a k", r=2, a=4)
        Gs = xt[0][:, 512:1024].rearrange("p (r a k) -> p r a k", r=2, a=4)
        sct = xt[1].rearrange("p (j a k) -> p j a k", j=4, a=4)
        Hf = hb.bitcast(F32).rearrange("p (r a k) -> p r a k", r=2, a=4)
        init4 = M("init4", [4, 4], F32)
        y2s = rt1.rearrange("p h d -> p (h d)")[:, 0:512]
        ysum = rt2.rearrange("p h d -> p (h d)")[:, 0:512]
        y2k = qk.rearrange("p h d -> p (h d)")[:, 0:512].bitcast(BF16).rearrange("p (t c) -> p t c", t=8)
        sig_t = qr.rearrange("p h d -> p (h d)")[:, 0:512]
        print("arena main words", aoff["main"])

        def load_T(dst, src, n, wkey):
            T.dma(ldT[0:n, :], src, w=["ldT"])
            tr(ps_z[:, 0:n], ldT[0:n, :], ident_f[0:n, 0:n], ["ldT", "ident_f"], ["ps_z"])
            dve(lambda e: e.tensor_copy(out=dst, in_=ps_z[:, 0:n]), ["ps_z"], [wkey])

        def sin_of(dst, src, n, shift, rk, wk, tmp=None):
            for c0 in range(0, n, 256):
                c1 = min(n, c0 + 256)
                m = c1 - c0
                ki, kf, yy = s_ki[:, 0:m], s_kf[:, 0:m], s_y[:, 0:m]
                sr = src[:, c0:c1]
                dve(lambda e: e.tensor_scalar(out=ki, in0=sr, scalar1=float(shift), scalar2=float(1.0 / TWO_PI), op0=ALU.add, op1=ALU.mult),
                    rk, ["sr_ki"])
                dve(lambda e: e.tensor_copy(out=kf, in_=ki), ["sr_ki"], ["sr_kf"])
                dve(lambda e: e.tensor_scalar(out=yy, in0=sr, scalar1=float(shift), scalar2=None, op0=ALU.add), rk, ["sr_y"])
                dve(lambda e: e.scalar_tensor_tensor(out=yy, in0=kf, scalar=float(-CW1), in1=yy, op0=ALU.mult, op1=ALU.add),
                    ["sr_kf", "sr_y"], ["sr_y"])
                dve(lambda e: e.scalar_tensor_tensor(out=yy, in0=kf, scalar=float(-CW2), in1=yy, op0=ALU.mult, op1=ALU.add),
                    ["sr_kf", "sr_y"], ["sr_y"])
                dve(lambda e: e.tensor_scalar(out=yy, in0=yy, scalar1=float(math.pi), scalar2=float(-math.pi), op0=ALU.min, op1=ALU.max),
                    ["sr_y"], ["sr_y"])
                act(lambda e: e.activation(out=dst[:, c0:c1], in_=yy, func=AF.Sin), ["sr_y"], wk)

        def rsqrt_of(dst, src, scale, rk, wk):
            dve(lambda e: e.tensor_scalar(out=dst, in0=src, scalar1=float(scale), scalar2=float(EPS), op0=ALU.mult, op1=ALU.add), rk, wk)
            act(lambda e: e.activation(out=dst, in_=dst, func=AF.Sqrt), wk, wk)
            dve(lambda e: e.reciprocal(out=dst, in_=dst), wk, wk)

        def join(keys, e="dve"):
            T.op(e, lambda en: en.memset(dummy[:, 0:1], 0.0), r=[], w=list(keys) + ["dummy"])

        s_ki = U("sr_ki", [256], I32); s_kf = U("sr_kf", [256], F32); s_y = U("sr_y", [256], F32)
        s_ang = U("s_ang", [256], F32)
        UB("sr_ki", [256], I32); UB("sr_kf", [256], F32); UB("sr_y", [256], F32); UB("s_ang", [256], F32)
        ones_f = U("ones_f", [128], F32)
        stage = [U("stage%d" % i, [1024], F32) for i in range(4)]
        s_posi = U("s_posi", [128], I32); s_posf = U("s_posf", [128], F32)
        s_posT = U("s_posT", [32], F32); s_inv = U("s_inv", [32], F32)
        s_memx = U("s_memx", [D], F32); s_memn = U("s_memn", [D], BF16); s_memT = U("s_memT", [8, NMEM], BF16)
        s_mkf = U("s_mkf", [4, 128], F32); s_mkb = U("s_mkb", [4, 128], BF16)
        s_smA = U("s_smA", [16], F32)
        s_sm = UB("s_sm", [16, 12], F32)
        s_e9 = UB("s_e9", [16, 9], F32); s_a9 = UB("s_a9", [16, 9], F32); s_mag9 = UB("s_mag9", [16, 9], F32)
        s_cos9 = UB("s_cos9", [16, 9], F32); s_sin9 = UB("s_sin9", [16, 9], F32)
        s_ar9 = UB("s_ar9", [16, 9], F32); s_ai9 = UB("s_ai9", [16, 9], F32)
        s_Br = UB("s_Br", [16, 16], F32); s_Bi = UB("s_Bi", [16, 16], F32)
        s_bbr = UB("s_bbr", [16, 16], F32); s_bbi = UB("s_bbi", [16, 16], F32); s_bt = UB("s_bt", [16, 16], F32)
        s_Cst = UB("s_Cst", [2, 128], F32)
        s_Cr = UB("s_Cr", [16, 16], F32); s_Ci = UB("s_Ci", [16, 16], F32)
        s_CAr = UB("s_CAr", [4, 9, 16], F32); s_CAi = UB("s_CAi", [4, 9, 16], F32); s_CAt = UB("s_CAt", [4, 9, 16], F32)
        s_CAzr = UB("s_CAzr", [4, 2, 8, 16], F32); s_CAzi = UB("s_CAzi", [4, 2, 8, 16], F32)
        s_Bzr = UB("s_Bzr", [4, 128], F32); s_Bzi = UB("s_Bzi", [4, 128], F32)
        s_Kt = UB("s_Kt", [2, 8, 16], F32); s_Kd = UB("s_Kd", [32], F32)
        s_Xr = UB("s_Xr", [4, 8, 16], F32); s_Xi = UB("s_Xi", [4, 8, 16], F32); s_Xt = UB("s_Xt", [4, 8, 16], F32)
        s_Xzr = UB("s_Xzr", [8, 4, 32], BF16); s_Xzi = UB("s_Xzi", [8, 4, 32], BF16)
        s_lg2 = UB("s_lg2", [2], F32)
        print("arena setup words", aoff["setup"], aoff["setupB"])
        ALLK = list(dict.fromkeys(akeys["main"] + akeys["setup"] + akeys["setupB"]))

        pool(lambda e: e.memset(ones_f, 1.0), [], ["ones_f"])
        pool(lambda e: e.memset(dummy[:], 0.0), [], ["dummy"])
        pool(lambda e: e.affine_select(out=ident_f[:], in_=ones_f, pattern=[[-1, 128]], compare_op=ALU.is_equal, fill=0.0,
                                       base=0, channel_multiplier=1), ["ones_f"], ["ident_f"])
        pool(lambda e: e.affine_select(out=ident_b[:], in_=ones_f, pattern=[[-1, 128]], compare_op=ALU.is_equal, fill=0.0,
                                       base=0, channel_multiplier=1), ["ones_f"], ["ident_b"])
        for j in range(4):
            pool(lambda e: e.affine_select(out=mask_cur[:, j, :], in_=ones_f, pattern=[[1, 128]], compare_op=ALU.is_ge, fill=0.0,
                                           base=0, channel_multiplier=-1), ["ones_f"], ["mask_cur"])
            pool(lambda e: e.affine_select(out=mask_prev[:, j, :], in_=ones_f, pattern=[[-1, 128]], compare_op=ALU.is_ge, fill=0.0,
                                           base=-1, channel_multiplier=1), ["ones_f"], ["mask_prev"])
        pool(lambda e: e.tensor_tensor(out=dmask[:], in0=ident_f[:, 0:32], in1=ident_f[:, 32:64], op=ALU.add), ["ident_f"], ["dmask"])
        pool(lambda e: e.tensor_tensor(out=dmask[:], in0=dmask[:], in1=ident_f[:, 64:96], op=ALU.add), ["ident_f", "dmask"], ["dmask"])
        pool(lambda e: e.tensor_tensor(out=dmask[:], in0=dmask[:], in1=ident_f[:, 96:128], op=ALU.add), ["ident_f", "dmask"], ["dmask"])
        pool(lambda e: e.iota(nvec[:], pattern=[[1, 9]], base=0, channel_multiplier=0, allow_small_or_imprecise_dtypes=True), [], ["nvec"])
        pool(lambda e: e.iota(kvec[:], pattern=[[1, NCH]], base=0, channel_multiplier=0, allow_small_or_imprecise_dtypes=True), [], ["kvec"])
        pool(lambda e: e.memset(mv_aug[:], 1.0), [], ["mv_aug"])
        pool(lambda e: e.memset(Kstrip[:], 0.0), [], ["Kstrip"])

        for bb in range(0, NB, 32):
            nbk = min(32, NB - bb)
            T.dma(s_posi[0:nbk, :], pos_d[bb * 128:(bb + nbk) * 128].rearrange("(b p) -> b p", p=128), w=["s_posi"])
            dve(lambda e: e.tensor_copy(out=s_posf[0:nbk, :], in_=s_posi[0:nbk, :]), ["s_posi"], ["s_posf"])
            tr(ps_z[:, 0:nbk], s_posf[0:nbk, :], ident_f[0:nbk, 0:nbk], ["s_posf", "ident_f"], ["ps_z"])
            dve(lambda e: e.tensor_copy(out=s_posT[:, 0:nbk], in_=ps_z[:, 0:nbk]), ["ps_z"], ["s_posT"])
        pool(lambda e: e.iota(s_inv, pattern=[[1, 32]], base=0, channel_multiplier=0, allow_small_or_imprecise_dtypes=True), [], ["s_inv"])
        act(lambda e: e.activation(out=s_inv, in_=s_inv, func=AF.Exp, scale=float(-math.log(10000.0) / 32.0)), ["s_inv"], ["s_inv"])
        for b0 in range(0, NB, 8):
            nb8 = min(8, NB - b0)
            ang3 = s_ang[:, 0:nb8 * 32].rearrange("p (b j) -> p b j", j=32)
            dve(lambda e: e.tensor_tensor(out=ang3, in0=s_posT[:, b0:b0 + nb8, None].to_broadcast([128, nb8, 32]),
                                          in1=s_inv[:, None, :].to_broadcast([128, nb8, 32]), op=ALU.mult), ["s_posT", "s_inv"], ["s_ang"])
            sin_of(sinT[:, b0:b0 + nb8, :].rearrange("p b j -> p (b j)"), s_ang[:, 0:nb8 * 32], nb8 * 32, 0.0, ["s_ang"], ["sinT"])
            sin_of(cosT[:, b0:b0 + nb8, :].rearrange("p b j -> p (b j)"), s_ang[:, 0:nb8 * 32], nb8 * 32, math.pi / 2, ["s_ang"], ["cosT"])

        def mark(name):
            if stop == name:
                raise StopBuild()

        def layer_setup(l):
            join(ALLK)
            load_T(g_col[:], norm_g_d[l].rearrange("(k p) -> k p", p=128), 8, "g_col")
            load_T(gm_col[:], mng_d[l].rearrange("(k p) -> k p", p=128), 8, "gm_col")
            load_T(bglu_col[:], bglu_d[l].rearrange("(k p) -> k p", p=128), 4, "bglu_col")
            load_T(d_col[:], dskip_d[l].rearrange("(k p) -> k p", p=128), 4, "d_col")
            load_T(gxq_col[:], xqg_d[l].rearrange("(k p) -> k p", p=128), 1, "gxq_col")
            dve(lambda e: e.tensor_scalar(out=gxq_col[:], in0=gxq_col[:], scalar1=float(1.0 / math.sqrt(128.0)), scalar2=None, op0=ALU.mult),
                ["gxq_col"], ["gxq_col"])
            T.dma(gq_bc[:], qg_d[l].partition_broadcast(128), w=["gq_bc"])
            T.dma(gk_bc[:], kg_d[l].partition_broadcast(128), w=["gk_bc"])
            T.dma(gxk_bc[:], xkg_d[l].partition_broadcast(128), w=["gxk_bc"])
            T.dma(esink[:], sinks_d[l].partition_broadcast(128), w=["esink"])
            act(lambda e: e.activation(out=esink[:], in_=esink[:], func=AF.Exp), ["esink"], ["esink"])
            dve(lambda e: e.tensor_copy(out=g10[:, 0:8, :], in_=gq_bc[:, None, :].to_broadcast([128, 8, 64])), ["gq_bc"], ["g10"])
            dve(lambda e: e.tensor_copy(out=g10[:, 8:10, :], in_=gk_bc[:, None, :].to_broadcast([128, 2, 64])), ["gk_bc", "g10"], ["g10"])

            mark('vecs')
            for mt in range(2):
                T.dma(s_memx, mem_d[mt * 128:(mt + 1) * 128, :], w=["s_memx"])
                act(lambda e: e.activation(out=s_memn, in_=s_memx, func=AF.Square, accum_out=s_smA[:, 0:1]), ["s_memx"], ["s_memn", "s_smA"])
                rsqrt_of(s_smA[:, 1:2], s_smA[:, 0:1], 1.0 / D, ["s_smA"], ["s_smA"])
                act(lambda e: e.activation(out=s_memn, in_=s_memx, func=AF.Copy, scale=s_smA[:, 1:2]), ["s_memx", "s_smA"], ["s_memn"])
                for kc in range(8):
                    tr(ps_t[:, kc, :], s_memn[:, kc * 128:(kc + 1) * 128], ident_b[:], ["s_memn", "ident_b"], ["ps_t"], sig=(kc == 7))
                dve(lambda e: e.tensor_copy(out=s_memT[:, :, mt * 128:(mt + 1) * 128], in_=ps_t[:]), ["ps_t"], ["s_memT"])
            Wkv = Wsb[:, :, 0:D]
            for kc in range(8):
                sg_ = stage[kc % 4]
                T.dma(sg_, wkv_d[l, kc * 128:(kc + 1) * 128, :], w=["stage%d" % (kc % 4)])
                eng = "act" if kc % 2 == 0 else "pool"
                if eng == "act":
                    act(lambda e: e.activation(out=Wkv[:, kc, :], in_=sg_, func=AF.Copy, scale=gm_col[:, kc:kc + 1]),
                        ["stage%d" % (kc % 4), "gm_col"], ["Wsb"])
                else:
                    dve(lambda e: e.tensor_scalar(out=Wkv[:, kc, :], in0=sg_, scalar1=gm_col[:, kc:kc + 1], scalar2=None, op0=ALU.mult),
                         ["stage%d" % (kc % 4), "gm_col"], ["Wsb"])
            for mt in range(2):
                for n in range(2):
                    for kc in range(8):
                        mm(ps_z[:], s_memT[:, kc, mt * 128:(mt + 1) * 128], Wkv[:, kc, n * 512:(n + 1) * 512], kc == 0, kc == 7,
                           ["s_memT", "Wsb"], ["ps_z"])
                    if n == 0:
                        act(lambda e: e.copy(out=s_mkf.rearrange("p h d -> p (h d)"), in_=ps_z[:]), ["ps_z"], ["s_mkf"])
                        dve(lambda e: e.tensor_tensor(out=s_memx[:, 0:512], in0=s_mkf.rearrange("p h d -> p (h d)"),
                                                      in1=s_mkf.rearrange("p h d -> p (h d)"), op=ALU.mult), ["s_mkf"], ["s_memx"])
                        dve(lambda e: e.tensor_reduce(out=s_smA[:, 4:8], in_=s_memx[:, 0:512].rearrange("p (h d) -> p h d", h=4),
                                                      axis=AX.X, op=ALU.add), ["s_memx"], ["s_smA"])
                        rsqrt_of(s_smA[:, 8:12], s_smA[:, 4:8], 1.0 / 128, ["s_smA"], ["s_smA"])
                        dve(lambda e: e.tensor_tensor(out=s_mkf, in0=s_mkf, in1=s_smA[:, 8:12, None].to_broadcast([128, 4, 128]), op=ALU.mult),
                            ["s_mkf", "s_smA"], ["s_mkf"])
                        dve(lambda e: e.tensor_tensor(out=s_mkb, in0=s_mkf, in1=gxk_bc[:, None, :].to_broadcast([128, 4, 128]), op=ALU.mult),
                            ["s_mkf", "gxk_bc"], ["s_mkb"])
                        for h in range(4):
                            tr(ps_t[:, h, :], s_mkb[:, h, :], ident_b[:], ["s_mkb", "ident_b"], ["ps_t"], sig=(h == 3))
                        dve(lambda e: e.tensor_scalar(out=mkT[:, :, mt * 128:(mt + 1) * 128], in0=ps_t[:, 0:4, :], scalar1=gxq_col[:, 0:1],
                                                      scalar2=None, op0=ALU.mult), ["ps_t", "gxq_col"], ["mkT"])
                    else:
                        act(lambda e: e.copy(out=mv_aug[:, mt, :, 0:128], in_=ps_z[:].rearrange("p (h d) -> p h d", h=4)), ["ps_z"], ["mv_aug"])

            mark('memkv')
            cnt = 0
            for kc in range(8):
                for (c0, c1) in ((0, 1024), (1024, 2048), (2048, 3072), (3072, INW)):
                    sg_ = stage[cnt % 4]; sk = "stage%d" % (cnt % 4)
                    T.dma(sg_[:, 0:c1 - c0], w_in_d[l, kc * 128:(kc + 1) * 128, c0:c1], w=[sk])
                    if cnt % 2 == 0:
                        act(lambda e: e.activation(out=Wsb[:, kc, c0:c1], in_=sg_[:, 0:c1 - c0], func=AF.Copy, scale=g_col[:, kc:kc + 1]),
                            [sk, "g_col"], ["Wsb"])
                    else:
                        dve(lambda e: e.tensor_scalar(out=Wsb[:, kc, c0:c1], in0=sg_[:, 0:c1 - c0], scalar1=g_col[:, kc:kc + 1], scalar2=None,
                                                       op0=ALU.mult), [sk, "g_col"], ["Wsb"])
                    cnt += 1
            for kc in range(12):
                sg_ = stage[cnt % 4]; sk = "stage%d" % (cnt % 4)
                T.dma(sg_, wout_d[l, kc * 128:(kc + 1) * 128, :], w=[sk])
                if cnt % 2 == 0:
                    act(lambda e: e.copy(out=Wout[:, kc, :], in_=sg_), [sk], ["Wout"])
                else:
                    dve(lambda e: e.tensor_copy(out=Wout[:, kc, :], in_=sg_), [sk], ["Wout"])
                cnt += 1
            for kc in range(4):
                sg_ = stage[cnt % 4]; sk = "stage%d" % (cnt % 4)
                T.dma(sg_[:, 0:512], wglu_d[l, kc * 128:(kc + 1) * 128, :], w=[sk])
                dve(lambda e: e.tensor_copy(out=Wglu[:, kc, :], in_=sg_[:, 0:512]), [sk], ["Wglu"])
                cnt += 1

            mark('weights')
            join(ALLK)
            lr = s_sm[:, :, 2]; li = s_sm[:, :, 3]; dtv = s_sm[:, :, 4]; lrdt = s_sm[:, :, 5]; lidt = s_sm[:, :, 6]
            t7 = s_sm[:, :, 7]; t8 = s_sm[:, :, 8]; fr = s_sm[:, :, 9]; fi = s_sm[:, :, 10]; t11 = s_sm[:, :, 11]
            load_T(lr, lre_d[l].rearrange("(pr two) p -> pr (two p)", two=2), 16, "s_sm")
            load_T(li, lim_d[l].rearrange("(pr two) p -> pr (two p)", two=2), 16, "s_sm")
            T.dma(s_lg2[0:16, :], ldt_d[l].rearrange("(pr two) -> pr two", two=2), w=["s_lg2"])
            dve(lambda e: e.tensor_copy(out=ldT[0:16, :].rearrange("q (two p) -> q two p", two=2),
                                        in_=s_lg2[0:16, :, None].to_broadcast([16, 2, 64])), ["s_lg2"], ["ldT"])
            tr(ps_z[:, 0:16], ldT[0:16, :], ident_f[0:16, 0:16], ["ldT", "ident_f"], ["ps_z"])
            act(lambda e: e.activation(out=dtv, in_=ps_z[:, 0:16], func=AF.Exp), ["ps_z"], ["s_sm"])
            dve(lambda e: e.tensor_tensor(out=lrdt, in0=lr, in1=dtv, op=ALU.mult), ["s_sm"], ["s_sm"])
            dve(lambda e: e.tensor_tensor(out=lidt, in0=li, in1=dtv, op=ALU.mult), ["s_sm"], ["s_sm"])
            dve(lambda e: e.tensor_tensor(out=s_e9, in0=lrdt[:, :, None].to_broadcast([128, 16, 9]), in1=nvec[:, None, :].to_broadcast([128, 16, 9]),
                                          op=ALU.mult), ["s_sm", "nvec"], ["s_e9"])
            dve(lambda e: e.tensor_tensor(out=s_a9, in0=lidt[:, :, None].to_broadcast([128, 16, 9]), in1=nvec[:, None, :].to_broadcast([128, 16, 9]),
                                          op=ALU.mult), ["s_sm", "nvec"], ["s_a9"])
            act(lambda e: e.activation(out=s_mag9, in_=s_e9, func=AF.Exp), ["s_e9"], ["s_mag9"])
            f2 = lambda v: v.rearrange("p a b -> p (a b)")
            sin_of(f2(s_sin9), f2(s_a9), 144, 0.0, ["s_a9"], ["s_sin9"])
            sin_of(f2(s_cos9), f2(s_a9), 144, math.pi / 2, ["s_a9"], ["s_cos9"])
            dve(lambda e: e.tensor_tensor(out=s_ar9, in0=s_mag9, in1=s_cos9, op=ALU.mult), ["s_mag9", "s_cos9"], ["s_ar9"])
            dve(lambda e: e.tensor_tensor(out=s_ai9, in0=s_mag9, in1=s_sin9, op=ALU.mult), ["s_mag9", "s_sin9"], ["s_ai9"])
            dve(lambda e: e.tensor_copy(out=magA[:], in_=s_mag9[:, :, 8]), ["s_mag9"], ["magA"])
            tmp16 = (s_ki[:, 0:16], s_kf[:, 0:16])
            dve(lambda e: e.tensor_scalar(out=tmp16[0], in0=lidt, scalar1=float(8.0 / TWO_PI), scalar2=None, op0=ALU.mult), ["s_sm"], ["sr_ki"])
            dve(lambda e: e.tensor_copy(out=tmp16[1], in_=tmp16[0]), ["sr_ki"], ["sr_kf"])
            dve(lambda e: e.tensor_scalar(out=t7, in0=lidt, scalar1=8.0, scalar2=None, op0=ALU.mult), ["s_sm"], ["s_sm"])
            dve(lambda e: e.scalar_tensor_tensor(out=t7, in0=tmp16[1], scalar=float(-CW1), in1=t7, op0=ALU.mult, op1=ALU.add), ["sr_kf", "s_sm"], ["s_sm"])
            dve(lambda e: e.scalar_tensor_tensor(out=t7, in0=tmp16[1], scalar=float(-CW2), in1=t7, op0=ALU.mult, op1=ALU.add), ["sr_kf", "s_sm"], ["s_sm"])
            for p0 in range(0, 16, 4):
                ta3 = s_ang[:, 0:4 * NCH].rearrange("p (a k) -> p a k", k=NCH)
                dve(lambda e: e.tensor_tensor(out=ta3, in0=t7[:, p0:p0 + 4, None].to_broadcast([128, 4, NCH]),
                                              in1=kvec[:, None, :].to_broadcast([128, 4, NCH]), op=ALU.mult), ["s_sm", "kvec"], ["s_ang"])
                sin_of(msT[:, p0:p0 + 4, :].rearrange("p a k -> p (a k)"), s_ang[:, 0:4 * NCH], 4 * NCH, 0.0, ["s_ang"], ["msT"])
                sin_of(mcT[:, p0:p0 + 4, :].rearrange("p a k -> p (a k)"), s_ang[:, 0:4 * NCH], 4 * NCH, math.pi / 2, ["s_ang"], ["mcT"])
            dve(lambda e: e.tensor_tensor(out=mcA[:], in0=mcT[:, :, 1], in1=magA[:], op=ALU.mult), ["mcT", "magA"], ["mcA"])
            dve(lambda e: e.tensor_tensor(out=msA[:], in0=msT[:, :, 1], in1=magA[:], op=ALU.mult), ["msT", "magA"], ["msA"])
            dve(lambda e: e.tensor_scalar(out=nmsA[:], in0=msA[:], scalar1=-1.0, scalar2=None, op0=ALU.mult), ["msA"], ["nmsA"])
            dve(lambda e: e.tensor_copy(out=magTab[:], in_=magA[:, :, None].to_broadcast([128, 16, NCH])), ["magA"], ["magTab"])
            dve(lambda e: e.memset(magTab[:, :, 0:1], 0.0), ["magTab"], ["magTab"])
            dve(lambda e: e.tensor_tensor(out=t8, in0=lr, in1=lr, op=ALU.mult), ["s_sm"], ["s_sm"])
            dve(lambda e: e.tensor_tensor(out=t11, in0=li, in1=li, op=ALU.mult), ["s_sm"], ["s_sm"])
            dve(lambda e: e.tensor_tensor(out=t8, in0=t8, in1=t11, op=ALU.add), ["s_sm"], ["s_sm"])
            dve(lambda e: e.reciprocal(out=t8, in_=t8), ["s_sm"], ["s_sm"])
            dve(lambda e: e.tensor_scalar(out=t11, in0=s_ar9[:, :, 1], scalar1=-1.0, scalar2=None, op0=ALU.add), ["s_ar9"], ["s_sm"])
            dve(lambda e: e.tensor_tensor(out=fr, in0=t11, in1=lr, op=ALU.mult), ["s_sm"], ["s_sm"])
            dve(lambda e: e.tensor_tensor(out=t7, in0=s_ai9[:, :, 1], in1=li, op=ALU.mult), ["s_sm", "s_ai9"], ["s_sm"])
            dve(lambda e: e.tensor_tensor(out=fr, in0=fr, in1=t7, op=ALU.add), ["s_sm"], ["s_sm"])
            dve(lambda e: e.tensor_tensor(out=fr, in0=fr, in1=t8, op=ALU.mult), ["s_sm"], ["s_sm"])
            dve(lambda e: e.tensor_tensor(out=fi, in0=s_ai9[:, :, 1], in1=lr, op=ALU.mult), ["s_sm", "s_ai9"], ["s_sm"])
            dve(lambda e: e.tensor_tensor(out=t7, in0=t11, in1=li, op=ALU.mult), ["s_sm"], ["s_sm"])
            dve(lambda e: e.tensor_tensor(out=fi, in0=fi, in1=t7, op=ALU.subtract), ["s_sm"], ["s_sm"])
            dve(lambda e: e.tensor_tensor(out=fi, in0=fi, in1=t8, op=ALU.mult), ["s_sm"], ["s_sm"])
            mark('ssm_tabs')
            T.dma(s_Br, bre_d[l].rearrange("(pr two) p c -> (two p) pr c", two=2), w=["s_Br"])
            T.dma(s_Bi, bim_d[l].rearrange("(pr two) p c -> (two p) pr c", two=2), w=["s_Bi"])
            frb = fr[:, :, None].to_broadcast([128, 16, 16]); fib = fi[:, :, None].to_broadcast([128, 16, 16])
            dve(lambda e: e.tensor_tensor(out=s_bbr, in0=s_Br, in1=frb, op=ALU.mult), ["s_Br", "s_sm"], ["s_bbr"])
            dve(lambda e: e.tensor_tensor(out=s_bt, in0=s_Bi, in1=fib, op=ALU.mult), ["s_Bi", "s_sm"], ["s_bt"])
            dve(lambda e: e.tensor_tensor(out=s_bbr, in0=s_bbr, in1=s_bt, op=ALU.subtract), ["s_bbr", "s_bt"], ["s_bbr"])
            dve(lambda e: e.tensor_tensor(out=s_bbi, in0=s_Bi, in1=frb, op=ALU.mult), ["s_Bi", "s_sm"], ["s_bbi"])
            dve(lambda e: e.tensor_tensor(out=s_bt, in0=s_Br, in1=fib, op=ALU.mult), ["s_Br", "s_sm", "s_bbr"], ["s_bt"])
            dve(lambda e: e.tensor_tensor(out=s_bbi, in0=s_bbi, in1=s_bt, op=ALU.add), ["s_bbi", "s_bt"], ["s_bbi"])
            for (cd, dst, dk) in ((cre_d, s_Cr, "s_Cr"), (cim_d, s_Ci, "s_Ci")):
                for pr in range(16):
                    T.dma(s_Cst[(pr % 8) * 16:(pr % 8) * 16 + 16, pr // 8, :].rearrange("c (two p) -> c two p", two=2),
                          cd[l, 2 * pr:2 * pr + 2].rearrange("two c p -> c two p"), w=["s_Cst"])
                for hh in range(2):
                    tr(ps_z[:, hh * 128:(hh + 1) * 128], s_Cst[:, hh, :], ident_f[:], ["s_Cst", "ident_f"], ["ps_z"], sig=(hh == 1))
                dve(lambda e: e.tensor_copy(out=dst.rearrange("p a c -> p (a c)"), in_=ps_z[:, 0:256]), ["ps_z"], [dk])
            mark('ssm_BC')
            for q in range(4):
                prs = slice(4 * q, 4 * q + 4)
                arb = s_ar9[:, prs, :, None].to_broadcast([128, 4, 9, 16]); aib = s_ai9[:, prs, :, None].to_broadcast([128, 4, 9, 16])
                crb = s_Cr[:, prs, None, :].to_broadcast([128, 4, 9, 16]); cib = s_Ci[:, prs, None, :].to_broadcast([128, 4, 9, 16])
                dve(lambda e: e.tensor_tensor(out=s_CAr, in0=crb, in1=arb, op=ALU.mult), ["s_Cr", "s_ar9"], ["s_CAr"])
                dve(lambda e: e.tensor_tensor(out=s_CAt, in0=cib, in1=aib, op=ALU.mult), ["s_Ci", "s_ai9"], ["s_CAt"])
                dve(lambda e: e.tensor_tensor(out=s_CAr, in0=s_CAr, in1=s_CAt, op=ALU.subtract), ["s_CAr", "s_CAt"], ["s_CAr"])
                dve(lambda e: e.tensor_tensor(out=s_CAi, in0=crb, in1=aib, op=ALU.mult), ["s_Cr", "s_ai9"], ["s_CAi"])
                dve(lambda e: e.tensor_tensor(out=s_CAt, in0=cib, in1=arb, op=ALU.mult), ["s_Ci", "s_ar9", "s_CAr"], ["s_CAt"])
                dve(lambda e: e.tensor_tensor(out=s_CAi, in0=s_CAi, in1=s_CAt, op=ALU.add), ["s_CAi", "s_CAt"], ["s_CAi"])
                pool(lambda e: e.memset(Vf[:, prs, :, :], 0.0), [], ["Vf"])
                for two in range(2):
                    rows = slice(64 * two, 64 * two + 64)
                    dve(lambda e: e.tensor_copy(out=Vf[rows, prs, 0, two * 128:(two + 1) * 128].rearrange("p a (t c) -> p a t c", c=16),
                                                in_=s_CAr[rows, :, 1:9, :]), ["s_CAr", "Vf"], ["Vf"])
                    dve(lambda e: e.tensor_scalar(out=Vf[rows, prs, 1, two * 128:(two + 1) * 128].rearrange("p a (t c) -> p a t c", c=16),
                                                  in0=s_CAi[rows, :, 1:9, :], scalar1=-1.0, scalar2=None, op0=ALU.mult), ["s_CAi", "Vf"], ["Vf"])
                pool(lambda e: e.memset(s_CAzr, 0.0), [], ["s_CAzr"])
                pool(lambda e: e.memset(s_CAzi, 0.0), [], ["s_CAzi"])
                pool(lambda e: e.memset(s_Bzr, 0.0), [], ["s_Bzr"])
                pool(lambda e: e.memset(s_Bzi, 0.0), [], ["s_Bzi"])
                for two in range(2):
                    rows = slice(64 * two, 64 * two + 64)
                    dve(lambda e: e.tensor_copy(out=s_CAzr[rows, :, two, :, :], in_=s_CAr[rows, :, 0:8, :]), ["s_CAr", "s_CAzr"], ["s_CAzr"])
                    dve(lambda e: e.tensor_scalar(out=s_CAzi[rows, :, two, :, :], in0=s_CAi[rows, :, 0:8, :], scalar1=-1.0, scalar2=None, op0=ALU.mult),
                        ["s_CAi", "s_CAzi"], ["s_CAzi"])
                    for j in range(4):
                        c0 = 32 * j + 16 * two
                        dve(lambda e: e.tensor_copy(out=s_Bzr[rows, j, c0:c0 + 16], in_=s_bbr[rows, 4 * q + j, :]), ["s_bbr", "s_Bzr"], ["s_Bzr"])
                        dve(lambda e: e.tensor_copy(out=s_Bzi[rows, j, c0:c0 + 16], in_=s_bbi[rows, 4 * q + j, :]), ["s_bbi", "s_Bzi"], ["s_Bzi"])
                for j in range(4):
                    mm(ps_z[:, 0:256], s_Bzr[:, j, :], s_CAzr[:, j].rearrange("p a t c -> p (a t c)"), j == 0, False, ["s_Bzr", "s_CAzr"], ["ps_z"])
                    mm(ps_z[:, 0:256], s_Bzi[:, j, :], s_CAzi[:, j].rearrange("p a t c -> p (a t c)"), False, j == 3, ["s_Bzi", "s_CAzi"], ["ps_z"])
                dve(lambda e: e.tensor_copy(out=s_Kt.rearrange("p a t c -> p (a t c)"), in_=ps_z[:, 0:256]), ["ps_z"], ["s_Kt"])
                dve(lambda e: e.tensor_scalar(out=s_Kd, in0=dmask[:], scalar1=d_col[:, q:q + 1], scalar2=None, op0=ALU.mult), ["dmask", "d_col"], ["s_Kd"])
                dve(lambda e: e.tensor_tensor(out=s_Kt[:, :, 0, :], in0=s_Kt[:, :, 0, :], in1=s_Kd.rearrange("p (a c) -> p a c", a=2), op=ALU.add),
                    ["s_Kt", "s_Kd"], ["s_Kt"])
                dve(lambda e: e.tensor_copy(out=Kstrip[:, q, :, 7:15, :], in_=s_Kt), ["s_Kt"], ["Kstrip"])
                brb = s_bbr[:, prs, None, :].to_broadcast([128, 4, 8, 16]); bib = s_bbi[:, prs, None, :].to_broadcast([128, 4, 8, 16])
                ar8 = s_ar9[:, prs, 0:8, None].to_broadcast([128, 4, 8, 16]); ai8 = s_ai9[:, prs, 0:8, None].to_broadcast([128, 4, 8, 16])
                dve(lambda e: e.tensor_tensor(out=s_Xr, in0=brb, in1=ar8, op=ALU.mult), ["s_bbr", "s_ar9"], ["s_Xr"])
                dve(lambda e: e.tensor_tensor(out=s_Xt, in0=bib, in1=ai8, op=ALU.mult), ["s_bbi", "s_ai9"], ["s_Xt"])
                dve(lambda e: e.tensor_tensor(out=s_Xr, in0=s_Xr, in1=s_Xt, op=ALU.subtract), ["s_Xr", "s_Xt"], ["s_Xr"])
                dve(lambda e: e.tensor_tensor(out=s_Xi, in0=bib, in1=ar8, op=ALU.mult), ["s_bbi", "s_ar9"], ["s_Xi"])
                dve(lambda e: e.tensor_tensor(out=s_Xt, in0=brb, in1=ai8, op=ALU.mult), ["s_bbr", "s_ai9", "s_Xr"], ["s_Xt"])
                dve(lambda e: e.tensor_tensor(out=s_Xi, in0=s_Xi, in1=s_Xt, op=ALU.add), ["s_Xi", "s_Xt"], ["s_Xi"])
                pool(lambda e: e.memset(s_Xzr, 0.0), [], ["s_Xzr"])
                pool(lambda e: e.memset(s_Xzi, 0.0), [], ["s_Xzi"])
                for two in range(2):
                    rows = slice(64 * two, 64 * two + 64)
                    dve(lambda e: e.tensor_copy(out=s_Xzr[rows, :, :, 16 * two:16 * two + 16].rearrange("p n a c -> p a n c"), in_=s_Xr[rows]), ["s_Xr", "s_Xzr"], ["s_Xzr"])
                    dve(lambda e: e.tensor_copy(out=s_Xzi[rows, :, :, 16 * two:16 * two + 16].rearrange("p n a c -> p a n c"), in_=s_Xi[rows]), ["s_Xi", "s_Xzi"], ["s_Xzi"])
                for hh in range(2):
                    for s4 in range(4):
                        s_ = 4 * hh + s4
                        for ri, xz in ((0, s_Xzr), (1, s_Xzi)):
                            tr(ps_t[:, 2 * s4 + ri, :], xz[:, 7 - s_, :, :].rearrange("p a c -> p (a c)"), ident_b[:], ["s_Xzr", "s_Xzi", "ident_b"], ["ps_t"],
                               sig=(s4 == 3 and ri == 1))
                    dve(lambda e: e.tensor_copy(out=Wssm[:, q, 4 * hh:4 * hh + 4, :, :].rearrange("p s r m -> p (s r) m"), in_=ps_t[:]),
                        ["ps_t"], ["Wssm"])
            pool(lambda e: e.memset(carry[:], 0.0), [], ["carry"])
            pool(lambda e: e.memset(Hb[:], 0.0), [], ["Hb"])
            join(ALLK)
            pool(lambda e: e.memset(vaug[0][:, :, 64:65], 1.0), [], ["vaug0"])
            pool(lambda e: e.memset(vaug[1][:, :, 64:65], 1.0), [], ["vaug1"])

        def block_front(l, b, src):
            slot = b % 2
            bl = b % BPS
            cols = slice(bl * 128, (bl + 1) * 128)
            xk, kTk, vk, pk = "xt%d" % slot, "kT%d" % slot, "vaug%d" % slot, None
            x_t = xt[slot]
            T.dma(x_t, src[b * 128:(b + 1) * 128, :], r=[("res", b)], w=[xk])
            act(lambda e: e.activation(out=junk, in_=x_t, func=AF.Square, accum_out=st1[:, 0:1]), [xk], ["junk", "st1"])
            rsqrt_of(st1[:, 1:2], st1[:, 0:1], 1.0 / D, ["st1"], ["st1"])
            act(lambda e: e.activation(out=hb, in_=x_t, func=AF.Copy, scale=st1[:, 1:2]), [xk, "st1"], ["hb"])
            for kc in range(8):
                tr(ps_t[:, kc, :], hb[:, kc * 128:(kc + 1) * 128], ident_b[:], ["hb", "ident_b"], ["ps_t"], sig=(kc == 7))
            dve(lambda e: e.tensor_copy(out=hTs[slot], in_=ps_t[:]), ["ps_t"], ["hT%d" % slot])


        def block_rest(l, b):
            slot = b % 2
            bl = b % BPS
            cols = slice(bl * 128, (bl + 1) * 128)
            kTk, vk = "kT%d" % slot, "vaug%d" % slot
            hT = hTs[slot]
            hTk = "hT%d" % slot

            def zgroup(c0, n):
                for kc in range(8):
                    mm(ps_z[:, 0:n], hT[:, kc, :], Wsb[:, kc, c0:c0 + n], kc == 0, kc == 7, [hTk, "Wsb"], ["ps_z"])

            zgroup(C_AQ, 512)
            act(lambda e: e.copy(out=qk[:, 0:8, :].rearrange("p h d -> p (h d)"), in_=ps_z[:]), ["ps_z"], ["qk"])
            zgroup(C_AK, 256)
            act(lambda e: e.copy(out=qk[:, 8:10, :].rearrange("p h d -> p (h d)"), in_=ps_z[:, 0:128]), ["ps_z", "qk"], ["qk"])
            act(lambda e: e.copy(out=vaug[slot][:, :, 0:64], in_=ps_z[:, 128:256].rearrange("p (h d) -> p h d", h=2)), ["ps_z"], [vk])
            zgroup(C_AG, 512)
            act(lambda e: e.activation(out=gate_a, in_=ps_z[:], func=AF.Silu), ["ps_z"], ["gate_a"])
            for (c0, dst, dk, fn) in ((C_SU, uT, "uT", None), (C_SG, gT, "gT", AF.Silu)):
                for ct in range(4):
                    for kc in range(8):
                        mm(ps_z[:, ct * 128:(ct + 1) * 128], Wsb[:, kc, c0 + ct * 128:c0 + (ct + 1) * 128], hT[:, kc, :], kc == 0, kc == 7,
                           [hTk, "Wsb"], ["ps_z"], sig=(ct == 3 and kc == 7))
                pz3 = ps_z[:].rearrange("p (c t) -> p c t", c=4)
                if fn is None:
                    act(lambda e: e.copy(out=dst[:, :, cols], in_=pz3), ["ps_z"], [dk])
                else:
                    act(lambda e: e.activation(out=dst[:, :, cols], in_=pz3, func=fn), ["ps_z"], [dk])

            if b == 0: mark('bm_z')
            dve(lambda e: e.tensor_tensor(out=sq, in0=qk, in1=qk, op=ALU.mult), ["qk"], ["sq"])
            dve(lambda e: e.tensor_reduce(out=st1[:, 2:12], in_=sq, axis=AX.X, op=ALU.add), ["sq"], ["st1"])
            if b == 0: mark("r1")
            rsqrt_of(st1[:, 2:12], st1[:, 2:12], 1.0 / 64, ["st1"], ["st1"])
            if b == 0: mark("r2")
            dve(lambda e: e.tensor_tensor(out=qk, in0=qk, in1=st1[:, 2:12, None].to_broadcast([128, 10, 64]), op=ALU.mult), ["qk", "st1"], ["qk"])
            dve(lambda e: e.tensor_tensor(out=qk, in0=qk, in1=g10[:], op=ALU.mult), ["qk", "g10"], ["qk"])
            if b == 0: mark("r3")
            qk4 = qk.rearrange("p h (a j) -> p h a j", a=2)
            r14 = rt1.rearrange("p h (a j) -> p h a j", a=2)
            r24 = rt2.rearrange("p h (a j) -> p h a j", a=2)
            qr4 = qr.rearrange("p h (a j) -> p h a j", a=2)
            cb = cosT[:, b, None, None, :].to_broadcast([128, 10, 2, 32])
            sb_ = sinT[:, b, None, :].to_broadcast([128, 10, 32])
            dve(lambda e: e.tensor_tensor(out=r14, in0=qk4, in1=cb, op=ALU.mult), ["qk", "cosT"], ["rt1"])
            if b == 0: mark("r4")
            dve(lambda e: e.tensor_tensor(out=r24[:, :, 0, :], in0=qk4[:, :, 1, :], in1=sb_, op=ALU.mult), ["qk", "sinT"], ["rt2"])
            dve(lambda e: e.tensor_tensor(out=r24[:, :, 1, :], in0=qk4[:, :, 0, :], in1=sb_, op=ALU.mult), ["qk", "sinT", "rt2"], ["rt2"])
            if b == 0: mark("r5")
            dve(lambda e: e.tensor_tensor(out=qr4[:, :, 0, :], in0=r14[:, :, 0, :], in1=r24[:, :, 0, :], op=ALU.subtract), ["rt1", "rt2"], ["qr"])
            dve(lambda e: e.tensor_tensor(out=qr4[:, :, 1, :], in0=r14[:, :, 1, :], in1=r24[:, :, 1, :], op=ALU.add), ["rt1", "rt2", "qr"], ["qr"])
            if b == 0: mark("r6")
            for j in range(5):
                tr(ps_t[:, j, :], qr[:, 2 * j:2 * j + 2, :].rearrange("p h d -> p (h d)"), ident_b[:], ["qr", "ident_b"], ["ps_t"], sig=(j == 4))
            if b == 0: mark("r7")
            dve(lambda e: e.tensor_copy(out=qT, in_=ps_t[:, 0:4, :]), ["ps_t"], ["qT"])
            dve(lambda e: e.tensor_copy(out=kT[slot], in_=ps_t[:, 4, :]), ["ps_t"], [kTk])
            if b == 0: mark('bm_rope')
            tiles = ([] if b == 0 else [(1 - slot, mask_prev, "mask_prev")]) + [(slot, mask_cur, "mask_cur")]
            for kvh in range(2):
                rows = slice(64 * kvh, 64 * kvh + 64)
                for ti, (sl, mk, mkk) in enumerate(tiles):
                    mm(ps_s[ti][:], kT[sl][rows, :], qT[rows, :, :].rearrange("p j q -> p (j q)"), True, True,
                       ["kT%d" % sl, "qT"], ["ps_s%d" % ti])
                    act(lambda e: e.activation(out=pTm[ti], in_=ps_s[ti][:], func=AF.Exp, scale=0.125), ["ps_s%d" % ti], ["pTm%d" % ti])
                    dve(lambda e: e.tensor_tensor(out=pTm[ti], in0=pTm[ti], in1=mk[:].rearrange("p j q -> p (j q)"), op=ALU.mult),
                         ["pTm%d" % ti, mkk], ["pTm%d" % ti])
                for j in range(4):
                    for ti, (sl, mk, mkk) in enumerate(tiles):
                        mm(ps_f[:, j * 65:(j + 1) * 65], pTm[ti][:, j * 128:(j + 1) * 128], vaug[sl][:, kvh, :], ti == 0, ti == len(tiles) - 1,
                           ["pTm%d" % ti, "vaug%d" % sl], ["ps_f"], sig=(j == 3 and ti == len(tiles) - 1))
                o4 = ps_f[:, 0:260].rearrange("p (j d) -> p j d", d=65)
                dve(lambda e: e.tensor_tensor(out=den[:, 0:4], in0=o4[:, :, 64], in1=esink[:, 4 * kvh:4 * kvh + 4], op=ALU.add), ["ps_f", "esink"], ["den"])
                dve(lambda e: e.reciprocal(out=den[:, 0:4], in_=den[:, 0:4]), ["den"], ["den"])
                for j in range(4):
                    h = 4 * kvh + j
                    dve(lambda e: e.scalar_tensor_tensor(out=mix_a[:, h * 64:(h + 1) * 64], in0=o4[:, j, 0:64], scalar=den[:, j:j + 1],
                                                         in1=gate_a[:, h * 64:(h + 1) * 64], op0=ALU.mult, op1=ALU.mult),
                        ["ps_f", "den", "gate_a"], ["mix_a"])
            for j in range(4):
                tr(ps_t[:, j, :], mix_a[:, j * 128:(j + 1) * 128], ident_b[:], ["mix_a", "ident_b"], ["ps_t"], sig=(j == 3))
            dve(lambda e: e.tensor_copy(out=mixT[:, 0:4, cols], in_=ps_t[:, 0:4, :]), ["ps_t"], ["mixT"])
            if b == 0: mark('bm_attn')
            zgroup(C_XQ, 512)
            act(lambda e: e.copy(out=xq_f.rearrange("p h d -> p (h d)"), in_=ps_z[:]), ["ps_z"], ["xq_f"])
            zgroup(C_XG, 512)
            act(lambda e: e.activation(out=gate_x, in_=ps_z[:], func=AF.Silu), ["ps_z"], ["gate_x"])
            sq4 = sq.rearrange("p h d -> p (h d)")[:, 0:512].rearrange("p (h d) -> p h d", h=4)
            dve(lambda e: e.tensor_tensor(out=sq4, in0=xq_f, in1=xq_f, op=ALU.mult), ["xq_f"], ["sq"])
            dve(lambda e: e.tensor_reduce(out=st1[:, 12:16], in_=sq4, axis=AX.X, op=ALU.add), ["sq"], ["st1"])
            rsqrt_of(st1[:, 12:16], st1[:, 12:16], 1.0 / 128, ["st1"], ["st1"])
            dve(lambda e: e.tensor_tensor(out=xq_b, in0=xq_f, in1=st1[:, 12:16, None].to_broadcast([128, 4, 128]), op=ALU.mult), ["xq_f", "st1"], ["xq_b"])
            for h in range(4):
                tr(ps_t[:, h, :], xq_b[:, h, :], ident_b[:], ["xq_b", "ident_b"], ["ps_t"], sig=(h == 3))
            dve(lambda e: e.tensor_copy(out=xqT, in_=ps_t[:, 0:4, :]), ["ps_t"], ["xqT"])
            for mt in range(2):
                for h in range(4):
                    mm(ps_s[mt][:, h * 128:(h + 1) * 128], mkT[:, h, mt * 128:(mt + 1) * 128], xqT[:, h, :], True, True, ["mkT", "xqT"],
                       ["ps_s%d" % mt], sig=(h == 3))
                act(lambda e: e.activation(out=pX[mt], in_=ps_s[mt][:], func=AF.Exp), ["ps_s%d" % mt], ["pX%d" % mt])
            for hp in range(2):
                for i in range(2):
                    h = 2 * hp + i
                    for mt in range(2):
                        mm(ps_f[:, i * 129:(i + 1) * 129], pX[mt][:, h * 128:(h + 1) * 128], mv_aug[:, mt, h, :], mt == 0, mt == 1,
                           ["pX%d" % mt, "mv_aug"], ["ps_f"], sig=(i == 1 and mt == 1))
                o2 = ps_f[:, 0:258].rearrange("p (i d) -> p i d", d=129)
                dve(lambda e: e.reciprocal(out=den[:, 4:6], in_=o2[:, :, 128]), ["ps_f"], ["den"])
                for i in range(2):
                    h = 2 * hp + i
                    dve(lambda e: e.scalar_tensor_tensor(out=mix_c[:, h * 128:(h + 1) * 128], in0=o2[:, i, 0:128], scalar=den[:, 4 + i:5 + i],
                                                         in1=gate_x[:, h * 128:(h + 1) * 128], op0=ALU.mult, op1=ALU.mult),
                        ["ps_f", "den", "gate_x"], ["mix_c"])
            for j in range(4):
                tr(ps_t[:, j, :], mix_c[:, j * 128:(j + 1) * 128], ident_b[:], ["mix_c", "ident_b"], ["ps_t"], sig=(j == 3))
            dve(lambda e: e.tensor_copy(out=mixT[:, 8:12, cols], in_=ps_t[:, 0:4, :]), ["ps_t"], ["mixT"])

        def ssm_superblock(l, sbi):
            ps_zs4 = ps_zs[:, 0:8 * NCH].rearrange("p (a r k) -> p a r k", a=4, r=2)
            for q in range(4):
                prs = slice(4 * q, 4 * q + 4)
                u3s = []
                for prl in range(4):
                    rows = slice(32 * prl, 32 * prl + 32)
                    kw = {"tile_position": (96, 0)} if prl == 3 else {}
                    u3 = uT[rows, q, :].rearrange("p (k s) -> p s k", s=8)
                    u3s.append((rows, kw, u3))
                    for ri in range(2):
                        for s_ in range(8):
                            mm(ps_zs4[:, prl, ri, :], Wssm[rows, q, s_, ri, :], u3[:, s_, :], s_ == 0, s_ == 7, ["Wssm", "uT"], ["ps_zs"],
                               sig=(ri == 1 and s_ == 7), **kw)
                zr = ps_zs4[:, :, 0, :]; zi = ps_zs4[:, :, 1, :]
                mc = mcT[:, prs, :]; ms = msT[:, prs, :]
                dve(lambda e: e.tensor_tensor(out=sct[:, 0], in0=zr, in1=mc, op=ALU.mult), ["ps_zs", "mcT"], ["sct"])
                dve(lambda e: e.tensor_tensor(out=sct[:, 1], in0=zi, in1=ms, op=ALU.mult), ["ps_zs", "msT", "sct"], ["sct"])
                dve(lambda e: e.tensor_tensor(out=sct[:, 2], in0=zi, in1=mc, op=ALU.mult), ["ps_zs", "mcT", "sct"], ["sct"])
                dve(lambda e: e.tensor_tensor(out=sct[:, 3], in0=zr, in1=ms, op=ALU.mult), ["ps_zs", "msT", "sct"], ["sct"])
                dve(lambda e: e.tensor_tensor(out=zt[:, 0], in0=sct[:, 0], in1=sct[:, 1], op=ALU.add), ["sct"], ["zt"])
                dve(lambda e: e.tensor_tensor(out=zt[:, 1], in0=sct[:, 2], in1=sct[:, 3], op=ALU.subtract), ["sct", "zt"], ["zt"])
                if sbi > 0:
                    hr_ = carry[:, prs, 0]; hi_ = carry[:, prs, 1]
                    dve(lambda e: e.tensor_tensor(out=init4[:, 0, :], in0=hr_, in1=mcA[:, prs], op=ALU.mult), ["carry", "mcA"], ["init4"])
                    dve(lambda e: e.tensor_tensor(out=init4[:, 1, :], in0=hi_, in1=nmsA[:, prs], op=ALU.mult), ["carry", "nmsA", "init4"], ["init4"])
                    dve(lambda e: e.tensor_tensor(out=init4[:, 2, :], in0=hi_, in1=mcA[:, prs], op=ALU.mult), ["carry", "mcA", "init4"], ["init4"])
                    dve(lambda e: e.tensor_tensor(out=init4[:, 3, :], in0=hr_, in1=msA[:, prs], op=ALU.mult), ["carry", "msA", "init4"], ["init4"])
                    for ri in range(2):
                        for jj in range(2):
                            dve(lambda e: e.tensor_tensor(out=zt[:, ri, :, 0], in0=zt[:, ri, :, 0], in1=init4[:, 2 * ri + jj, :], op=ALU.add),
                                ["zt", "init4"], ["zt"])
                mg = magTab[:, prs, :].rearrange("p a k -> p (a k)")
                for ri in range(2):
                    dve(lambda e: e.tensor_tensor_scan(out=Gs[:, ri].rearrange("p a k -> p (a k)"), data0=mg,
                                                      data1=zt[:, ri].rearrange("p a k -> p (a k)"), initial=0.0, op0=ALU.mult, op1=ALU.add),
                        ["zt", "magTab", "Gs"], ["Gs"])
                f2 = lambda v: v.rearrange("p a k -> p (a k)")
                dve(lambda e: e.tensor_tensor(out=f2(sct[:, 0]), in0=f2(Gs[:, 0]), in1=f2(mc), op=ALU.mult), ["Gs", "mcT"], ["sct"])
                dve(lambda e: e.tensor_tensor(out=f2(sct[:, 1]), in0=f2(Gs[:, 1]), in1=f2(ms), op=ALU.mult), ["Gs", "msT", "sct"], ["sct"])
                dve(lambda e: e.tensor_tensor(out=f2(sct[:, 2]), in0=f2(Gs[:, 1]), in1=f2(mc), op=ALU.mult), ["Gs", "mcT", "sct"], ["sct"])
                dve(lambda e: e.tensor_tensor(out=f2(sct[:, 3]), in0=f2(Gs[:, 0]), in1=f2(ms), op=ALU.mult), ["Gs", "msT", "sct"], ["sct"])
                dve(lambda e: e.tensor_tensor(out=f2(Hf[:, 0]), in0=f2(sct[:, 0]), in1=f2(sct[:, 1]), op=ALU.subtract), ["sct"], ["Hf"])
                dve(lambda e: e.tensor_tensor(out=f2(Hf[:, 1]), in0=f2(sct[:, 2]), in1=f2(sct[:, 3]), op=ALU.add), ["sct", "Hf"], ["Hf"])
                Hfp = Hf.rearrange("p r a k -> p a r k")
                act(lambda e: e.copy(out=Hb[:, prs, :, 0:1], in_=carry[:, prs, :, None]), ["carry"], ["Hb"])
                act(lambda e: e.copy(out=Hb[:, prs, :, 1:NCH], in_=Hfp[:, :, :, 0:NCH - 1]), ["Hf", "Hb"], ["Hb"])
                act(lambda e: e.copy(out=carry[:, prs, :], in_=Hfp[:, :, :, NCH - 1]), ["Hf", "Hb"], ["carry"])
                for h2 in range(2):
                    for i2 in range(2):
                        prl = 2 * h2 + i2
                        rows, kw, u3 = u3s[prl]
                        for s_ in range(8):
                            mm(ps_y[0:NCH, i2 * 256:(i2 + 1) * 256].rearrange("k (a t c) -> k a t c", a=2, t=8), u3[:, s_, :],
                               Kstrip[rows, q, :, 7 - s_:15 - s_, :], s_ == 0, s_ == 7, ["uT", "Kstrip"], ["ps_y"], **kw)
                    for i2 in range(2):
                        pr = 4 * q + 2 * h2 + i2
                        for ri in range(2):
                            mm(ps_y2[0:NCH, i2 * 256:(i2 + 1) * 256], Hb[:, pr, ri, :], Vf[:, pr, ri, :], ri == 0, ri == 1, ["Hb", "Vf"], ["ps_y2"],
                               sig=(i2 == 1 and ri == 1))
                    act(lambda e: e.copy(out=y2s[0:NCH, :], in_=ps_y2[0:NCH, 0:512]), ["ps_y2"], ["y2s"])
                    dve(lambda e: e.tensor_tensor(out=ysum[0:NCH, :], in0=ps_y[0:NCH, 0:512], in1=y2s[0:NCH, :], op=ALU.add), ["ps_y", "y2s"], ["ysum"])
                    for i2 in range(2):
                        prl = 2 * h2 + i2
                        act(lambda e: e.activation(out=y2k[0:NCH, :, 32 * prl:32 * prl + 32].rearrange("k t (a c) -> k t a c", a=2),
                                                   in_=ysum[0:NCH, i2 * 256:(i2 + 1) * 256].rearrange("k (a t c) -> k t a c", a=2, t=8),
                                                   func=AF.Gelu_apprx_tanh), ["ysum", "y2k"], ["y2k"])
                for t in range(8):
                    tr(ps_t[:, t, 0:NCH], y2k[0:NCH, t, :], ident_b[0:NCH, 0:NCH], ["y2k", "ident_b"], ["ps_t"], sig=(t == 7))
                dve(lambda e: e.tensor_copy(out=uT[:, q, :].rearrange("c (k t) -> c t k", t=8), in_=ps_t[:, :, 0:NCH]), ["ps_t", "uT"], ["uT"])
            for oc in range(4):
                for kc in range(4):
                    mm(ps_y[:, 0:SBT], Wglu[:, kc, oc * 128:(oc + 1) * 128], uT[:, kc, :], kc == 0, kc == 3, ["Wglu", "uT"], ["ps_y"])
                act(lambda e: e.activation(out=sig_t, in_=ps_y[:, 0:SBT], func=AF.Sigmoid, bias=bglu_col[:, oc:oc + 1]), ["ps_y", "bglu_col"], ["sig_t"])
                dve(lambda e: e.tensor_tensor(out=sig_t, in0=sig_t, in1=gT[:, oc, :], op=ALU.mult), ["sig_t", "gT"], ["sig_t"])
                dve(lambda e: e.tensor_tensor(out=mixT[:, 4 + oc, :], in0=sig_t, in1=uT[:, oc, :], op=ALU.mult), ["sig_t", "uT"], ["mixT"])

        def block_out(l, b, src):
            slot = b % 2
            bl = b % BPS
            cols = slice(bl * 128, (bl + 1) * 128)
            xk = "xr0"
            slot = 0
            T.dma(xr[slot], src[b * 128:(b + 1) * 128, :], r=[("res", b)], w=[xk], q="sp")
            for half in range(2):
                for kc in range(12):
                    mm(ps_f[:], mixT[:, kc, cols], Wout[:, kc, half * 512:(half + 1) * 512], kc == 0, kc == 11, ["mixT", "Wout"], ["ps_f"])
                dve(lambda e: e.tensor_tensor(out=xr[slot][:, half * 512:(half + 1) * 512], in0=ps_f[:], in1=xr[slot][:, half * 512:(half + 1) * 512],
                                              op=ALU.add), ["ps_f", xk], [xk])
            T.dma(out_d[b * 128:(b + 1) * 128, :], xr[slot], r=[xk], w=[("res", b)], q="sp")

        try:
            mark('consts')
            for l in range(n_layers):
                src = x_d if l == 0 else out_d
                layer_setup(l)
                mark('setup')
                for sbi in range(n_sb):
                    block_front(l, sbi * BPS, src)
                    for bl in range(BPS):
                        if bl + 1 < BPS:
                            block_front(l, sbi * BPS + bl + 1, src)
                        block_rest(l, sbi * BPS + bl)
                        mark('main%d' % bl)
                    ssm_superblock(l, sbi)
                    mark('ssm')
                    for bl in range(BPS):
                        block_out(l, sbi * BPS + bl, src)
        except StopBuild:
            print("build stopped at", stop)
        T.drain("sp")
        print("ops", T.nops, "waits", T.nwaits)
    return nc


Q_PERM = [0, 4, 1, 5, 2, 6, 3, 7]


def prep_inputs(inputs, n_sb=S // SBT, layers=None, x_override=None):
    SEQ = n_sb * SBT
    lsl = slice(None) if layers is None else slice(layers[0], layers[1])
    w_in = np.asarray(inputs["w_in"], dtype=np.float32)[lsl]
    qcols = np.concatenate([np.arange(h * 64, (h + 1) * 64) for h in Q_PERM])
    perm = np.concatenate([qcols, np.arange(512, INW)])
    w_in_p = np.ascontiguousarray(w_in[:, :, perm])
    shared = {k: np.ascontiguousarray(np.asarray(v)[lsl]) for k, v in inputs.items() if k not in ("x", "mem", "positions", "w_in")}
    shared["w_in"] = w_in_p
    xs = np.asarray(inputs["x"]) if x_override is None else x_override
    maps = []
    for c in range(8):
        m = dict(shared)
        m["x"] = np.ascontiguousarray(xs[c, :SEQ])
        m["mem"] = np.ascontiguousarray(np.asarray(inputs["mem"])[c])
        m["positions"] = np.ascontiguousarray(np.asarray(inputs["positions"])[c, :SEQ]).astype(np.int32)
        maps.append(m)
    return maps


LAYERS_PER_LAUNCH = 4


def kernel(**inputs):
    nc = build_nc(n_layers=LAYERS_PER_LAUNCH, wd=LAYERS_PER_LAUNCH)
    x = np.asarray(inputs["x"], dtype=np.float32)
    for l0 in range(0, DEPTH, LAYERS_PER_LAUNCH):
        maps = prep_inputs(inputs, layers=(l0, l0 + LAYERS_PER_LAUNCH), x_override=x)
        res = run_bass_kernel_spmd(nc, maps, core_ids=list(range(8)))
        x = np.stack([np.asarray(r["out"]) for r in res.results], axis=0).astype(np.float32)
    return x
```

```python
import math
import numpy as np
from contextlib import ExitStack
import concourse.bass as bass
import concourse.mybir as mybir
from concourse.bass_utils import run_bass_kernel_spmd

F32 = mybir.dt.float32
BF16 = mybir.dt.bfloat16
I32 = mybir.dt.int32
ALU = mybir.AluOpType
AF = mybir.ActivationFunctionType
AX = mybir.AxisListType

D = 1024
S = 4096
NMEM = 256
INW = 3328
DEPTH = 4
C_AQ, C_AK, C_AV, C_AG, C_SU, C_SG, C_XQ, C_XG = 0, 512, 640, 768, 1280, 1792, 2304, 2816
EPS = 1e-6
SBT = 512
BPS = SBT // 128
NCH = SBT // 8
N_DMA_SEMS = 40
TWO_PI = 2.0 * math.pi
CW1 = 6.28125
CW2 = TWO_PI - 6.28125


class Trk:
    def __init__(self, nc, stack):
        self.nc = nc
        self.eng = {"pe": nc.tensor, "act": nc.scalar, "dve": nc.vector, "pool": nc.gpsimd, "sp": nc.sync}
        self.sem = {k: stack.enter_context(nc.semaphore("s_" + k)) for k in ("pe", "act", "dve", "pool")}
        self.cnt = {k: 0 for k in self.sem}
        self.dsem = [stack.enter_context(nc.semaphore("d%d" % i)) for i in range(N_DMA_SEMS)]
        self.dcnt = [0] * N_DMA_SEMS
        self.dnext = 0
        self.known = {k: {} for k in self.eng}
        self.lastw = {}
        self.reads = {}
        self.nwaits = 0
        self.nops = 0

    def _wait(self, e, tok):
        kind, key, val = tok
        if kind == "E" and key == e and val > self.cnt[e]:
            return
        kn = self.known[e]
        if kn.get((kind, key), 0) >= val:
            return
        kn[(kind, key)] = val
        sem = self.sem[key] if kind == "E" else self.dsem[key]
        self.eng[e].wait_ge(sem, val)
        self.nwaits += 1

    ALIAS = {"junk": "hb", "sq": "rt1", "xq_f": "rt2", "y2s": "rt1", "ysum": "rt2", "y2k": "qk", "sig_t": "qr",
             "zt": "xt0", "Gs": "xt0", "sct": "xt1", "Hf": "hb",
             "pX0": "pTm0", "pX1": "pTm1", "mix_c": "mix_a"}

    def _deps(self, e, r, w):
        r = [self.ALIAS.get(k, k) for k in r]
        w = [self.ALIAS.get(k, k) for k in w]
        for k in r:
            t = self.lastw.get(k)
            if t is not None:
                self._wait(e, t)
        for k in w:
            t = self.lastw.get(k)
            if t is not None:
                self._wait(e, t)
            for t in self.reads.get(k, ()):
                self._wait(e, t)

    def _commit(self, tok, r, w):
        r = [self.ALIAS.get(k, k) for k in r]
        w = [self.ALIAS.get(k, k) for k in w]
        for k in r:
            self.reads.setdefault(k, []).append(tok)
        for k in w:
            self.lastw[k] = tok
            self.reads[k] = []

    def op(self, e, fn, r=(), w=(), sig=True):
        self._deps(e, r, w)
        inst = fn(self.eng[e])
        self.nops += 1
        if sig:
            self.cnt[e] += 1
            inst.then_inc(self.sem[e], 1)
            tok = ("E", e, self.cnt[e])
        else:
            tok = ("E", e, self.cnt[e] + 1)
        self._commit(tok, r, w)

    def dma(self, out, in_, r=(), w=(), q="sp"):
        self._deps(q, r, w)
        i = self.dnext
        self.dnext = (self.dnext + 1) % N_DMA_SEMS
        if self.dcnt[i] > 0:
            self._wait(q, ("D", i, self.dcnt[i]))
        self.dcnt[i] += 16
        self.eng[q].dma_start(out=out, in_=in_).then_inc(self.dsem[i], 16)
        tok = ("D", i, self.dcnt[i])
        self.nops += 1
        self._commit(tok, r, w)

    def drain(self, e="sp"):
        for k in self.sem:
            if self.cnt[k] > 0:
                self._wait(e, ("E", k, self.cnt[k]))
        for i in range(N_DMA_SEMS):
            if self.dcnt[i] > 0:
                self._wait(e, ("D", i, self.dcnt[i]))

    def finish(self, keys, e="sp"):
        for k in keys:
            t = self.lastw.get(k)
            if t is not None:
                self._wait(e, t)


class StopBuild(Exception):
    pass


def build_nc(n_layers=DEPTH, n_sb=S // SBT, dbg=False, stop=None, wd=DEPTH):
    nc = bass.Bass("TRN2", target_bir_lowering=False)
    SEQ = n_sb * SBT
    di = lambda n, s, dt=F32: nc.dram_tensor(n, s, dt, kind="ExternalInput").ap()
    x_d = di("x", [SEQ, D]); mem_d = di("mem", [NMEM, D]); pos_d = di("positions", [SEQ], I32)
    norm_g_d = di("norm_g", [wd, D]); w_in_d = di("w_in", [wd, D, INW])
    qg_d = di("q_norm_g", [wd, 64]); kg_d = di("k_norm_g", [wd, 64]); sinks_d = di("sinks", [wd, 8])
    lre_d = di("lam_re", [wd, 32, 64]); lim_d = di("lam_im", [wd, 32, 64]); ldt_d = di("log_dt", [wd, 32])
    bre_d = di("b_re", [wd, 32, 64, 16]); bim_d = di("b_im", [wd, 32, 64, 16])
    cre_d = di("c_re", [wd, 32, 16, 64]); cim_d = di("c_im", [wd, 32, 16, 64])
    dskip_d = di("d_skip", [wd, 512]); wglu_d = di("w_glu", [wd, 512, 512]); bglu_d = di("b_glu", [wd, 512])
    mng_d = di("mem_norm_g", [wd, D]); wkv_d = di("w_mem_kv", [wd, D, D])
    xqg_d = di("xq_norm_g", [wd, 128]); xkg_d = di("xk_norm_g", [wd, 128]); wout_d = di("w_out", [wd, 1536, D])
    out_d = nc.dram_tensor("out", [SEQ, D], F32, kind="ExternalOutput").ap()
    NB = SEQ // 128

    with ExitStack() as st:
        T = Trk(nc, st)
        sbt = lambda name, shape, dt: st.enter_context(nc.sbuf_tensor(name, shape, dt))
        pst = lambda name, shape, dt: st.enter_context(nc.psum_tensor(name, shape, dt))
        dve = lambda fn, r, w: T.op("dve", fn, r=r, w=w)
        act = lambda fn, r, w: T.op("act", fn, r=r, w=w)
        pool = lambda fn, r, w: T.op("pool", fn, r=r, w=w)

        def mm(out, lhsT, rhs, start, stop, r, w, sig=None, **kw):
            T.op("pe", lambda e: e.matmul(out, lhsT=lhsT, rhs=rhs, start=start, stop=stop, **kw), r=r, w=w,
                 sig=(stop if sig is None else sig))

        def tr(out, in_, ident, r, w, sig=True):
            T.op("pe", lambda e: e.transpose(out=out, in_=in_, identity=ident), r=r, w=w, sig=sig)

        ps_t = pst("ps_t", [128, 8, 128], BF16)
        ps_z = pst("ps_z", [128, 512], F32)
        ps_s = [pst("ps_s0", [128, 512], F32), pst("ps_s1", [128, 512], F32)]
        ps_f = pst("ps_f", [128, 512], F32)
        ps_y = pst("ps_y", [128, 512], F32)
        ps_y2 = pst("ps_y2", [128, 512], F32)
        ps_zs = pst("ps_zs", [128, 512], F32)

        Wsb = sbt("Wsb", [128, 8, INW], BF16)
        Wout = sbt("Wout", [128, 12, D], BF16)
        Wglu = sbt("Wglu", [128, 4, 512], BF16)
        Kstrip = sbt("Kstrip", [128, 4, 2, 15, 16], BF16)
        Wssm = sbt("Wssm", [128, 4, 8, 2, 128], BF16)
        Vf = sbt("Vf", [128, 16, 2, 256], BF16)
        mcT = sbt("mcT", [128, 16, NCH], F32)
        msT = sbt("msT", [128, 16, NCH], F32)
        Hb = sbt("Hb", [128, 16, 2, NCH], BF16)
        carry = sbt("carry", [128, 16, 2], F32)
        magA = sbt("magA", [128, 16], F32)
        mcA = sbt("mcA", [128, 16], F32)
        msA = sbt("msA", [128, 16], F32)
        nmsA = sbt("nmsA", [128, 16], F32)
        magTab = sbt("magTab", [128, 16, NCH], F32)
        mkT = sbt("mkT", [128, 4, NMEM], BF16)
        mv_aug = sbt("mv_aug", [128, 2, 4, 129], BF16)
        ident_b = sbt("ident_b", [128, 128], BF16)
        ident_f = sbt("ident_f", [128, 128], F32)
        mask_cur = sbt("mask_cur", [128, 4, 128], BF16)
        mask_prev = sbt("mask_prev", [128, 4, 128], BF16)
        dmask = sbt("dmask", [128, 32], F32)
        cosT = sbt("cosT", [128, NB, 32], F32)
        sinT = sbt("sinT", [128, NB, 32], F32)
        g10 = sbt("g10", [128, 10, 64], F32)
        gq_bc = sbt("gq_bc", [128, 64], F32)
        gk_bc = sbt("gk_bc", [128, 64], F32)
        esink = sbt("esink", [128, 8], F32)
        gxk_bc = sbt("gxk_bc", [128, 128], F32)
        gxq_col = sbt("gxq_col", [128, 1], F32)
        bglu_col = sbt("bglu_col", [128, 4], F32)
        d_col = sbt("d_col", [128, 4], F32)
        g_col = sbt("g_col", [128, 8], F32)
        gm_col = sbt("gm_col", [128, 8], F32)
        ldT = sbt("ldT", [32, 128], F32)
        nvec = sbt("nvec", [128, 9], F32)
        kvec = sbt("kvec", [128, NCH], F32)
        dummy = sbt("dummy_t", [128, 4], F32)

        ARW = 14330
        arena = sbt("arena", [128, ARW], F32)
        aoff = {"main": 0, "setup": 0, "setupB": 0}
        akeys = {"main": [], "setup": [], "setupB": []}

        def carve(phase, name, shape, dt):
            n = int(np.prod(shape))
            words = n if dt in (F32, I32) else (n + 1) // 2
            o = aoff[phase]
            aoff[phase] = o + words
            assert aoff[phase] <= ARW, (phase, name, aoff[phase])
            v = arena[:, o:o + words]
            if dt != F32:
                v = v.bitcast(dt)
                if dt == BF16 and n % 2:
                    v = v[:, 0:n]
            if len(shape) > 1:
                names = " ".join("d%d" % i for i in range(len(shape)))
                v = v.rearrange("p (%s) -> p %s" % (names, names), **{"d%d" % i: shape[i] for i in range(1, len(shape))})
            if name not in akeys[phase]:
                akeys[phase].append(name)
            return v

        M = lambda name, shape, dt: carve("main", name, shape, dt)
        U = lambda name, shape, dt: carve("setup", name, shape, dt)
        UB = lambda name, shape, dt: carve("setupB", name, shape, dt)

        uT = M("uT", [4, SBT], BF16)
        gT = M("gT", [4, SBT], BF16)
        mixT = M("mixT", [12, SBT], BF16)
        xt = [M("xt0", [D], F32), M("xt1", [D], F32)]
        xr = [M("xr0", [D], F32)]
        hb = M("hb", [D], BF16)
        junk = hb
        hTs = [M("hT0", [8, 128], BF16), M("hT1", [8, 128], BF16)]
        st1 = M("st1", [16], F32)
        qk = M("qk", [10, 64], F32)
        rt1 = M("rt1", [10, 64], F32)
        rt2 = M("rt2", [10, 64], F32)
        sq = rt1
        qr = M("qr", [10, 64], BF16)
        qT = M("qT", [4, 128], BF16)
        kT = [M("kT0", [128], BF16), M("kT1", [128], BF16)]
        vaug = [M("vaug0", [2, 65], BF16), M("vaug1", [2, 65], BF16)]
        pTm = [M("pTm0", [512], BF16), M("pTm1", [512], BF16)]
        pX = pTm
        gate_a = M("gate_a", [512], BF16)
        gate_x = M("gate_x", [512], BF16)
        mix_a = M("mix_a", [512], BF16)
        mix_c = mix_a
        xq_f = rt2.rearrange("p h d -> p (h d)")[:, 0:512].rearrange("p (h d) -> p h d", h=4)
        xq_b = M("xq_b", [4, 128], BF16)
        xqT = M("xqT", [4, 128], BF16)
        den = M("den", [8], F32)
        zt = xt[0][:, 0:512].rearrange("p (r a k) -> p r a k", r=2, a=4)
        Gs = xt[0][:, 512:1024].rearrange("p (r a k) -> p r a k", r=2, a=4)
        sct = xt[1].rearrange("p (j a k) -> p j a k", j=4, a=4)
        Hf = hb.bitcast(F32).rearrange("p (r a k) -> p r a k", r=2, a=4)
        init4 = M("init4", [4, 4], F32)
        y2s = rt1.rearrange("p h d -> p (h d)")[:, 0:512]
        ysum = rt2.rearrange("p h d -> p (h d)")[:, 0:512]
        y2k = qk.rearrange("p h d -> p (h d)")[:, 0:512].bitcast(BF16).rearrange("p (t c) -> p t c", t=8)
        sig_t = qr.rearrange("p h d -> p (h d)")[:, 0:512]
        print("arena main words", aoff["main"])

        def load_T(dst, src, n, wkey):
            T.dma(ldT[0:n, :], src, w=["ldT"])
            tr(ps_z[:, 0:n], ldT[0:n, :], ident_f[0:n, 0:n], ["ldT", "ident_f"], ["ps_z"])
            dve(lambda e: e.tensor_copy(out=dst, in_=ps_z[:, 0:n]), ["ps_z"], [wkey])

        def sin_of(dst, src, n, shift, rk, wk, tmp=None):
            for c0 in range(0, n, 256):
                c1 = min(n, c0 + 256)
                m = c1 - c0
                ki, kf, yy = s_ki[:, 0:m], s_kf[:, 0:m], s_y[:, 0:m]
                sr = src[:, c0:c1]
                dve(lambda e: e.tensor_scalar(out=ki, in0=sr, scalar1=float(shift), scalar2=float(1.0 / TWO_PI), op0=ALU.add, op1=ALU.mult),
                    rk, ["sr_ki"])
                dve(lambda e: e.tensor_copy(out=kf, in_=ki), ["sr_ki"], ["sr_kf"])
                dve(lambda e: e.tensor_scalar(out=yy, in0=sr, scalar1=float(shift), scalar2=None, op0=ALU.add), rk, ["sr_y"])
                dve(lambda e: e.scalar_tensor_tensor(out=yy, in0=kf, scalar=float(-CW1), in1=yy, op0=ALU.mult, op1=ALU.add),
                    ["sr_kf", "sr_y"], ["sr_y"])
                dve(lambda e: e.scalar_tensor_tensor(out=yy, in0=kf, scalar=float(-CW2), in1=yy, op0=ALU.mult, op1=ALU.add),
                    ["sr_kf", "sr_y"], ["sr_y"])
                dve(lambda e: e.tensor_scalar(out=yy, in0=yy, scalar1=float(math.pi), scalar2=float(-math.pi), op0=ALU.min, op1=ALU.max),
                    ["sr_y"], ["sr_y"])
                act(lambda e: e.activation(out=dst[:, c0:c1], in_=yy, func=AF.Sin), ["sr_y"], wk)

        def rsqrt_of(dst, src, scale, rk, wk):
            dve(lambda e: e.tensor_scalar(out=dst, in0=src, scalar1=float(scale), scalar2=float(EPS), op0=ALU.mult, op1=ALU.add), rk, wk)
            act(lambda e: e.activation(out=dst, in_=dst, func=AF.Sqrt), wk, wk)
            dve(lambda e: e.reciprocal(out=dst, in_=dst), wk, wk)

        def join(keys, e="dve"):
            T.op(e, lambda en: en.memset(dummy[:, 0:1], 0.0), r=[], w=list(keys) + ["dummy"])

        s_ki = U("sr_ki", [256], I32); s_kf = U("sr_kf", [256], F32); s_y = U("sr_y", [256], F32)
        s_ang = U("s_ang", [256], F32)
        UB("sr_ki", [256], I32); UB("sr_kf", [256], F32); UB("sr_y", [256], F32); UB("s_ang", [256], F32)
        ones_f = U("ones_f", [128], F32)
        stage = [U("stage%d" % i, [1024], F32) for i in range(4)]
        s_posi = U("s_posi", [128], I32); s_posf = U("s_posf", [128], F32)
        s_posT = U("s_posT", [32], F32); s_inv = U("s_inv", [32], F32)
        s_memx = U("s_memx", [D], F32); s_memn = U("s_memn", [D], BF16); s_memT = U("s_memT", [8, NMEM], BF16)
        s_mkf = U("s_mkf", [4, 128], F32); s_mkb = U("s_mkb", [4, 128], BF16)
        s_smA = U("s_smA", [16], F32)
        s_sm = UB("s_sm", [16, 12], F32)
        s_e9 = UB("s_e9", [16, 9], F32); s_a9 = UB("s_a9", [16, 9], F32); s_mag9 = UB("s_mag9", [16, 9], F32)
        s_cos9 = UB("s_cos9", [16, 9], F32); s_sin9 = UB("s_sin9", [16, 9], F32)
        s_ar9 = UB("s_ar9", [16, 9], F32); s_ai9 = UB("s_ai9", [16, 9], F32)
        s_Br = UB("s_Br", [16, 16], F32); s_Bi = UB("s_Bi", [16, 16], F32)
        s_bbr = UB("s_bbr", [16, 16], F32); s_bbi = UB("s_bbi", [16, 16], F32); s_bt = UB("s_bt", [16, 16], F32)
        s_Cst = UB("s_Cst", [2, 128], F32)
        s_Cr = UB("s_Cr", [16, 16], F32); s_Ci = UB("s_Ci", [16, 16], F32)
        s_CAr = UB("s_CAr", [4, 9, 16], F32); s_CAi = UB("s_CAi", [4, 9, 16], F32); s_CAt = UB("s_CAt", [4, 9, 16], F32)
        s_CAzr = UB("s_CAzr", [4, 2, 8, 16], F32); s_CAzi = UB("s_CAzi", [4, 2, 8, 16], F32)
        s_Bzr = UB("s_Bzr", [4, 128], F32); s_Bzi = UB("s_Bzi", [4, 128], F32)
        s_Kt = UB("s_Kt", [2, 8, 16], F32); s_Kd = UB("s_Kd", [32], F32)
        s_Xr = UB("s_Xr", [4, 8, 16], F32); s_Xi = UB("s_Xi", [4, 8, 16], F32); s_Xt = UB("s_Xt", [4, 8, 16], F32)
        s_Xzr = UB("s_Xzr", [8, 4, 32], BF16); s_Xzi = UB("s_Xzi", [8, 4, 32], BF16)
        s_lg2 = UB("s_lg2", [2], F32)
        print("arena setup words", aoff["setup"], aoff["setupB"])
        ALLK = list(dict.fromkeys(akeys["main"] + akeys["setup"] + akeys["setupB"]))

        pool(lambda e: e.memset(ones_f, 1.0), [], ["ones_f"])
        pool(lambda e: e.memset(dummy[:], 0.0), [], ["dummy"])
        pool(lambda e: e.affine_select(out=ident_f[:], in_=ones_f, pattern=[[-1, 128]], compare_op=ALU.is_equal, fill=0.0,
                                       base=0, channel_multiplier=1), ["ones_f"], ["ident_f"])
        pool(lambda e: e.affine_select(out=ident_b[:], in_=ones_f, pattern=[[-1, 128]], compare_op=ALU.is_equal, fill=0.0,
                                       base=0, channel_multiplier=1), ["ones_f"], ["ident_b"])
        for j in range(4):
            pool(lambda e: e.affine_select(out=mask_cur[:, j, :], in_=ones_f, pattern=[[1, 128]], compare_op=ALU.is_ge, fill=0.0,
                                           base=0, channel_multiplier=-1), ["ones_f"], ["mask_cur"])
            pool(lambda e: e.affine_select(out=mask_prev[:, j, :], in_=ones_f, pattern=[[-1, 128]], compare_op=ALU.is_ge, fill=0.0,
                                           base=-1, channel_multiplier=1), ["ones_f"], ["mask_prev"])
        pool(lambda e: e.tensor_tensor(out=dmask[:], in0=ident_f[:, 0:32], in1=ident_f[:, 32:64], op=ALU.add), ["ident_f"], ["dmask"])
        pool(lambda e: e.tensor_tensor(out=dmask[:], in0=dmask[:], in1=ident_f[:, 64:96], op=ALU.add), ["ident_f", "dmask"], ["dmask"])
        pool(lambda e: e.tensor_tensor(out=dmask[:], in0=dmask[:], in1=ident_f[:, 96:128], op=ALU.add), ["ident_f", "dmask"], ["dmask"])
        pool(lambda e: e.iota(nvec[:], pattern=[[1, 9]], base=0, channel_multiplier=0, allow_small_or_imprecise_dtypes=True), [], ["nvec"])
        pool(lambda e: e.iota(kvec[:], pattern=[[1, NCH]], base=0, channel_multiplier=0, allow_small_or_imprecise_dtypes=True), [], ["kvec"])
        pool(lambda e: e.memset(mv_aug[:], 1.0), [], ["mv_aug"])
        pool(lambda e: e.memset(Kstrip[:], 0.0), [], ["Kstrip"])

        for bb in range(0, NB, 32):
            nbk = min(32, NB - bb)
            T.dma(s_posi[0:nbk, :], pos_d[bb * 128:(bb + nbk) * 128].rearrange("(b p) -> b p", p=128), w=["s_posi"])
            dve(lambda e: e.tensor_copy(out=s_posf[0:nbk, :], in_=s_posi[0:nbk, :]), ["s_posi"], ["s_posf"])
            tr(ps_z[:, 0:nbk], s_posf[0:nbk, :], ident_f[0:nbk, 0:nbk], ["s_posf", "ident_f"], ["ps_z"])
            dve(lambda e: e.tensor_copy(out=s_posT[:, 0:nbk], in_=ps_z[:, 0:nbk]), ["ps_z"], ["s_posT"])
        pool(lambda e: e.iota(s_inv, pattern=[[1, 32]], base=0, channel_multiplier=0, allow_small_or_imprecise_dtypes=True), [], ["s_inv"])
        act(lambda e: e.activation(out=s_inv, in_=s_inv, func=AF.Exp, scale=float(-math.log(10000.0) / 32.0)), ["s_inv"], ["s_inv"])
        for b0 in range(0, NB, 8):
            nb8 = min(8, NB - b0)
            ang3 = s_ang[:, 0:nb8 * 32].rearrange("p (b j) -> p b j", j=32)
            dve(lambda e: e.tensor_tensor(out=ang3, in0=s_posT[:, b0:b0 + nb8, None].to_broadcast([128, nb8, 32]),
                                          in1=s_inv[:, None, :].to_broadcast([128, nb8, 32]), op=ALU.mult), ["s_posT", "s_inv"], ["s_ang"])
            sin_of(sinT[:, b0:b0 + nb8, :].rearrange("p b j -> p (b j)"), s_ang[:, 0:nb8 * 32], nb8 * 32, 0.0, ["s_ang"], ["sinT"])
            sin_of(cosT[:, b0:b0 + nb8, :].rearrange("p b j -> p (b j)"), s_ang[:, 0:nb8 * 32], nb8 * 32, math.pi / 2, ["s_ang"], ["cosT"])

        def mark(name):
            if stop == name:
                raise StopBuild()

        def layer_setup(l):
            join(ALLK)
            load_T(g_col[:], norm_g_d[l].rearrange("(k p) -> k p", p=128), 8, "g_col")
            load_T(gm_col[:], mng_d[l].rearrange("(k p) -> k p", p=128), 8, "gm_col")
            load_T(bglu_col[:], bglu_d[l].rearrange("(k p) -> k p", p=128), 4, "bglu_col")
            load_T(d_col[:], dskip_d[l].rearrange("(k p) -> k p", p=128), 4, "d_col")
            load_T(gxq_col[:], xqg_d[l].rearrange("(k p) -> k p", p=128), 1, "gxq_col")
            dve(lambda e: e.tensor_scalar(out=gxq_col[:], in0=gxq_col[:], scalar1=float(1.0 / math.sqrt(128.0)), scalar2=None, op0=ALU.mult),
                ["gxq_col"], ["gxq_col"])
            T.dma(gq_bc[:], qg_d[l].partition_broadcast(128), w=["gq_bc"])
            T.dma(gk_bc[:], kg_d[l].partition_broadcast(128), w=["gk_bc"])
            T.dma(gxk_bc[:], xkg_d[l].partition_broadcast(128), w=["gxk_bc"])
            T.dma(esink[:], sinks_d[l].partition_broadcast(128), w=["esink"])
            act(lambda e: e.activation(out=esink[:], in_=esink[:], func=AF.Exp), ["esink"], ["esink"])
            dve(lambda e: e.tensor_copy(out=g10[:, 0:8, :], in_=gq_bc[:, None, :].to_broadcast([128, 8, 64])), ["gq_bc"], ["g10"])
            dve(lambda e: e.tensor_copy(out=g10[:, 8:10, :], in_=gk_bc[:, None, :].to_broadcast([128, 2, 64])), ["gk_bc", "g10"], ["g10"])

            mark('vecs')
            for mt in range(2):
                T.dma(s_memx, mem_d[mt * 128:(mt + 1) * 128, :], w=["s_memx"])
                act(lambda e: e.activation(out=s_memn, in_=s_memx, func=AF.Square, accum_out=s_smA[:, 0:1]), ["s_memx"], ["s_memn", "s_smA"])
                rsqrt_of(s_smA[:, 1:2], s_smA[:, 0:1], 1.0 / D, ["s_smA"], ["s_smA"])
                act(lambda e: e.activation(out=s_memn, in_=s_memx, func=AF.Copy, scale=s_smA[:, 1:2]), ["s_memx", "s_smA"], ["s_memn"])
                for kc in range(8):
                    tr(ps_t[:, kc, :], s_memn[:, kc * 128:(kc + 1) * 128], ident_b[:], ["s_memn", "ident_b"], ["ps_t"], sig=(kc == 7))
                dve(lambda e: e.tensor_copy(out=s_memT[:, :, mt * 128:(mt + 1) * 128], in_=ps_t[:]), ["ps_t"], ["s_memT"])
            Wkv = Wsb[:, :, 0:D]
            for kc in range(8):
                sg_ = stage[kc % 4]
                T.dma(sg_, wkv_d[l, kc * 128:(kc + 1) * 128, :], w=["stage%d" % (kc % 4)])
                eng = "act" if kc % 2 == 0 else "pool"
                if eng == "act":
                    act(lambda e: e.activation(out=Wkv[:, kc, :], in_=sg_, func=AF.Copy, scale=gm_col[:, kc:kc + 1]),
                        ["stage%d" % (kc % 4), "gm_col"], ["Wsb"])
                else:
                    dve(lambda e: e.tensor_scalar(out=Wkv[:, kc, :], in0=sg_, scalar1=gm_col[:, kc:kc + 1], scalar2=None, op0=ALU.mult),
                         ["stage%d" % (kc % 4), "gm_col"], ["Wsb"])
            for mt in range(2):
                for n in range(2):
                    for kc in range(8):
                        mm(ps_z[:], s_memT[:, kc, mt * 128:(mt + 1) * 128], Wkv[:, kc, n * 512:(n + 1) * 512], kc == 0, kc == 7,
                           ["s_memT", "Wsb"], ["ps_z"])
                    if n == 0:
                        act(lambda e: e.copy(out=s_mkf.rearrange("p h d -> p (h d)"), in_=ps_z[:]), ["ps_z"], ["s_mkf"])
                        dve(lambda e: e.tensor_tensor(out=s_memx[:, 0:512], in0=s_mkf.rearrange("p h d -> p (h d)"),
                                                      in1=s_mkf.rearrange("p h d -> p (h d)"), op=ALU.mult), ["s_mkf"], ["s_memx"])
                        dve(lambda e: e.tensor_reduce(out=s_smA[:, 4:8], in_=s_memx[:, 0:512].rearrange("p (h d) -> p h d", h=4),
                                                      axis=AX.X, op=ALU.add), ["s_memx"], ["s_smA"])
                        rsqrt_of(s_smA[:, 8:12], s_smA[:, 4:8], 1.0 / 128, ["s_smA"], ["s_smA"])
                        dve(lambda e: e.tensor_tensor(out=s_mkf, in0=s_mkf, in1=s_smA[:, 8:12, None].to_broadcast([128, 4, 128]), op=ALU.mult),
                            ["s_mkf", "s_smA"], ["s_mkf"])
                        dve(lambda e: e.tensor_tensor(out=s_mkb, in0=s_mkf, in1=gxk_bc[:, None, :].to_broadcast([128, 4, 128]), op=ALU.mult),
                            ["s_mkf", "gxk_bc"], ["s_mkb"])
                        for h in range(4):
                            tr(ps_t[:, h, :], s_mkb[:, h, :], ident_b[:], ["s_mkb", "ident_b"], ["ps_t"], sig=(h == 3))
                        dve(lambda e: e.tensor_scalar(out=mkT[:, :, mt * 128:(mt + 1) * 128], in0=ps_t[:, 0:4, :], scalar1=gxq_col[:, 0:1],
                                                      scalar2=None, op0=ALU.mult), ["ps_t", "gxq_col"], ["mkT"])
                    else:
                        act(lambda e: e.copy(out=mv_aug[:, mt, :, 0:128], in_=ps_z[:].rearrange("p (h d) -> p h d", h=4)), ["ps_z"], ["mv_aug"])

            mark('memkv')
            cnt = 0
            for kc in range(8):
                for (c0, c1) in ((0, 1024), (1024, 2048), (2048, 3072), (3072, INW)):
                    sg_ = stage[cnt % 4]; sk = "stage%d" % (cnt % 4)
                    T.dma(sg_[:, 0:c1 - c0], w_in_d[l, kc * 128:(kc + 1) * 128, c0:c1], w=[sk])
                    if cnt % 2 == 0:
                        act(lambda e: e.activation(out=Wsb[:, kc, c0:c1], in_=sg_[:, 0:c1 - c0], func=AF.Copy, scale=g_col[:, kc:kc + 1]),
                            [sk, "g_col"], ["Wsb"])
                    else:
                        dve(lambda e: e.tensor_scalar(out=Wsb[:, kc, c0:c1], in0=sg_[:, 0:c1 - c0], scalar1=g_col[:, kc:kc + 1], scalar2=None,
                                                       op0=ALU.mult), [sk, "g_col"], ["Wsb"])
                    cnt += 1
            for kc in range(12):
                sg_ = stage[cnt % 4]; sk = "stage%d" % (cnt % 4)
                T.dma(sg_, wout_d[l, kc * 128:(kc + 1) * 128, :], w=[sk])
                if cnt % 2 == 0:
                    act(lambda e: e.copy(out=Wout[:, kc, :], in_=sg_), [sk], ["Wout"])
                else:
                    dve(lambda e: e.tensor_copy(out=Wout[:, kc, :], in_=sg_), [sk], ["Wout"])
                cnt += 1
            for kc in range(4):
                sg_ = stage[cnt % 4]; sk = "stage%d" % (cnt % 4)
                T.dma(sg_[:, 0:512], wglu_d[l, kc * 128:(kc + 1) * 128, :], w=[sk])
                dve(lambda e: e.tensor_copy(out=Wglu[:, kc, :], in_=sg_[:, 0:512]), [sk], ["Wglu"])
                cnt += 1

            mark('weights')
            join(ALLK)
            lr = s_sm[:, :, 2]; li = s_sm[:, :, 3]; dtv = s_sm[:, :, 4]; lrdt = s_sm[:, :, 5]; lidt = s_sm[:, :, 6]
            t7 = s_sm[:, :, 7]; t8 = s_sm[:, :, 8]; fr = s_sm[:, :, 9]; fi = s_sm[:, :, 10]; t11 = s_sm[:, :, 11]
            load_T(lr, lre_d[l].rearrange("(pr two) p -> pr (two p)", two=2), 16, "s_sm")
            load_T(li, lim_d[l].rearrange("(pr two) p -> pr (two p)", two=2), 16, "s_sm")
            T.dma(s_lg2[0:16, :], ldt_d[l].rearrange("(pr two) -> pr two", two=2), w=["s_lg2"])
            dve(lambda e: e.tensor_copy(out=ldT[0:16, :].rearrange("q (two p) -> q two p", two=2),
                                        in_=s_lg2[0:16, :, None].to_broadcast([16, 2, 64])), ["s_lg2"], ["ldT"])
            tr(ps_z[:, 0:16], ldT[0:16, :], ident_f[0:16, 0:16], ["ldT", "ident_f"], ["ps_z"])
            act(lambda e: e.activation(out=dtv, in_=ps_z[:, 0:16], func=AF.Exp), ["ps_z"], ["s_sm"])
            dve(lambda e: e.tensor_tensor(out=lrdt, in0=lr, in1=dtv, op=ALU.mult), ["s_sm"], ["s_sm"])
            dve(lambda e: e.tensor_tensor(out=lidt, in0=li, in1=dtv, op=ALU.mult), ["s_sm"], ["s_sm"])
            dve(lambda e: e.tensor_tensor(out=s_e9, in0=lrdt[:, :, None].to_broadcast([128, 16, 9]), in1=nvec[:, None, :].to_broadcast([128, 16, 9]),
                                          op=ALU.mult), ["s_sm", "nvec"], ["s_e9"])
            dve(lambda e: e.tensor_tensor(out=s_a9, in0=lidt[:, :, None].to_broadcast([128, 16, 9]), in1=nvec[:, None, :].to_broadcast([128, 16, 9]),
                                          op=ALU.mult), ["s_sm", "nvec"], ["s_a9"])
            act(lambda e: e.activation(out=s_mag9, in_=s_e9, func=AF.Exp), ["s_e9"], ["s_mag9"])
            f2 = lambda v: v.rearrange("p a b -> p (a b)")
            sin_of(f2(s_sin9), f2(s_a9), 144, 0.0, ["s_a9"], ["s_sin9"])
            sin_of(f2(s_cos9), f2(s_a9), 144, math.pi / 2, ["s_a9"], ["s_cos9"])
            dve(lambda e: e.tensor_tensor(out=s_ar9, in0=s_mag9, in1=s_cos9, op=ALU.mult), ["s_mag9", "s_cos9"], ["s_ar9"])
            dve(lambda e: e.tensor_tensor(out=s_ai9, in0=s_mag9, in1=s_sin9, op=ALU.mult), ["s_mag9", "s_sin9"], ["s_ai9"])
            dve(lambda e: e.tensor_copy(out=magA[:], in_=s_mag9[:, :, 8]), ["s_mag9"], ["magA"])
            tmp16 = (s_ki[:, 0:16], s_kf[:, 0:16])
            dve(lambda e: e.tensor_scalar(out=tmp16[0], in0=lidt, scalar1=float(8.0 / TWO_PI), scalar2=None, op0=ALU.mult), ["s_sm"], ["sr_ki"])
            dve(lambda e: e.tensor_copy(out=tmp16[1], in_=tmp16[0]), ["sr_ki"], ["sr_kf"])
            dve(lambda e: e.tensor_scalar(out=t7, in0=lidt, scalar1=8.0, scalar2=None, op0=ALU.mult), ["s_sm"], ["s_sm"])
            dve(lambda e: e.scalar_tensor_tensor(out=t7, in0=tmp16[1], scalar=float(-CW1), in1=t7, op0=ALU.mult, op1=ALU.add), ["sr_kf", "s_sm"], ["s_sm"])
            dve(lambda e: e.scalar_tensor_tensor(out=t7, in0=tmp16[1], scalar=float(-CW2), in1=t7, op0=ALU.mult, op1=ALU.add), ["sr_kf", "s_sm"], ["s_sm"])
            for p0 in range(0, 16, 4):
                ta3 = s_ang[:, 0:4 * NCH].rearrange("p (a k) -> p a k", k=NCH)
                dve(lambda e: e.tensor_tensor(out=ta3, in0=t7[:, p0:p0 + 4, None].to_broadcast([128, 4, NCH]),
                                              in1=kvec[:, None, :].to_broadcast([128, 4, NCH]), op=ALU.mult), ["s_sm", "kvec"], ["s_ang"])
                sin_of(msT[:, p0:p0 + 4, :].rearrange("p a k -> p (a k)"), s_ang[:, 0:4 * NCH], 4 * NCH, 0.0, ["s_ang"], ["msT"])
                sin_of(mcT[:, p0:p0 + 4, :].rearrange("p a k -> p (a k)"), s_ang[:, 0:4 * NCH], 4 * NCH, math.pi / 2, ["s_ang"], ["mcT"])
            dve(lambda e: e.tensor_tensor(out=mcA[:], in0=mcT[:, :, 1], in1=magA[:], op=ALU.mult), ["mcT", "magA"], ["mcA"])
            dve(lambda e: e.tensor_tensor(out=msA[:], in0=msT[:, :, 1], in1=magA[:], op=ALU.mult), ["msT", "magA"], ["msA"])
            dve(lambda e: e.tensor_scalar(out=nmsA[:], in0=msA[:], scalar1=-1.0, scalar2=None, op0=ALU.mult), ["msA"], ["nmsA"])
            dve(lambda e: e.tensor_copy(out=magTab[:], in_=magA[:, :, None].to_broadcast([128, 16, NCH])), ["magA"], ["magTab"])
            dve(lambda e: e.memset(magTab[:, :, 0:1], 0.0), ["magTab"], ["magTab"])
            dve(lambda e: e.tensor_tensor(out=t8, in0=lr, in1=lr, op=ALU.mult), ["s_sm"], ["s_sm"])
            dve(lambda e: e.tensor_tensor(out=t11, in0=li, in1=li, op=ALU.mult), ["s_sm"], ["s_sm"])
            dve(lambda e: e.tensor_tensor(out=t8, in0=t8, in1=t11, op=ALU.add), ["s_sm"], ["s_sm"])
            dve(lambda e: e.reciprocal(out=t8, in_=t8), ["s_sm"], ["s_sm"])
            dve(lambda e: e.tensor_scalar(out=t11, in0=s_ar9[:, :, 1], scalar1=-1.0, scalar2=None, op0=ALU.add), ["s_ar9"], ["s_sm"])
            dve(lambda e: e.tensor_tensor(out=fr, in0=t11, in1=lr, op=ALU.mult), ["s_sm"], ["s_sm"])
            dve(lambda e: e.tensor_tensor(out=t7, in0=s_ai9[:, :, 1], in1=li, op=ALU.mult), ["s_sm", "s_ai9"], ["s_sm"])
            dve(lambda e: e.tensor_tensor(out=fr, in0=fr, in1=t7, op=ALU.add), ["s_sm"], ["s_sm"])
            dve(lambda e: e.tensor_tensor(out=fr, in0=fr, in1=t8, op=ALU.mult), ["s_sm"], ["s_sm"])
            dve(lambda e: e.tensor_tensor(out=fi, in0=s_ai9[:, :, 1], in1=lr, op=ALU.mult), ["s_sm", "s_ai9"], ["s_sm"])
            dve(lambda e: e.tensor_tensor(out=t7, in0=t11, in1=li, op=ALU.mult), ["s_sm"], ["s_sm"])
            dve(lambda e: e.tensor_tensor(out=fi, in0=fi, in1=t7, op=ALU.subtract), ["s_sm"], ["s_sm"])
            dve(lambda e: e.tensor_tensor(out=fi, in0=fi, in1=t8, op=ALU.mult), ["s_sm"], ["s_sm"])
            mark('ssm_tabs')
            T.dma(s_Br, bre_d[l].rearrange("(pr two) p c -> (two p) pr c", two=2), w=["s_Br"])
            T.dma(s_Bi, bim_d[l].rearrange("(pr two) p c -> (two p) pr c", two=2), w=["s_Bi"])
            frb = fr[:, :, None].to_broadcast([128, 16, 16]); fib = fi[:, :, None].to_broadcast([128, 16, 16])
            dve(lambda e: e.tensor_tensor(out=s_bbr, in0=s_Br, in1=frb, op=ALU.mult), ["s_Br", "s_sm"], ["s_bbr"])
            dve(lambda e: e.tensor_tensor(out=s_bt, in0=s_Bi, in1=fib, op=ALU.mult), ["s_Bi", "s_sm"], ["s_bt"])
            dve(lambda e: e.tensor_tensor(out=s_bbr, in0=s_bbr, in1=s_bt, op=ALU.subtract), ["s_bbr", "s_bt"], ["s_bbr"])
            dve(lambda e: e.tensor_tensor(out=s_bbi, in0=s_Bi, in1=frb, op=ALU.mult), ["s_Bi", "s_sm"], ["s_bbi"])
            dve(lambda e: e.tensor_tensor(out=s_bt, in0=s_Br, in1=fib, op=ALU.mult), ["s_Br", "s_sm", "s_bbr"], ["s_bt"])
            dve(lambda e: e.tensor_tensor(out=s_bbi, in0=s_bbi, in1=s_bt, op=ALU.add), ["s_bbi", "s_bt"], ["s_bbi"])
            for (cd, dst, dk) in ((cre_d, s_Cr, "s_Cr"), (cim_d, s_Ci, "s_Ci")):
                for pr in range(16):
                    T.dma(s_Cst[(pr % 8) * 16:(pr % 8) * 16 + 16, pr // 8, :].rearrange("c (two p) -> c two p", two=2),
                          cd[l, 2 * pr:2 * pr + 2].rearrange("two c p -> c two p"), w=["s_Cst"])
                for hh in range(2):
                    tr(ps_z[:, hh * 128:(hh + 1) * 128], s_Cst[:, hh, :], ident_f[:], ["s_Cst", "ident_f"], ["ps_z"], sig=(hh == 1))
                dve(lambda e: e.tensor_copy(out=dst.rearrange("p a c -> p (a c)"), in_=ps_z[:, 0:256]), ["ps_z"], [dk])
            mark('ssm_BC')
            for q in range(4):
                prs = slice(4 * q, 4 * q + 4)
                arb = s_ar9[:, prs, :, None].to_broadcast([128, 4, 9, 16]); aib = s_ai9[:, prs, :, None].to_broadcast([128, 4, 9, 16])
                crb = s_Cr[:, prs, None, :].to_broadcast([128, 4, 9, 16]); cib = s_Ci[:, prs, None, :].to_broadcast([128, 4, 9, 16])
                dve(lambda e: e.tensor_tensor(out=s_CAr, in0=crb, in1=arb, op=ALU.mult), ["s_Cr", "s_ar9"], ["s_CAr"])
                dve(lambda e: e.tensor_tensor(out=s_CAt, in0=cib, in1=aib, op=ALU.mult), ["s_Ci", "s_ai9"], ["s_CAt"])
                dve(lambda e: e.tensor_tensor(out=s_CAr, in0=s_CAr, in1=s_CAt, op=ALU.subtract), ["s_CAr", "s_CAt"], ["s_CAr"])
                dve(lambda e: e.tensor_tensor(out=s_CAi, in0=crb, in1=aib, op=ALU.mult), ["s_Cr", "s_ai9"], ["s_CAi"])
                dve(lambda e: e.tensor_tensor(out=s_CAt, in0=cib, in1=arb, op=ALU.mult), ["s_Ci", "s_ar9", "s_CAr"], ["s_CAt"])
                dve(lambda e: e.tensor_tensor(out=s_CAi, in0=s_CAi, in1=s_CAt, op=ALU.add), ["s_CAi", "s_CAt"], ["s_CAi"])
                pool(lambda e: e.memset(Vf[:, prs, :, :], 0.0), [], ["Vf"])
                for two in range(2):
                    rows = slice(64 * two, 64 * two + 64)
                    dve(lambda e: e.tensor_copy(out=Vf[rows, prs, 0, two * 128:(two + 1) * 128].rearrange("p a (t c) -> p a t c", c=16),
                                                in_=s_CAr[rows, :, 1:9, :]), ["s_CAr", "Vf"], ["Vf"])
                    dve(lambda e: e.tensor_scalar(out=Vf[rows, prs, 1, two * 128:(two + 1) * 128].rearrange("p a (t c) -> p a t c", c=16),
                                                  in0=s_CAi[rows, :, 1:9, :], scalar1=-1.0, scalar2=None, op0=ALU.mult), ["s_CAi", "Vf"], ["Vf"])
                pool(lambda e: e.memset(s_CAzr, 0.0), [], ["s_CAzr"])
                pool(lambda e: e.memset(s_CAzi, 0.0), [], ["s_CAzi"])
                pool(lambda e: e.memset(s_Bzr, 0.0), [], ["s_Bzr"])
                pool(lambda e: e.memset(s_Bzi, 0.0), [], ["s_Bzi"])
                for two in range(2):
                    rows = slice(64 * two, 64 * two + 64)
                    dve(lambda e: e.tensor_copy(out=s_CAzr[rows, :, two, :, :], in_=s_CAr[rows, :, 0:8, :]), ["s_CAr", "s_CAzr"], ["s_CAzr"])
                    dve(lambda e: e.tensor_scalar(out=s_CAzi[rows, :, two, :, :], in0=s_CAi[rows, :, 0:8, :], scalar1=-1.0, scalar2=None, op0=ALU.mult),
                        ["s_CAi", "s_CAzi"], ["s_CAzi"])
                    for j in range(4):
                        c0 = 32 * j + 16 * two
                        dve(lambda e: e.tensor_copy(out=s_Bzr[rows, j, c0:c0 + 16], in_=s_bbr[rows, 4 * q + j, :]), ["s_bbr", "s_Bzr"], ["s_Bzr"])
                        dve(lambda e: e.tensor_copy(out=s_Bzi[rows, j, c0:c0 + 16], in_=s_bbi[rows, 4 * q + j, :]), ["s_bbi", "s_Bzi"], ["s_Bzi"])
                for j in range(4):
                    mm(ps_z[:, 0:256], s_Bzr[:, j, :], s_CAzr[:, j].rearrange("p a t c -> p (a t c)"), j == 0, False, ["s_Bzr", "s_CAzr"], ["ps_z"])
                    mm(ps_z[:, 0:256], s_Bzi[:, j, :], s_CAzi[:, j].rearrange("p a t c -> p (a t c)"), False, j == 3, ["s_Bzi", "s_CAzi"], ["ps_z"])
                dve(lambda e: e.tensor_copy(out=s_Kt.rearrange("p a t c -> p (a t c)"), in_=ps_z[:, 0:256]), ["ps_z"], ["s_Kt"])
                dve(lambda e: e.tensor_scalar(out=s_Kd, in0=dmask[:], scalar1=d_col[:, q:q + 1], scalar2=None, op0=ALU.mult), ["dmask", "d_col"], ["s_Kd"])
                dve(lambda e: e.tensor_tensor(out=s_Kt[:, :, 0, :], in0=s_Kt[:, :, 0, :], in1=s_Kd.rearrange("p (a c) -> p a c", a=2), op=ALU.add),
                    ["s_Kt", "s_Kd"], ["s_Kt"])
                dve(lambda e: e.tensor_copy(out=Kstrip[:, q, :, 7:15, :], in_=s_Kt), ["s_Kt"], ["Kstrip"])
                brb = s_bbr[:, prs, None, :].to_broadcast([128, 4, 8, 16]); bib = s_bbi[:, prs, None, :].to_broadcast([128, 4, 8, 16])
                ar8 = s_ar9[:, prs, 0:8, None].to_broadcast([128, 4, 8, 16]); ai8 = s_ai9[:, prs, 0:8, None].to_broadcast([128, 4, 8, 16])
                dve(lambda e: e.tensor_tensor(out=s_Xr, in0=brb, in1=ar8, op=ALU.mult), ["s_bbr", "s_ar9"], ["s_Xr"])
                dve(lambda e: e.tensor_tensor(out=s_Xt, in0=bib, in1=ai8, op=ALU.mult), ["s_bbi", "s_ai9"], ["s_Xt"])
                dve(lambda e: e.tensor_tensor(out=s_Xr, in0=s_Xr, in1=s_Xt, op=ALU.subtract), ["s_Xr", "s_Xt"], ["s_Xr"])
                dve(lambda e: e.tensor_tensor(out=s_Xi, in0=bib, in1=ar8, op=ALU.mult), ["s_bbi", "s_ar9"], ["s_Xi"])
                dve(lambda e: e.tensor_tensor(out=s_Xt, in0=brb, in1=ai8, op=ALU.mult), ["s_bbr", "s_ai9", "s_Xr"], ["s_Xt"])
                dve(lambda e: e.tensor_tensor(out=s_Xi, in0=s_Xi, in1=s_Xt, op=ALU.add), ["s_Xi", "s_Xt"], ["s_Xi"])
                pool(lambda e: e.memset(s_Xzr, 0.0), [], ["s_Xzr"])
                pool(lambda e: e.memset(s_Xzi, 0.0), [], ["s_Xzi"])
                for two in range(2):
                    rows = slice(64 * two, 64 * two + 64)
                    dve(lambda e: e.tensor_copy(out=s_Xzr[rows, :, :, 16 * two:16 * two + 16].rearrange("p n a c -> p a n c"), in_=s_Xr[rows]), ["s_Xr", "s_Xzr"], ["s_Xzr"])
                    dve(lambda e: e.tensor_copy(out=s_Xzi[rows, :, :, 16 * two:16 * two + 16].rearrange("p n a c -> p a n c"), in_=s_Xi[rows]), ["s_Xi", "s_Xzi"], ["s_Xzi"])
                for hh in range(2):
                    for s4 in range(4):
                        s_ = 4 * hh + s4
                        for ri, xz in ((0, s_Xzr), (1, s_Xzi)):
                            tr(ps_t[:, 2 * s4 + ri, :], xz[:, 7 - s_, :, :].rearrange("p a c -> p (a c)"), ident_b[:], ["s_Xzr", "s_Xzi", "ident_b"], ["ps_t"],
                               sig=(s4 == 3 and ri == 1))
                    dve(lambda e: e.tensor_copy(out=Wssm[:, q, 4 * hh:4 * hh + 4, :, :].rearrange("p s r m -> p (s r) m"), in_=ps_t[:]),
                        ["ps_t"], ["Wssm"])
            pool(lambda e: e.memset(carry[:], 0.0), [], ["carry"])
            pool(lambda e: e.memset(Hb[:], 0.0), [], ["Hb"])
            join(ALLK)
            pool(lambda e: e.memset(vaug[0][:, :, 64:65], 1.0), [], ["vaug0"])
            pool(lambda e: e.memset(vaug[1][:, :, 64:65], 1.0), [], ["vaug1"])

        def block_front(l, b, src):
            slot = b % 2
            bl = b % BPS
            cols = slice(bl * 128, (bl + 1) * 128)
            xk, kTk, vk, pk = "xt%d" % slot, "kT%d" % slot, "vaug%d" % slot, None
            x_t = xt[slot]
            T.dma(x_t, src[b * 128:(b + 1) * 128, :], r=[("res", b)], w=[xk])
            act(lambda e: e.activation(out=junk, in_=x_t, func=AF.Square, accum_out=st1[:, 0:1]), [xk], ["junk", "st1"])
            rsqrt_of(st1[:, 1:2], st1[:, 0:1], 1.0 / D, ["st1"], ["st1"])
            act(lambda e: e.activation(out=hb, in_=x_t, func=AF.Copy, scale=st1[:, 1:2]), [xk, "st1"], ["hb"])
            for kc in range(8):
                tr(ps_t[:, kc, :], hb[:, kc * 128:(kc + 1) * 128], ident_b[:], ["hb", "ident_b"], ["ps_t"], sig=(kc == 7))
            dve(lambda e: e.tensor_copy(out=hTs[slot], in_=ps_t[:]), ["ps_t"], ["hT%d" % slot])


        def block_rest(l, b):
            slot = b % 2
            bl = b % BPS
            cols = slice(bl * 128, (bl + 1) * 128)
            kTk, vk = "kT%d" % slot, "vaug%d" % slot
            hT = hTs[slot]
            hTk = "hT%d" % slot

            def zgroup(c0, n):
                for kc in range(8):
                    mm(ps_z[:, 0:n], hT[:, kc, :], Wsb[:, kc, c0:c0 + n], kc == 0, kc == 7, [hTk, "Wsb"], ["ps_z"])

            zgroup(C_AQ, 512)
            act(lambda e: e.copy(out=qk[:, 0:8, :].rearrange("p h d -> p (h d)"), in_=ps_z[:]), ["ps_z"], ["qk"])
            zgroup(C_AK, 256)
            act(lambda e: e.copy(out=qk[:, 8:10, :].rearrange("p h d -> p (h d)"), in_=ps_z[:, 0:128]), ["ps_z", "qk"], ["qk"])
            act(lambda e: e.copy(out=vaug[slot][:, :, 0:64], in_=ps_z[:, 128:256].rearrange("p (h d) -> p h d", h=2)), ["ps_z"], [vk])
            zgroup(C_AG, 512)
            act(lambda e: e.activation(out=gate_a, in_=ps_z[:], func=AF.Silu), ["ps_z"], ["gate_a"])
            for (c0, dst, dk, fn) in ((C_SU, uT, "uT", None), (C_SG, gT, "gT", AF.Silu)):
                for ct in range(4):
                    for kc in range(8):
                        mm(ps_z[:, ct * 128:(ct + 1) * 128], Wsb[:, kc, c0 + ct * 128:c0 + (ct + 1) * 128], hT[:, kc, :], kc == 0, kc == 7,
                           [hTk, "Wsb"], ["ps_z"], sig=(ct == 3 and kc == 7))
                pz3 = ps_z[:].rearrange("p (c t) -> p c t", c=4)
                if fn is None:
                    act(lambda e: e.copy(out=dst[:, :, cols], in_=pz3), ["ps_z"], [dk])
                else:
                    act(lambda e: e.activation(out=dst[:, :, cols], in_=pz3, func=fn), ["ps_z"], [dk])

            if b == 0: mark('bm_z')
            dve(lambda e: e.tensor_tensor(out=sq, in0=qk, in1=qk, op=ALU.mult), ["qk"], ["sq"])
            dve(lambda e: e.tensor_reduce(out=st1[:, 2:12], in_=sq, axis=AX.X, op=ALU.add), ["sq"], ["st1"])
            if b == 0: mark("r1")
            rsqrt_of(st1[:, 2:12], st1[:, 2:12], 1.0 / 64, ["st1"], ["st1"])
            if b == 0: mark("r2")
            dve(lambda e: e.tensor_tensor(out=qk, in0=qk, in1=st1[:, 2:12, None].to_broadcast([128, 10, 64]), op=ALU.mult), ["qk", "st1"], ["qk"])
            dve(lambda e: e.tensor_tensor(out=qk, in0=qk, in1=g10[:], op=ALU.mult), ["qk", "g10"], ["qk"])
            if b == 0: mark("r3")
            qk4 = qk.rearrange("p h (a j) -> p h a j", a=2)
            r14 = rt1.rearrange("p h (a j) -> p h a j", a=2)
            r24 = rt2.rearrange("p h (a j) -> p h a j", a=2)
            qr4 = qr.rearrange("p h (a j) -> p h a j", a=2)
            cb = cosT[:, b, None, None, :].to_broadcast([128, 10, 2, 32])
            sb_ = sinT[:, b, None, :].to_broadcast([128, 10, 32])
            dve(lambda e: e.tensor_tensor(out=r14, in0=qk4, in1=cb, op=ALU.mult), ["qk", "cosT"], ["rt1"])
            if b == 0: mark("r4")
            dve(lambda e: e.tensor_tensor(out=r24[:, :, 0, :], in0=qk4[:, :, 1, :], in1=sb_, op=ALU.mult), ["qk", "sinT"], ["rt2"])
            dve(lambda e: e.tensor_tensor(out=r24[:, :, 1, :], in0=qk4[:, :, 0, :], in1=sb_, op=ALU.mult), ["qk", "sinT", "rt2"], ["rt2"])
            if b == 0: mark("r5")
            dve(lambda e: e.tensor_tensor(out=qr4[:, :, 0, :], in0=r14[:, :, 0, :], in1=r24[:, :, 0, :], op=ALU.subtract), ["rt1", "rt2"], ["qr"])
            dve(lambda e: e.tensor_tensor(out=qr4[:, :, 1, :], in0=r14[:, :, 1, :], in1=r24[:, :, 1, :], op=ALU.add), ["rt1", "rt2", "qr"], ["qr"])
            if b == 0: mark("r6")
            for j in range(5):
                tr(ps_t[:, j, :], qr[:, 2 * j:2 * j + 2, :].rearrange("p h d -> p (h d)"), ident_b[:], ["qr", "ident_b"], ["ps_t"], sig=(j == 4))
            if b == 0: mark("r7")
            dve(lambda e: e.tensor_copy(out=qT, in_=ps_t[:, 0:4, :]), ["ps_t"], ["qT"])
            dve(lambda e: e.tensor_copy(out=kT[slot], in_=ps_t[:, 4, :]), ["ps_t"], [kTk])
            if b == 0: mark('bm_rope')
            tiles = ([] if b == 0 else [(1 - slot, mask_prev, "mask_prev")]) + [(slot, mask_cur, "mask_cur")]
            for kvh in range(2):
                rows = slice(64 * kvh, 64 * kvh + 64)
                for ti, (sl, mk, mkk) in enumerate(tiles):
                    mm(ps_s[ti][:], kT[sl][rows, :], qT[rows, :, :].rearrange("p j q -> p (j q)"), True, True,
                       ["kT%d" % sl, "qT"], ["ps_s%d" % ti])
                    act(lambda e: e.activation(out=pTm[ti], in_=ps_s[ti][:], func=AF.Exp, scale=0.125), ["ps_s%d" % ti], ["pTm%d" % ti])
                    dve(lambda e: e.tensor_tensor(out=pTm[ti], in0=pTm[ti], in1=mk[:].rearrange("p j q -> p (j q)"), op=ALU.mult),
                         ["pTm%d" % ti, mkk], ["pTm%d" % ti])
                for j in range(4):
                    for ti, (sl, mk, mkk) in enumerate(tiles):
                        mm(ps_f[:, j * 65:(j + 1) * 65], pTm[ti][:, j * 128:(j + 1) * 128], vaug[sl][:, kvh, :], ti == 0, ti == len(tiles) - 1,
                           ["pTm%d" % ti, "vaug%d" % sl], ["ps_f"], sig=(j == 3 and ti == len(tiles) - 1))
                o4 = ps_f[:, 0:260].rearrange("p (j d) -> p j d", d=65)
                dve(lambda e: e.tensor_tensor(out=den[:, 0:4], in0=o4[:, :, 64], in1=esink[:, 4 * kvh:4 * kvh + 4], op=ALU.add), ["ps_f", "esink"], ["den"])
                dve(lambda e: e.reciprocal(out=den[:, 0:4], in_=den[:, 0:4]), ["den"], ["den"])
                for j in range(4):
                    h = 4 * kvh + j
                    dve(lambda e: e.scalar_tensor_tensor(out=mix_a[:, h * 64:(h + 1) * 64], in0=o4[:, j, 0:64], scalar=den[:, j:j + 1],
                                                         in1=gate_a[:, h * 64:(h + 1) * 64], op0=ALU.mult, op1=ALU.mult),
                        ["ps_f", "den", "gate_a"], ["mix_a"])
            for j in range(4):
                tr(ps_t[:, j, :], mix_a[:, j * 128:(j + 1) * 128], ident_b[:], ["mix_a", "ident_b"], ["ps_t"], sig=(j == 3))
            dve(lambda e: e.tensor_copy(out=mixT[:, 0:4, cols], in_=ps_t[:, 0:4, :]), ["ps_t"], ["mixT"])
            if b == 0: mark('bm_attn')
            zgroup(C_XQ, 512)
            act(lambda e: e.copy(out=xq_f.rearrange("p h d -> p (h d)"), in_=ps_z[:]), ["ps_z"], ["xq_f"])
            zgroup(C_XG, 512)
            act(lambda e: e.activation(out=gate_x, in_=ps_z[:], func=AF.Silu), ["ps_z"], ["gate_x"])
            sq4 = sq.rearrange("p h d -> p (h d)")[:, 0:512].rearrange("p (h d) -> p h d", h=4)
            dve(lambda e: e.tensor_tensor(out=sq4, in0=xq_f, in1=xq_f, op=ALU.mult), ["xq_f"], ["sq"])
            dve(lambda e: e.tensor_reduce(out=st1[:, 12:16], in_=sq4, axis=AX.X, op=ALU.add), ["sq"], ["st1"])
            rsqrt_of(st1[:, 12:16], st1[:, 12:16], 1.0 / 128, ["st1"], ["st1"])
            dve(lambda e: e.tensor_tensor(out=xq_b, in0=xq_f, in1=st1[:, 12:16, None].to_broadcast([128, 4, 128]), op=ALU.mult), ["xq_f", "st1"], ["xq_b"])
            for h in range(4):
                tr(ps_t[:, h, :], xq_b[:, h, :], ident_b[:], ["xq_b", "ident_b"], ["ps_t"], sig=(h == 3))
            dve(lambda e: e.tensor_copy(out=xqT, in_=ps_t[:, 0:4, :]), ["ps_t"], ["xqT"])
            for mt in range(2):
                for h in range(4):
                    mm(ps_s[mt][:, h * 128:(h + 1) * 128], mkT[:, h, mt * 128:(mt + 1) * 128], xqT[:, h, :], True, True, ["mkT", "xqT"],
                       ["ps_s%d" % mt], sig=(h == 3))
                act(lambda e: e.activation(out=pX[mt], in_=ps_s[mt][:], func=AF.Exp), ["ps_s%d" % mt], ["pX%d" % mt])
            for hp in range(2):
                for i in range(2):
                    h = 2 * hp + i
                    for mt in range(2):
                        mm(ps_f[:, i * 129:(i + 1) * 129], pX[mt][:, h * 128:(h + 1) * 128], mv_aug[:, mt, h, :], mt == 0, mt == 1,
                           ["pX%d" % mt, "mv_aug"], ["ps_f"], sig=(i == 1 and mt == 1))
                o2 = ps_f[:, 0:258].rearrange("p (i d) -> p i d", d=129)
                dve(lambda e: e.reciprocal(out=den[:, 4:6], in_=o2[:, :, 128]), ["ps_f"], ["den"])
                for i in range(2):
                    h = 2 * hp + i
                    dve(lambda e: e.scalar_tensor_tensor(out=mix_c[:, h * 128:(h + 1) * 128], in0=o2[:, i, 0:128], scalar=den[:, 4 + i:5 + i],
                                                         in1=gate_x[:, h * 128:(h + 1) * 128], op0=ALU.mult, op1=ALU.mult),
                        ["ps_f", "den", "gate_x"], ["mix_c"])
            for j in range(4):
                tr(ps_t[:, j, :], mix_c[:, j * 128:(j + 1) * 128], ident_b[:], ["mix_c", "ident_b"], ["ps_t"], sig=(j == 3))
            dve(lambda e: e.tensor_copy(out=mixT[:, 8:12, cols], in_=ps_t[:, 0:4, :]), ["ps_t"], ["mixT"])

        def ssm_superblock(l, sbi):
            ps_zs4 = ps_zs[:, 0:8 * NCH].rearrange("p (a r k) -> p a r k", a=4, r=2)
            for q in range(4):
                prs = slice(4 * q, 4 * q + 4)
                u3s = []
                for prl in range(4):
                    rows = slice(32 * prl, 32 * prl + 32)
                    kw = {"tile_position": (96, 0)} if prl == 3 else {}
                    u3 = uT[rows, q, :].rearrange("p (k s) -> p s k", s=8)
                    u3s.append((rows, kw, u3))
                    for ri in range(2):
                        for s_ in range(8):
                            mm(ps_zs4[:, prl, ri, :], Wssm[rows, q, s_, ri, :], u3[:, s_, :], s_ == 0, s_ == 7, ["Wssm", "uT"], ["ps_zs"],
                               sig=(ri == 1 and s_ == 7), **kw)
                zr = ps_zs4[:, :, 0, :]; zi = ps_zs4[:, :, 1, :]
                mc = mcT[:, prs, :]; ms = msT[:, prs, :]
                dve(lambda e: e.tensor_tensor(out=sct[:, 0], in0=zr, in1=mc, op=ALU.mult), ["ps_zs", "mcT"], ["sct"])
                dve(lambda e: e.tensor_tensor(out=sct[:, 1], in0=zi, in1=ms, op=ALU.mult), ["ps_zs", "msT", "sct"], ["sct"])
                dve(lambda e: e.tensor_tensor(out=sct[:, 2], in0=zi, in1=mc, op=ALU.mult), ["ps_zs", "mcT", "sct"], ["sct"])
                dve(lambda e: e.tensor_tensor(out=sct[:, 3], in0=zr, in1=ms, op=ALU.mult), ["ps_zs", "msT", "sct"], ["sct"])
                dve(lambda e: e.tensor_tensor(out=zt[:, 0], in0=sct[:, 0], in1=sct[:, 1], op=ALU.add), ["sct"], ["zt"])
                dve(lambda e: e.tensor_tensor(out=zt[:, 1], in0=sct[:, 2], in1=sct[:, 3], op=ALU.subtract), ["sct", "zt"], ["zt"])
                if sbi > 0:
                    hr_ = carry[:, prs, 0]; hi_ = carry[:, prs, 1]
                    dve(lambda e: e.tensor_tensor(out=init4[:, 0, :], in0=hr_, in1=mcA[:, prs], op=ALU.mult), ["carry", "mcA"], ["init4"])
                    dve(lambda e: e.tensor_tensor(out=init4[:, 1, :], in0=hi_, in1=nmsA[:, prs], op=ALU.mult), ["carry", "nmsA", "init4"], ["init4"])
                    dve(lambda e: e.tensor_tensor(out=init4[:, 2, :], in0=hi_, in1=mcA[:, prs], op=ALU.mult), ["carry", "mcA", "init4"], ["init4"])
                    dve(lambda e: e.tensor_tensor(out=init4[:, 3, :], in0=hr_, in1=msA[:, prs], op=ALU.mult), ["carry", "msA", "init4"], ["init4"])
                    for ri in range(2):
                        for jj in range(2):
                            dve(lambda e: e.tensor_tensor(out=zt[:, ri, :, 0], in0=zt[:, ri, :, 0], in1=init4[:, 2 * ri + jj, :], op=ALU.add),
                                ["zt", "init4"], ["zt"])
                mg = magTab[:, prs, :].rearrange("p a k -> p (a k)")
                for ri in range(2):
                    dve(lambda e: e.tensor_tensor_scan(out=Gs[:, ri].rearrange("p a k -> p (a k)"), data0=mg,
                                                      data1=zt[:, ri].rearrange("p a k -> p (a k)"), initial=0.0, op0=ALU.mult, op1=ALU.add),
                        ["zt", "magTab", "Gs"], ["Gs"])
                f2 = lambda v: v.rearrange("p a k -> p (a k)")
                dve(lambda e: e.tensor_tensor(out=f2(sct[:, 0]), in0=f2(Gs[:, 0]), in1=f2(mc), op=ALU.mult), ["Gs", "mcT"], ["sct"])
                dve(lambda e: e.tensor_tensor(out=f2(sct[:, 1]), in0=f2(Gs[:, 1]), in1=f2(ms), op=ALU.mult), ["Gs", "msT", "sct"], ["sct"])
                dve(lambda e: e.tensor_tensor(out=f2(sct[:, 2]), in0=f2(Gs[:, 1]), in1=f2(mc), op=ALU.mult), ["Gs", "mcT", "sct"], ["sct"])
                dve(lambda e: e.tensor_tensor(out=f2(sct[:, 3]), in0=f2(Gs[:, 0]), in1=f2(ms), op=ALU.mult), ["Gs", "msT", "sct"], ["sct"])
                dve(lambda e: e.tensor_tensor(out=f2(Hf[:, 0]), in0=f2(sct[:, 0]), in1=f2(sct[:, 1]), op=ALU.subtract), ["sct"], ["Hf"])
                dve(lambda e: e.tensor_tensor(out=f2(Hf[:, 1]), in0=f2(sct[:, 2]), in1=f2(sct[:, 3]), op=ALU.add), ["sct", "Hf"], ["Hf"])
                Hfp = Hf.rearrange("p r a k -> p a r k")
                act(lambda e: e.copy(out=Hb[:, prs, :, 0:1], in_=carry[:, prs, :, None]), ["carry"], ["Hb"])
                act(lambda e: e.copy(out=Hb[:, prs, :, 1:NCH], in_=Hfp[:, :, :, 0:NCH - 1]), ["Hf", "Hb"], ["Hb"])
                act(lambda e: e.copy(out=carry[:, prs, :], in_=Hfp[:, :, :, NCH - 1]), ["Hf", "Hb"], ["carry"])
                for h2 in range(2):
                    for i2 in range(2):
                        prl = 2 * h2 + i2
                        rows, kw, u3 = u3s[prl]
                        for s_ in range(8):
                            mm(ps_y[0:NCH, i2 * 256:(i2 + 1) * 256].rearrange("k (a t c) -> k a t c", a=2, t=8), u3[:, s_, :],
                               Kstrip[rows, q, :, 7 - s_:15 - s_, :], s_ == 0, s_ == 7, ["uT", "Kstrip"], ["ps_y"], **kw)
                    for i2 in range(2):
                        pr = 4 * q + 2 * h2 + i2
                        for ri in range(2):
                            mm(ps_y2[0:NCH, i2 * 256:(i2 + 1) * 256], Hb[:, pr, ri, :], Vf[:, pr, ri, :], ri == 0, ri == 1, ["Hb", "Vf"], ["ps_y2"],
                               sig=(i2 == 1 and ri == 1))
                    act(lambda e: e.copy(out=y2s[0:NCH, :], in_=ps_y2[0:NCH, 0:512]), ["ps_y2"], ["y2s"])
                    dve(lambda e: e.tensor_tensor(out=ysum[0:NCH, :], in0=ps_y[0:NCH, 0:512], in1=y2s[0:NCH, :], op=ALU.add), ["ps_y", "y2s"], ["ysum"])
                    for i2 in range(2):
                        prl = 2 * h2 + i2
                        act(lambda e: e.activation(out=y2k[0:NCH, :, 32 * prl:32 * prl + 32].rearrange("k t (a c) -> k t a c", a=2),
                                                   in_=ysum[0:NCH, i2 * 256:(i2 + 1) * 256].rearrange("k (a t c) -> k t a c", a=2, t=8),
                                                   func=AF.Gelu_apprx_tanh), ["ysum", "y2k"], ["y2k"])
                for t in range(8):
                    tr(ps_t[:, t, 0:NCH], y2k[0:NCH, t, :], ident_b[0:NCH, 0:NCH], ["y2k", "ident_b"], ["ps_t"], sig=(t == 7))
                dve(lambda e: e.tensor_copy(out=uT[:, q, :].rearrange("c (k t) -> c t k", t=8), in_=ps_t[:, :, 0:NCH]), ["ps_t", "uT"], ["uT"])
            for oc in range(4):
                for kc in range(4):
                    mm(ps_y[:, 0:SBT], Wglu[:, kc, oc * 128:(oc + 1) * 128], uT[:, kc, :], kc == 0, kc == 3, ["Wglu", "uT"], ["ps_y"])
                act(lambda e: e.activation(out=sig_t, in_=ps_y[:, 0:SBT], func=AF.Sigmoid, bias=bglu_col[:, oc:oc + 1]), ["ps_y", "bglu_col"], ["sig_t"])
                dve(lambda e: e.tensor_tensor(out=sig_t, in0=sig_t, in1=gT[:, oc, :], op=ALU.mult), ["sig_t", "gT"], ["sig_t"])
                dve(lambda e: e.tensor_tensor(out=mixT[:, 4 + oc, :], in0=sig_t, in1=uT[:, oc, :], op=ALU.mult), ["sig_t", "uT"], ["mixT"])

        xr_buf = [xr[0], xt[1]]
        xr_key = ["xr0", "xt1"]

        def out_load(l, b, src):
            sl = b % 2
            T.dma(xr_buf[sl], src[b * 128:(b + 1) * 128, :], r=[("res", b)], w=[xr_key[sl]], q="sp")

        def block_out(l, b, src, nxt):
            sl = b % 2
            bl = b % BPS
            cols = slice(bl * 128, (bl + 1) * 128)
            xk = xr_key[sl]
            if nxt is not None:
                out_load(l, nxt, src)
            for half in range(2):
                for kc in range(12):
                    mm(ps_f[:], mixT[:, kc, cols], Wout[:, kc, half * 512:(half + 1) * 512], kc == 0, kc == 11, ["mixT", "Wout"], ["ps_f"])
                dve(lambda e: e.tensor_tensor(out=xr_buf[sl][:, half * 512:(half + 1) * 512], in0=ps_f[:], in1=xr_buf[sl][:, half * 512:(half + 1) * 512],
                                              op=ALU.add), ["ps_f", xk], [xk])
            T.dma(out_d[b * 128:(b + 1) * 128, :], xr_buf[sl], r=[xk], w=[("res", b)], q="sp")

        try:
            mark('consts')
            for l in range(n_layers):
                src = x_d if l == 0 else out_d
                layer_setup(l)
                mark('setup')
                for sbi in range(n_sb):
                    block_front(l, sbi * BPS, src)
                    for bl in range(BPS):
                        if bl + 1 < BPS:
                            block_front(l, sbi * BPS + bl + 1, src)
                        block_rest(l, sbi * BPS + bl)
                        mark('main%d' % bl)
                    ssm_superblock(l, sbi)
                    mark('ssm')
                    out_load(l, sbi * BPS, src)
                    for bl in range(BPS):
                        block_out(l, sbi * BPS + bl, src, (sbi * BPS + bl + 1) if bl + 1 < BPS else None)
        except StopBuild:
            print("build stopped at", stop)
        T.drain("sp")
        print("ops", T.nops, "waits", T.nwaits)
    return nc


Q_PERM = [0, 4, 1, 5, 2, 6, 3, 7]


def prep_inputs(inputs, n_sb=S // SBT, layers=None, x_override=None):
    SEQ = n_sb * SBT
    lsl = slice(None) if layers is None else slice(layers[0], layers[1])
    w_in = np.asarray(inputs["w_in"], dtype=np.float32)[lsl]
    qcols = np.concatenate([np.arange(h * 64, (h + 1) * 64) for h in Q_PERM])
    perm = np.concatenate([qcols, np.arange(512, INW)])
    w_in_p = np.ascontiguousarray(w_in[:, :, perm])
    shared = {k: np.ascontiguousarray(np.asarray(v)[lsl]) for k, v in inputs.items() if k not in ("x", "mem", "positions", "w_in")}
    shared["w_in"] = w_in_p
    xs = np.asarray(inputs["x"]) if x_override is None else x_override
    maps = []
    for c in range(8):
        m = dict(shared)
        m["x"] = np.ascontiguousarray(xs[c, :SEQ])
        m["mem"] = np.ascontiguousarray(np.asarray(inputs["mem"])[c])
        m["positions"] = np.ascontiguousarray(np.asarray(inputs["positions"])[c, :SEQ]).astype(np.int32)
        maps.append(m)
    return maps


LAYERS_PER_LAUNCH = 4


def kernel(**inputs):
    nc = build_nc(n_layers=LAYERS_PER_LAUNCH, wd=LAYERS_PER_LAUNCH)
    x = np.asarray(inputs["x"], dtype=np.float32)
    for l0 in range(0, DEPTH, LAYERS_PER_LAUNCH):
        maps = prep_inputs(inputs, layers=(l0, l0 + LAYERS_PER_LAUNCH), x_override=x)
        res = run_bass_kernel_spmd(nc, maps, core_ids=list(range(8)))
        x = np.stack([np.asarray(r["out"]) for r in res.results], axis=0).astype(np.float32)
    return x
```
